# Optimizing a Trainium2 kernel written in Bass

```python
import math
import jax
import jax.numpy as jnp
from jax import lax
import numpy as np

D_MODEL = 1024
BATCH = 2
SEQ = 8192
DEPTH = 2

N_A_LAYERS = DEPTH // 2
N_B_LAYERS = DEPTH - N_A_LAYERS
EPS = 1e-6
NEG_INF = -1e30

GDN_QK_HEADS = 8
GDN_V_HEADS = 16
GDN_HEAD_DIM = 128
GDN_QK_DIM = GDN_QK_HEADS * GDN_HEAD_DIM
GDN_V_DIM = GDN_V_HEADS * GDN_HEAD_DIM
GDN_CONV_DIM = 2 * GDN_QK_DIM + GDN_V_DIM
GDN_IN_DIM = GDN_CONV_DIM + GDN_V_DIM + 2 * GDN_V_HEADS
GDN_CONV_WIDTH = 4
GDN_CHUNK = 64

SWA_Q_HEADS = 16
SWA_KV_HEADS = 4
SWA_GROUP = SWA_Q_HEADS // SWA_KV_HEADS
SWA_HEAD_DIM = 64
SWA_WINDOW = 128
SWA_BLOCK = 128

REL_BUCKETS = 32
REL_MAX_DISTANCE = 128

D_FF = 2816
FFN_CONV_WIDTH = 3

kernel_name = 'hybrid_gdn_swa_yoco'


def rmsnorm(x, w):
    xf = x.astype(jnp.float32)
    y = xf * lax.rsqrt(jnp.mean(xf * xf, axis=-1, keepdims=True) + EPS)
    return (y * w.astype(jnp.float32)).astype(x.dtype)


def l2norm(x):
    return x * lax.rsqrt(jnp.sum(x * x, axis=-1, keepdims=True) + EPS)


def causal_dwconv(x, w, b=None):
    width = w.shape[0]
    t = x.shape[1]
    xp = jnp.pad(x, ((0, 0), (width - 1, 0), (0, 0)))
    y = sum(xp[:, j:j + t] * w[j] for j in range(width))
    return y if b is None else y + b


def chunked_gated_delta_rule(q, k, v, g, beta):
    bsz, t, h, dk = q.shape
    dv = v.shape[-1]
    c = GDN_CHUNK
    n = t // c
    f32 = jnp.float32
    q = q.astype(f32).reshape(bsz, n, c, h, dk).transpose(0, 1, 3, 2, 4)
    k = k.astype(f32).reshape(bsz, n, c, h, dk).transpose(0, 1, 3, 2, 4)
    v = v.astype(f32).reshape(bsz, n, c, h, dv).transpose(0, 1, 3, 2, 4)
    g = g.astype(f32).reshape(bsz, n, c, h).transpose(0, 1, 3, 2)
    beta = beta.astype(f32).reshape(bsz, n, c, h).transpose(0, 1, 3, 2)

    gc = jnp.cumsum(g, axis=-1)
    tril = jnp.tril(jnp.ones((c, c), dtype=bool))
    strict = jnp.tril(jnp.ones((c, c), dtype=bool), -1)
    diff = gc[..., :, None] - gc[..., None, :]
    decay = jnp.where(tril, jnp.exp(jnp.where(tril, diff, 0.0)), 0.0)

    k_beta = k * beta[..., None]
    v_beta = v * beta[..., None]
    m = jnp.where(strict, jnp.einsum('bnhcd,bnhsd->bnhcs', k_beta, k) * decay, 0.0)
    eye = jnp.broadcast_to(jnp.eye(c, dtype=f32), m.shape)
    t_mat = lax.linalg.triangular_solve(m + eye, eye, left_side=True, lower=True,
                                        unit_diagonal=True)
    u = jnp.einsum('bnhcs,bnhse->bnhce', t_mat, v_beta)
    w = jnp.einsum('bnhcs,bnhsd->bnhcd', t_mat, k_beta * jnp.exp(gc)[..., None])
    a_intra = jnp.einsum('bnhcd,bnhsd->bnhcs', q, k) * decay
    q_dec = q * jnp.exp(gc)[..., None]
    k_dec = k * jnp.exp(gc[..., -1:] - gc)[..., None]
    g_last = jnp.exp(gc[..., -1])

    def step(state, inp):
        w_i, u_i, qd_i, kd_i, a_i, gl_i = inp
        v_new = u_i - jnp.einsum('bhcd,bhde->bhce', w_i, state)
        o_i = (jnp.einsum('bhcd,bhde->bhce', qd_i, state)
               + jnp.einsum('bhcs,bhse->bhce', a_i, v_new))
        state = state * gl_i[..., None, None] + jnp.einsum('bhcd,bhce->bhde', kd_i, v_new)
        return state, o_i

    xs = tuple(jnp.moveaxis(z, 1, 0) for z in (w, u, q_dec, k_dec, a_intra, g_last))
    s0 = jnp.zeros((bsz, h, dk, dv), f32)
    _, o = lax.scan(step, s0, xs)
    return o.transpose(1, 0, 3, 2, 4).reshape(bsz, t, h, dv)


def gated_deltanet(hn, w_in, conv_w, a_log, dt_bias, out_norm_w, w_out):
    bsz, t, _ = hn.shape
    proj = hn @ w_in
    qkv, z, b, a = jnp.split(
        proj, [GDN_CONV_DIM, GDN_CONV_DIM + GDN_V_DIM, GDN_CONV_DIM + GDN_V_DIM + GDN_V_HEADS],
        axis=-1)
    qkv = jax.nn.silu(causal_dwconv(qkv, conv_w))
    q, k, v = jnp.split(qkv, [GDN_QK_DIM, 2 * GDN_QK_DIM], axis=-1)
    q = l2norm(q.reshape(bsz, t, GDN_QK_HEADS, GDN_HEAD_DIM)) * (GDN_HEAD_DIM ** -0.5)
    k = l2norm(k.reshape(bsz, t, GDN_QK_HEADS, GDN_HEAD_DIM))
    v = v.reshape(bsz, t, GDN_V_HEADS, GDN_HEAD_DIM)
    rep = GDN_V_HEADS // GDN_QK_HEADS
    q = jnp.repeat(q, rep, axis=2)
    k = jnp.repeat(k, rep, axis=2)
    beta = jax.nn.sigmoid(b.astype(jnp.float32))
    g = -jnp.exp(a_log.astype(jnp.float32)) * jax.nn.softplus(
        a.astype(jnp.float32) + dt_bias.astype(jnp.float32))
    o = chunked_gated_delta_rule(q, k, v, g, beta)
    z = z.reshape(bsz, t, GDN_V_HEADS, GDN_HEAD_DIM).astype(jnp.float32)
    o = rmsnorm(o, out_norm_w) * jax.nn.silu(z)
    return o.reshape(bsz, t, GDN_V_DIM).astype(hn.dtype) @ w_out


def shared_kv(h, kv_norm_w, w_kv):
    bsz, t, _ = h.shape
    kv = rmsnorm(h, kv_norm_w) @ w_kv
    k, v = jnp.split(kv, 2, axis=-1)
    shape = (bsz, t, SWA_KV_HEADS, SWA_HEAD_DIM)
    return k.reshape(shape), v.reshape(shape)


def t5_bucket(dist):
    n = jnp.maximum(dist, 0)
    max_exact = REL_BUCKETS // 2
    nf = jnp.maximum(n, 1).astype(jnp.float32)
    large = max_exact + (jnp.log(nf / max_exact) / math.log(REL_MAX_DISTANCE / max_exact)
                         * (REL_BUCKETS - max_exact)).astype(jnp.int32)
    large = jnp.minimum(large, REL_BUCKETS - 1)
    return jnp.where(n < max_exact, n, large)


def band_bias_and_mask(rel_table, nb):
    qi = jnp.arange(SWA_BLOCK)[:, None]
    ki = jnp.arange(2 * SWA_BLOCK)[None, :]
    dist = qi + SWA_BLOCK - ki
    bias = rel_table.astype(jnp.float32)[t5_bucket(dist)]
    bias = jnp.transpose(bias, (2, 0, 1)).reshape(
        SWA_KV_HEADS, SWA_GROUP, SWA_BLOCK, 2 * SWA_BLOCK)
    in_window = (dist >= 0) & (dist < SWA_WINDOW)
    key_pos = jnp.arange(nb)[:, None, None] * SWA_BLOCK + ki[None] - SWA_BLOCK
    mask = in_window[None] & (key_pos >= 0)
    return bias, mask


def sliding_window_sink_attention(hn, k, v, w_q, sinks, w_o, bias, mask):
    bsz, t, _ = hn.shape
    nb = t // SWA_BLOCK
    q = (hn @ w_q).reshape(bsz, nb, SWA_BLOCK, SWA_KV_HEADS, SWA_GROUP, SWA_HEAD_DIM)
    q = q * (SWA_HEAD_DIM ** -0.5)

    def band(z):
        prev = jnp.concatenate([jnp.zeros_like(z[:, :SWA_BLOCK]), z[:, :-SWA_BLOCK]], axis=1)
        shape = (bsz, nb, SWA_BLOCK, SWA_KV_HEADS, SWA_HEAD_DIM)
        return jnp.concatenate([prev.reshape(shape), z.reshape(shape)], axis=2)

    kb, vb = band(k), band(v)
    s = jnp.einsum('bnqhgd,bnkhd->bnhgqk', q, kb).astype(jnp.float32) + bias
    s = jnp.where(mask[None, :, None, None], s, NEG_INF)
    sink = sinks.astype(jnp.float32).reshape(SWA_KV_HEADS, SWA_GROUP)[:, :, None]
    m = jnp.maximum(jnp.max(s, axis=-1), sink)
    p = jnp.exp(s - m[..., None])
    denom = jnp.sum(p, axis=-1) + jnp.exp(sink - m)
    probs = (p / denom[..., None]).astype(vb.dtype)
    o = jnp.einsum('bnhgqk,bnkhd->bnqhgd', probs, vb)
    return o.reshape(bsz, t, SWA_Q_HEADS * SWA_HEAD_DIM) @ w_o


def conv_gated_mlp(hn, w_up, conv_w, conv_b, w_down):
    u = causal_dwconv(hn @ w_up, conv_w, conv_b)
    gate, val = jnp.split(u, 2, axis=-1)
    return (jax.nn.silu(gate) * val) @ w_down


def setup_inputs(seed: int = 0) -> dict:
    key = jax.random.key(seed)
    ks = jax.random.split(key, 24)
    f32 = jnp.float32

    def nrm(k, shape, scale):
        return jax.random.normal(k, shape, f32) * scale

    def gain(k, shape):
        return 1.0 + 0.02 * jax.random.normal(k, shape, f32)

    dt = jnp.exp(jax.random.uniform(ks[5], (N_A_LAYERS, GDN_V_HEADS), f32,
                                    minval=math.log(1e-3), maxval=math.log(1e-1)))
    return {
        'x': nrm(ks[0], (BATCH, SEQ, D_MODEL), 1.0),
        'a_norm_w': gain(ks[1], (N_A_LAYERS, D_MODEL)),
        'a_w_in': nrm(ks[2], (N_A_LAYERS, D_MODEL, GDN_IN_DIM), D_MODEL ** -0.5),
        'a_conv_w': nrm(ks[3], (N_A_LAYERS, GDN_CONV_WIDTH, GDN_CONV_DIM), GDN_CONV_WIDTH ** -0.5),
        'a_a_log': jnp.log(jax.random.uniform(ks[4], (N_A_LAYERS, GDN_V_HEADS), f32,
                                              minval=1.0, maxval=16.0)),
        'a_dt_bias': dt + jnp.log(-jnp.expm1(-dt)),
        'a_out_norm_w': gain(ks[6], (N_A_LAYERS, GDN_HEAD_DIM)),
        'a_w_out': nrm(ks[7], (N_A_LAYERS, GDN_V_DIM, D_MODEL), GDN_V_DIM ** -0.5),
        'kv_norm_w': gain(ks[8], (D_MODEL,)),
        'w_kv': nrm(ks[9], (D_MODEL, 2 * SWA_KV_HEADS * SWA_HEAD_DIM), D_MODEL ** -0.5),
        'b_norm_w': gain(ks[10], (N_B_LAYERS, D_MODEL)),
        'b_w_q': nrm(ks[11], (N_B_LAYERS, D_MODEL, SWA_Q_HEADS * SWA_HEAD_DIM), D_MODEL ** -0.5),
        'b_sinks': nrm(ks[12], (N_B_LAYERS, SWA_Q_HEADS), 0.5),
        'b_w_o': nrm(ks[13], (N_B_LAYERS, SWA_Q_HEADS * SWA_HEAD_DIM, D_MODEL),
                     (SWA_Q_HEADS * SWA_HEAD_DIM) ** -0.5),
        'rel_bias_table': nrm(ks[14], (REL_BUCKETS, SWA_Q_HEADS), 0.5),
        'ffn_norm_w': gain(ks[15], (DEPTH, D_MODEL)),
        'ffn_w_up': nrm(ks[16], (DEPTH, D_MODEL, 2 * D_FF), D_MODEL ** -0.5),
        'ffn_conv_w': nrm(ks[17], (DEPTH, FFN_CONV_WIDTH, 2 * D_FF), FFN_CONV_WIDTH ** -0.5),
        'ffn_conv_b': nrm(ks[18], (DEPTH, 2 * D_FF), 0.02),
        'ffn_w_down': nrm(ks[19], (DEPTH, D_FF, D_MODEL), D_FF ** -0.5),
        'final_norm_w': gain(ks[20], (D_MODEL,)),
    }


def reference(x, a_norm_w, a_w_in, a_conv_w, a_a_log, a_dt_bias, a_out_norm_w, a_w_out,
              kv_norm_w, w_kv, b_norm_w, b_w_q, b_sinks, b_w_o, rel_bias_table,
              ffn_norm_w, ffn_w_up, ffn_conv_w, ffn_conv_b, ffn_w_down, final_norm_w):
    nb = x.shape[1] // SWA_BLOCK
    bias, mask = band_bias_and_mask(rel_bias_table, nb)
    h = x
    k_sh = None
    v_sh = None
    for layer in range(DEPTH):
        if layer < N_A_LAYERS:
            i = layer
            h = h + gated_deltanet(rmsnorm(h, a_norm_w[i]), a_w_in[i], a_conv_w[i], a_a_log[i],
                                   a_dt_bias[i], a_out_norm_w[i], a_w_out[i])
        else:
            j = layer - N_A_LAYERS
            if j == 0:
                k_sh, v_sh = shared_kv(h, kv_norm_w, w_kv)
            h = h + sliding_window_sink_attention(rmsnorm(h, b_norm_w[j]), k_sh, v_sh, b_w_q[j],
                                                  b_sinks[j], b_w_o[j], bias, mask)
        h = h + conv_gated_mlp(rmsnorm(h, ffn_norm_w[layer]), ffn_w_up[layer], ffn_conv_w[layer],
                               ffn_conv_b[layer], ffn_w_down[layer])
    return rmsnorm(h, final_norm_w)
```

```python
import contextlib
import numpy as np
import concourse.bass as bass
import concourse.mybir as mybir
from concourse.bass_utils import run_bass_kernel_spmd

F32 = mybir.dt.float32
BF16 = mybir.dt.bfloat16
AF = mybir.ActivationFunctionType
ALU = mybir.AluOpType
AX = mybir.AxisListType

D = 1024
NFULL = 18
NHALO = 2
NOWN = 16
NEG = -1.0e30
EPOCH = 30000


class Buf:
    __slots__ = ("name", "w", "r", "excl")

    def __init__(self, name="", excl=False):
        self.name = name
        self.w = None
        self.r = []
        self.excl = excl


class Sched:
    ENGS = ("pe", "act", "dve", "pool", "sp")

    def __init__(self, nc, n_dma_sems=10):
        self.nc = nc
        self.prog = {e: [] for e in self.ENGS}
        self.cnt = {e: 0 for e in self.ENGS}
        self.sems = {}
        self.seen = {e: {} for e in self.ENGS}
        self.dma_sems = {}
        self.n_dma_sems = n_dma_sems
        self.dma_rr = {e: 0 for e in self.ENGS}
        self._semctx = []
        self.last_tok = {}

    def _new_sem(self, name):
        ctx = self.nc.semaphore(name)
        h = ctx.__enter__()
        self._semctx.append(ctx)
        return h

    def _eng_sem(self, eng, idx):
        key = (eng, idx // EPOCH)
        if key not in self.sems:
            self.sems[key] = self._new_sem(f"s_{eng}_{idx // EPOCH}")
        return self.sems[key], (idx % EPOCH) + 1

    def _wait(self, eng, tok):
        teng, sem, val = tok
        if teng == eng and eng == "pe":
            return
        seen = self.seen[eng]
        if seen.get(sem.name, 0) >= val:
            return
        seen[sem.name] = val
        self.prog[eng].append(lambda e, sem=sem, val=val: e.wait_ge(sem, val))

    def _deps(self, eng, reads, writes):
        for b in reads:
            if b.w is not None:
                self._wait(eng, b.w)
            if b.excl:
                for t in b.r:
                    if t[0] != eng:
                        self._wait(eng, t)
        for b in writes:
            if b.w is not None and b.w[0] != eng:
                self._wait(eng, b.w)
            for t in b.r:
                if t[0] != eng:
                    self._wait(eng, t)

    def _commit(self, tok, reads, writes):
        for b in reads:
            b.r.append(tok)
        for b in writes:
            b.w = tok
            b.r = []

    def op(self, eng, fn, R=(), W=()):
        self._deps(eng, R, W)
        idx = self.cnt[eng]
        self.cnt[eng] += 1
        sem, val = self._eng_sem(eng, idx)
        self.prog[eng].append(lambda e, fn=fn, sem=sem: fn(e).then_inc(sem, 1))
        tok = (eng, sem, val)
        self.last_tok[eng] = tok
        self._commit(tok, R, W)
        return tok

    def dma(self, eng, out, in_, R=(), W=()):
        k = self.dma_rr[eng]
        self.dma_rr[eng] = (k + 1) % self.n_dma_sems
        key = (eng, k)
        if key not in self.dma_sems:
            self.dma_sems[key] = [self._new_sem(f"d_{eng}_{k}"), 0]
        ent = self.dma_sems[key]
        sem, tot = ent
        if tot > 0:
            self._wait(eng, ("dma", sem, tot))
        self._deps(eng, R, W)
        ent[1] = tot + 16
        self.prog[eng].append(
            lambda e, out=out, in_=in_, sem=sem: e.dma_start(out=out, in_=in_).then_inc(sem, 16))
        tok = ("dma", sem, tot + 16)
        self._commit(tok, R, W)
        return tok

    def barrier(self):
        toks = list(self.last_tok.values())
        for (eng, k), (sem, tot) in self.dma_sems.items():
            if tot > 0:
                toks.append(("dma", sem, tot))
        for e in self.ENGS:
            for t in toks:
                if t[0] != e or e != "pe":
                    self._wait(e, t)

    def finish(self, final_toks):
        for t in final_toks:
            self._wait("sp", t)
        nc = self.nc
        with nc.Block() as block:
            @block.tensor
            def _(e):
                for f in self.prog["pe"]:
                    f(e)

            @block.scalar
            def _(e):
                for f in self.prog["act"]:
                    f(e)

            @block.vector
            def _(e):
                for f in self.prog["dve"]:
                    f(e)

            @block.gpsimd
            def _(e):
                for f in self.prog["pool"]:
                    f(e)

            @block.sync
            def _(e):
                for f in self.prog["sp"]:
                    f(e)
        for ctx in reversed(self._semctx):
            ctx.__exit__(None, None, None)


class T:
    def __init__(self, t, name, nslots=0):
        self.t = t
        self.b = Buf(name)
        self.bs = [Buf(f"{name}{i}") for i in range(nslots)]

    def __getitem__(self, k):
        return self.t[k]


def build_program(nhist, stage=99, cut=99):
    nc = bass.Bass("TRN2", target_bir_lowering=False)
    S = Sched(nc)
    NT_ALL = nhist + NFULL

    def din(name, shape, dt=F32):
        return nc.dram_tensor(name, list(shape), dt, kind="ExternalInput").ap()

    x_ext = din("x_ext", [NT_ALL * 128, D])
    valid_d = din("valid", [128, NFULL])
    kbias_d = din("kbias", [1, (NFULL + 1) * 128])
    w_in_d = din("a_w_in", [D, 6176])
    w_out_d = din("a_w_out", [2048, D])
    w_up_d = [din(f"ffn_w_up{l}", [D, 5632]) for l in range(2)]
    w_dn_d = [din(f"ffn_w_down{l}", [2816, D]) for l in range(2)]
    w_kv_d = din("w_kv_dup", [D, 768])
    w_q_d = din("b_w_q", [D, D])
    w_o_d = din("b_w_o", [D, D])
    a_nw_d = din("a_norm_wT", [128, 8])
    f_nw_d = [din(f"ffn_norm_wT{l}", [128, 8]) for l in range(2)]
    kv_nw_d = din("kv_norm_wT", [128, 8])
    b_nw_d = din("b_norm_wT", [128, 8])
    on_w_d = din("out_norm_wT", [128, 1])
    fin_w_d = din("final_norm_wb", [128, D])
    a_cw_d = din("a_conv_wT", [128, 32, 4])
    f_cw_d = [din(f"ffn_conv_wT{l}", [128, 44, 3]) for l in range(2)]
    f_cb_d = [din(f"ffn_conv_bT{l}", [128, 44]) for l in range(2)]
    alog_d = din("a_log_b", [128, 16])
    dtb_d = din("dt_bias_b", [128, 16])
    sink_d = din("sinks_b", [128, 16])
    band_d = din("biasband", [128, 16, 256])
    amask_d = din("attnmask", [128, 256])
    ident_d = din("ident", [128, 128])
    triu_d = din("triu", [128, 128])
    maskL_d = din("maskL", [128, 128])
    maskU_d = din("maskU", [128, 128])
    out_d = nc.dram_tensor("out", [NOWN * 128, D], F32, kind="ExternalOutput").ap()

    def dscr(name, shape):
        return nc.dram_tensor(name, list(shape), BF16).ap()

    Win_s = dscr("Win_s", [128, 8, 6176])
    Wout_s = dscr("Wout_s", [128, 16, D])
    Wup_s = [dscr(f"Wup_s{l}", [128, 8, 5632]) for l in range(2)]
    Wdn_s = [dscr(f"Wdn_s{l}", [128, 22, D]) for l in range(2)]
    Wkv_s = dscr("Wkv_s", [128, 8, 768])
    Wq_s = dscr("Wq_s", [128, 8, D])
    Wo_s = dscr("Wo_s", [128, 8, D])
    scrB = {id(a): Buf("scr") for a in [Win_s, Wout_s, Wkv_s, Wq_s, Wo_s] + Wup_s + Wdn_s}

    uid = [0]

    def sb(stk, name, shape, dt=F32, nslots=0):
        uid[0] += 1
        t = stk.enter_context(nc.sbuf_tensor(f"{name}_{uid[0]}", list(shape), dt))
        return T(t, name, nslots)

    root = contextlib.ExitStack()
    PB = [T(root.enter_context(nc.psum_tensor(f"pb{i}", [128, 512], F32)), f"pb{i}") for i in range(8)]
    for p_ in PB:
        p_.b.excl = True
    pbi = [0]

    reserved = set()

    def pb():
        while True:
            k = pbi[0] % 8
            pbi[0] += 1
            if k not in reserved:
                return PB[k]

    identf = sb(root, "identf", [128, 128])
    identb = sb(root, "identb", [128, 128], BF16)
    onesb = sb(root, "onesb", [128, 128], BF16)
    triub = sb(root, "triub", [128, 128], BF16)
    maskLt = sb(root, "maskLt", [128, 128])
    maskUt = sb(root, "maskUt", [128, 128])
    cst = sb(root, "cst", [128, 8])
    validt = sb(root, "validt", [128, NFULL])
    tmpc = sb(root, "tmpc", [128, 128])

    S.dma("sp", identf[:], ident_d[:, :], W=[identf.b])
    S.op("dve", lambda e: e.tensor_copy(identb[:], identf[:]), R=[identf.b], W=[identb.b])
    S.op("pool", lambda e: e.memset(onesb[:], 1.0), W=[onesb.b])
    S.dma("sp", tmpc[:], triu_d[:, :], W=[tmpc.b])
    S.op("dve", lambda e: e.tensor_copy(triub[:], tmpc[:]), R=[tmpc.b], W=[triub.b])
    S.dma("sp", maskLt[:], maskL_d[:, :], W=[maskLt.b])
    S.dma("sp", maskUt[:], maskU_d[:, :], W=[maskUt.b])
    S.op("pool", lambda e: e.memset(cst[:, 0:1], 1e-6), W=[cst.b])
    S.op("pool", lambda e: e.memset(cst[:, 1:2], 1.0), W=[cst.b])
    S.op("pool", lambda e: e.memset(cst[:, 2:3], float(np.log(128.0 ** -0.5))), W=[cst.b])
    S.op("pool", lambda e: e.memset(cst[:, 3:4], 0.0), W=[cst.b])
    S.dma("sp", validt[:], valid_d[:, :], W=[validt.b])
    EPS = cst[:, 0:1]
    ONE = cst[:, 1:2]

    with contextlib.ExitStack() as ph:
        stg = [sb(ph, f"stg{i}", [128, 2048]) for i in range(2)]
        cvt = [sb(ph, f"cvt{i}", [128, 2048], BF16) for i in range(2)]
        nws = sb(ph, "nws", [128, 8 * 5 + 1])
        nw_ap = {}
        for i, (nm, d_) in enumerate([("a", a_nw_d), ("f0", f_nw_d[0]), ("f1", f_nw_d[1]), ("kv", kv_nw_d), ("b", b_nw_d)]):
            S.dma("sp", nws[:, i * 8:(i + 1) * 8], d_[:, :], W=[nws.b])
            nw_ap[nm] = (i * 8)
        S.dma("sp", nws[:, 40:41], on_w_d[:, :], W=[nws.b])
        blk = [0]

        def convert(src3, dst3, C, N, scale=None):
            ng = min(N, 512)
            cg = max(1, min(C, 2048 // ng))
            for c0 in range(0, C, cg):
                cc = min(cg, C - c0)
                for n0 in range(0, N, ng):
                    nn = min(ng, N - n0)
                    i = blk[0] % 2
                    blk[0] += 1
                    st, cv = stg[i], cvt[i]
                    sv = st[:, 0:cc * nn].rearrange("p (c n) -> p c n", c=cc)
                    cvv = cv[:, 0:cc * nn].rearrange("p (c n) -> p c n", c=cc)
                    S.dma("sp", sv, src3[:, c0:c0 + cc, n0:n0 + nn], W=[st.b])
                    eng = "dve" if (blk[0] % 2 == 0) else "pool"
                    if scale is None:
                        S.op(eng, lambda e, cvv=cvv, sv=sv: e.tensor_copy(cvv, sv), R=[st.b], W=[cv.b])
                    elif scale[0] == "pc":
                        g = nws[:, scale[1] + c0:scale[1] + c0 + cc].unsqueeze(2).to_broadcast([128, cc, nn])
                        S.op(eng, lambda e, cvv=cvv, sv=sv, g=g: e.tensor_tensor(cvv, sv, g, ALU.mult),
                             R=[st.b, nws.b], W=[cv.b])
                    else:
                        g = nws[:, scale[1]:scale[1] + 1].unsqueeze(2).to_broadcast([128, cc, nn])
                        S.op(eng, lambda e, cvv=cvv, sv=sv, g=g: e.tensor_tensor(cvv, sv, g, ALU.mult),
                             R=[st.b, nws.b], W=[cv.b])
                    S.dma("pool", dst3[:, c0:c0 + cc, n0:n0 + nn], cvv, R=[cv.b], W=[scrB[id(dst3)]])

        def pcn(ap):
            return ap.rearrange("(c p) n -> p c n", p=128)

        convert(pcn(w_in_d), Win_s, 8, 6176, ("pc", nw_ap["a"]))
        convert(pcn(w_out_d), Wout_s, 16, D, ("p", 40))
        if stage >= 2:
            for l in range(2):
                convert(pcn(w_up_d[l]), Wup_s[l], 8, 5632, ("pc", nw_ap[f"f{l}"]))
                convert(pcn(w_dn_d[l]), Wdn_s[l], 22, D, None)
        if stage >= 3:
            convert(pcn(w_kv_d), Wkv_s, 8, 768, ("pc", nw_ap["kv"]))
            convert(pcn(w_q_d), Wq_s, 8, D, ("pc", nw_ap["b"]))
            convert(pcn(w_o_d), Wo_s, 8, D, None)
        S.barrier()

    gdn = root
    S32 = sb(gdn, "S32", [128, 16, 128], F32, nslots=4)
    Sbf = sb(gdn, "Sbf", [128, 16, 128], BF16, nslots=4)
    carry = sb(gdn, "carry", [128, 32, 3])
    convw = sb(gdn, "convw", [128, 32, 4])
    wba = sb(gdn, "wba", [128, 8, 32], BF16)
    negA = sb(gdn, "negA", [128, 16])
    dtb = sb(gdn, "dtb", [128, 16])
    S.op("pool", lambda e: e.memset(S32[:], 0.0), W=[S32.b] + S32.bs)
    S.op("pool", lambda e: e.memset(Sbf[:], 0.0), W=[Sbf.b] + Sbf.bs)
    S.op("pool", lambda e: e.memset(carry[:], 0.0), W=[carry.b])
    S.dma("sp", convw[:], a_cw_d[:, :, :], W=[convw.b])
    S.dma("sp", wba[:], Win_s[:, :, 6144:6176], R=[scrB[id(Win_s)]], W=[wba.b])
    S.dma("sp", negA[:], alog_d[:, :], W=[negA.b])
    S.dma("sp", dtb[:], dtb_d[:, :], W=[dtb.b])
    S.op("act", lambda e: e.activation(negA[:], negA[:], AF.Exp), R=[negA.b], W=[negA.b])
    S.op("dve", lambda e: e.tensor_scalar(negA[:], negA[:], -1.0, None, ALU.mult), R=[negA.b], W=[negA.b])

    def rmsnorm_T(ph_bufs, src_ap, src_bufs, xnT, col0):
        junk, ms, xn = ph_bufs
        S.op("act", lambda e: e.activation(junk[:], src_ap, AF.Square, scale=1.0 / 32.0, accum_out=ms[:, 0:1]),
             R=src_bufs, W=[junk.b, ms.b])
        S.op("act", lambda e: e.activation(ms[:, 1:2], ms[:, 0:1], AF.Ln, bias=EPS), R=[ms.b, cst.b], W=[ms.b])
        S.op("act", lambda e: e.activation(ms[:, 2:3], ms[:, 1:2], AF.Exp, scale=-0.5), R=[ms.b], W=[ms.b])
        S.op("dve", lambda e: e.tensor_scalar(xn[:], src_ap, ms[:, 2:3], None, ALU.mult), R=src_bufs + [ms.b], W=[xn.b])
        p = pb()
        pv = p[:].bitcast(BF16)
        for c in range(8):
            S.op("pe", lambda e, c=c: e.transpose(pv[:, c * 128:(c + 1) * 128], xn[:, c * 128:(c + 1) * 128], identb[:]),
                 R=[xn.b, identb.b], W=[p.b])
        S.op("act", lambda e: e.copy(xnT[:, :, col0:col0 + 128], pv.rearrange("p (c t) -> p c t", c=8)),
             R=[p.b], W=[xnT.b])

    def wload(B, k, src3, nparts):
        wb = B["wblk"][k % len(B["wblk"])]
        n = src3.shape[2]
        wv = wb[:, 0:nparts * n].rearrange("p (c n) -> p c n", c=nparts)
        return wb, wv

    def gdn_st_norm(B, tiles, full, hbuf, sp):
        xnT = B["xnT"][sp % 2]
        for i, t in enumerate(tiles):
            if full:
                ft = t - nhist
                src, sbufs = hbuf[:, ft, :], [hbuf.bs[ft]]
            else:
                xs = B["xs"][t % 2]
                src, sbufs = xs[:], [xs.b]
            S.dma("sp", src, x_ext[t * 128:(t + 1) * 128, :], W=sbufs)
            rmsnorm_T((B["junk"], B["ms"], B["xn"][i % 2]), src, sbufs, xnT, i * 128)

    def gdn_st_rest(B, tiles, full, hbuf, sp, has_next=False):
        NT = len(tiles)
        N = NT * 128
        WB = B["WB"]
        CPB = WB // 128
        xnT, qkvT = B["xnT"][sp % 2], B["qkvT"]
        gdn_st_small(B, xnT, NT, sp)
        f0 = 0 if full else 8
        wb = None
        for f in range(f0, 32):
            if f % CPB == 0 or wb is None:
                blk = f // CPB
                if blk in B["pref"]:
                    wb, wv = B["pref"].pop(blk)
                else:
                    wb, wv = wload(B, blk, Win_s[:, :, blk * WB:(blk + 1) * WB], 8)
                    S.dma("sp", wv, Win_s[:, :, blk * WB:(blk + 1) * WB], R=[scrB[id(Win_s)]], W=[wb.b])
            p = pb()
            for c in range(8):
                S.op("pe", lambda e, c=c, wv=wv, f=f, p=p: e.matmul(p[:, 0:N], wv[:, c, (f % CPB) * 128:(f % CPB + 1) * 128],
                                                                    xnT[:, c, 0:N], start=(c == 0), stop=(c == 7)),
                     R=[wb.b, xnT.b], W=[p.b])
            u = B["u"][f % 2]
            acc = B["acc"][f % 2]
            S.op("pool", lambda e, u=u, f=f: e.tensor_copy(u[:, 0:3], carry[:, f, :]), R=[carry.b], W=[u.b])
            S.op("act", lambda e, u=u, p=p: e.copy(u[:, 3:3 + N], p[:, 0:N]), R=[p.b], W=[u.b])
            S.op("pool", lambda e, u=u, f=f: e.tensor_copy(carry[:, f, :], u[:, N:N + 3]), R=[u.b], W=[carry.b])
            S.op("dve", lambda e, u=u, acc=acc, f=f: e.tensor_scalar(acc[:, 0:N], u[:, 3:3 + N], convw[:, f, 3:4], None, ALU.mult),
                 R=[u.b, convw.b], W=[acc.b])
            for j in (2, 1, 0):
                S.op("dve", lambda e, u=u, acc=acc, f=f, j=j: e.scalar_tensor_tensor(
                    acc[:, 0:N], u[:, j:j + N], convw[:, f, j:j + 1], acc[:, 0:N], ALU.mult, ALU.add),
                    R=[u.b, convw.b, acc.b], W=[acc.b])
            S.op("act", lambda e, acc=acc, f=f: e.activation(qkvT[:, f, 0:N], acc[:, 0:N], AF.Silu),
                 R=[acc.b], W=[qkvT.bs[f]])
        for f in range(f0, 16):
            sq = B["sq"][f % 2]
            S.op("pool", lambda e, sq=sq, f=f: e.tensor_tensor(sq[:, 0:N], qkvT[:, f, 0:N], qkvT[:, f, 0:N], ALU.mult),
                 R=[qkvT.bs[f]], W=[sq.b])
            p = pb()
            S.op("pe", lambda e, p=p, sq=sq: e.matmul(p[:, 0:N], onesb[:], sq[:, 0:N], start=True, stop=True),
                 R=[onesb.b, sq.b], W=[p.b])
            rn = B["acc"][f % 2]
            S.op("act", lambda e, p=p, rn=rn: e.activation(rn[:, 0:N], p[:, 0:N], AF.Ln, bias=EPS), R=[p.b, cst.b], W=[rn.b])
            bias = cst[:, 2:3] if f < 8 else cst[:, 3:4]
            S.op("act", lambda e, rn=rn, bias=bias: e.activation(rn[:, 0:N], rn[:, 0:N], AF.Exp, scale=-0.5, bias=bias),
                 R=[rn.b, cst.b], W=[rn.b])
            S.op("dve", lambda e, rn=rn, f=f: e.tensor_tensor(qkvT[:, f, 0:N], qkvT[:, f, 0:N], rn[:, 0:N], ALU.mult),
                 R=[rn.b, qkvT.bs[f]], W=[qkvT.bs[f]])
        if cut < 2:
            return
        if has_next and not full:
            for blk in range(f0 // CPB, f0 // CPB + len(B["wblk"])):
                wb, wv = wload(B, blk, Win_s[:, :, blk * WB:(blk + 1) * WB], 8)
                S.dma("sp", wv, Win_s[:, :, blk * WB:(blk + 1) * WB], R=[scrB[id(Win_s)]], W=[wb.b])
                B["pref"][blk] = (wb, wv)
        for i, t in enumerate(tiles):
            gdn_tile(B, xnT, i, t, full, hbuf, sp)

    def gdn_st_small(B, xnT, NT, sp):
        sm = B["sm"][sp % 2]
        smb = B["smb"][sp % 2]
        W_ = NT * 16
        SM = lambda k: sm[:, k, 0:W_].rearrange("p (i h) -> p i h", h=16)
        SB = lambda k: smb[:, k, 0:W_].rearrange("p (i h) -> p i h", h=16)
        r = lambda k: sm.bs[k]
        rb = lambda k: smb.bs[k]
        bc = lambda t_: t_[:].unsqueeze(1).to_broadcast([128, NT, 16])
        pba = pb()
        for i in range(NT):
            for c in range(8):
                S.op("pe", lambda e, c=c, i=i: e.matmul(pba[:, i * 32:(i + 1) * 32], xnT[:, c, i * 128:(i + 1) * 128], wba[:, c, :],
                                                        start=(c == 0), stop=(c == 7)), R=[xnT.b, wba.b], W=[pba.b])
        pbv = pba[:, 0:NT * 32].rearrange("p (i c) -> p i c", c=32)
        S.op("dve", lambda e: e.tensor_tensor(SM(0), pbv[:, :, 16:32], bc(dtb), ALU.add), R=[pba.b, dtb.b], W=[r(0)])
        S.op("act", lambda e: e.activation(SM(4), pbv[:, :, 0:16], AF.Exp, scale=-1.0), R=[pba.b], W=[r(4)])
        S.op("act", lambda e: e.activation(SM(1), SM(0), AF.Abs), R=[r(0)], W=[r(1)])
        S.op("act", lambda e: e.activation(SM(1), SM(1), AF.Exp, scale=-1.0), R=[r(1)], W=[r(1)])
        S.op("act", lambda e: e.activation(SM(1), SM(1), AF.Ln, bias=ONE), R=[r(1), cst.b], W=[r(1)])
        S.op("dve", lambda e: e.tensor_scalar(SM(4), SM(4), 1.0, None, ALU.add), R=[r(4)], W=[r(4)])
        S.op("dve", lambda e: e.reciprocal(SM(5), SM(4)), R=[r(4)], W=[r(5)])
        S.op("dve", lambda e: e.scalar_tensor_tensor(SM(2), SM(0), 0.0, SM(1), ALU.max, ALU.add), R=[r(0), r(1)], W=[r(2)])
        S.op("dve", lambda e: e.tensor_tensor(SM(3), SM(2), bc(negA), ALU.mult), R=[r(2), negA.b], W=[r(3)])
        S.op("dve", lambda e: e.tensor_copy(SB(0), SM(3)), R=[r(3)], W=[rb(0)])
        S.op("dve", lambda e: e.tensor_copy(SM(6), SB(0)), R=[rb(0)], W=[r(6)])
        S.op("dve", lambda e: e.tensor_tensor(SB(1), SM(3), SM(6), ALU.subtract), R=[r(3), r(6)], W=[rb(1)])
        pc = pb()
        for i in range(NT):
            for k in range(2):
                S.op("pe", lambda e, i=i, k=k: e.matmul(pc[:, i * 32:i * 32 + 16], triub[:], smb[:, k, i * 16:(i + 1) * 16], start=(k == 0), stop=(k == 1)),
                     R=[triub.b, rb(k)], W=[pc.b])
            for k in range(2):
                S.op("pe", lambda e, i=i, k=k: e.matmul(pc[:, i * 32 + 16:i * 32 + 32], onesb[:], smb[:, k, i * 16:(i + 1) * 16], start=(k == 0), stop=(k == 1)),
                     R=[onesb.b, rb(k)], W=[pc.b])
        pcv = pc[:, 0:NT * 32].rearrange("p (i c) -> p i c", c=32)
        S.op("dve", lambda e: e.tensor_copy(SM(7), pcv[:, :, 0:16]), R=[pc.b], W=[r(7)])
        S.op("dve", lambda e: e.tensor_copy(SM(8), pcv[:, :, 16:32]), R=[pc.b], W=[r(8)])
        S.op("act", lambda e: e.activation(SM(9), SM(8), AF.Exp), R=[r(8)], W=[r(9)])
        S.op("dve", lambda e: e.tensor_tensor(SM(10), SM(8), SM(7), ALU.subtract), R=[r(7), r(8)], W=[r(10)])
        S.op("act", lambda e: e.activation(SM(15), SM(7), AF.Exp), R=[r(7)], W=[r(15)])
        S.op("act", lambda e: e.activation(SM(10), SM(10), AF.Exp), R=[r(10)], W=[r(10)])
        S.op("dve", lambda e: e.tensor_scalar(SM(12), SM(7), -1.0, None, ALU.mult), R=[r(7)], W=[r(12)])
        S.op("dve", lambda e: e.tensor_copy(SB(2), SM(12)), R=[r(12)], W=[rb(2)])
        S.op("dve", lambda e: e.tensor_copy(SM(13), SB(2)), R=[rb(2)], W=[r(13)])
        S.op("dve", lambda e: e.tensor_tensor(SM(14), SM(12), SM(13), ALU.subtract), R=[r(12), r(13)], W=[r(14)])
        S.op("dve", lambda e: e.tensor_tensor(SM(11), SM(5), SM(15), ALU.mult), R=[r(5), r(15)], W=[r(11)])

    def gdn_tile(B, xnT, i, t, full, hbuf, sp):
        qkvT = B["qkvT"]
        WB = B["WB"]
        c0 = i * 128
        sm = B["sm"][sp % 2]
        o16 = i * 16
        SM = lambda k: sm[:, k, o16:o16 + 16]
        if full:
            zs = B["zs"]
            nzb = 2048 // WB
            for blk in range(nzb):
                wb, wv = wload(B, blk, Win_s[:, :, 4096 + blk * WB:4096 + (blk + 1) * WB], 8)
                S.dma("sp", wv, Win_s[:, :, 4096 + blk * WB:4096 + (blk + 1) * WB], R=[scrB[id(Win_s)]], W=[wb.b])
                p = pb()
                for c in range(8):
                    S.op("pe", lambda e, c=c, wv=wv, p=p: e.matmul(p[:, 0:WB], xnT[:, c, c0:c0 + 128], wv[:, c, :],
                                                                   start=(c == 0), stop=(c == 7)),
                         R=[wb.b, xnT.b], W=[p.b])
                S.op("act", lambda e, p=p, blk=blk: e.activation(zs[:, blk * WB:(blk + 1) * WB], p[:, 0:WB], AF.Silu),
                     R=[p.b], W=[zs.b])
        ktm, kbg, kd, vb = B["ktm"], B["kbg"], B["kd"], B["vb"]
        p = pb()
        pv = p[:].bitcast(BF16)
        for kh in range(8):
            S.op("pe", lambda e, kh=kh, pv=pv: e.transpose(pv[:, kh * 128:(kh + 1) * 128], qkvT[:, 8 + kh, c0:c0 + 128], identb[:]),
                 R=[qkvT.bs[8 + kh], identb.b], W=[p.b])
        S.op("act", lambda e, pv=pv: e.copy(ktm[:], pv.rearrange("p (k d) -> p k d", k=8)), R=[p.b], W=[ktm.b])
        if cut < 3.2:
            return
        k4 = ktm[:].unsqueeze(2).to_broadcast([128, 8, 2, 128])
        S.op("pool", lambda e: e.tensor_tensor(kbg[:].rearrange("p (k r) d -> p k r d", r=2), k4,
                                               SM(11).rearrange("p (k r) -> p k r", r=2).unsqueeze(3).to_broadcast([128, 8, 2, 128]),
                                               ALU.mult), R=[ktm.b, sm.bs[11]], W=[kbg.b])
        S.op("pool", lambda e: e.tensor_tensor(kd[:].rearrange("p (k r) d -> p k r d", r=2), k4,
                                               SM(10).rearrange("p (k r) -> p k r", r=2).unsqueeze(3).to_broadcast([128, 8, 2, 128]),
                                               ALU.mult), R=[ktm.b, sm.bs[10]], W=[kd.b])
        if cut < 3.4:
            return
        for half in range(2):
            p = pb()
            pv = p[:].bitcast(BF16)
            for j in range(8):
                h_ = half * 8 + j
                S.op("pe", lambda e, j=j, h_=h_, pv=pv: e.transpose(pv[:, j * 128:(j + 1) * 128], qkvT[:, 16 + h_, c0:c0 + 128], identb[:]),
                     R=[qkvT.bs[16 + h_], identb.b], W=[p.b])
            S.op("dve", lambda e, half=half, pv=pv: e.tensor_tensor(
                vb[:, half * 8:(half + 1) * 8, :], pv.rearrange("p (k d) -> p k d", k=8),
                sm[:, 5, o16 + half * 8:o16 + (half + 1) * 8].unsqueeze(2).to_broadcast([128, 8, 128]), ALU.mult),
                R=[p.b, sm.bs[5]], W=[vb.b])
        if cut < 3.6:
            return
        Asb = B["Asb"]
        pA = [pb(), pb()]
        for kh in range(8):
            S.op("pe", lambda e, kh=kh: e.matmul(pA[kh // 4][:, (kh % 4) * 128:(kh % 4 + 1) * 128], qkvT[:, 8 + kh, c0:c0 + 128],
                                                 qkvT[:, 8 + kh, c0:c0 + 128], start=True, stop=True),
                 R=[qkvT.bs[8 + kh]], W=[pA[kh // 4].b])
        for j in range(2):
            S.op("act", lambda e, j=j: e.copy(Asb[:, j * 4:(j + 1) * 4, :], pA[j][:].rearrange("p (k s) -> p k s", k=4)),
                 R=[pA[j].b], W=[Asb.b])
        if cut < 3.8:
            return
        if full:
            KQsb = B["KQsb"]
            pK = [pb(), pb()]
            for kh in range(8):
                S.op("pe", lambda e, kh=kh: e.matmul(pK[kh // 4][:, (kh % 4) * 128:(kh % 4 + 1) * 128], qkvT[:, 8 + kh, c0:c0 + 128],
                                                     qkvT[:, kh, c0:c0 + 128], start=True, stop=True),
                     R=[qkvT.bs[8 + kh], qkvT.bs[kh]], W=[pK[kh // 4].b])
            for j in range(2):
                S.op("dve", lambda e, j=j: e.tensor_copy(KQsb[:, j * 4:(j + 1) * 4, :], pK[j][:].rearrange("p (k s) -> p k s", k=4)),
                     R=[pK[j].b], W=[KQsb.b])
        if cut < 4:
            return
        G = B["G"]
        b4 = lambda ap2: ap2.unsqueeze(1).to_broadcast([128, 4, 128])
        col4 = lambda k, h0: sm[:, k, o16 + h0:o16 + h0 + 4].unsqueeze(2).to_broadcast([128, 4, 128])
        flat = lambda t_: t_[:].rearrange("p j s -> p (j s)")
        kr = lambda ap3: ap3.rearrange("p (k r) s -> p k r s", r=2)
        NG = len(G)
        for gp in range(4 // NG):
            grp = list(range(NG * gp, NG * gp + NG))
            GB = {g: G[g % NG] for g in grp}
            H0 = {g: g * 4 for g in grp}
            for g in grp:
                gb, h0 = GB[g], H0[g]
                S.op("dve", lambda e, gb=gb, h0=h0: e.tensor_tensor(gb["dgh"][:], b4(identf[:]), col4(13, h0), ALU.mult),
                     R=[identf.b, sm.bs[13]], W=[gb["dgh"].b])
                S.op("pool", lambda e, gb=gb, h0=h0: e.tensor_tensor(gb["dgl"][:], b4(identf[:]), col4(14, h0), ALU.mult),
                     R=[identf.b, sm.bs[14]], W=[gb["dgl"].b])
            PR = {}
            for g in grp:
                gb = GB[g]
                pR = pb()
                PR[g] = pR
                S.op("pe", lambda e, pR=pR, gb=gb: e.matmul(pR[:], onesb[:], flat(gb["dgh"]), start=True, stop=False),
                     R=[onesb.b, gb["dgh"].b], W=[pR.b])
                S.op("pe", lambda e, pR=pR, gb=gb: e.matmul(pR[:], onesb[:], flat(gb["dgl"]), start=False, stop=True),
                     R=[onesb.b, gb["dgl"].b], W=[pR.b])
            for g in grp:
                gb, h0, pR = GB[g], H0[g], PR[g]
                S.op("dve", lambda e, pR=pR, gb=gb, h0=h0: e.tensor_tensor(gb["Z"][:], pR[:].rearrange("p (j s) -> p j s", j=4), col4(7, h0), ALU.add),
                     R=[pR.b, sm.bs[7]], W=[gb["Z"].b])
            if full:
                for g in grp:
                    gb, pR = GB[g], PR[g]
                    S.op("act", lambda e, pR=pR, gb=gb: e.activation(flat(gb["ER"]), pR[:], AF.Exp, scale=-1.0), R=[pR.b], W=[gb["ER"].b])
                for g in grp:
                    gb = GB[g]
                    S.op("dve", lambda e, gb=gb: e.scalar_tensor_tensor(gb["DT"][:], gb["Z"][:], -1.0, b4(maskUt[:]), ALU.mult, ALU.add),
                         R=[gb["Z"].b, maskUt.b], W=[gb["DT"].b])
            for g in grp:
                gb = GB[g]
                S.op("pool", lambda e, gb=gb: e.tensor_tensor(gb["Z"][:], gb["Z"][:], b4(maskLt[:]), ALU.add), R=[gb["Z"].b, maskLt.b], W=[gb["Z"].b])
            if full:
                for g in grp:
                    gb = GB[g]
                    S.op("act", lambda e, gb=gb: e.activation(gb["DT"][:], gb["DT"][:], AF.Exp), R=[gb["DT"].b], W=[gb["DT"].b])
            for g in grp:
                gb = GB[g]
                S.op("act", lambda e, gb=gb: e.activation(gb["Z"][:], gb["Z"][:], AF.Exp), R=[gb["Z"].b], W=[gb["Z"].b])
            for g in grp:
                gb, h0 = GB[g], H0[g]
                S.op("pool", lambda e, gb=gb, h0=h0: e.tensor_tensor(gb["Z"][:], gb["Z"][:], col4(5, h0), ALU.mult), R=[gb["Z"].b, sm.bs[5]], W=[gb["Z"].b])
            if full:
                for g in grp:
                    gb, kh0 = GB[g], H0[g] // 2
                    S.op("pool", lambda e, gb=gb, kh0=kh0: e.tensor_tensor(
                        kr(gb["aT"][:]), kr(gb["DT"][:]), KQsb[:, kh0:kh0 + 2, :].unsqueeze(2).to_broadcast([128, 2, 2, 128]), ALU.mult),
                        R=[gb["DT"].b, KQsb.b], W=[gb["aT"].b])
                    S.op("pool", lambda e, gb=gb, kh0=kh0: e.tensor_tensor(
                        kr(gb["qdT"][:]), kr(gb["ER"][:]), qkvT[:, kh0:kh0 + 2, c0:c0 + 128].unsqueeze(2).to_broadcast([128, 2, 2, 128]), ALU.mult),
                        R=[gb["ER"].b, qkvT.bs[kh0], qkvT.bs[kh0 + 1]], W=[gb["qdT"].b])
            for g in grp:
                gb, kh0 = GB[g], H0[g] // 2
                S.op("dve", lambda e, gb=gb, kh0=kh0: e.tensor_tensor(
                    kr(gb["Z"][:]), kr(gb["Z"][:]), Asb[:, kh0:kh0 + 2, :].unsqueeze(2).to_broadcast([128, 2, 2, 128]), ALU.mult),
                    R=[gb["Z"].b, Asb.b], W=[gb["Z"].b])
            for g in grp:
                gb = GB[g]
                S.op("act", lambda e, gb=gb: e.copy(gb["Mk"][:], gb["Z"][:]), R=[gb["Z"].b], W=[gb["Mk"].b])
            for g in grp:
                gb = GB[g]
                S.op("pool", lambda e, gb=gb: e.tensor_copy(gb["Mh"][:], gb["Mk"][:]), R=[gb["Mk"].b], W=[gb["Mh"].b])
                S.op("pool", lambda e, gb=gb: e.tensor_tensor(gb["Ml"][:], gb["Z"][:], gb["Mh"][:], ALU.subtract),
                     R=[gb["Z"].b, gb["Mh"].b], W=[gb["Ml"].b])
            PT = {}
            for g in grp:
                gb = GB[g]
                p = pb()
                PT[g] = p
                pv = p[:].bitcast(BF16)
                for j in range(4):
                    S.op("pe", lambda e, j=j, pv=pv, gb=gb: e.transpose(pv[:, j * 128:(j + 1) * 128], gb["Mk"][:, j, :], identb[:]),
                         R=[gb["Mk"].b, identb.b], W=[p.b])
            for g in grp:
                gb, p = GB[g], PT[g]
                pv = p[:].bitcast(BF16)
                S.op("act", lambda e, gb=gb, pv=pv: e.copy(flat(gb["Nk"]), pv[:, 0:512]), R=[p.b], W=[gb["Nk"].b])
                S.op("dve", lambda e, gb=gb, pv=pv: e.scalar_tensor_tensor(
                    gb["V"][:], pv[:, 0:512].rearrange("p (j s) -> p j s", j=4), -1.0, b4(identb[:]), ALU.mult, ALU.add),
                    R=[p.b, identb.b], W=[gb["V"].b])
            for r in range(1, 8):
                for g in grp:
                    gb = GB[g]
                    Mk, Nk, V = gb["Mk"], gb["Nk"], gb["V"]
                    pM = pN = pV = None
                    if r <= 6:
                        pM = pb()
                        for j in range(4):
                            S.op("pe", lambda e, j=j, pM=pM, Mk=Mk, Nk=Nk: e.matmul(
                                pM[:, j * 128:(j + 1) * 128], Nk[:, j, :], Mk[:, j, :], start=True, stop=True),
                                R=[Nk.b, Mk.b], W=[pM.b])
                    if r <= 5:
                        pN = pb()
                        for j in range(4):
                            S.op("pe", lambda e, j=j, pN=pN, Mk=Mk, Nk=Nk: e.matmul(
                                pN[:, j * 128:(j + 1) * 128], Mk[:, j, :], Nk[:, j, :], start=True, stop=True),
                                R=[Nk.b, Mk.b], W=[pN.b])
                    if r >= 2:
                        pV = pb()
                        for j in range(4):
                            S.op("pe", lambda e, j=j, pV=pV, Mk=Mk, V=V: e.matmul(
                                pV[:, j * 128:(j + 1) * 128], Mk[:, j, :], V[:, j, :], start=True, stop=True),
                                R=[Mk.b, V.b], W=[pV.b])
                        S.op("dve", lambda e, pV=pV, V=V: e.tensor_tensor(flat(V), pV[:], flat(V), ALU.add),
                             R=[pV.b, V.b], W=[V.b])
                    if pM is not None:
                        S.op("act", lambda e, pM=pM, Mk=Mk: e.copy(flat(Mk), pM[:]), R=[pM.b], W=[Mk.b])
                    if pN is not None:
                        S.op("act", lambda e, pN=pN, Nk=Nk: e.copy(flat(Nk), pN[:]), R=[pN.b], W=[Nk.b])
            PNV, PT0 = {}, {}
            for g in grp:
                gb = GB[g]
                V, Mh, Ml = gb["V"], gb["Mh"], gb["Ml"]
                pNV = pb()
                PNV[g] = pNV
                for j in range(4):
                    S.op("pe", lambda e, j=j, pNV=pNV, Mh=Mh, V=V: e.matmul(pNV[:, j * 128:(j + 1) * 128], Mh[:, j, :], V[:, j, :], start=True, stop=False),
                         R=[Mh.b, V.b], W=[pNV.b])
                    S.op("pe", lambda e, j=j, pNV=pNV, Ml=Ml, V=V: e.matmul(pNV[:, j * 128:(j + 1) * 128], Ml[:, j, :], V[:, j, :], start=False, stop=True),
                         R=[Ml.b, V.b], W=[pNV.b])
                pT0 = pb()
                PT0[g] = pT0
                pT0v = pT0[:].bitcast(BF16)
                for j in range(4):
                    S.op("pe", lambda e, j=j, pT0v=pT0v, V=V: e.transpose(pT0v[:, j * 128:(j + 1) * 128], V[:, j, :], identb[:]),
                         R=[V.b, identb.b], W=[pT0.b])
            for g in grp:
                gb = GB[g]
                V, Z, Rv, T0 = gb["V"], gb["Z"], gb["dgh"], gb["Nk"]
                pNV, pT0 = PNV[g], PT0[g]
                pT0v = pT0[:].bitcast(BF16)
                S.op("dve", lambda e, pNV=pNV, Z=Z, V=V: e.scalar_tensor_tensor(flat(Z), pNV[:], -1.0, flat(V), ALU.mult, ALU.subtract),
                     R=[pNV.b, V.b], W=[Z.b])
                S.op("pool", lambda e, Z=Z, Rv=Rv: e.tensor_tensor(Rv[:], Z[:], b4(identf[:]), ALU.add), R=[Z.b, identf.b], W=[Rv.b])
                S.op("act", lambda e, pT0v=pT0v, T0=T0: e.copy(flat(T0), pT0v[:, 0:512]), R=[pT0.b], W=[T0.b])
            PVR = {}
            for g in grp:
                gb = GB[g]
                Rv, T0 = gb["dgh"], gb["Nk"]
                pVR = pb()
                PVR[g] = pVR
                for j in range(4):
                    S.op("pe", lambda e, j=j, pVR=pVR, T0=T0, Rv=Rv: e.matmul(pVR[:, j * 128:(j + 1) * 128], T0[:, j, :], Rv[:, j, :], start=True, stop=True),
                         R=[T0.b, Rv.b], W=[pVR.b])
            for g in grp:
                gb = GB[g]
                V, Z, Vl, pVR = gb["V"], gb["Z"], gb["Mk"], PVR[g]
                S.op("dve", lambda e, pVR=pVR, Z=Z, V=V: e.tensor_tensor(flat(Z), pVR[:], flat(V), ALU.add), R=[pVR.b, V.b], W=[Z.b])
                S.op("act", lambda e, Z=Z, V=V: e.copy(V[:], Z[:]), R=[Z.b], W=[V.b])
                S.op("pool", lambda e, Z=Z, V=V, Vl=Vl: e.tensor_tensor(Vl[:], Z[:], V[:], ALU.subtract), R=[Z.b, V.b], W=[Vl.b])
            for g in grp:
                gb, h0 = GB[g], H0[g]
                V, Vl = gb["V"], gb["Mk"]
                pU = pb()
                pW = pb()
                for j in range(4):
                    S.op("pe", lambda e, j=j, pU=pU, V=V, h0=h0: e.matmul(pU[:, j * 128:(j + 1) * 128], V[:, j, :], vb[:, h0 + j, :], start=True, stop=False),
                         R=[V.b, vb.b], W=[pU.b])
                    S.op("pe", lambda e, j=j, pU=pU, Vl=Vl, h0=h0: e.matmul(pU[:, j * 128:(j + 1) * 128], Vl[:, j, :], vb[:, h0 + j, :], start=False, stop=True),
                         R=[Vl.b, vb.b], W=[pU.b])
                for j in range(4):
                    S.op("pe", lambda e, j=j, pW=pW, V=V, h0=h0: e.matmul(pW[:, j * 128:(j + 1) * 128], kbg[:, h0 + j, :], V[:, j, :], start=True, stop=False),
                         R=[V.b, kbg.b], W=[pW.b])
                    S.op("pe", lambda e, j=j, pW=pW, Vl=Vl, h0=h0: e.matmul(pW[:, j * 128:(j + 1) * 128], kbg[:, h0 + j, :], Vl[:, j, :], start=False, stop=True),
                         R=[Vl.b, kbg.b], W=[pW.b])
                S.op("act", lambda e, gb=gb, pU=pU: e.copy(flat(gb["u"]), pU[:]), R=[pU.b], W=[gb["u"].b])
                S.op("dve", lambda e, gb=gb, pW=pW: e.tensor_copy(flat(gb["wT"]), pW[:]), R=[pW.b], W=[gb["wT"].b])
            PWS = {}
            for g in grp:
                gb, h0 = GB[g], H0[g]
                pWS = pb()
                PWS[g] = pWS
                for j in range(4):
                    S.op("pe", lambda e, j=j, pWS=pWS, gb=gb, h0=h0: e.matmul(pWS[:, j * 128:(j + 1) * 128], gb["wT"][:, j, :], Sbf[:, h0 + j, :], start=True, stop=True),
                         R=[gb["wT"].b, Sbf.bs[g]], W=[pWS.b])
            for g in grp:
                gb, pWS = GB[g], PWS[g]
                S.op("dve", lambda e, gb=gb, pWS=pWS: e.tensor_tensor(flat(gb["vn"]), flat(gb["u"]), pWS[:], ALU.subtract),
                     R=[pWS.b, gb["u"].b], W=[gb["vn"].b])
            PO, PS = {}, {}
            for g in grp:
                gb, h0 = GB[g], H0[g]
                if full:
                    pO = pb()
                    PO[g] = pO
                    for j in range(4):
                        S.op("pe", lambda e, j=j, pO=pO, gb=gb, h0=h0: e.matmul(pO[:, j * 128:(j + 1) * 128], gb["qdT"][:, j, :], Sbf[:, h0 + j, :], start=True, stop=False),
                             R=[gb["qdT"].b, Sbf.bs[g]], W=[pO.b])
                        S.op("pe", lambda e, j=j, pO=pO, gb=gb: e.matmul(pO[:, j * 128:(j + 1) * 128], gb["aT"][:, j, :], gb["vn"][:, j, :], start=False, stop=True),
                             R=[gb["aT"].b, gb["vn"].b], W=[pO.b])
                pS = pb()
                PS[g] = pS
                for j in range(4):
                    S.op("pe", lambda e, j=j, pS=pS, gb=gb, h0=h0: e.matmul(pS[:, j * 128:(j + 1) * 128], kd[:, h0 + j, :], gb["vn"][:, j, :], start=True, stop=True),
                         R=[kd.b, gb["vn"].b], W=[pS.b])
            for g in grp:
                h0 = H0[g]
                S.op("pool", lambda e, h0=h0: e.tensor_tensor(S32[:, h0:h0 + 4, :], S32[:, h0:h0 + 4, :], col4(9, h0), ALU.mult),
                     R=[sm.bs[9], S32.bs[g]], W=[S32.bs[g]])
            for g in grp:
                h0, pS = H0[g], PS[g]
                S.op("dve", lambda e, h0=h0, pS=pS: e.tensor_tensor(S32[:, h0:h0 + 4, :], S32[:, h0:h0 + 4, :], pS[:].rearrange("p (j s) -> p j s", j=4), ALU.add),
                     R=[pS.b, S32.bs[g]], W=[S32.bs[g]])
            for g in grp:
                h0 = H0[g]
                S.op("act", lambda e, h0=h0: e.copy(Sbf[:, h0:h0 + 4, :], S32[:, h0:h0 + 4, :]), R=[S32.bs[g]], W=[Sbf.bs[g]])
            if full:
                for g in grp:
                    h0, pO = H0[g], PO[g]
                    oss, og, on, zs = B["oss"], B["og"][g % 2], B["on"], B["zs"]
                    for j in range(4):
                        S.op("act", lambda e, j=j, pO=pO, h0=h0: e.activation(B["junk"][:, 0:128], pO[:, j * 128:(j + 1) * 128], AF.Square,
                                                                             scale=float(128.0 ** -0.5), accum_out=oss[:, h0 + j:h0 + j + 1]),
                             R=[pO.b], W=[B["junk"].b, oss.b])
                    S.op("act", lambda e, h0=h0: e.activation(oss[:, 16 + h0:20 + h0], oss[:, h0:h0 + 4], AF.Ln, bias=EPS), R=[oss.b, cst.b], W=[oss.b])
                    S.op("act", lambda e, h0=h0: e.activation(oss[:, 16 + h0:20 + h0], oss[:, 16 + h0:20 + h0], AF.Exp, scale=-0.5), R=[oss.b], W=[oss.b])
                    S.op("dve", lambda e, pO=pO, og=og, h0=h0: e.tensor_tensor(og[:], pO[:].rearrange("p (j s) -> p j s", j=4),
                                                                               oss[:, 16 + h0:20 + h0].unsqueeze(2).to_broadcast([128, 4, 128]), ALU.mult),
                         R=[pO.b, oss.b], W=[og.b])
                    S.op("pool", lambda e, og=og, h0=h0: e.tensor_tensor(on[:, h0:h0 + 4, :], og[:], zs[:, h0 * 128:(h0 + 4) * 128].rearrange("p (h e) -> p h e", h=4), ALU.mult),
                         R=[og.b, zs.b], W=[on.b])
        if not full:
            return
        on, onT = B["on"], B["onT"]
        for half in range(2):
            p = pb()
            pv = p[:].bitcast(BF16)
            for j in range(8):
                S.op("pe", lambda e, j=j, pv=pv, half=half: e.transpose(pv[:, j * 128:(j + 1) * 128], on[:, half * 8 + j, :], identb[:]),
                     R=[on.b, identb.b], W=[p.b])
            if half == 0:
                S.op("act", lambda e, pv=pv, half=half: e.copy(onT[:, half * 8:(half + 1) * 8, :], pv.rearrange("p (k d) -> p k d", k=8)),
                     R=[p.b], W=[onT.b])
            else:
                S.op("dve", lambda e, pv=pv, half=half: e.tensor_copy(onT[:, half * 8:(half + 1) * 8, :], pv.rearrange("p (k d) -> p k d", k=8)),
                     R=[p.b], W=[onT.b])
        ft = t - nhist
        py = [pb(), pb()]
        HPB = WB // 256
        for hb in range(16 // HPB):
            wb = B["wblk"][hb % len(B["wblk"])]
            wv = wb[:, 0:HPB * 1024].rearrange("p (h n) -> p h n", h=HPB)
            S.dma("sp", wv, Wout_s[:, hb * HPB:(hb + 1) * HPB, :], R=[scrB[id(Wout_s)]], W=[wb.b])
            for hl in range(HPB):
                h_ = hb * HPB + hl
                for half in range(2):
                    S.op("pe", lambda e, hl=hl, h_=h_, half=half, wv=wv: e.matmul(
                        py[half][:], onT[:, h_, :], wv[:, hl, half * 512:(half + 1) * 512],
                        start=(h_ == 0), stop=(h_ == 15)), R=[wb.b, onT.b], W=[py[half].b])
        for half in range(2):
            S.op("dve", lambda e, half=half: e.tensor_tensor(
                hbuf[:, ft, half * 512:(half + 1) * 512], hbuf[:, ft, half * 512:(half + 1) * 512],
                py[half][:], ALU.add), R=[py[half].b, hbuf.bs[ft]], W=[hbuf.bs[ft]])

    def gdn_bufs(ph, N, full):
        B = {}
        B["WB"] = 256 if full else 512
        B["pref"] = {}
        B["xnT"] = [sb(ph, f"xnT{i}", [128, 8, N], BF16) for i in range(2)]
        B["qkvT"] = sb(ph, "qkvT", [128, 32, N], BF16, nslots=32)
        B["xs"] = [sb(ph, f"xs{i}", [128, D]) for i in range(2)] if not full else None
        B["xn"] = [sb(ph, f"xn{i}", [128, D], BF16) for i in range(2)]
        B["junk"] = sb(ph, "junk", [128, D], BF16)
        B["ms"] = sb(ph, "ms", [128, 4])
        B["wblk"] = [sb(ph, f"wblk{i}", [128, 8 * B["WB"]], BF16) for i in range(2 if full else 3)]
        B["u"] = [sb(ph, f"u{i}", [128, N + 3]) for i in range(2)]
        B["acc"] = [sb(ph, f"acc{i}", [128, N]) for i in range(2)]
        B["sq"] = [sb(ph, f"sq{i}", [128, N], BF16) for i in range(2)]
        B["sm"] = [sb(ph, f"sm{i}", [128, 16, (N // 128) * 16], F32, nslots=16) for i in range(2)]
        B["smb"] = [sb(ph, f"smb{i}", [128, 4, (N // 128) * 16], BF16, nslots=4) for i in range(2)]
        B["ktm"] = sb(ph, "ktm", [128, 8, 128], BF16)
        B["kbg"] = sb(ph, "kbg", [128, 16, 128], BF16)
        B["kd"] = sb(ph, "kd", [128, 16, 128], BF16)
        B["vb"] = sb(ph, "vb", [128, 16, 128], BF16)
        B["Asb"] = sb(ph, "Asb", [128, 8, 128], BF16)
        if full:
            B["KQsb"] = sb(ph, "KQsb", [128, 8, 128], BF16)
        G = []
        for g in range(2 if full else 4):
            gb = {}
            gb["dgh"] = sb(ph, f"dgh{g}", [128, 4, 128], BF16)
            gb["dgl"] = sb(ph, f"dgl{g}", [128, 4, 128], BF16)
            gb["Z"] = sb(ph, f"Z{g}", [128, 4, 128])
            gb["Mk"] = sb(ph, f"Mk{g}", [128, 4, 128], BF16)
            gb["Nk"] = sb(ph, f"Nk{g}", [128, 4, 128], BF16)
            gb["V"] = sb(ph, f"V{g}", [128, 4, 128], BF16)
            gb["Mh"] = sb(ph, f"Mh{g}", [128, 4, 128], BF16)
            gb["Ml"] = sb(ph, f"Ml{g}", [128, 4, 128], BF16)
            gb["u"] = sb(ph, f"ug{g}", [128, 4, 128])
            gb["wT"] = sb(ph, f"wT{g}", [128, 4, 128], BF16)
            gb["vn"] = sb(ph, f"vn{g}", [128, 4, 128], BF16)
            if full:
                gb["ER"] = sb(ph, f"ER{g}", [128, 4, 128], BF16)
                gb["DT"] = sb(ph, f"DT{g}", [128, 4, 128])
                gb["aT"] = sb(ph, f"aT{g}", [128, 4, 128], BF16)
                gb["qdT"] = sb(ph, f"qdT{g}", [128, 4, 128], BF16)
            G.append(gb)
        B["G"] = G
        if full:
            B["zs"] = sb(ph, "zs", [128, 2048], BF16)
            B["og"] = [sb(ph, f"og{i}", [128, 4, 128]) for i in range(2)]
            B["oss"] = sb(ph, "oss", [128, 32])
            B["on"] = sb(ph, "on", [128, 16, 128], BF16)
            B["onT"] = sb(ph, "onT", [128, 16, 128], BF16)
        return B

    if nhist > 0:
        with contextlib.ExitStack() as ph:
            B = gdn_bufs(ph, 512, False)
            sts = [list(range(t, min(t + 4, nhist))) for t in range(0, nhist, 4)]
            gdn_st_norm(B, sts[0], False, None, 0)
            for k, tl in enumerate(sts):
                if k + 1 < len(sts):
                    gdn_st_norm(B, sts[k + 1], False, None, k + 1)
                gdn_st_rest(B, tl, False, None, k, k + 1 < len(sts))
            S.barrier()

    hbuf = sb(root, "h", [128, NFULL, D], F32, nslots=NFULL)

    with contextlib.ExitStack() as ph:
        B = gdn_bufs(ph, 256, True)
        sts = [[nhist + 2 * st, nhist + 2 * st + 1] for st in range(NFULL // 2)]
        gdn_st_norm(B, sts[0], True, hbuf, 0)
        for k, tl in enumerate(sts):
            if k + 1 < len(sts):
                gdn_st_norm(B, sts[k + 1], True, hbuf, k + 1)
            gdn_st_rest(B, tl, True, hbuf, k)
        S.barrier()

    def ffn_phase(l):
        with contextlib.ExitStack() as ph:
            xnT = sb(ph, "f_xnT", [128, 8, 512], BF16)
            xn = [sb(ph, f"f_xn{i}", [128, D], BF16) for i in range(2)]
            junk = sb(ph, "f_junk", [128, D], BF16)
            ms = sb(ph, "f_ms", [128, 4])
            wblk = [sb(ph, f"f_w{i}", [128, 4096], BF16) for i in range(3)]
            u = [sb(ph, f"f_u{i}", [128, 514]) for i in range(4)]
            acc = [sb(ph, f"f_acc{i}", [128, 512]) for i in range(4)]
            gs = [sb(ph, f"f_gs{i}", [128, 512]) for i in range(2)]
            act = sb(ph, "f_act", [128, 22, 512], BF16, nslots=22)
            fcarry = sb(ph, "f_carry", [128, 44, 2])
            cw = sb(ph, "f_cw", [128, 44, 3])
            cb = sb(ph, "f_cb", [128, 44])
            S.op("pool", lambda e: e.memset(fcarry[:], 0.0), W=[fcarry.b])
            S.dma("sp", cw[:], f_cw_d[l][:, :, :], W=[cw.b])
            S.dma("sp", cb[:], f_cb_d[l][:, :], W=[cb.b])
            sts = [list(range(s, min(s + 4, NFULL))) for s in range(0, NFULL, 4)]

            def ffn_supertile(tiles):
                NT = len(tiles)
                N = NT * 128
                for i, ft in enumerate(tiles):
                    rmsnorm_T((junk, ms, xn[i % 2]), hbuf[:, ft, :], [hbuf.bs[ft]], xnT, i * 128)
                for b in range(11):
                    wb = wblk[b % 3]
                    wv = wb[:].rearrange("p (c n) -> p c n", c=8)
                    S.dma("sp", wv[:, :, 0:256], Wup_s[l][:, :, b * 256:(b + 1) * 256], R=[scrB[id(Wup_s[l])]], W=[wb.b])
                    S.dma("sp", wv[:, :, 256:512], Wup_s[l][:, :, 2816 + b * 256:2816 + (b + 1) * 256], R=[scrB[id(Wup_s[l])]], W=[wb.b])
                    for jj in range(2):
                        res = []
                        for which in range(2):
                            f = which * 22 + b * 2 + jj
                            col = which * 256 + jj * 128
                            p = pb()
                            for c in range(8):
                                S.op("pe", lambda e, c=c, p=p, wv=wv, col=col: e.matmul(p[:, 0:N], wv[:, c, col:col + 128], xnT[:, c, 0:N],
                                                                                      start=(c == 0), stop=(c == 7)),
                                     R=[wb.b, xnT.b], W=[p.b])
                            k = (jj * 2 + which)
                            uu, aa = u[k], acc[k]
                            S.op("pool", lambda e, uu=uu, f=f: e.tensor_copy(uu[:, 0:2], fcarry[:, f, :]), R=[fcarry.b], W=[uu.b])
                            S.op("act", lambda e, uu=uu, p=p: e.copy(uu[:, 2:2 + N], p[:, 0:N]), R=[p.b], W=[uu.b])
                            S.op("pool", lambda e, uu=uu, f=f: e.tensor_copy(fcarry[:, f, :], uu[:, N:N + 2]), R=[uu.b], W=[fcarry.b])
                            S.op("dve", lambda e, uu=uu, aa=aa, f=f: e.tensor_scalar(aa[:, 0:N], uu[:, 2:2 + N], cw[:, f, 2:3], cb[:, f:f + 1], ALU.mult, ALU.add),
                                 R=[uu.b, cw.b, cb.b], W=[aa.b])
                            for j in (1, 0):
                                S.op("dve", lambda e, uu=uu, aa=aa, f=f, j=j: e.scalar_tensor_tensor(
                                    aa[:, 0:N], uu[:, j:j + N], cw[:, f, j:j + 1], aa[:, 0:N], ALU.mult, ALU.add),
                                    R=[uu.b, cw.b, aa.b], W=[aa.b])
                            res.append(aa)
                        g_ = gs[jj]
                        S.op("act", lambda e, g_=g_, a0=res[0]: e.activation(g_[:, 0:N], a0[:, 0:N], AF.Silu), R=[res[0].b], W=[g_.b])
                        S.op("pool", lambda e, g_=g_, a1=res[1], b=b, jj=jj: e.tensor_tensor(act[:, b * 2 + jj, 0:N], g_[:, 0:N], a1[:, 0:N], ALU.mult),
                             R=[g_.b, res[1].b], W=[act.bs[b * 2 + jj]])
                pys = [[pb(), pb()] for _ in range(NT)]
                for jb in range(6):
                    nj = 4 if jb < 5 else 2
                    wb = wblk[jb % 3]
                    wv = wb[:].rearrange("p (j n) -> p j n", j=4)
                    S.dma("sp", wv[:, 0:nj, :], Wdn_s[l][:, jb * 4:jb * 4 + nj, :], R=[scrB[id(Wdn_s[l])]], W=[wb.b])
                    for jl in range(nj):
                        j = jb * 4 + jl
                        for i in range(NT):
                            for half in range(2):
                                S.op("pe", lambda e, i=i, j=j, jl=jl, half=half, wv=wv: e.matmul(
                                    pys[i][half][:], act[:, j, i * 128:(i + 1) * 128], wv[:, jl, half * 512:(half + 1) * 512],
                                    start=(j == 0), stop=(j == 21)), R=[wb.b, act.bs[j]], W=[pys[i][half].b])
                for i, ft in enumerate(tiles):
                    for half in range(2):
                        S.op("dve", lambda e, i=i, ft=ft, half=half: e.scalar_tensor_tensor(
                            hbuf[:, ft, half * 512:(half + 1) * 512], pys[i][half][:], validt[:, ft:ft + 1],
                            hbuf[:, ft, half * 512:(half + 1) * 512], ALU.mult, ALU.add),
                            R=[pys[i][half].b, hbuf.bs[ft], validt.b], W=[hbuf.bs[ft]])

            for tiles in sts:
                ffn_supertile(tiles)
            S.barrier()

    if stage >= 2:
        ffn_phase(0)

    def attn_phase():
        with contextlib.ExitStack() as ph:
            NK = NFULL + 1
            xnT = sb(ph, "a_xnT", [128, 8, 512], BF16)
            xn = [sb(ph, f"a_xn{i}", [128, D], BF16) for i in range(2)]
            ms = sb(ph, "a_ms", [128, 4])
            wblk = [sb(ph, f"a_w{i}", [128, 8, 512], BF16) for i in range(3)]
            wk = [0]

            def wnext(src3, n):
                wb = wblk[wk[0] % 3]
                wk[0] += 1
                S.dma("sp", wb[:, :, 0:n], src3, R=[scrB[id(Wkv_s)], scrB[id(Wq_s)], scrB[id(Wo_s)]], W=[wb.b])
                return wb

            KT = sb(ph, "a_KT", [128, 4, NK * 128], BF16)
            Vt = sb(ph, "a_V", [128, NK, 256], BF16)
            QT = sb(ph, "a_QT", [128, 8, 512], BF16)
            BMf = sb(ph, "a_BMf", [128, 4, 256])
            BM = sb(ph, "a_BM", [128, 16, 256], BF16)
            am = sb(ph, "a_am", [128, 256])
            kbf = sb(ph, "a_kbf", [1, 512])
            kbb = sb(ph, "a_kbb", [1, NK * 128], BF16)
            sinkb = sb(ph, "a_sink", [128, 16])
            sc = [sb(ph, f"a_sc{i}", [128, 2, 256]) for i in range(3)]
            pr = [sb(ph, f"a_pr{i}", [128, 2, 256], BF16) for i in range(3)]
            pT = [sb(ph, f"a_pT{i}", [128, 4, 128], BF16) for i in range(3)]
            st_ = sb(ph, "a_st", [128, 6, 16])
            obf = sb(ph, "a_obf", [128, D], BF16)
            junk = obf
            oT = sb(ph, "a_oT", [128, 8, 128], BF16)
            S.dma("sp", am[:], amask_d[:, :], W=[am.b])
            S.dma("sp", sinkb[:], sink_d[:, :], W=[sinkb.b])
            for q4 in range(4):
                S.dma("sp", BMf[:], band_d[:, q4 * 4:(q4 + 1) * 4, :], W=[BMf.b])
                S.op("dve", lambda e, q4=q4: e.tensor_tensor(BM[:, q4 * 4:(q4 + 1) * 4, :], BMf[:], am[:].unsqueeze(1).to_broadcast([128, 4, 256]), ALU.add),
                     R=[BMf.b, am.b], W=[BM.b])
            for k0_ in range(0, NK * 128, 512):
                kn = min(512, NK * 128 - k0_)
                S.dma("sp", kbf[:, 0:kn], kbias_d[:, k0_:k0_ + kn], W=[kbf.b])
                S.op("dve", lambda e, k0_=k0_, kn=kn: e.tensor_copy(kbb[:, k0_:k0_ + kn], kbf[:, 0:kn]), R=[kbf.b], W=[kbb.b])
            S.op("pool", lambda e: e.memset(KT[:, :, 0:128], 0.0), W=[KT.b])
            S.op("pool", lambda e: e.memset(Vt[:, 0, :], 0.0), W=[Vt.b])
            sts = [list(range(s, min(s + 4, NFULL))) for s in range(0, NFULL, 4)]

            def attn_supertile(tiles):
                NT = len(tiles)
                N = NT * 128
                for i, ft in enumerate(tiles):
                    rmsnorm_T((junk, ms, xn[i % 2]), hbuf[:, ft, :], [hbuf.bs[ft]], xnT, i * 128)
                kc0 = (tiles[0] + 1) * 128
                wkK = wnext(Wkv_s[:, :, 0:512], 512)
                for j in range(4):
                    p = pb()
                    for c in range(8):
                        S.op("pe", lambda e, c=c, p=p, j=j, wkK=wkK: e.matmul(p[:, 0:N], wkK[:, c, j * 128:(j + 1) * 128], xnT[:, c, 0:N],
                                                                     start=(c == 0), stop=(c == 7)), R=[wkK.b, xnT.b], W=[p.b])
                    S.op("act", lambda e, p=p, j=j: e.copy(KT[:, j, kc0:kc0 + N], p[:, 0:N]), R=[p.b], W=[KT.b])
                wkV = wnext(Wkv_s[:, :, 512:768], 256)
                for i, ft in enumerate(tiles):
                    p = pb()
                    for c in range(8):
                        S.op("pe", lambda e, c=c, p=p, i=i, wkV=wkV: e.matmul(p[:, 0:256], xnT[:, c, i * 128:(i + 1) * 128], wkV[:, c, 0:256],
                                                                     start=(c == 0), stop=(c == 7)), R=[wkV.b, xnT.b], W=[p.b])
                    S.op("dve", lambda e, p=p, ft=ft: e.tensor_copy(Vt[:, ft + 1, :], p[:, 0:256]), R=[p.b], W=[Vt.b])
                for f in range(8):
                    if f % 4 == 0:
                        wqb = wnext(Wq_s[:, :, (f // 4) * 512:(f // 4 + 1) * 512], 512)
                    p = pb()
                    for c in range(8):
                        S.op("pe", lambda e, c=c, p=p, f=f, wqb=wqb: e.matmul(p[:, 0:N], wqb[:, c, (f % 4) * 128:(f % 4 + 1) * 128], xnT[:, c, 0:N],
                                                                     start=(c == 0), stop=(c == 7)), R=[wqb.b, xnT.b], W=[p.b])
                    S.op("act", lambda e, p=p, f=f: e.activation(QT[:, f, 0:N], p[:, 0:N], AF.Copy, scale=0.125), R=[p.b], W=[QT.b])
                for i, ft in enumerate(tiles):
                    attn_tile(i, ft)

            def attn_tile(i, ft):
                if True:
                    k0 = ft * 128
                    pO = [PB[6], PB[7]]
                    reserved.update((6, 7))
                    for f in range(8):
                        kv = f // 2
                        ps = pb()
                        for hh in range(2):
                            lo = hh * 64
                            S.op("pe", lambda e, ps=ps, hh=hh, lo=lo, f=f, kv=kv, i=i, k0=k0: e.matmul(
                                ps[:, hh * 256:(hh + 1) * 256], QT[lo:lo + 64, f, i * 128:(i + 1) * 128], KT[lo:lo + 64, kv, k0:k0 + 256],
                                start=True, stop=False), R=[QT.b, KT.b], W=[ps.b])
                            S.op("pe", lambda e, ps=ps, hh=hh, k0=k0: e.matmul(
                                ps[:, hh * 256:(hh + 1) * 256], onesb[0:1, 0:128], kbb[0:1, k0:k0 + 256], start=False, stop=True),
                                R=[onesb.b, kbb.b], W=[ps.b])
                        s_, p_, t_ = sc[f % 3], pr[f % 3], pT[f % 3]
                        S.op("dve", lambda e, ps=ps, s_=s_, f=f: e.tensor_tensor(s_[:], ps[:].rearrange("p (h k) -> p h k", h=2), BM[:, 2 * f:2 * f + 2, :], ALU.add),
                             R=[ps.b, BM.b], W=[s_.b])
                        S.op("dve", lambda e, s_=s_, f=f: e.tensor_reduce(st_[:, 0, 2 * f:2 * f + 2], s_[:], AX.X, ALU.max), R=[s_.b], W=[st_.b])
                        S.op("dve", lambda e, f=f: e.tensor_tensor(st_[:, 0, 2 * f:2 * f + 2], st_[:, 0, 2 * f:2 * f + 2], sinkb[:, 2 * f:2 * f + 2], ALU.max),
                             R=[st_.b, sinkb.b], W=[st_.b])
                        S.op("dve", lambda e, f=f: e.tensor_scalar(st_[:, 1, 2 * f:2 * f + 2], st_[:, 0, 2 * f:2 * f + 2], -1.0, None, ALU.mult),
                             R=[st_.b], W=[st_.b])
                        for hh in range(2):
                            h_ = 2 * f + hh
                            S.op("act", lambda e, s_=s_, p_=p_, hh=hh, h_=h_: e.activation(p_[:, hh, :], s_[:, hh, :], AF.Exp, bias=st_[:, 1, h_:h_ + 1],
                                                                                           accum_out=st_[:, 2, h_:h_ + 1]),
                                 R=[s_.b, st_.b], W=[p_.b, st_.b])
                        pt = pb()
                        ptv = pt[:].bitcast(BF16)
                        for hh in range(2):
                            for kb in range(2):
                                S.op("pe", lambda e, ptv=ptv, pt=pt, p_=p_, hh=hh, kb=kb: e.transpose(
                                    ptv[:, (hh * 2 + kb) * 128:(hh * 2 + kb + 1) * 128], p_[:, hh, kb * 128:(kb + 1) * 128], identb[:]),
                                    R=[p_.b, identb.b], W=[pt.b])
                        if f % 2 == 0:
                            S.op("act", lambda e, ptv=ptv, t_=t_: e.copy(t_[:].rearrange("p a b -> p (a b)"), ptv[:, 0:512]), R=[pt.b], W=[t_.b])
                        else:
                            S.op("dve", lambda e, ptv=ptv, t_=t_: e.tensor_copy(t_[:].rearrange("p a b -> p (a b)"), ptv[:, 0:512]), R=[pt.b], W=[t_.b])
                        for hh in range(2):
                            h_ = 2 * f + hh
                            for kb in range(2):
                                S.op("pe", lambda e, t_=t_, hh=hh, kb=kb, h_=h_, kv=kv, ft=ft: e.matmul(
                                    pO[h_ // 8][:, (h_ % 8) * 64:(h_ % 8 + 1) * 64], t_[:, hh * 2 + kb, :], Vt[:, ft + kb, kv * 64:(kv + 1) * 64],
                                    start=(kb == 0), stop=(kb == 1)), R=[t_.b, Vt.b], W=[pO[h_ // 8].b])
                    reserved.clear()
                    S.op("dve", lambda e: e.tensor_tensor(st_[:, 3, :], sinkb[:], st_[:, 1, :], ALU.add), R=[st_.b, sinkb.b], W=[st_.b])
                    S.op("act", lambda e: e.activation(st_[:, 3, :], st_[:, 3, :], AF.Exp), R=[st_.b], W=[st_.b])
                    S.op("dve", lambda e: e.tensor_tensor(st_[:, 3, :], st_[:, 3, :], st_[:, 2, :], ALU.add), R=[st_.b], W=[st_.b])
                    S.op("dve", lambda e: e.reciprocal(st_[:, 4, :], st_[:, 3, :]), R=[st_.b], W=[st_.b])
                    for half in range(2):
                        S.op("dve", lambda e, half=half: e.tensor_tensor(
                            obf[:, half * 512:(half + 1) * 512].rearrange("p (h d) -> p h d", h=8), pO[half][:].rearrange("p (h d) -> p h d", h=8),
                            st_[:, 4, half * 8:(half + 1) * 8].unsqueeze(2).to_broadcast([128, 8, 64]), ALU.mult),
                            R=[pO[half].b, st_.b], W=[obf.b])
                    p = pb()
                    pv = p[:].bitcast(BF16)
                    for c in range(8):
                        S.op("pe", lambda e, c=c, pv=pv, p=p: e.transpose(pv[:, c * 128:(c + 1) * 128], obf[:, c * 128:(c + 1) * 128], identb[:]),
                             R=[obf.b, identb.b], W=[p.b])
                    S.op("act", lambda e, pv=pv: e.copy(oT[:].rearrange("p c t -> p (c t)"), pv), R=[p.b], W=[oT.b])
                    py = [pb(), pb()]
                    for half in range(2):
                        wob = wnext(Wo_s[:, :, half * 512:(half + 1) * 512], 512)
                        for c in range(8):
                            S.op("pe", lambda e, c=c, half=half, wob=wob: e.matmul(py[half][:], oT[:, c, :], wob[:, c, :],
                                                                           start=(c == 0), stop=(c == 7)), R=[oT.b, wob.b], W=[py[half].b])
                        S.op("dve", lambda e, half=half, ft=ft: e.scalar_tensor_tensor(
                            hbuf[:, ft, half * 512:(half + 1) * 512], py[half][:], validt[:, ft:ft + 1],
                            hbuf[:, ft, half * 512:(half + 1) * 512], ALU.mult, ALU.add),
                            R=[py[half].b, hbuf.bs[ft], validt.b], W=[hbuf.bs[ft]])

            for tiles in sts:
                attn_supertile(tiles)
            S.barrier()

    if stage >= 3:
        attn_phase()
    if stage >= 4:
        ffn_phase(1)

    toks = []
    with contextlib.ExitStack() as ph:
        fnw = sb(ph, "fnw", [128, D])
        junk = sb(ph, "o_junk", [128, D], BF16)
        ms = sb(ph, "o_ms", [128, 4])
        ob = [sb(ph, f"o_b{i}", [128, D]) for i in range(2)]
        S.dma("sp", fnw[:], fin_w_d[:, :], W=[fnw.b])
        for t in range(NOWN):
            ft = t + NHALO
            o_ = ob[t % 2]
            if stage >= 5:
                S.op("act", lambda e, ft=ft: e.activation(junk[:], hbuf[:, ft, :], AF.Square, scale=1.0 / 32.0, accum_out=ms[:, 0:1]),
                     R=[hbuf.bs[ft]], W=[junk.b, ms.b])
                S.op("act", lambda e: e.activation(ms[:, 1:2], ms[:, 0:1], AF.Ln, bias=EPS), R=[ms.b, cst.b], W=[ms.b])
                S.op("act", lambda e: e.activation(ms[:, 2:3], ms[:, 1:2], AF.Exp, scale=-0.5), R=[ms.b], W=[ms.b])
                S.op("dve", lambda e, ft=ft, o_=o_: e.scalar_tensor_tensor(o_[:], hbuf[:, ft, :], ms[:, 2:3], fnw[:], ALU.mult, ALU.mult),
                     R=[hbuf.bs[ft], ms.b, fnw.b], W=[o_.b])
            else:
                S.op("dve", lambda e, ft=ft, o_=o_: e.tensor_copy(o_[:], hbuf[:, ft, :]), R=[hbuf.bs[ft]], W=[o_.b])
            toks.append(S.dma("sp", out_d[t * 128:(t + 1) * 128, :], o_[:], R=[o_.b]))
        S.finish(toks)
    root.close()
    return nc


def _t5_bucket(dist):
    n = np.maximum(dist, 0)
    nf = np.maximum(n, 1).astype(np.float32)
    large = 16 + (np.log(nf / 16) / np.log(128 / 16) * 16).astype(np.int32)
    large = np.minimum(large, 31)
    return np.where(n < 16, n, large)


def make_inputs(inp, nhist, cores):
    f32 = np.float32
    A = lambda v: np.ascontiguousarray(np.asarray(v), dtype=f32)
    x = A(inp["x"])

    def pc(v):
        return np.ascontiguousarray(A(v).reshape(8, 128).T)

    common = {
        "a_w_in": A(inp["a_w_in"][0]),
        "a_w_out": A(inp["a_w_out"][0]),
        "ffn_w_up0": A(inp["ffn_w_up"][0]), "ffn_w_up1": A(inp["ffn_w_up"][1]),
        "ffn_w_down0": A(inp["ffn_w_down"][0]), "ffn_w_down1": A(inp["ffn_w_down"][1]),
        "b_w_q": A(inp["b_w_q"][0]), "b_w_o": A(inp["b_w_o"][0]),
        "a_norm_wT": pc(inp["a_norm_w"][0]),
        "ffn_norm_wT0": pc(inp["ffn_norm_w"][0]), "ffn_norm_wT1": pc(inp["ffn_norm_w"][1]),
        "kv_norm_wT": pc(inp["kv_norm_w"]), "b_norm_wT": pc(inp["b_norm_w"][0]),
        "out_norm_wT": A(inp["a_out_norm_w"][0]).reshape(128, 1),
        "final_norm_wb": np.ascontiguousarray(np.broadcast_to(A(inp["final_norm_w"])[None, :], (128, D))),
        "a_conv_wT": np.ascontiguousarray(A(inp["a_conv_w"][0]).T.reshape(32, 128, 4).transpose(1, 0, 2)),
        "a_log_b": np.ascontiguousarray(np.broadcast_to(A(inp["a_a_log"][0])[None, :], (128, 16))),
        "dt_bias_b": np.ascontiguousarray(np.broadcast_to(A(inp["a_dt_bias"][0])[None, :], (128, 16))),
        "sinks_b": np.ascontiguousarray(np.broadcast_to(A(inp["b_sinks"][0])[None, :], (128, 16))),
    }
    for l in range(2):
        common[f"ffn_conv_wT{l}"] = np.ascontiguousarray(A(inp["ffn_conv_w"][l]).T.reshape(44, 128, 3).transpose(1, 0, 2))
        common[f"ffn_conv_bT{l}"] = np.ascontiguousarray(A(inp["ffn_conv_b"][l]).reshape(44, 128).T)
    wkv = A(inp["w_kv"])
    cols = []
    for j in range(4):
        cols += [wkv[:, j * 64:(j + 1) * 64], wkv[:, j * 64:(j + 1) * 64]]
    cols.append(wkv[:, 256:512])
    common["w_kv_dup"] = np.ascontiguousarray(np.concatenate(cols, axis=1))
    qi = np.arange(128)[:, None]
    ki = np.arange(256)[None, :]
    dist = qi + 128 - ki
    bucket = _t5_bucket(dist)
    tab = A(inp["rel_bias_table"])
    common["biasband"] = np.ascontiguousarray(tab[bucket].transpose(0, 2, 1))
    inwin = (dist >= 0) & (dist < 128)
    common["attnmask"] = np.where(inwin, 0.0, NEG).astype(f32)
    common["ident"] = np.eye(128, dtype=f32)
    p_ = np.arange(128)[:, None]
    j_ = np.arange(128)[None, :]
    common["triu"] = (p_ <= j_).astype(f32)
    common["maskL"] = np.where(p_ > j_, 0.0, NEG).astype(f32)
    common["maskU"] = np.where(j_ >= p_, 0.0, NEG).astype(f32)
    NT_ALL = nhist + NFULL
    maps = []
    for c in cores:
        b, j = c // 4, c % 4
        end = 2048 * (j + 1)
        start = end - NT_ALL * 128
        xe = np.zeros((NT_ALL * 128, D), f32)
        s0 = max(start, 0)
        xe[s0 - start:] = x[b, s0:end]
        pos_full = np.arange(end - NFULL * 128, end)
        valid = (pos_full >= 0).astype(f32).reshape(NFULL, 128).T
        kb = np.concatenate([np.full(128, NEG, f32), np.where(pos_full >= 0, 0.0, NEG).astype(f32)])[None, :]
        m = dict(common)
        m["x_ext"] = xe
        m["valid"] = np.ascontiguousarray(valid)
        m["kbias"] = np.ascontiguousarray(kb)
        maps.append(m)
    return maps


_NHIST = 46


def kernel(**inputs):
    nc = build_program(_NHIST)
    maps = make_inputs(inputs, _NHIST, list(range(8)))
    res = run_bass_kernel_spmd(nc, maps, core_ids=list(range(8)))
    out = np.empty((2, 8192, D), np.float32)
    for c in range(8):
        b, j = c // 4, c % 4
        out[b, 2048 * j:2048 * (j + 1)] = res.results[c]["out"]
    return out
```

```python
import contextlib
import numpy as np
import concourse.bass as bass
import concourse.mybir as mybir
from concourse.bass_utils import run_bass_kernel_spmd

F32 = mybir.dt.float32
BF16 = mybir.dt.bfloat16
AF = mybir.ActivationFunctionType
ALU = mybir.AluOpType
AX = mybir.AxisListType

D = 1024
NFULL = 18
NHALO = 2
NOWN = 16
NEG = -1.0e30
EPOCH = 30000


class Buf:
    __slots__ = ("name", "w", "r", "excl")

    def __init__(self, name="", excl=False):
        self.name = name
        self.w = None
        self.r = []
        self.excl = excl


class Sched:
    ENGS = ("pe", "act", "dve", "pool", "sp")

    def __init__(self, nc, n_dma_sems=10):
        self.nc = nc
        self.prog = {e: [] for e in self.ENGS}
        self.cnt = {e: 0 for e in self.ENGS}
        self.sems = {}
        self.seen = {e: {} for e in self.ENGS}
        self.dma_sems = {}
        self.n_dma_sems = n_dma_sems
        self.dma_rr = {e: 0 for e in self.ENGS}
        self._semctx = []
        self.last_tok = {}

    def _new_sem(self, name):
        ctx = self.nc.semaphore(name)
        h = ctx.__enter__()
        self._semctx.append(ctx)
        return h

    def _eng_sem(self, eng, idx):
        key = (eng, idx // EPOCH)
        if key not in self.sems:
            self.sems[key] = self._new_sem(f"s_{eng}_{idx // EPOCH}")
        return self.sems[key], (idx % EPOCH) + 1

    def _wait(self, eng, tok):
        teng, sem, val = tok
        if teng == eng and eng == "pe":
            return
        seen = self.seen[eng]
        if seen.get(sem.name, 0) >= val:
            return
        seen[sem.name] = val
        self.prog[eng].append(lambda e, sem=sem, val=val: e.wait_ge(sem, val))

    def _deps(self, eng, reads, writes):
        for b in reads:
            if b.w is not None:
                self._wait(eng, b.w)
            if b.excl:
                for t in b.r:
                    if t[0] != eng:
                        self._wait(eng, t)
        for b in writes:
            if b.w is not None and b.w[0] != eng:
                self._wait(eng, b.w)
            for t in b.r:
                if t[0] != eng:
                    self._wait(eng, t)

    def _commit(self, tok, reads, writes):
        for b in reads:
            b.r.append(tok)
        for b in writes:
            b.w = tok
            b.r = []

    def op(self, eng, fn, R=(), W=()):
        self._deps(eng, R, W)
        idx = self.cnt[eng]
        self.cnt[eng] += 1
        sem, val = self._eng_sem(eng, idx)
        self.prog[eng].append(lambda e, fn=fn, sem=sem: fn(e).then_inc(sem, 1))
        tok = (eng, sem, val)
        self.last_tok[eng] = tok
        self._commit(tok, R, W)
        return tok

    def dma(self, eng, out, in_, R=(), W=()):
        k = self.dma_rr[eng]
        self.dma_rr[eng] = (k + 1) % self.n_dma_sems
        key = (eng, k)
        if key not in self.dma_sems:
            self.dma_sems[key] = [self._new_sem(f"d_{eng}_{k}"), 0]
        ent = self.dma_sems[key]
        sem, tot = ent
        if tot > 0:
            self._wait(eng, ("dma", sem, tot))
        self._deps(eng, R, W)
        ent[1] = tot + 16
        self.prog[eng].append(
            lambda e, out=out, in_=in_, sem=sem: e.dma_start(out=out, in_=in_).then_inc(sem, 16))
        tok = ("dma", sem, tot + 16)
        self._commit(tok, R, W)
        return tok

    def barrier(self):
        toks = list(self.last_tok.values())
        for (eng, k), (sem, tot) in self.dma_sems.items():
            if tot > 0:
                toks.append(("dma", sem, tot))
        for e in self.ENGS:
            for t in toks:
                if t[0] != e or e != "pe":
                    self._wait(e, t)

    def finish(self, final_toks):
        for t in final_toks:
            self._wait("sp", t)
        nc = self.nc
        with nc.Block() as block:
            @block.tensor
            def _(e):
                for f in self.prog["pe"]:
                    f(e)

            @block.scalar
            def _(e):
                for f in self.prog["act"]:
                    f(e)

            @block.vector
            def _(e):
                for f in self.prog["dve"]:
                    f(e)

            @block.gpsimd
            def _(e):
                for f in self.prog["pool"]:
                    f(e)

            @block.sync
            def _(e):
                for f in self.prog["sp"]:
                    f(e)
        for ctx in reversed(self._semctx):
            ctx.__exit__(None, None, None)


class T:
    def __init__(self, t, name, nslots=0):
        self.t = t
        self.b = Buf(name)
        self.bs = [Buf(f"{name}{i}") for i in range(nslots)]

    def __getitem__(self, k):
        return self.t[k]


def build_program(nhist, stage=99, cut=99):
    nc = bass.Bass("TRN2", target_bir_lowering=False)
    S = Sched(nc)
    NT_ALL = nhist + NFULL

    def din(name, shape, dt=F32):
        return nc.dram_tensor(name, list(shape), dt, kind="ExternalInput").ap()

    x_ext = din("x_ext", [NT_ALL * 128, D])
    valid_d = din("valid", [128, NFULL])
    kbias_d = din("kbias", [1, (NFULL + 1) * 128])
    w_in_d = din("a_w_in", [D, 6176])
    w_out_d = din("a_w_out", [2048, D])
    w_up_d = [din(f"ffn_w_up{l}", [D, 5632]) for l in range(2)]
    w_dn_d = [din(f"ffn_w_down{l}", [2816, D]) for l in range(2)]
    w_kv_d = din("w_kv_dup", [D, 768])
    w_q_d = din("b_w_q", [D, D])
    w_o_d = din("b_w_o", [D, D])
    a_nw_d = din("a_norm_wT", [128, 8])
    f_nw_d = [din(f"ffn_norm_wT{l}", [128, 8]) for l in range(2)]
    kv_nw_d = din("kv_norm_wT", [128, 8])
    b_nw_d = din("b_norm_wT", [128, 8])
    on_w_d = din("out_norm_wT", [128, 1])
    fin_w_d = din("final_norm_wb", [128, D])
    a_cw_d = din("a_conv_wT", [128, 32, 4])
    f_cw_d = [din(f"ffn_conv_wT{l}", [128, 44, 3]) for l in range(2)]
    f_cb_d = [din(f"ffn_conv_bT{l}", [128, 44]) for l in range(2)]
    alog_d = din("a_log_b", [128, 16])
    dtb_d = din("dt_bias_b", [128, 16])
    sink_d = din("sinks_b", [128, 16])
    band_d = din("biasband", [128, 16, 256])
    amask_d = din("attnmask", [128, 256])
    ident_d = din("ident", [128, 128])
    triu_d = din("triu", [128, 128])
    maskL_d = din("maskL", [128, 128])
    maskU_d = din("maskU", [128, 128])
    out_d = nc.dram_tensor("out", [NOWN * 128, D], F32, kind="ExternalOutput").ap()

    def dscr(name, shape):
        return nc.dram_tensor(name, list(shape), BF16).ap()

    Win_s = dscr("Win_b", [25, 128, 8, 256])
    Wout_s = dscr("Wout_s", [128, 16, D])
    Wup_s = [dscr(f"Wup_b{l}", [11, 128, 8, 512]) for l in range(2)]
    Wdn_s = [dscr(f"Wdn_s{l}", [128, 22, D]) for l in range(2)]
    Wkv_s = dscr("Wkv_b", [2, 128, 8, 512])
    Wq_s = dscr("Wq_b", [2, 128, 8, 512])
    Wo_s = dscr("Wo_b", [2, 128, 8, 512])
    scrB = {id(a): Buf("scr") for a in [Win_s, Wout_s, Wkv_s, Wq_s, Wo_s] + Wup_s + Wdn_s}

    uid = [0]

    def sb(stk, name, shape, dt=F32, nslots=0):
        uid[0] += 1
        t = stk.enter_context(nc.sbuf_tensor(f"{name}_{uid[0]}", list(shape), dt))
        return T(t, name, nslots)

    root = contextlib.ExitStack()
    PB = [T(root.enter_context(nc.psum_tensor(f"pb{i}", [128, 512], F32)), f"pb{i}") for i in range(8)]
    for p_ in PB:
        p_.b.excl = True
    pbi = [0]

    reserved = set()

    def pb():
        while True:
            k = pbi[0] % 8
            pbi[0] += 1
            if k not in reserved:
                return PB[k]

    identf = sb(root, "identf", [128, 128])
    identb = sb(root, "identb", [128, 128], BF16)
    onesb = sb(root, "onesb", [128, 128], BF16)
    triub = sb(root, "triub", [128, 128], BF16)
    maskLt = sb(root, "maskLt", [128, 128])
    maskUt = sb(root, "maskUt", [128, 128])
    cst = sb(root, "cst", [128, 8])
    validt = sb(root, "validt", [128, NFULL])
    tmpc = sb(root, "tmpc", [128, 128])

    S.dma("sp", identf[:], ident_d[:, :], W=[identf.b])
    S.op("dve", lambda e: e.tensor_copy(identb[:], identf[:]), R=[identf.b], W=[identb.b])
    S.op("pool", lambda e: e.memset(onesb[:], 1.0), W=[onesb.b])
    S.dma("sp", tmpc[:], triu_d[:, :], W=[tmpc.b])
    S.op("dve", lambda e: e.tensor_copy(triub[:], tmpc[:]), R=[tmpc.b], W=[triub.b])
    S.dma("sp", maskLt[:], maskL_d[:, :], W=[maskLt.b])
    S.dma("sp", maskUt[:], maskU_d[:, :], W=[maskUt.b])
    S.op("pool", lambda e: e.memset(cst[:, 0:1], 1e-6), W=[cst.b])
    S.op("pool", lambda e: e.memset(cst[:, 1:2], 1.0), W=[cst.b])
    S.op("pool", lambda e: e.memset(cst[:, 2:3], float(np.log(128.0 ** -0.5))), W=[cst.b])
    S.op("pool", lambda e: e.memset(cst[:, 3:4], 0.0), W=[cst.b])
    S.dma("sp", validt[:], valid_d[:, :], W=[validt.b])
    EPS = cst[:, 0:1]
    ONE = cst[:, 1:2]

    with contextlib.ExitStack() as ph:
        stg = [sb(ph, f"stg{i}", [128, 2048]) for i in range(2)]
        cvt = [sb(ph, f"cvt{i}", [128, 2048], BF16) for i in range(2)]
        nws = sb(ph, "nws", [128, 8 * 5 + 1])
        nw_ap = {}
        for i, (nm, d_) in enumerate([("a", a_nw_d), ("f0", f_nw_d[0]), ("f1", f_nw_d[1]), ("kv", kv_nw_d), ("b", b_nw_d)]):
            S.dma("sp", nws[:, i * 8:(i + 1) * 8], d_[:, :], W=[nws.b])
            nw_ap[nm] = (i * 8)
        S.dma("sp", nws[:, 40:41], on_w_d[:, :], W=[nws.b])
        blk = [0]

        def convert(src3, dst3, C, N, scale=None, dstfn=None, ng=512):
            ng = min(N, ng)
            cg = max(1, min(C, 2048 // ng))
            for c0 in range(0, C, cg):
                cc = min(cg, C - c0)
                for n0 in range(0, N, ng):
                    nn = min(ng, N - n0)
                    i = blk[0] % 2
                    blk[0] += 1
                    st, cv = stg[i], cvt[i]
                    sv = st[:, 0:cc * nn].rearrange("p (c n) -> p c n", c=cc)
                    cvv = cv[:, 0:cc * nn].rearrange("p (c n) -> p c n", c=cc)
                    S.dma("sp", sv, src3[:, c0:c0 + cc, n0:n0 + nn], W=[st.b])
                    eng = "dve" if (blk[0] % 2 == 0) else "pool"
                    if scale is None:
                        S.op(eng, lambda e, cvv=cvv, sv=sv: e.tensor_copy(cvv, sv), R=[st.b], W=[cv.b])
                    elif scale[0] == "pc":
                        g = nws[:, scale[1] + c0:scale[1] + c0 + cc].unsqueeze(2).to_broadcast([128, cc, nn])
                        S.op(eng, lambda e, cvv=cvv, sv=sv, g=g: e.tensor_tensor(cvv, sv, g, ALU.mult),
                             R=[st.b, nws.b], W=[cv.b])
                    else:
                        g = nws[:, scale[1]:scale[1] + 1].unsqueeze(2).to_broadcast([128, cc, nn])
                        S.op(eng, lambda e, cvv=cvv, sv=sv, g=g: e.tensor_tensor(cvv, sv, g, ALU.mult),
                             R=[st.b, nws.b], W=[cv.b])
                    dap = dst3[:, c0:c0 + cc, n0:n0 + nn] if dstfn is None else dstfn(c0, cc, n0, nn)
                    S.dma("pool", dap, cvv, R=[cv.b], W=[scrB[id(dst3)]])

        def pcn(ap):
            return ap.rearrange("(c p) n -> p c n", p=128)

        convert(pcn(w_in_d), Win_s, 8, 6176, ("pc", nw_ap["a"]), ng=256,
                dstfn=lambda c0, cc, n0, nn: Win_s[n0 // 256, :, c0:c0 + cc, 0:nn])
        convert(pcn(w_out_d), Wout_s, 16, D, ("p", 40))
        if stage >= 2:
            for l in range(2):
                convert(pcn(w_up_d[l]), Wup_s[l], 8, 5632, ("pc", nw_ap[f"f{l}"]), ng=256,
                        dstfn=lambda c0, cc, n0, nn, l=l: (Wup_s[l][n0 // 256, :, c0:c0 + cc, 0:256] if n0 < 2816
                                                           else Wup_s[l][(n0 - 2816) // 256, :, c0:c0 + cc, 256:512]))
                convert(pcn(w_dn_d[l]), Wdn_s[l], 22, D, None)
        if stage >= 3:
            convert(pcn(w_kv_d), Wkv_s, 8, 768, ("pc", nw_ap["kv"]), ng=256,
                    dstfn=lambda c0, cc, n0, nn: (Wkv_s[0, :, c0:c0 + cc, n0:n0 + nn] if n0 < 512 else Wkv_s[1, :, c0:c0 + cc, 0:nn]))
            convert(pcn(w_q_d), Wq_s, 8, D, ("pc", nw_ap["b"]), ng=512,
                    dstfn=lambda c0, cc, n0, nn: Wq_s[n0 // 512, :, c0:c0 + cc, 0:nn])
            convert(pcn(w_o_d), Wo_s, 8, D, None, ng=512,
                    dstfn=lambda c0, cc, n0, nn: Wo_s[n0 // 512, :, c0:c0 + cc, 0:nn])
        S.barrier()

    gdn = root
    S32 = sb(gdn, "S32", [128, 16, 128], F32, nslots=4)
    Sbf = sb(gdn, "Sbf", [128, 16, 128], BF16, nslots=4)
    carry = sb(gdn, "carry", [128, 32, 3])
    convw = sb(gdn, "convw", [128, 32, 4])
    wba = sb(gdn, "wba", [128, 8, 32], BF16)
    negA = sb(gdn, "negA", [128, 16])
    dtb = sb(gdn, "dtb", [128, 16])
    S.op("pool", lambda e: e.memset(S32[:], 0.0), W=[S32.b] + S32.bs)
    S.op("pool", lambda e: e.memset(Sbf[:], 0.0), W=[Sbf.b] + Sbf.bs)
    S.op("pool", lambda e: e.memset(carry[:], 0.0), W=[carry.b])
    S.dma("sp", convw[:], a_cw_d[:, :, :], W=[convw.b])
    S.dma("sp", wba[:], Win_s[24, :, :, 0:32], R=[scrB[id(Win_s)]], W=[wba.b])
    S.dma("sp", negA[:], alog_d[:, :], W=[negA.b])
    S.dma("sp", dtb[:], dtb_d[:, :], W=[dtb.b])
    S.op("act", lambda e: e.activation(negA[:], negA[:], AF.Exp), R=[negA.b], W=[negA.b])
    S.op("dve", lambda e: e.tensor_scalar(negA[:], negA[:], -1.0, None, ALU.mult), R=[negA.b], W=[negA.b])

    def rmsnorm_T(ph_bufs, src_ap, src_bufs, xnT, col0):
        junk, ms, xn = ph_bufs
        S.op("act", lambda e: e.activation(junk[:], src_ap, AF.Square, scale=1.0 / 32.0, accum_out=ms[:, 0:1]),
             R=src_bufs, W=[junk.b, ms.b])
        S.op("act", lambda e: e.activation(ms[:, 1:2], ms[:, 0:1], AF.Ln, bias=EPS), R=[ms.b, cst.b], W=[ms.b])
        S.op("act", lambda e: e.activation(ms[:, 2:3], ms[:, 1:2], AF.Exp, scale=-0.5), R=[ms.b], W=[ms.b])
        S.op("dve", lambda e: e.tensor_scalar(xn[:], src_ap, ms[:, 2:3], None, ALU.mult), R=src_bufs + [ms.b], W=[xn.b])
        p = pb()
        pv = p[:].bitcast(BF16)
        for c in range(8):
            S.op("pe", lambda e, c=c: e.transpose(pv[:, c * 128:(c + 1) * 128], xn[:, c * 128:(c + 1) * 128], identb[:]),
                 R=[xn.b, identb.b], W=[p.b])
        S.op("act", lambda e: e.copy(xnT[:, :, col0:col0 + 128], pv.rearrange("p (c t) -> p c t", c=8)),
             R=[p.b], W=[xnT.b])

    def wload(B, k, src3, nparts):
        wb = B["wblk"][k % len(B["wblk"])]
        n = src3.shape[2]
        wv = wb[:, 0:nparts * n].rearrange("p (c n) -> p c n", c=nparts)
        return wb, wv

    def gdn_st_norm(B, tiles, full, hbuf, sp):
        xnT = B["xnT"][sp % 2]
        for i, t in enumerate(tiles):
            if full:
                ft = t - nhist
                src, sbufs = hbuf[:, ft, :], [hbuf.bs[ft]]
            else:
                xs = B["xs"][t % 2]
                src, sbufs = xs[:], [xs.b]
            S.dma("sp", src, x_ext[t * 128:(t + 1) * 128, :], W=sbufs)
            rmsnorm_T((B["junk"], B["ms"], B["xn"][i % 2]), src, sbufs, xnT, i * 128)

    def gdn_st_rest(B, tiles, full, hbuf, sp, has_next=False):
        NT = len(tiles)
        N = NT * 128
        WB = B["WB"]
        CPB = WB // 128
        xnT, qkvT = B["xnT"][sp % 2], B["qkvT"]
        gdn_st_small(B, xnT, NT, sp)
        f0 = 0 if full else 8
        wb = None
        for f in range(f0, 32):
            if f % CPB == 0 or wb is None:
                blk = f // CPB
                if blk in B["pref"]:
                    wb, wv = B["pref"].pop(blk)
                else:
                    wb, wv = wload(B, blk, Win_s[blk], 8)
                    S.dma("sp", wv, Win_s[blk], R=[scrB[id(Win_s)]], W=[wb.b])
            p = pb()
            for c in range(8):
                S.op("pe", lambda e, c=c, wv=wv, f=f, p=p: e.matmul(p[:, 0:N], wv[:, c, (f % CPB) * 128:(f % CPB + 1) * 128],
                                                                    xnT[:, c, 0:N], start=(c == 0), stop=(c == 7)),
                     R=[wb.b, xnT.b], W=[p.b])
            u = B["u"][f % 2]
            acc = B["acc"][f % 2]
            S.op("pool", lambda e, u=u, f=f: e.tensor_copy(u[:, 0:3], carry[:, f, :]), R=[carry.b], W=[u.b])
            S.op("act", lambda e, u=u, p=p: e.copy(u[:, 3:3 + N], p[:, 0:N]), R=[p.b], W=[u.b])
            S.op("pool", lambda e, u=u, f=f: e.tensor_copy(carry[:, f, :], u[:, N:N + 3]), R=[u.b], W=[carry.b])
            S.op("dve", lambda e, u=u, acc=acc, f=f: e.tensor_scalar(acc[:, 0:N], u[:, 3:3 + N], convw[:, f, 3:4], None, ALU.mult),
                 R=[u.b, convw.b], W=[acc.b])
            for j in (2, 1, 0):
                S.op("dve", lambda e, u=u, acc=acc, f=f, j=j: e.scalar_tensor_tensor(
                    acc[:, 0:N], u[:, j:j + N], convw[:, f, j:j + 1], acc[:, 0:N], ALU.mult, ALU.add),
                    R=[u.b, convw.b, acc.b], W=[acc.b])
            S.op("act", lambda e, acc=acc, f=f: e.activation(qkvT[:, f, 0:N], acc[:, 0:N], AF.Silu),
                 R=[acc.b], W=[qkvT.bs[f]])
        for f in range(f0, 16):
            sq = B["sq"][f % 2]
            S.op("pool", lambda e, sq=sq, f=f: e.tensor_tensor(sq[:, 0:N], qkvT[:, f, 0:N], qkvT[:, f, 0:N], ALU.mult),
                 R=[qkvT.bs[f]], W=[sq.b])
            p = pb()
            S.op("pe", lambda e, p=p, sq=sq: e.matmul(p[:, 0:N], onesb[:], sq[:, 0:N], start=True, stop=True),
                 R=[onesb.b, sq.b], W=[p.b])
            rn = B["acc"][f % 2]
            S.op("act", lambda e, p=p, rn=rn: e.activation(rn[:, 0:N], p[:, 0:N], AF.Ln, bias=EPS), R=[p.b, cst.b], W=[rn.b])
            bias = cst[:, 2:3] if f < 8 else cst[:, 3:4]
            S.op("act", lambda e, rn=rn, bias=bias: e.activation(rn[:, 0:N], rn[:, 0:N], AF.Exp, scale=-0.5, bias=bias),
                 R=[rn.b, cst.b], W=[rn.b])
            S.op("dve", lambda e, rn=rn, f=f: e.tensor_tensor(qkvT[:, f, 0:N], qkvT[:, f, 0:N], rn[:, 0:N], ALU.mult),
                 R=[rn.b, qkvT.bs[f]], W=[qkvT.bs[f]])
        if cut < 2:
            return
        if has_next and not full:
            for blk in range(f0 // CPB, f0 // CPB + len(B["wblk"])):
                wb, wv = wload(B, blk, Win_s[blk], 8)
                S.dma("sp", wv, Win_s[blk], R=[scrB[id(Win_s)]], W=[wb.b])
                B["pref"][blk] = (wb, wv)
        for i, t in enumerate(tiles):
            gdn_tile(B, xnT, i, t, full, hbuf, sp)

    def gdn_st_small(B, xnT, NT, sp):
        sm = B["sm"][sp % 2]
        smb = B["smb"][sp % 2]
        W_ = NT * 16
        SM = lambda k: sm[:, k, 0:W_].rearrange("p (i h) -> p i h", h=16)
        SB = lambda k: smb[:, k, 0:W_].rearrange("p (i h) -> p i h", h=16)
        r = lambda k: sm.bs[k]
        rb = lambda k: smb.bs[k]
        bc = lambda t_: t_[:].unsqueeze(1).to_broadcast([128, NT, 16])
        pba = pb()
        for i in range(NT):
            for c in range(8):
                S.op("pe", lambda e, c=c, i=i: e.matmul(pba[:, i * 32:(i + 1) * 32], xnT[:, c, i * 128:(i + 1) * 128], wba[:, c, :],
                                                        start=(c == 0), stop=(c == 7)), R=[xnT.b, wba.b], W=[pba.b])
        pbv = pba[:, 0:NT * 32].rearrange("p (i c) -> p i c", c=32)
        S.op("dve", lambda e: e.tensor_tensor(SM(0), pbv[:, :, 16:32], bc(dtb), ALU.add), R=[pba.b, dtb.b], W=[r(0)])
        S.op("act", lambda e: e.activation(SM(4), pbv[:, :, 0:16], AF.Exp, scale=-1.0), R=[pba.b], W=[r(4)])
        S.op("act", lambda e: e.activation(SM(1), SM(0), AF.Abs), R=[r(0)], W=[r(1)])
        S.op("act", lambda e: e.activation(SM(1), SM(1), AF.Exp, scale=-1.0), R=[r(1)], W=[r(1)])
        S.op("act", lambda e: e.activation(SM(1), SM(1), AF.Ln, bias=ONE), R=[r(1), cst.b], W=[r(1)])
        S.op("dve", lambda e: e.tensor_scalar(SM(4), SM(4), 1.0, None, ALU.add), R=[r(4)], W=[r(4)])
        S.op("dve", lambda e: e.reciprocal(SM(5), SM(4)), R=[r(4)], W=[r(5)])
        S.op("dve", lambda e: e.scalar_tensor_tensor(SM(2), SM(0), 0.0, SM(1), ALU.max, ALU.add), R=[r(0), r(1)], W=[r(2)])
        S.op("dve", lambda e: e.tensor_tensor(SM(3), SM(2), bc(negA), ALU.mult), R=[r(2), negA.b], W=[r(3)])
        S.op("dve", lambda e: e.tensor_copy(SB(0), SM(3)), R=[r(3)], W=[rb(0)])
        S.op("dve", lambda e: e.tensor_copy(SM(6), SB(0)), R=[rb(0)], W=[r(6)])
        S.op("dve", lambda e: e.tensor_tensor(SB(1), SM(3), SM(6), ALU.subtract), R=[r(3), r(6)], W=[rb(1)])
        pc = pb()
        for i in range(NT):
            for k in range(2):
                S.op("pe", lambda e, i=i, k=k: e.matmul(pc[:, i * 32:i * 32 + 16], triub[:], smb[:, k, i * 16:(i + 1) * 16], start=(k == 0), stop=(k == 1)),
                     R=[triub.b, rb(k)], W=[pc.b])
            for k in range(2):
                S.op("pe", lambda e, i=i, k=k: e.matmul(pc[:, i * 32 + 16:i * 32 + 32], onesb[:], smb[:, k, i * 16:(i + 1) * 16], start=(k == 0), stop=(k == 1)),
                     R=[onesb.b, rb(k)], W=[pc.b])
        pcv = pc[:, 0:NT * 32].rearrange("p (i c) -> p i c", c=32)
        S.op("dve", lambda e: e.tensor_copy(SM(7), pcv[:, :, 0:16]), R=[pc.b], W=[r(7)])
        S.op("dve", lambda e: e.tensor_copy(SM(8), pcv[:, :, 16:32]), R=[pc.b], W=[r(8)])
        S.op("act", lambda e: e.activation(SM(9), SM(8), AF.Exp), R=[r(8)], W=[r(9)])
        S.op("dve", lambda e: e.tensor_tensor(SM(10), SM(8), SM(7), ALU.subtract), R=[r(7), r(8)], W=[r(10)])
        S.op("act", lambda e: e.activation(SM(15), SM(7), AF.Exp), R=[r(7)], W=[r(15)])
        S.op("act", lambda e: e.activation(SM(10), SM(10), AF.Exp), R=[r(10)], W=[r(10)])
        S.op("dve", lambda e: e.tensor_scalar(SM(12), SM(7), -1.0, None, ALU.mult), R=[r(7)], W=[r(12)])
        S.op("dve", lambda e: e.tensor_copy(SB(2), SM(12)), R=[r(12)], W=[rb(2)])
        S.op("dve", lambda e: e.tensor_copy(SM(13), SB(2)), R=[rb(2)], W=[r(13)])
        S.op("dve", lambda e: e.tensor_tensor(SM(14), SM(12), SM(13), ALU.subtract), R=[r(12), r(13)], W=[r(14)])
        S.op("dve", lambda e: e.tensor_tensor(SM(11), SM(5), SM(15), ALU.mult), R=[r(5), r(15)], W=[r(11)])

    def gdn_tile(B, xnT, i, t, full, hbuf, sp):
        qkvT = B["qkvT"]
        WB = B["WB"]
        c0 = i * 128
        sm = B["sm"][sp % 2]
        o16 = i * 16
        SM = lambda k: sm[:, k, o16:o16 + 16]
        if full:
            zs = B["zs"]
            nzb = 2048 // WB
            for blk in range(nzb):
                wb, wv = wload(B, blk, Win_s[16 + blk], 8)
                S.dma("sp", wv, Win_s[16 + blk], R=[scrB[id(Win_s)]], W=[wb.b])
                p = pb()
                for c in range(8):
                    S.op("pe", lambda e, c=c, wv=wv, p=p: e.matmul(p[:, 0:WB], xnT[:, c, c0:c0 + 128], wv[:, c, :],
                                                                   start=(c == 0), stop=(c == 7)),
                         R=[wb.b, xnT.b], W=[p.b])
                S.op("act", lambda e, p=p, blk=blk: e.activation(zs[:, blk * WB:(blk + 1) * WB], p[:, 0:WB], AF.Silu),
                     R=[p.b], W=[zs.b])
        ktm, kbg, kd, vb = B["ktm"], B["kbg"], B["kd"], B["vb"]
        p = pb()
        pv = p[:].bitcast(BF16)
        for kh in range(8):
            S.op("pe", lambda e, kh=kh, pv=pv: e.transpose(pv[:, kh * 128:(kh + 1) * 128], qkvT[:, 8 + kh, c0:c0 + 128], identb[:]),
                 R=[qkvT.bs[8 + kh], identb.b], W=[p.b])
        S.op("act", lambda e, pv=pv: e.copy(ktm[:], pv.rearrange("p (k d) -> p k d", k=8)), R=[p.b], W=[ktm.b])
        if cut < 3.2:
            return
        k4 = ktm[:].unsqueeze(2).to_broadcast([128, 8, 2, 128])
        S.op("pool", lambda e: e.tensor_tensor(kbg[:].rearrange("p (k r) d -> p k r d", r=2), k4,
                                               SM(11).rearrange("p (k r) -> p k r", r=2).unsqueeze(3).to_broadcast([128, 8, 2, 128]),
                                               ALU.mult), R=[ktm.b, sm.bs[11]], W=[kbg.b])
        S.op("pool", lambda e: e.tensor_tensor(kd[:].rearrange("p (k r) d -> p k r d", r=2), k4,
                                               SM(10).rearrange("p (k r) -> p k r", r=2).unsqueeze(3).to_broadcast([128, 8, 2, 128]),
                                               ALU.mult), R=[ktm.b, sm.bs[10]], W=[kd.b])
        if cut < 3.4:
            return
        for half in range(2):
            p = pb()
            pv = p[:].bitcast(BF16)
            for j in range(8):
                h_ = half * 8 + j
                S.op("pe", lambda e, j=j, h_=h_, pv=pv: e.transpose(pv[:, j * 128:(j + 1) * 128], qkvT[:, 16 + h_, c0:c0 + 128], identb[:]),
                     R=[qkvT.bs[16 + h_], identb.b], W=[p.b])
            S.op("dve", lambda e, half=half, pv=pv: e.tensor_tensor(
                vb[:, half * 8:(half + 1) * 8, :], pv.rearrange("p (k d) -> p k d", k=8),
                sm[:, 5, o16 + half * 8:o16 + (half + 1) * 8].unsqueeze(2).to_broadcast([128, 8, 128]), ALU.mult),
                R=[p.b, sm.bs[5]], W=[vb.b])
        if cut < 3.6:
            return
        Asb = B["Asb"]
        pA = [pb(), pb()]
        for kh in range(8):
            S.op("pe", lambda e, kh=kh: e.matmul(pA[kh // 4][:, (kh % 4) * 128:(kh % 4 + 1) * 128], qkvT[:, 8 + kh, c0:c0 + 128],
                                                 qkvT[:, 8 + kh, c0:c0 + 128], start=True, stop=True),
                 R=[qkvT.bs[8 + kh]], W=[pA[kh // 4].b])
        for j in range(2):
            S.op("act", lambda e, j=j: e.copy(Asb[:, j * 4:(j + 1) * 4, :], pA[j][:].rearrange("p (k s) -> p k s", k=4)),
                 R=[pA[j].b], W=[Asb.b])
        if cut < 3.8:
            return
        if full:
            KQsb = B["KQsb"]
            pK = [pb(), pb()]
            for kh in range(8):
                S.op("pe", lambda e, kh=kh: e.matmul(pK[kh // 4][:, (kh % 4) * 128:(kh % 4 + 1) * 128], qkvT[:, 8 + kh, c0:c0 + 128],
                                                     qkvT[:, kh, c0:c0 + 128], start=True, stop=True),
                     R=[qkvT.bs[8 + kh], qkvT.bs[kh]], W=[pK[kh // 4].b])
            for j in range(2):
                S.op("dve", lambda e, j=j: e.tensor_copy(KQsb[:, j * 4:(j + 1) * 4, :], pK[j][:].rearrange("p (k s) -> p k s", k=4)),
                     R=[pK[j].b], W=[KQsb.b])
        if cut < 4:
            return
        G = B["G"]
        b4 = lambda ap2: ap2.unsqueeze(1).to_broadcast([128, 4, 128])
        col4 = lambda k, h0: sm[:, k, o16 + h0:o16 + h0 + 4].unsqueeze(2).to_broadcast([128, 4, 128])
        flat = lambda t_: t_[:].rearrange("p j s -> p (j s)")
        kr = lambda ap3: ap3.rearrange("p (k r) s -> p k r s", r=2)
        NG = len(G)
        for gp in range(4 // NG):
            grp = list(range(NG * gp, NG * gp + NG))
            GB = {g: G[g % NG] for g in grp}
            H0 = {g: g * 4 for g in grp}
            for g in grp:
                gb, h0 = GB[g], H0[g]
                S.op("dve", lambda e, gb=gb, h0=h0: e.tensor_tensor(gb["dgh"][:], b4(identf[:]), col4(13, h0), ALU.mult),
                     R=[identf.b, sm.bs[13]], W=[gb["dgh"].b])
                S.op("pool", lambda e, gb=gb, h0=h0: e.tensor_tensor(gb["dgl"][:], b4(identf[:]), col4(14, h0), ALU.mult),
                     R=[identf.b, sm.bs[14]], W=[gb["dgl"].b])
            PR = {}
            for g in grp:
                gb = GB[g]
                pR = pb()
                PR[g] = pR
                S.op("pe", lambda e, pR=pR, gb=gb: e.matmul(pR[:], onesb[:], flat(gb["dgh"]), start=True, stop=False),
                     R=[onesb.b, gb["dgh"].b], W=[pR.b])
                S.op("pe", lambda e, pR=pR, gb=gb: e.matmul(pR[:], onesb[:], flat(gb["dgl"]), start=False, stop=True),
                     R=[onesb.b, gb["dgl"].b], W=[pR.b])
            for g in grp:
                gb, h0, pR = GB[g], H0[g], PR[g]
                S.op("dve", lambda e, pR=pR, gb=gb, h0=h0: e.tensor_tensor(gb["Z"][:], pR[:].rearrange("p (j s) -> p j s", j=4), col4(7, h0), ALU.add),
                     R=[pR.b, sm.bs[7]], W=[gb["Z"].b])
            if full:
                for g in grp:
                    gb, pR = GB[g], PR[g]
                    S.op("act", lambda e, pR=pR, gb=gb: e.activation(flat(gb["ER"]), pR[:], AF.Exp, scale=-1.0), R=[pR.b], W=[gb["ER"].b])
                for g in grp:
                    gb = GB[g]
                    S.op("dve", lambda e, gb=gb: e.scalar_tensor_tensor(gb["DT"][:], gb["Z"][:], -1.0, b4(maskUt[:]), ALU.mult, ALU.add),
                         R=[gb["Z"].b, maskUt.b], W=[gb["DT"].b])
            for g in grp:
                gb = GB[g]
                S.op("pool", lambda e, gb=gb: e.tensor_tensor(gb["Z"][:], gb["Z"][:], b4(maskLt[:]), ALU.add), R=[gb["Z"].b, maskLt.b], W=[gb["Z"].b])
            if full:
                for g in grp:
                    gb = GB[g]
                    S.op("act", lambda e, gb=gb: e.activation(gb["DT"][:], gb["DT"][:], AF.Exp), R=[gb["DT"].b], W=[gb["DT"].b])
            for g in grp:
                gb = GB[g]
                S.op("act", lambda e, gb=gb: e.activation(gb["Z"][:], gb["Z"][:], AF.Exp), R=[gb["Z"].b], W=[gb["Z"].b])
            for g in grp:
                gb, h0 = GB[g], H0[g]
                S.op("pool", lambda e, gb=gb, h0=h0: e.tensor_tensor(gb["Z"][:], gb["Z"][:], col4(5, h0), ALU.mult), R=[gb["Z"].b, sm.bs[5]], W=[gb["Z"].b])
            if full:
                for g in grp:
                    gb, kh0 = GB[g], H0[g] // 2
                    S.op("pool", lambda e, gb=gb, kh0=kh0: e.tensor_tensor(
                        kr(gb["aT"][:]), kr(gb["DT"][:]), KQsb[:, kh0:kh0 + 2, :].unsqueeze(2).to_broadcast([128, 2, 2, 128]), ALU.mult),
                        R=[gb["DT"].b, KQsb.b], W=[gb["aT"].b])
                    S.op("pool", lambda e, gb=gb, kh0=kh0: e.tensor_tensor(
                        kr(gb["qdT"][:]), kr(gb["ER"][:]), qkvT[:, kh0:kh0 + 2, c0:c0 + 128].unsqueeze(2).to_broadcast([128, 2, 2, 128]), ALU.mult),
                        R=[gb["ER"].b, qkvT.bs[kh0], qkvT.bs[kh0 + 1]], W=[gb["qdT"].b])
            for g in grp:
                gb, kh0 = GB[g], H0[g] // 2
                S.op("dve", lambda e, gb=gb, kh0=kh0: e.tensor_tensor(
                    kr(gb["Z"][:]), kr(gb["Z"][:]), Asb[:, kh0:kh0 + 2, :].unsqueeze(2).to_broadcast([128, 2, 2, 128]), ALU.mult),
                    R=[gb["Z"].b, Asb.b], W=[gb["Z"].b])
            for g in grp:
                gb = GB[g]
                S.op("act", lambda e, gb=gb: e.copy(gb["Mk"][:], gb["Z"][:]), R=[gb["Z"].b], W=[gb["Mk"].b])
            for g in grp:
                gb = GB[g]
                S.op("pool", lambda e, gb=gb: e.tensor_copy(gb["Mh"][:], gb["Mk"][:]), R=[gb["Mk"].b], W=[gb["Mh"].b])
                S.op("pool", lambda e, gb=gb: e.tensor_tensor(gb["Ml"][:], gb["Z"][:], gb["Mh"][:], ALU.subtract),
                     R=[gb["Z"].b, gb["Mh"].b], W=[gb["Ml"].b])
            PT = {}
            for g in grp:
                gb = GB[g]
                p = pb()
                PT[g] = p
                pv = p[:].bitcast(BF16)
                for j in range(4):
                    S.op("pe", lambda e, j=j, pv=pv, gb=gb: e.transpose(pv[:, j * 128:(j + 1) * 128], gb["Mk"][:, j, :], identb[:]),
                         R=[gb["Mk"].b, identb.b], W=[p.b])
            for g in grp:
                gb, p = GB[g], PT[g]
                pv = p[:].bitcast(BF16)
                S.op("act", lambda e, gb=gb, pv=pv: e.copy(flat(gb["Nk"]), pv[:, 0:512]), R=[p.b], W=[gb["Nk"].b])
                S.op("dve", lambda e, gb=gb, pv=pv: e.scalar_tensor_tensor(
                    gb["V"][:], pv[:, 0:512].rearrange("p (j s) -> p j s", j=4), -1.0, b4(identb[:]), ALU.mult, ALU.add),
                    R=[p.b, identb.b], W=[gb["V"].b])
            for r in range(1, 7):
                for g in grp:
                    gb = GB[g]
                    Mk, Nk, V = gb["Mk"], gb["Nk"], gb["V"]
                    pM = pN = pV = None
                    if r <= 5:
                        pM = pb()
                        for j in range(4):
                            S.op("pe", lambda e, j=j, pM=pM, Mk=Mk, Nk=Nk: e.matmul(
                                pM[:, j * 128:(j + 1) * 128], Nk[:, j, :], Mk[:, j, :], start=True, stop=True),
                                R=[Nk.b, Mk.b], W=[pM.b])
                    if r <= 5:
                        pN = pb()
                        for j in range(4):
                            S.op("pe", lambda e, j=j, pN=pN, Mk=Mk, Nk=Nk: e.matmul(
                                pN[:, j * 128:(j + 1) * 128], Mk[:, j, :], Nk[:, j, :], start=True, stop=True),
                                R=[Nk.b, Mk.b], W=[pN.b])
                    if r >= 2:
                        pV = pb()
                        for j in range(4):
                            S.op("pe", lambda e, j=j, pV=pV, Mk=Mk, V=V: e.matmul(
                                pV[:, j * 128:(j + 1) * 128], Mk[:, j, :], V[:, j, :], start=True, stop=True),
                                R=[Mk.b, V.b], W=[pV.b])
                        S.op("dve", lambda e, pV=pV, V=V: e.tensor_tensor(flat(V), pV[:], flat(V), ALU.add),
                             R=[pV.b, V.b], W=[V.b])
                    if pM is not None:
                        S.op("act", lambda e, pM=pM, Mk=Mk: e.copy(flat(Mk), pM[:]), R=[pM.b], W=[Mk.b])
                    if pN is not None:
                        S.op("act", lambda e, pN=pN, Nk=Nk: e.copy(flat(Nk), pN[:]), R=[pN.b], W=[Nk.b])
            PNV, PT0 = {}, {}
            for g in grp:
                gb = GB[g]
                V, Mh, Ml = gb["V"], gb["Mh"], gb["Ml"]
                pNV = pb()
                PNV[g] = pNV
                for j in range(4):
                    S.op("pe", lambda e, j=j, pNV=pNV, Mh=Mh, V=V: e.matmul(pNV[:, j * 128:(j + 1) * 128], Mh[:, j, :], V[:, j, :], start=True, stop=False),
                         R=[Mh.b, V.b], W=[pNV.b])
                    S.op("pe", lambda e, j=j, pNV=pNV, Ml=Ml, V=V: e.matmul(pNV[:, j * 128:(j + 1) * 128], Ml[:, j, :], V[:, j, :], start=False, stop=True),
                         R=[Ml.b, V.b], W=[pNV.b])
                pT0 = pb()
                PT0[g] = pT0
                pT0v = pT0[:].bitcast(BF16)
                for j in range(4):
                    S.op("pe", lambda e, j=j, pT0v=pT0v, V=V: e.transpose(pT0v[:, j * 128:(j + 1) * 128], V[:, j, :], identb[:]),
                         R=[V.b, identb.b], W=[pT0.b])
            for g in grp:
                gb = GB[g]
                V, Z, Rv, T0 = gb["V"], gb["Z"], gb["dgh"], gb["Nk"]
                pNV, pT0 = PNV[g], PT0[g]
                pT0v = pT0[:].bitcast(BF16)
                S.op("dve", lambda e, pNV=pNV, Z=Z, V=V: e.scalar_tensor_tensor(flat(Z), pNV[:], -1.0, flat(V), ALU.mult, ALU.subtract),
                     R=[pNV.b, V.b], W=[Z.b])
                S.op("pool", lambda e, Z=Z, Rv=Rv: e.tensor_tensor(Rv[:], Z[:], b4(identf[:]), ALU.add), R=[Z.b, identf.b], W=[Rv.b])
                S.op("act", lambda e, pT0v=pT0v, T0=T0: e.copy(flat(T0), pT0v[:, 0:512]), R=[pT0.b], W=[T0.b])
            PVR = {}
            for g in grp:
                gb = GB[g]
                Rv, T0 = gb["dgh"], gb["Nk"]
                pVR = pb()
                PVR[g] = pVR
                for j in range(4):
                    S.op("pe", lambda e, j=j, pVR=pVR, T0=T0, Rv=Rv: e.matmul(pVR[:, j * 128:(j + 1) * 128], T0[:, j, :], Rv[:, j, :], start=True, stop=True),
                         R=[T0.b, Rv.b], W=[pVR.b])
            for g in grp:
                gb = GB[g]
                V, Z, Vl, pVR = gb["V"], gb["Z"], gb["Mk"], PVR[g]
                S.op("dve", lambda e, pVR=pVR, Z=Z, V=V: e.tensor_tensor(flat(Z), pVR[:], flat(V), ALU.add), R=[pVR.b, V.b], W=[Z.b])
                S.op("act", lambda e, Z=Z, V=V: e.copy(V[:], Z[:]), R=[Z.b], W=[V.b])
                S.op("pool", lambda e, Z=Z, V=V, Vl=Vl: e.tensor_tensor(Vl[:], Z[:], V[:], ALU.subtract), R=[Z.b, V.b], W=[Vl.b])
            for g in grp:
                gb, h0 = GB[g], H0[g]
                V, Vl = gb["V"], gb["Mk"]
                pU = pb()
                pW = pb()
                for j in range(4):
                    S.op("pe", lambda e, j=j, pU=pU, V=V, h0=h0: e.matmul(pU[:, j * 128:(j + 1) * 128], V[:, j, :], vb[:, h0 + j, :], start=True, stop=False),
                         R=[V.b, vb.b], W=[pU.b])
                    S.op("pe", lambda e, j=j, pU=pU, Vl=Vl, h0=h0: e.matmul(pU[:, j * 128:(j + 1) * 128], Vl[:, j, :], vb[:, h0 + j, :], start=False, stop=True),
                         R=[Vl.b, vb.b], W=[pU.b])
                for j in range(4):
                    S.op("pe", lambda e, j=j, pW=pW, V=V, h0=h0: e.matmul(pW[:, j * 128:(j + 1) * 128], kbg[:, h0 + j, :], V[:, j, :], start=True, stop=False),
                         R=[V.b, kbg.b], W=[pW.b])
                    S.op("pe", lambda e, j=j, pW=pW, Vl=Vl, h0=h0: e.matmul(pW[:, j * 128:(j + 1) * 128], kbg[:, h0 + j, :], Vl[:, j, :], start=False, stop=True),
                         R=[Vl.b, kbg.b], W=[pW.b])
                S.op("act", lambda e, gb=gb, pU=pU: e.copy(flat(gb["u"]), pU[:]), R=[pU.b], W=[gb["u"].b])
                S.op("dve", lambda e, gb=gb, pW=pW: e.tensor_copy(flat(gb["wT"]), pW[:]), R=[pW.b], W=[gb["wT"].b])
            PWS = {}
            for g in grp:
                gb, h0 = GB[g], H0[g]
                pWS = pb()
                PWS[g] = pWS
                for j in range(4):
                    S.op("pe", lambda e, j=j, pWS=pWS, gb=gb, h0=h0: e.matmul(pWS[:, j * 128:(j + 1) * 128], gb["wT"][:, j, :], Sbf[:, h0 + j, :], start=True, stop=True),
                         R=[gb["wT"].b, Sbf.bs[g]], W=[pWS.b])
            for g in grp:
                gb, pWS = GB[g], PWS[g]
                S.op("dve", lambda e, gb=gb, pWS=pWS: e.tensor_tensor(flat(gb["vn"]), flat(gb["u"]), pWS[:], ALU.subtract),
                     R=[pWS.b, gb["u"].b], W=[gb["vn"].b])
            PO, PS = {}, {}
            for g in grp:
                gb, h0 = GB[g], H0[g]
                if full:
                    pO = pb()
                    PO[g] = pO
                    for j in range(4):
                        S.op("pe", lambda e, j=j, pO=pO, gb=gb, h0=h0: e.matmul(pO[:, j * 128:(j + 1) * 128], gb["qdT"][:, j, :], Sbf[:, h0 + j, :], start=True, stop=False),
                             R=[gb["qdT"].b, Sbf.bs[g]], W=[pO.b])
                        S.op("pe", lambda e, j=j, pO=pO, gb=gb: e.matmul(pO[:, j * 128:(j + 1) * 128], gb["aT"][:, j, :], gb["vn"][:, j, :], start=False, stop=True),
                             R=[gb["aT"].b, gb["vn"].b], W=[pO.b])
                pS = pb()
                PS[g] = pS
                for j in range(4):
                    S.op("pe", lambda e, j=j, pS=pS, gb=gb, h0=h0: e.matmul(pS[:, j * 128:(j + 1) * 128], kd[:, h0 + j, :], gb["vn"][:, j, :], start=True, stop=True),
                         R=[kd.b, gb["vn"].b], W=[pS.b])
            for g in grp:
                h0 = H0[g]
                S.op("pool", lambda e, h0=h0: e.tensor_tensor(S32[:, h0:h0 + 4, :], S32[:, h0:h0 + 4, :], col4(9, h0), ALU.mult),
                     R=[sm.bs[9], S32.bs[g]], W=[S32.bs[g]])
            for g in grp:
                h0, pS = H0[g], PS[g]
                S.op("dve", lambda e, h0=h0, pS=pS: e.tensor_tensor(S32[:, h0:h0 + 4, :], S32[:, h0:h0 + 4, :], pS[:].rearrange("p (j s) -> p j s", j=4), ALU.add),
                     R=[pS.b, S32.bs[g]], W=[S32.bs[g]])
            for g in grp:
                h0 = H0[g]
                S.op("act", lambda e, h0=h0: e.copy(Sbf[:, h0:h0 + 4, :], S32[:, h0:h0 + 4, :]), R=[S32.bs[g]], W=[Sbf.bs[g]])
            if full:
                for g in grp:
                    h0, pO = H0[g], PO[g]
                    oss, og, on, zs = B["oss"], B["og"][g % 2], B["on"], B["zs"]
                    for j in range(4):
                        S.op("act", lambda e, j=j, pO=pO, h0=h0: e.activation(B["junk"][:, 0:128], pO[:, j * 128:(j + 1) * 128], AF.Square,
                                                                             scale=float(128.0 ** -0.5), accum_out=oss[:, h0 + j:h0 + j + 1]),
                             R=[pO.b], W=[B["junk"].b, oss.b])
                    S.op("act", lambda e, h0=h0: e.activation(oss[:, 16 + h0:20 + h0], oss[:, h0:h0 + 4], AF.Ln, bias=EPS), R=[oss.b, cst.b], W=[oss.b])
                    S.op("act", lambda e, h0=h0: e.activation(oss[:, 16 + h0:20 + h0], oss[:, 16 + h0:20 + h0], AF.Exp, scale=-0.5), R=[oss.b], W=[oss.b])
                    S.op("dve", lambda e, pO=pO, og=og, h0=h0: e.tensor_tensor(og[:], pO[:].rearrange("p (j s) -> p j s", j=4),
                                                                               oss[:, 16 + h0:20 + h0].unsqueeze(2).to_broadcast([128, 4, 128]), ALU.mult),
                         R=[pO.b, oss.b], W=[og.b])
                    S.op("pool", lambda e, og=og, h0=h0: e.tensor_tensor(on[:, h0:h0 + 4, :], og[:], zs[:, h0 * 128:(h0 + 4) * 128].rearrange("p (h e) -> p h e", h=4), ALU.mult),
                         R=[og.b, zs.b], W=[on.b])
        if not full:
            return
        on, onT = B["on"], B["onT"]
        for half in range(2):
            p = pb()
            pv = p[:].bitcast(BF16)
            for j in range(8):
                S.op("pe", lambda e, j=j, pv=pv, half=half: e.transpose(pv[:, j * 128:(j + 1) * 128], on[:, half * 8 + j, :], identb[:]),
                     R=[on.b, identb.b], W=[p.b])
            if half == 0:
                S.op("act", lambda e, pv=pv, half=half: e.copy(onT[:, half * 8:(half + 1) * 8, :], pv.rearrange("p (k d) -> p k d", k=8)),
                     R=[p.b], W=[onT.b])
            else:
                S.op("dve", lambda e, pv=pv, half=half: e.tensor_copy(onT[:, half * 8:(half + 1) * 8, :], pv.rearrange("p (k d) -> p k d", k=8)),
                     R=[p.b], W=[onT.b])
        ft = t - nhist
        py = [pb(), pb()]
        HPB = WB // 256
        for hb in range(16 // HPB):
            wb = B["wblk"][hb % len(B["wblk"])]
            wv = wb[:, 0:HPB * 1024].rearrange("p (h n) -> p h n", h=HPB)
            S.dma("sp", wv, Wout_s[:, hb * HPB:(hb + 1) * HPB, :], R=[scrB[id(Wout_s)]], W=[wb.b])
            for hl in range(HPB):
                h_ = hb * HPB + hl
                for half in range(2):
                    S.op("pe", lambda e, hl=hl, h_=h_, half=half, wv=wv: e.matmul(
                        py[half][:], onT[:, h_, :], wv[:, hl, half * 512:(half + 1) * 512],
                        start=(h_ == 0), stop=(h_ == 15)), R=[wb.b, onT.b], W=[py[half].b])
        for half in range(2):
            S.op("dve", lambda e, half=half: e.tensor_tensor(
                hbuf[:, ft, half * 512:(half + 1) * 512], hbuf[:, ft, half * 512:(half + 1) * 512],
                py[half][:], ALU.add), R=[py[half].b, hbuf.bs[ft]], W=[hbuf.bs[ft]])

    def gdn_bufs(ph, N, full):
        B = {}
        B["WB"] = 256
        B["pref"] = {}
        B["xnT"] = [sb(ph, f"xnT{i}", [128, 8, N], BF16) for i in range(2)]
        B["qkvT"] = sb(ph, "qkvT", [128, 32, N], BF16, nslots=32)
        B["xs"] = [sb(ph, f"xs{i}", [128, D]) for i in range(2)] if not full else None
        B["xn"] = [sb(ph, f"xn{i}", [128, D], BF16) for i in range(2)]
        B["junk"] = sb(ph, "junk", [128, D], BF16)
        B["ms"] = sb(ph, "ms", [128, 4])
        B["wblk"] = [sb(ph, f"wblk{i}", [128, 8 * B["WB"]], BF16) for i in range(2 if full else 6)]
        B["u"] = [sb(ph, f"u{i}", [128, N + 3]) for i in range(2)]
        B["acc"] = [sb(ph, f"acc{i}", [128, N]) for i in range(2)]
        B["sq"] = [sb(ph, f"sq{i}", [128, N], BF16) for i in range(2)]
        B["sm"] = [sb(ph, f"sm{i}", [128, 16, (N // 128) * 16], F32, nslots=16) for i in range(2)]
        B["smb"] = [sb(ph, f"smb{i}", [128, 4, (N // 128) * 16], BF16, nslots=4) for i in range(2)]
        B["ktm"] = sb(ph, "ktm", [128, 8, 128], BF16)
        B["kbg"] = sb(ph, "kbg", [128, 16, 128], BF16)
        B["kd"] = sb(ph, "kd", [128, 16, 128], BF16)
        B["vb"] = sb(ph, "vb", [128, 16, 128], BF16)
        B["Asb"] = sb(ph, "Asb", [128, 8, 128], BF16)
        if full:
            B["KQsb"] = sb(ph, "KQsb", [128, 8, 128], BF16)
        G = []
        for g in range(2 if full else 4):
            gb = {}
            gb["dgh"] = sb(ph, f"dgh{g}", [128, 4, 128], BF16)
            gb["dgl"] = sb(ph, f"dgl{g}", [128, 4, 128], BF16)
            gb["Z"] = sb(ph, f"Z{g}", [128, 4, 128])
            gb["Mk"] = sb(ph, f"Mk{g}", [128, 4, 128], BF16)
            gb["Nk"] = sb(ph, f"Nk{g}", [128, 4, 128], BF16)
            gb["V"] = sb(ph, f"V{g}", [128, 4, 128], BF16)
            gb["Mh"] = sb(ph, f"Mh{g}", [128, 4, 128], BF16)
            gb["Ml"] = sb(ph, f"Ml{g}", [128, 4, 128], BF16)
            gb["u"] = sb(ph, f"ug{g}", [128, 4, 128])
            gb["wT"] = sb(ph, f"wT{g}", [128, 4, 128], BF16)
            gb["vn"] = sb(ph, f"vn{g}", [128, 4, 128], BF16)
            if full:
                gb["ER"] = sb(ph, f"ER{g}", [128, 4, 128], BF16)
                gb["DT"] = sb(ph, f"DT{g}", [128, 4, 128])
                gb["aT"] = sb(ph, f"aT{g}", [128, 4, 128], BF16)
                gb["qdT"] = sb(ph, f"qdT{g}", [128, 4, 128], BF16)
            G.append(gb)
        B["G"] = G
        if full:
            B["zs"] = sb(ph, "zs", [128, 2048], BF16)
            B["og"] = [sb(ph, f"og{i}", [128, 4, 128]) for i in range(2)]
            B["oss"] = sb(ph, "oss", [128, 32])
            B["on"] = sb(ph, "on", [128, 16, 128], BF16)
            B["onT"] = sb(ph, "onT", [128, 16, 128], BF16)
        return B

    if nhist > 0:
        with contextlib.ExitStack() as ph:
            B = gdn_bufs(ph, 512, False)
            sts = [list(range(t, min(t + 4, nhist))) for t in range(0, nhist, 4)]
            gdn_st_norm(B, sts[0], False, None, 0)
            for k, tl in enumerate(sts):
                if k + 1 < len(sts):
                    gdn_st_norm(B, sts[k + 1], False, None, k + 1)
                gdn_st_rest(B, tl, False, None, k, k + 1 < len(sts))
            S.barrier()

    hbuf = sb(root, "h", [128, NFULL, D], F32, nslots=NFULL)

    with contextlib.ExitStack() as ph:
        B = gdn_bufs(ph, 256, True)
        sts = [[nhist + 2 * st, nhist + 2 * st + 1] for st in range(NFULL // 2)]
        gdn_st_norm(B, sts[0], True, hbuf, 0)
        for k, tl in enumerate(sts):
            if k + 1 < len(sts):
                gdn_st_norm(B, sts[k + 1], True, hbuf, k + 1)
            gdn_st_rest(B, tl, True, hbuf, k)
        S.barrier()

    def ffn_phase(l):
        with contextlib.ExitStack() as ph:
            xnT = sb(ph, "f_xnT", [128, 8, 512], BF16)
            xn = [sb(ph, f"f_xn{i}", [128, D], BF16) for i in range(2)]
            junk = sb(ph, "f_junk", [128, D], BF16)
            ms = sb(ph, "f_ms", [128, 4])
            wblk = [sb(ph, f"f_w{i}", [128, 4096], BF16) for i in range(3)]
            u = [sb(ph, f"f_u{i}", [128, 514]) for i in range(4)]
            acc = [sb(ph, f"f_acc{i}", [128, 512]) for i in range(4)]
            gs = [sb(ph, f"f_gs{i}", [128, 512]) for i in range(2)]
            act = sb(ph, "f_act", [128, 22, 512], BF16, nslots=22)
            fcarry = sb(ph, "f_carry", [128, 44, 2])
            cw = sb(ph, "f_cw", [128, 44, 3])
            cb = sb(ph, "f_cb", [128, 44])
            S.op("pool", lambda e: e.memset(fcarry[:], 0.0), W=[fcarry.b])
            S.dma("sp", cw[:], f_cw_d[l][:, :, :], W=[cw.b])
            S.dma("sp", cb[:], f_cb_d[l][:, :], W=[cb.b])
            sts = [list(range(s, min(s + 4, NFULL))) for s in range(0, NFULL, 4)]

            def ffn_supertile(tiles):
                NT = len(tiles)
                N = NT * 128
                for i, ft in enumerate(tiles):
                    rmsnorm_T((junk, ms, xn[i % 2]), hbuf[:, ft, :], [hbuf.bs[ft]], xnT, i * 128)
                for b in range(11):
                    wb = wblk[b % 3]
                    wv = wb[:].rearrange("p (c n) -> p c n", c=8)
                    S.dma("sp", wv, Wup_s[l][b], R=[scrB[id(Wup_s[l])]], W=[wb.b])
                    for jj in range(2):
                        res = []
                        for which in range(2):
                            f = which * 22 + b * 2 + jj
                            col = which * 256 + jj * 128
                            p = pb()
                            for c in range(8):
                                S.op("pe", lambda e, c=c, p=p, wv=wv, col=col: e.matmul(p[:, 0:N], wv[:, c, col:col + 128], xnT[:, c, 0:N],
                                                                                      start=(c == 0), stop=(c == 7)),
                                     R=[wb.b, xnT.b], W=[p.b])
                            k = (jj * 2 + which)
                            uu, aa = u[k], acc[k]
                            S.op("pool", lambda e, uu=uu, f=f: e.tensor_copy(uu[:, 0:2], fcarry[:, f, :]), R=[fcarry.b], W=[uu.b])
                            S.op("act", lambda e, uu=uu, p=p: e.copy(uu[:, 2:2 + N], p[:, 0:N]), R=[p.b], W=[uu.b])
                            S.op("pool", lambda e, uu=uu, f=f: e.tensor_copy(fcarry[:, f, :], uu[:, N:N + 2]), R=[uu.b], W=[fcarry.b])
                            S.op("dve", lambda e, uu=uu, aa=aa, f=f: e.tensor_scalar(aa[:, 0:N], uu[:, 2:2 + N], cw[:, f, 2:3], cb[:, f:f + 1], ALU.mult, ALU.add),
                                 R=[uu.b, cw.b, cb.b], W=[aa.b])
                            for j in (1, 0):
                                S.op("dve", lambda e, uu=uu, aa=aa, f=f, j=j: e.scalar_tensor_tensor(
                                    aa[:, 0:N], uu[:, j:j + N], cw[:, f, j:j + 1], aa[:, 0:N], ALU.mult, ALU.add),
                                    R=[uu.b, cw.b, aa.b], W=[aa.b])
                            res.append(aa)
                        g_ = gs[jj]
                        S.op("act", lambda e, g_=g_, a0=res[0]: e.activation(g_[:, 0:N], a0[:, 0:N], AF.Silu), R=[res[0].b], W=[g_.b])
                        S.op("pool", lambda e, g_=g_, a1=res[1], b=b, jj=jj: e.tensor_tensor(act[:, b * 2 + jj, 0:N], g_[:, 0:N], a1[:, 0:N], ALU.mult),
                             R=[g_.b, res[1].b], W=[act.bs[b * 2 + jj]])
                pys = [[pb(), pb()] for _ in range(NT)]
                for jb in range(6):
                    nj = 4 if jb < 5 else 2
                    wb = wblk[jb % 3]
                    wv = wb[:].rearrange("p (j n) -> p j n", j=4)
                    S.dma("sp", wv[:, 0:nj, :], Wdn_s[l][:, jb * 4:jb * 4 + nj, :], R=[scrB[id(Wdn_s[l])]], W=[wb.b])
                    for jl in range(nj):
                        j = jb * 4 + jl
                        for i in range(NT):
                            for half in range(2):
                                S.op("pe", lambda e, i=i, j=j, jl=jl, half=half, wv=wv: e.matmul(
                                    pys[i][half][:], act[:, j, i * 128:(i + 1) * 128], wv[:, jl, half * 512:(half + 1) * 512],
                                    start=(j == 0), stop=(j == 21)), R=[wb.b, act.bs[j]], W=[pys[i][half].b])
                for i, ft in enumerate(tiles):
                    for half in range(2):
                        S.op("dve", lambda e, i=i, ft=ft, half=half: e.scalar_tensor_tensor(
                            hbuf[:, ft, half * 512:(half + 1) * 512], pys[i][half][:], validt[:, ft:ft + 1],
                            hbuf[:, ft, half * 512:(half + 1) * 512], ALU.mult, ALU.add),
                            R=[pys[i][half].b, hbuf.bs[ft], validt.b], W=[hbuf.bs[ft]])

            for tiles in sts:
                ffn_supertile(tiles)
            S.barrier()

    if stage >= 2:
        ffn_phase(0)

    def attn_phase():
        with contextlib.ExitStack() as ph:
            NK = NFULL + 1
            xnT = sb(ph, "a_xnT", [128, 8, 512], BF16)
            xn = [sb(ph, f"a_xn{i}", [128, D], BF16) for i in range(2)]
            ms = sb(ph, "a_ms", [128, 4])
            wblk = [sb(ph, f"a_w{i}", [128, 8, 512], BF16) for i in range(3)]
            wk = [0]

            def wnext(src3, n):
                wb = wblk[wk[0] % 3]
                wk[0] += 1
                S.dma("sp", wb[:, :, 0:n], src3, R=[scrB[id(Wkv_s)], scrB[id(Wq_s)], scrB[id(Wo_s)]], W=[wb.b])
                return wb

            KT = sb(ph, "a_KT", [128, 4, NK * 128], BF16)
            Vt = sb(ph, "a_V", [128, NK, 256], BF16)
            QT = sb(ph, "a_QT", [128, 8, 512], BF16)
            BMf = sb(ph, "a_BMf", [128, 4, 256])
            BM = sb(ph, "a_BM", [128, 16, 256], BF16)
            am = sb(ph, "a_am", [128, 256])
            kbf = sb(ph, "a_kbf", [1, 512])
            kbb = sb(ph, "a_kbb", [1, NK * 128], BF16)
            sinkb = sb(ph, "a_sink", [128, 16])
            sc = [sb(ph, f"a_sc{i}", [128, 2, 256]) for i in range(3)]
            pr = [sb(ph, f"a_pr{i}", [128, 2, 256], BF16) for i in range(3)]
            pT = [sb(ph, f"a_pT{i}", [128, 4, 128], BF16) for i in range(3)]
            st_ = sb(ph, "a_st", [128, 6, 16])
            obf = sb(ph, "a_obf", [128, D], BF16)
            junk = obf
            oT = sb(ph, "a_oT", [128, 8, 128], BF16)
            S.dma("sp", am[:], amask_d[:, :], W=[am.b])
            S.dma("sp", sinkb[:], sink_d[:, :], W=[sinkb.b])
            for q4 in range(4):
                S.dma("sp", BMf[:], band_d[:, q4 * 4:(q4 + 1) * 4, :], W=[BMf.b])
                S.op("dve", lambda e, q4=q4: e.tensor_tensor(BM[:, q4 * 4:(q4 + 1) * 4, :], BMf[:], am[:].unsqueeze(1).to_broadcast([128, 4, 256]), ALU.add),
                     R=[BMf.b, am.b], W=[BM.b])
            for k0_ in range(0, NK * 128, 512):
                kn = min(512, NK * 128 - k0_)
                S.dma("sp", kbf[:, 0:kn], kbias_d[:, k0_:k0_ + kn], W=[kbf.b])
                S.op("dve", lambda e, k0_=k0_, kn=kn: e.tensor_copy(kbb[:, k0_:k0_ + kn], kbf[:, 0:kn]), R=[kbf.b], W=[kbb.b])
            S.op("pool", lambda e: e.memset(KT[:, :, 0:128], 0.0), W=[KT.b])
            S.op("pool", lambda e: e.memset(Vt[:, 0, :], 0.0), W=[Vt.b])
            sts = [list(range(s, min(s + 4, NFULL))) for s in range(0, NFULL, 4)]

            def attn_supertile(tiles):
                NT = len(tiles)
                N = NT * 128
                for i, ft in enumerate(tiles):
                    rmsnorm_T((junk, ms, xn[i % 2]), hbuf[:, ft, :], [hbuf.bs[ft]], xnT, i * 128)
                kc0 = (tiles[0] + 1) * 128
                wkK = wnext(Wkv_s[0], 512)
                for j in range(4):
                    p = pb()
                    for c in range(8):
                        S.op("pe", lambda e, c=c, p=p, j=j, wkK=wkK: e.matmul(p[:, 0:N], wkK[:, c, j * 128:(j + 1) * 128], xnT[:, c, 0:N],
                                                                     start=(c == 0), stop=(c == 7)), R=[wkK.b, xnT.b], W=[p.b])
                    S.op("act", lambda e, p=p, j=j: e.copy(KT[:, j, kc0:kc0 + N], p[:, 0:N]), R=[p.b], W=[KT.b])
                wkV = wnext(Wkv_s[1, :, :, 0:256], 256)
                for i, ft in enumerate(tiles):
                    p = pb()
                    for c in range(8):
                        S.op("pe", lambda e, c=c, p=p, i=i, wkV=wkV: e.matmul(p[:, 0:256], xnT[:, c, i * 128:(i + 1) * 128], wkV[:, c, 0:256],
                                                                     start=(c == 0), stop=(c == 7)), R=[wkV.b, xnT.b], W=[p.b])
                    S.op("dve", lambda e, p=p, ft=ft: e.tensor_copy(Vt[:, ft + 1, :], p[:, 0:256]), R=[p.b], W=[Vt.b])
                for f in range(8):
                    if f % 4 == 0:
                        wqb = wnext(Wq_s[f // 4], 512)
                    p = pb()
                    for c in range(8):
                        S.op("pe", lambda e, c=c, p=p, f=f, wqb=wqb: e.matmul(p[:, 0:N], wqb[:, c, (f % 4) * 128:(f % 4 + 1) * 128], xnT[:, c, 0:N],
                                                                     start=(c == 0), stop=(c == 7)), R=[wqb.b, xnT.b], W=[p.b])
                    S.op("act", lambda e, p=p, f=f: e.activation(QT[:, f, 0:N], p[:, 0:N], AF.Copy, scale=0.125), R=[p.b], W=[QT.b])
                for i, ft in enumerate(tiles):
                    attn_tile(i, ft)

            def attn_tile(i, ft):
                if True:
                    k0 = ft * 128
                    pO = [PB[6], PB[7]]
                    reserved.update((6, 7))
                    for f in range(8):
                        kv = f // 2
                        ps = pb()
                        for hh in range(2):
                            lo = hh * 64
                            S.op("pe", lambda e, ps=ps, hh=hh, lo=lo, f=f, kv=kv, i=i, k0=k0: e.matmul(
                                ps[:, hh * 256:(hh + 1) * 256], QT[lo:lo + 64, f, i * 128:(i + 1) * 128], KT[lo:lo + 64, kv, k0:k0 + 256],
                                start=True, stop=False), R=[QT.b, KT.b], W=[ps.b])
                            S.op("pe", lambda e, ps=ps, hh=hh, k0=k0: e.matmul(
                                ps[:, hh * 256:(hh + 1) * 256], onesb[0:1, 0:128], kbb[0:1, k0:k0 + 256], start=False, stop=True),
                                R=[onesb.b, kbb.b], W=[ps.b])
                        s_, p_, t_ = sc[f % 3], pr[f % 3], pT[f % 3]
                        S.op("dve", lambda e, ps=ps, s_=s_, f=f: e.tensor_tensor(s_[:], ps[:].rearrange("p (h k) -> p h k", h=2), BM[:, 2 * f:2 * f + 2, :], ALU.add),
                             R=[ps.b, BM.b], W=[s_.b])
                        S.op("dve", lambda e, s_=s_, f=f: e.tensor_reduce(st_[:, 0, 2 * f:2 * f + 2], s_[:], AX.X, ALU.max), R=[s_.b], W=[st_.b])
                        S.op("dve", lambda e, f=f: e.tensor_tensor(st_[:, 0, 2 * f:2 * f + 2], st_[:, 0, 2 * f:2 * f + 2], sinkb[:, 2 * f:2 * f + 2], ALU.max),
                             R=[st_.b, sinkb.b], W=[st_.b])
                        S.op("dve", lambda e, f=f: e.tensor_scalar(st_[:, 1, 2 * f:2 * f + 2], st_[:, 0, 2 * f:2 * f + 2], -1.0, None, ALU.mult),
                             R=[st_.b], W=[st_.b])
                        for hh in range(2):
                            h_ = 2 * f + hh
                            S.op("act", lambda e, s_=s_, p_=p_, hh=hh, h_=h_: e.activation(p_[:, hh, :], s_[:, hh, :], AF.Exp, bias=st_[:, 1, h_:h_ + 1],
                                                                                           accum_out=st_[:, 2, h_:h_ + 1]),
                                 R=[s_.b, st_.b], W=[p_.b, st_.b])
                        pt = pb()
                        ptv = pt[:].bitcast(BF16)
                        for hh in range(2):
                            for kb in range(2):
                                S.op("pe", lambda e, ptv=ptv, pt=pt, p_=p_, hh=hh, kb=kb: e.transpose(
                                    ptv[:, (hh * 2 + kb) * 128:(hh * 2 + kb + 1) * 128], p_[:, hh, kb * 128:(kb + 1) * 128], identb[:]),
                                    R=[p_.b, identb.b], W=[pt.b])
                        if f % 2 == 0:
                            S.op("act", lambda e, ptv=ptv, t_=t_: e.copy(t_[:].rearrange("p a b -> p (a b)"), ptv[:, 0:512]), R=[pt.b], W=[t_.b])
                        else:
                            S.op("dve", lambda e, ptv=ptv, t_=t_: e.tensor_copy(t_[:].rearrange("p a b -> p (a b)"), ptv[:, 0:512]), R=[pt.b], W=[t_.b])
                        for hh in range(2):
                            h_ = 2 * f + hh
                            for kb in range(2):
                                S.op("pe", lambda e, t_=t_, hh=hh, kb=kb, h_=h_, kv=kv, ft=ft: e.matmul(
                                    pO[h_ // 8][:, (h_ % 8) * 64:(h_ % 8 + 1) * 64], t_[:, hh * 2 + kb, :], Vt[:, ft + kb, kv * 64:(kv + 1) * 64],
                                    start=(kb == 0), stop=(kb == 1)), R=[t_.b, Vt.b], W=[pO[h_ // 8].b])
                    reserved.clear()
                    S.op("dve", lambda e: e.tensor_tensor(st_[:, 3, :], sinkb[:], st_[:, 1, :], ALU.add), R=[st_.b, sinkb.b], W=[st_.b])
                    S.op("act", lambda e: e.activation(st_[:, 3, :], st_[:, 3, :], AF.Exp), R=[st_.b], W=[st_.b])
                    S.op("dve", lambda e: e.tensor_tensor(st_[:, 3, :], st_[:, 3, :], st_[:, 2, :], ALU.add), R=[st_.b], W=[st_.b])
                    S.op("dve", lambda e: e.reciprocal(st_[:, 4, :], st_[:, 3, :]), R=[st_.b], W=[st_.b])
                    for half in range(2):
                        S.op("dve", lambda e, half=half: e.tensor_tensor(
                            obf[:, half * 512:(half + 1) * 512].rearrange("p (h d) -> p h d", h=8), pO[half][:].rearrange("p (h d) -> p h d", h=8),
                            st_[:, 4, half * 8:(half + 1) * 8].unsqueeze(2).to_broadcast([128, 8, 64]), ALU.mult),
                            R=[pO[half].b, st_.b], W=[obf.b])
                    p = pb()
                    pv = p[:].bitcast(BF16)
                    for c in range(8):
                        S.op("pe", lambda e, c=c, pv=pv, p=p: e.transpose(pv[:, c * 128:(c + 1) * 128], obf[:, c * 128:(c + 1) * 128], identb[:]),
                             R=[obf.b, identb.b], W=[p.b])
                    S.op("act", lambda e, pv=pv: e.copy(oT[:].rearrange("p c t -> p (c t)"), pv), R=[p.b], W=[oT.b])
                    py = [pb(), pb()]
                    for half in range(2):
                        wob = wnext(Wo_s[half], 512)
                        for c in range(8):
                            S.op("pe", lambda e, c=c, half=half, wob=wob: e.matmul(py[half][:], oT[:, c, :], wob[:, c, :],
                                                                           start=(c == 0), stop=(c == 7)), R=[oT.b, wob.b], W=[py[half].b])
                        S.op("dve", lambda e, half=half, ft=ft: e.scalar_tensor_tensor(
                            hbuf[:, ft, half * 512:(half + 1) * 512], py[half][:], validt[:, ft:ft + 1],
                            hbuf[:, ft, half * 512:(half + 1) * 512], ALU.mult, ALU.add),
                            R=[py[half].b, hbuf.bs[ft], validt.b], W=[hbuf.bs[ft]])

            for tiles in sts:
                attn_supertile(tiles)
            S.barrier()

    if stage >= 3:
        attn_phase()
    if stage >= 4:
        ffn_phase(1)

    toks = []
    with contextlib.ExitStack() as ph:
        fnw = sb(ph, "fnw", [128, D])
        junk = sb(ph, "o_junk", [128, D], BF16)
        ms = sb(ph, "o_ms", [128, 4])
        ob = [sb(ph, f"o_b{i}", [128, D]) for i in range(2)]
        S.dma("sp", fnw[:], fin_w_d[:, :], W=[fnw.b])
        for t in range(NOWN):
            ft = t + NHALO
            o_ = ob[t % 2]
            if stage >= 5:
                S.op("act", lambda e, ft=ft: e.activation(junk[:], hbuf[:, ft, :], AF.Square, scale=1.0 / 32.0, accum_out=ms[:, 0:1]),
                     R=[hbuf.bs[ft]], W=[junk.b, ms.b])
                S.op("act", lambda e: e.activation(ms[:, 1:2], ms[:, 0:1], AF.Ln, bias=EPS), R=[ms.b, cst.b], W=[ms.b])
                S.op("act", lambda e: e.activation(ms[:, 2:3], ms[:, 1:2], AF.Exp, scale=-0.5), R=[ms.b], W=[ms.b])
                S.op("dve", lambda e, ft=ft, o_=o_: e.scalar_tensor_tensor(o_[:], hbuf[:, ft, :], ms[:, 2:3], fnw[:], ALU.mult, ALU.mult),
                     R=[hbuf.bs[ft], ms.b, fnw.b], W=[o_.b])
            else:
                S.op("dve", lambda e, ft=ft, o_=o_: e.tensor_copy(o_[:], hbuf[:, ft, :]), R=[hbuf.bs[ft]], W=[o_.b])
            toks.append(S.dma("sp", out_d[t * 128:(t + 1) * 128, :], o_[:], R=[o_.b]))
        S.finish(toks)
    root.close()
    return nc


def _t5_bucket(dist):
    n = np.maximum(dist, 0)
    nf = np.maximum(n, 1).astype(np.float32)
    large = 16 + (np.log(nf / 16) / np.log(128 / 16) * 16).astype(np.int32)
    large = np.minimum(large, 31)
    return np.where(n < 16, n, large)


def make_inputs(inp, nhist, cores):
    f32 = np.float32
    A = lambda v: np.ascontiguousarray(np.asarray(v), dtype=f32)
    x = A(inp["x"])

    def pc(v):
        return np.ascontiguousarray(A(v).reshape(8, 128).T)

    common = {
        "a_w_in": A(inp["a_w_in"][0]),
        "a_w_out": A(inp["a_w_out"][0]),
        "ffn_w_up0": A(inp["ffn_w_up"][0]), "ffn_w_up1": A(inp["ffn_w_up"][1]),
        "ffn_w_down0": A(inp["ffn_w_down"][0]), "ffn_w_down1": A(inp["ffn_w_down"][1]),
        "b_w_q": A(inp["b_w_q"][0]), "b_w_o": A(inp["b_w_o"][0]),
        "a_norm_wT": pc(inp["a_norm_w"][0]),
        "ffn_norm_wT0": pc(inp["ffn_norm_w"][0]), "ffn_norm_wT1": pc(inp["ffn_norm_w"][1]),
        "kv_norm_wT": pc(inp["kv_norm_w"]), "b_norm_wT": pc(inp["b_norm_w"][0]),
        "out_norm_wT": A(inp["a_out_norm_w"][0]).reshape(128, 1),
        "final_norm_wb": np.ascontiguousarray(np.broadcast_to(A(inp["final_norm_w"])[None, :], (128, D))),
        "a_conv_wT": np.ascontiguousarray(A(inp["a_conv_w"][0]).T.reshape(32, 128, 4).transpose(1, 0, 2)),
        "a_log_b": np.ascontiguousarray(np.broadcast_to(A(inp["a_a_log"][0])[None, :], (128, 16))),
        "dt_bias_b": np.ascontiguousarray(np.broadcast_to(A(inp["a_dt_bias"][0])[None, :], (128, 16))),
        "sinks_b": np.ascontiguousarray(np.broadcast_to(A(inp["b_sinks"][0])[None, :], (128, 16))),
    }
    for l in range(2):
        common[f"ffn_conv_wT{l}"] = np.ascontiguousarray(A(inp["ffn_conv_w"][l]).T.reshape(44, 128, 3).transpose(1, 0, 2))
        common[f"ffn_conv_bT{l}"] = np.ascontiguousarray(A(inp["ffn_conv_b"][l]).reshape(44, 128).T)
    wkv = A(inp["w_kv"])
    cols = []
    for j in range(4):
        cols += [wkv[:, j * 64:(j + 1) * 64], wkv[:, j * 64:(j + 1) * 64]]
    cols.append(wkv[:, 256:512])
    common["w_kv_dup"] = np.ascontiguousarray(np.concatenate(cols, axis=1))
    qi = np.arange(128)[:, None]
    ki = np.arange(256)[None, :]
    dist = qi + 128 - ki
    bucket = _t5_bucket(dist)
    tab = A(inp["rel_bias_table"])
    common["biasband"] = np.ascontiguousarray(tab[bucket].transpose(0, 2, 1))
    inwin = (dist >= 0) & (dist < 128)
    common["attnmask"] = np.where(inwin, 0.0, NEG).astype(f32)
    common["ident"] = np.eye(128, dtype=f32)
    p_ = np.arange(128)[:, None]
    j_ = np.arange(128)[None, :]
    common["triu"] = (p_ <= j_).astype(f32)
    common["maskL"] = np.where(p_ > j_, 0.0, NEG).astype(f32)
    common["maskU"] = np.where(j_ >= p_, 0.0, NEG).astype(f32)
    NT_ALL = nhist + NFULL
    maps = []
    for c in cores:
        b, j = c // 4, c % 4
        end = 2048 * (j + 1)
        start = end - NT_ALL * 128
        xe = np.zeros((NT_ALL * 128, D), f32)
        s0 = max(start, 0)
        xe[s0 - start:] = x[b, s0:end]
        pos_full = np.arange(end - NFULL * 128, end)
        valid = (pos_full >= 0).astype(f32).reshape(NFULL, 128).T
        kb = np.concatenate([np.full(128, NEG, f32), np.where(pos_full >= 0, 0.0, NEG).astype(f32)])[None, :]
        m = dict(common)
        m["x_ext"] = xe
        m["valid"] = np.ascontiguousarray(valid)
        m["kbias"] = np.ascontiguousarray(kb)
        maps.append(m)
    return maps


_NHIST = 46


def kernel(**inputs):
    nc = build_program(_NHIST)
    maps = make_inputs(inputs, _NHIST, list(range(8)))
    res = run_bass_kernel_spmd(nc, maps, core_ids=list(range(8)))
    out = np.empty((2, 8192, D), np.float32)
    for c in range(8):
        b, j = c // 4, c % 4
        out[b, 2048 * j:2048 * (j + 1)] = res.results[c]["out"]
    return out
```

```python
import contextlib
import numpy as np
import concourse.bass as bass
import concourse.mybir as mybir
from concourse.bass_utils import run_bass_kernel_spmd

F32 = mybir.dt.float32
BF16 = mybir.dt.bfloat16
AF = mybir.ActivationFunctionType
ALU = mybir.AluOpType
AX = mybir.AxisListType

D = 1024
NFULL = 18
NHALO = 2
NOWN = 16
NEG = -1.0e30
EPOCH = 30000


class Buf:
    __slots__ = ("name", "w", "r", "excl")

    def __init__(self, name="", excl=False):
        self.name = name
        self.w = None
        self.r = []
        self.excl = excl


class Sched:
    ENGS = ("pe", "act", "dve", "pool", "sp")

    def __init__(self, nc, n_dma_sems=10):
        self.nc = nc
        self.prog = {e: [] for e in self.ENGS}
        self.cnt = {e: 0 for e in self.ENGS}
        self.sems = {}
        self.seen = {e: {} for e in self.ENGS}
        self.dma_sems = {}
        self.n_dma_sems = n_dma_sems
        self.dma_rr = {e: 0 for e in self.ENGS}
        self._semctx = []
        self.last_tok = {}

    def _new_sem(self, name):
        ctx = self.nc.semaphore(name)
        h = ctx.__enter__()
        self._semctx.append(ctx)
        return h

    def _eng_sem(self, eng, idx):
        key = (eng, idx // EPOCH)
        if key not in self.sems:
            self.sems[key] = self._new_sem(f"s_{eng}_{idx // EPOCH}")
        return self.sems[key], (idx % EPOCH) + 1

    def _wait(self, eng, tok):
        teng, sem, val = tok
        if teng == eng and eng == "pe":
            return
        seen = self.seen[eng]
        if seen.get(sem.name, 0) >= val:
            return
        seen[sem.name] = val
        self.prog[eng].append(lambda e, sem=sem, val=val: e.wait_ge(sem, val))

    def _deps(self, eng, reads, writes):
        for b in reads:
            if b.w is not None:
                self._wait(eng, b.w)
            if b.excl:
                for t in b.r:
                    if t[0] != eng:
                        self._wait(eng, t)
        for b in writes:
            if b.w is not None and b.w[0] != eng:
                self._wait(eng, b.w)
            for t in b.r:
                if t[0] != eng:
                    self._wait(eng, t)

    def _commit(self, tok, reads, writes):
        for b in reads:
            b.r.append(tok)
        for b in writes:
            b.w = tok
            b.r = []

    def op(self, eng, fn, R=(), W=()):
        self._deps(eng, R, W)
        idx = self.cnt[eng]
        self.cnt[eng] += 1
        sem, val = self._eng_sem(eng, idx)
        self.prog[eng].append(lambda e, fn=fn, sem=sem: fn(e).then_inc(sem, 1))
        tok = (eng, sem, val)
        self.last_tok[eng] = tok
        self._commit(tok, R, W)
        return tok

    def dma(self, eng, out, in_, R=(), W=()):
        k = self.dma_rr[eng]
        self.dma_rr[eng] = (k + 1) % self.n_dma_sems
        key = (eng, k)
        if key not in self.dma_sems:
            self.dma_sems[key] = [self._new_sem(f"d_{eng}_{k}"), 0]
        ent = self.dma_sems[key]
        sem, tot = ent
        if tot > 0:
            self._wait(eng, ("dma", sem, tot))
        self._deps(eng, R, W)
        ent[1] = tot + 16
        self.prog[eng].append(
            lambda e, out=out, in_=in_, sem=sem: e.dma_start(out=out, in_=in_).then_inc(sem, 16))
        tok = ("dma", sem, tot + 16)
        self._commit(tok, R, W)
        return tok

    def barrier(self):
        toks = list(self.last_tok.values())
        for (eng, k), (sem, tot) in self.dma_sems.items():
            if tot > 0:
                toks.append(("dma", sem, tot))
        for e in self.ENGS:
            for t in toks:
                if t[0] != e or e != "pe":
                    self._wait(e, t)

    def finish(self, final_toks):
        for t in final_toks:
            self._wait("sp", t)
        nc = self.nc
        with nc.Block() as block:
            @block.tensor
            def _(e):
                for f in self.prog["pe"]:
                    f(e)

            @block.scalar
            def _(e):
                for f in self.prog["act"]:
                    f(e)

            @block.vector
            def _(e):
                for f in self.prog["dve"]:
                    f(e)

            @block.gpsimd
            def _(e):
                for f in self.prog["pool"]:
                    f(e)

            @block.sync
            def _(e):
                for f in self.prog["sp"]:
                    f(e)
        for ctx in reversed(self._semctx):
            ctx.__exit__(None, None, None)


class T:
    def __init__(self, t, name, nslots=0):
        self.t = t
        self.b = Buf(name)
        self.bs = [Buf(f"{name}{i}") for i in range(nslots)]

    def __getitem__(self, k):
        return self.t[k]


def build_program(nhist, stage=99, cut=99):
    nc = bass.Bass("TRN2", target_bir_lowering=False)
    S = Sched(nc)
    NT_ALL = nhist + NFULL

    def din(name, shape, dt=F32):
        return nc.dram_tensor(name, list(shape), dt, kind="ExternalInput").ap()

    x_ext = din("x_ext", [NT_ALL * 128, D])
    valid_d = din("valid", [128, NFULL])
    kbias_d = din("kbias", [1, (NFULL + 1) * 128])
    w_in_d = din("a_w_in", [D, 6176])
    w_out_d = din("a_w_out", [2048, D])
    w_up_d = [din(f"ffn_w_up{l}", [D, 5632]) for l in range(2)]
    w_dn_d = [din(f"ffn_w_down{l}", [2816, D]) for l in range(2)]
    w_kv_d = din("w_kv_dup", [D, 768])
    w_q_d = din("b_w_q", [D, D])
    w_o_d = din("b_w_o", [D, D])
    a_nw_d = din("a_norm_wT", [128, 8])
    f_nw_d = [din(f"ffn_norm_wT{l}", [128, 8]) for l in range(2)]
    kv_nw_d = din("kv_norm_wT", [128, 8])
    b_nw_d = din("b_norm_wT", [128, 8])
    on_w_d = din("out_norm_wT", [128, 1])
    fin_w_d = din("final_norm_wb", [128, D])
    a_cw_d = din("a_conv_wT", [128, 32, 4])
    f_cw_d = [din(f"ffn_conv_wT{l}", [128, 44, 3]) for l in range(2)]
    f_cb_d = [din(f"ffn_conv_bT{l}", [128, 44]) for l in range(2)]
    alog_d = din("a_log_b", [128, 16])
    dtb_d = din("dt_bias_b", [128, 16])
    sink_d = din("sinks_b", [128, 16])
    band_d = din("biasband", [128, 16, 256])
    amask_d = din("attnmask", [128, 256])
    ident_d = din("ident", [128, 128])
    triu_d = din("triu", [128, 128])
    maskL_d = din("maskL", [128, 128])
    maskU_d = din("maskU", [128, 128])
    out_d = nc.dram_tensor("out", [NOWN * 128, D], F32, kind="ExternalOutput").ap()

    def dscr(name, shape):
        return nc.dram_tensor(name, list(shape), BF16).ap()

    Win_s = dscr("Win_b", [25, 128, 8, 256])
    Wout_s = dscr("Wout_s", [128, 16, D])
    Wup_s = [dscr(f"Wup_b{l}", [11, 128, 8, 512]) for l in range(2)]
    Wdn_s = [dscr(f"Wdn_s{l}", [128, 22, D]) for l in range(2)]
    Wkv_s = dscr("Wkv_b", [2, 128, 8, 512])
    Wq_s = dscr("Wq_b", [2, 128, 8, 512])
    Wo_s = dscr("Wo_b", [2, 128, 8, 512])
    scrB = {id(a): Buf("scr") for a in [Win_s, Wout_s, Wkv_s, Wq_s, Wo_s] + Wup_s + Wdn_s}

    uid = [0]

    def sb(stk, name, shape, dt=F32, nslots=0):
        uid[0] += 1
        t = stk.enter_context(nc.sbuf_tensor(f"{name}_{uid[0]}", list(shape), dt))
        return T(t, name, nslots)

    root = contextlib.ExitStack()
    PB = [T(root.enter_context(nc.psum_tensor(f"pb{i}", [128, 512], F32)), f"pb{i}") for i in range(8)]
    for p_ in PB:
        p_.b.excl = True
    pbi = [0]

    reserved = set()

    def pb():
        while True:
            k = pbi[0] % 8
            pbi[0] += 1
            if k not in reserved:
                return PB[k]

    identf = sb(root, "identf", [128, 128])
    identb = sb(root, "identb", [128, 128], BF16)
    onesb = sb(root, "onesb", [128, 128], BF16)
    triub = sb(root, "triub", [128, 128], BF16)
    maskLt = sb(root, "maskLt", [128, 128])
    maskUt = sb(root, "maskUt", [128, 128])
    cst = sb(root, "cst", [128, 8])
    validt = sb(root, "validt", [128, NFULL])
    tmpc = sb(root, "tmpc", [128, 128])

    S.dma("sp", identf[:], ident_d[:, :], W=[identf.b])
    S.op("dve", lambda e: e.tensor_copy(identb[:], identf[:]), R=[identf.b], W=[identb.b])
    S.op("pool", lambda e: e.memset(onesb[:], 1.0), W=[onesb.b])
    S.dma("sp", tmpc[:], triu_d[:, :], W=[tmpc.b])
    S.op("dve", lambda e: e.tensor_copy(triub[:], tmpc[:]), R=[tmpc.b], W=[triub.b])
    S.dma("sp", maskLt[:], maskL_d[:, :], W=[maskLt.b])
    S.dma("sp", maskUt[:], maskU_d[:, :], W=[maskUt.b])
    S.op("pool", lambda e: e.memset(cst[:, 0:1], 1e-6), W=[cst.b])
    S.op("pool", lambda e: e.memset(cst[:, 1:2], 1.0), W=[cst.b])
    S.op("pool", lambda e: e.memset(cst[:, 2:3], float(np.log(128.0 ** -0.5))), W=[cst.b])
    S.op("pool", lambda e: e.memset(cst[:, 3:4], 0.0), W=[cst.b])
    S.dma("sp", validt[:], valid_d[:, :], W=[validt.b])
    EPS = cst[:, 0:1]
    ONE = cst[:, 1:2]

    gdn = root
    S32 = sb(gdn, "S32", [128, 16, 128], F32, nslots=4)
    Sbf = sb(gdn, "Sbf", [128, 16, 128], BF16, nslots=4)
    carry = sb(gdn, "carry", [128, 32, 3])
    convw = sb(gdn, "convw", [128, 32, 4])
    wba = sb(gdn, "wba", [128, 8, 32], BF16)
    negA = sb(gdn, "negA", [128, 16])
    dtb = sb(gdn, "dtb", [128, 16])
    S.op("pool", lambda e: e.memset(S32[:], 0.0), W=[S32.b] + S32.bs)
    S.op("pool", lambda e: e.memset(Sbf[:], 0.0), W=[Sbf.b] + Sbf.bs)
    S.op("pool", lambda e: e.memset(carry[:], 0.0), W=[carry.b])
    S.dma("sp", convw[:], a_cw_d[:, :, :], W=[convw.b])
    S.dma("sp", negA[:], alog_d[:, :], W=[negA.b])
    S.dma("sp", dtb[:], dtb_d[:, :], W=[dtb.b])
    S.op("act", lambda e: e.activation(negA[:], negA[:], AF.Exp), R=[negA.b], W=[negA.b])
    S.op("dve", lambda e: e.tensor_scalar(negA[:], negA[:], -1.0, None, ALU.mult), R=[negA.b], W=[negA.b])

    ph0 = contextlib.ExitStack()
    bgq = []
    if True:
        ph = ph0
        stg = [sb(ph, f"stg{i}", [128, 2048]) for i in range(2)]
        cvt = [sb(ph, f"cvt{i}", [128, 2048], BF16) for i in range(2)]
        nws = sb(ph, "nws", [128, 8 * 5 + 1])
        nw_ap = {}
        for i, (nm, d_) in enumerate([("a", a_nw_d), ("f0", f_nw_d[0]), ("f1", f_nw_d[1]), ("kv", kv_nw_d), ("b", b_nw_d)]):
            S.dma("sp", nws[:, i * 8:(i + 1) * 8], d_[:, :], W=[nws.b])
            nw_ap[nm] = (i * 8)
        S.dma("sp", nws[:, 40:41], on_w_d[:, :], W=[nws.b])
        blk = [0]

        def convert(src3, dst3, C, N, scale=None, dstfn=None, ng=512, defer=False):
            ng = min(N, ng)
            cg = max(1, min(C, 2048 // ng))

            def emit(c0, cc, n0, nn):
                i = blk[0] % 2
                blk[0] += 1
                st, cv = stg[i], cvt[i]
                sv = st[:, 0:cc * nn].rearrange("p (c n) -> p c n", c=cc)
                cvv = cv[:, 0:cc * nn].rearrange("p (c n) -> p c n", c=cc)
                S.dma("sp", sv, src3[:, c0:c0 + cc, n0:n0 + nn], W=[st.b])
                eng = "dve" if (blk[0] % 2 == 0) else "pool"
                if scale is None:
                    S.op(eng, lambda e: e.tensor_copy(cvv, sv), R=[st.b], W=[cv.b])
                elif scale[0] == "pc":
                    g = nws[:, scale[1] + c0:scale[1] + c0 + cc].unsqueeze(2).to_broadcast([128, cc, nn])
                    S.op(eng, lambda e: e.tensor_tensor(cvv, sv, g, ALU.mult), R=[st.b, nws.b], W=[cv.b])
                else:
                    g = nws[:, scale[1]:scale[1] + 1].unsqueeze(2).to_broadcast([128, cc, nn])
                    S.op(eng, lambda e: e.tensor_tensor(cvv, sv, g, ALU.mult), R=[st.b, nws.b], W=[cv.b])
                dap = dst3[:, c0:c0 + cc, n0:n0 + nn] if dstfn is None else dstfn(c0, cc, n0, nn)
                S.dma("pool", dap, cvv, R=[cv.b], W=[scrB[id(dst3)]])

            for c0 in range(0, C, cg):
                cc = min(cg, C - c0)
                for n0 in range(0, N, ng):
                    nn = min(ng, N - n0)
                    if defer:
                        bgq.append(lambda c0=c0, cc=cc, n0=n0, nn=nn: emit(c0, cc, n0, nn))
                    else:
                        emit(c0, cc, n0, nn)

        def pcn(ap):
            return ap.rearrange("(c p) n -> p c n", p=128)

        convert(pcn(w_in_d), Win_s, 8, 6176, ("pc", nw_ap["a"]), ng=256,
                dstfn=lambda c0, cc, n0, nn: Win_s[n0 // 256, :, c0:c0 + cc, 0:nn])
        convert(pcn(w_out_d), Wout_s, 16, D, ("p", 40))
        if stage >= 2:
            for l in range(2):
                convert(pcn(w_up_d[l]), Wup_s[l], 8, 5632, ("pc", nw_ap[f"f{l}"]), ng=256,
                        dstfn=lambda c0, cc, n0, nn, l=l: (Wup_s[l][n0 // 256, :, c0:c0 + cc, 0:256] if n0 < 2816
                                                           else Wup_s[l][(n0 - 2816) // 256, :, c0:c0 + cc, 256:512]), defer=True)
                convert(pcn(w_dn_d[l]), Wdn_s[l], 22, D, None, defer=True)
        if stage >= 3:
            convert(pcn(w_kv_d), Wkv_s, 8, 768, ("pc", nw_ap["kv"]), ng=256,
                    dstfn=lambda c0, cc, n0, nn: (Wkv_s[0, :, c0:c0 + cc, n0:n0 + nn] if n0 < 512 else Wkv_s[1, :, c0:c0 + cc, 0:nn]), defer=True)
            convert(pcn(w_q_d), Wq_s, 8, D, ("pc", nw_ap["b"]), ng=512,
                    dstfn=lambda c0, cc, n0, nn: Wq_s[n0 // 512, :, c0:c0 + cc, 0:nn], defer=True)
            convert(pcn(w_o_d), Wo_s, 8, D, None, ng=512,
                    dstfn=lambda c0, cc, n0, nn: Wo_s[n0 // 512, :, c0:c0 + cc, 0:nn], defer=True)
        S.dma("sp", wba[:], Win_s[24, :, :, 0:32], R=[scrB[id(Win_s)]], W=[wba.b])

    def run_bg(n):
        for _ in range(min(n, len(bgq))):
            bgq.pop(0)()


    def rmsnorm_T(ph_bufs, src_ap, src_bufs, xnT, col0):
        junk, ms, xn = ph_bufs
        S.op("act", lambda e: e.activation(junk[:], src_ap, AF.Square, scale=1.0 / 32.0, accum_out=ms[:, 0:1]),
             R=src_bufs, W=[junk.b, ms.b])
        S.op("act", lambda e: e.activation(ms[:, 1:2], ms[:, 0:1], AF.Ln, bias=EPS), R=[ms.b, cst.b], W=[ms.b])
        S.op("act", lambda e: e.activation(ms[:, 2:3], ms[:, 1:2], AF.Exp, scale=-0.5), R=[ms.b], W=[ms.b])
        S.op("dve", lambda e: e.tensor_scalar(xn[:], src_ap, ms[:, 2:3], None, ALU.mult), R=src_bufs + [ms.b], W=[xn.b])
        p = pb()
        pv = p[:].bitcast(BF16)
        for c in range(8):
            S.op("pe", lambda e, c=c: e.transpose(pv[:, c * 128:(c + 1) * 128], xn[:, c * 128:(c + 1) * 128], identb[:]),
                 R=[xn.b, identb.b], W=[p.b])
        S.op("act", lambda e: e.copy(xnT[:, :, col0:col0 + 128], pv.rearrange("p (c t) -> p c t", c=8)),
             R=[p.b], W=[xnT.b])

    def wload(B, k, src3, nparts):
        wb = B["wblk"][k % len(B["wblk"])]
        n = src3.shape[2]
        wv = wb[:, 0:nparts * n].rearrange("p (c n) -> p c n", c=nparts)
        return wb, wv

    def gdn_st_norm(B, tiles, full, hbuf, sp):
        xnT = B["xnT"][sp % 2]
        for i, t in enumerate(tiles):
            if full:
                ft = t - nhist
                src, sbufs = hbuf[:, ft, :], [hbuf.bs[ft]]
            else:
                xs = B["xs"][t % 2]
                src, sbufs = xs[:], [xs.b]
            S.dma("sp", src, x_ext[t * 128:(t + 1) * 128, :], W=sbufs)
            rmsnorm_T((B["junk"], B["ms"], B["xn"][i % 2]), src, sbufs, xnT, i * 128)

    def gdn_st_rest(B, tiles, full, hbuf, sp, has_next=False):
        NT = len(tiles)
        N = NT * 128
        WB = B["WB"]
        CPB = WB // 128
        xnT, qkvT = B["xnT"][sp % 2], B["qkvT"]
        gdn_st_small(B, xnT, NT, sp)
        f0 = 0 if full else 8
        wb = None
        for f in range(f0, 32):
            if f % CPB == 0 or wb is None:
                blk = f // CPB
                if blk in B["pref"]:
                    wb, wv = B["pref"].pop(blk)
                else:
                    wb, wv = wload(B, blk, Win_s[blk], 8)
                    S.dma("sp", wv, Win_s[blk], R=[scrB[id(Win_s)]], W=[wb.b])
            p = pb()
            for c in range(8):
                S.op("pe", lambda e, c=c, wv=wv, f=f, p=p: e.matmul(p[:, 0:N], wv[:, c, (f % CPB) * 128:(f % CPB + 1) * 128],
                                                                    xnT[:, c, 0:N], start=(c == 0), stop=(c == 7)),
                     R=[wb.b, xnT.b], W=[p.b])
            u = B["u"][f % 2]
            acc = B["acc"][f % 2]
            S.op("pool", lambda e, u=u, f=f: e.tensor_copy(u[:, 0:3], carry[:, f, :]), R=[carry.b], W=[u.b])
            S.op("act", lambda e, u=u, p=p: e.copy(u[:, 3:3 + N], p[:, 0:N]), R=[p.b], W=[u.b])
            S.op("pool", lambda e, u=u, f=f: e.tensor_copy(carry[:, f, :], u[:, N:N + 3]), R=[u.b], W=[carry.b])
            S.op("dve", lambda e, u=u, acc=acc, f=f: e.tensor_scalar(acc[:, 0:N], u[:, 3:3 + N], convw[:, f, 3:4], None, ALU.mult),
                 R=[u.b, convw.b], W=[acc.b])
            for j in (2, 1, 0):
                S.op("dve", lambda e, u=u, acc=acc, f=f, j=j: e.scalar_tensor_tensor(
                    acc[:, 0:N], u[:, j:j + N], convw[:, f, j:j + 1], acc[:, 0:N], ALU.mult, ALU.add),
                    R=[u.b, convw.b, acc.b], W=[acc.b])
            S.op("act", lambda e, acc=acc, f=f: e.activation(qkvT[:, f, 0:N], acc[:, 0:N], AF.Silu),
                 R=[acc.b], W=[qkvT.bs[f]])
        for f in range(f0, 16):
            sq = B["sq"][f % 2]
            S.op("pool", lambda e, sq=sq, f=f: e.tensor_tensor(sq[:, 0:N], qkvT[:, f, 0:N], qkvT[:, f, 0:N], ALU.mult),
                 R=[qkvT.bs[f]], W=[sq.b])
            p = pb()
            S.op("pe", lambda e, p=p, sq=sq: e.matmul(p[:, 0:N], onesb[:], sq[:, 0:N], start=True, stop=True),
                 R=[onesb.b, sq.b], W=[p.b])
            rn = B["acc"][f % 2]
            S.op("act", lambda e, p=p, rn=rn: e.activation(rn[:, 0:N], p[:, 0:N], AF.Ln, bias=EPS), R=[p.b, cst.b], W=[rn.b])
            bias = cst[:, 2:3] if f < 8 else cst[:, 3:4]
            S.op("act", lambda e, rn=rn, bias=bias: e.activation(rn[:, 0:N], rn[:, 0:N], AF.Exp, scale=-0.5, bias=bias),
                 R=[rn.b, cst.b], W=[rn.b])
            S.op("dve", lambda e, rn=rn, f=f: e.tensor_tensor(qkvT[:, f, 0:N], qkvT[:, f, 0:N], rn[:, 0:N], ALU.mult),
                 R=[rn.b, qkvT.bs[f]], W=[qkvT.bs[f]])
        if cut < 2:
            return
        if has_next and not full:
            for blk in range(f0 // CPB, f0 // CPB + len(B["wblk"])):
                wb, wv = wload(B, blk, Win_s[blk], 8)
                S.dma("sp", wv, Win_s[blk], R=[scrB[id(Win_s)]], W=[wb.b])
                B["pref"][blk] = (wb, wv)
        for i, t in enumerate(tiles):
            gdn_tile(B, xnT, i, t, full, hbuf, sp)

    def gdn_st_small(B, xnT, NT, sp):
        sm = B["sm"][sp % 2]
        smb = B["smb"][sp % 2]
        W_ = NT * 16
        SM = lambda k: sm[:, k, 0:W_].rearrange("p (i h) -> p i h", h=16)
        SB = lambda k: smb[:, k, 0:W_].rearrange("p (i h) -> p i h", h=16)
        r = lambda k: sm.bs[k]
        rb = lambda k: smb.bs[k]
        bc = lambda t_: t_[:].unsqueeze(1).to_broadcast([128, NT, 16])
        pba = pb()
        for i in range(NT):
            for c in range(8):
                S.op("pe", lambda e, c=c, i=i: e.matmul(pba[:, i * 32:(i + 1) * 32], xnT[:, c, i * 128:(i + 1) * 128], wba[:, c, :],
                                                        start=(c == 0), stop=(c == 7)), R=[xnT.b, wba.b], W=[pba.b])
        pbv = pba[:, 0:NT * 32].rearrange("p (i c) -> p i c", c=32)
        S.op("dve", lambda e: e.tensor_tensor(SM(0), pbv[:, :, 16:32], bc(dtb), ALU.add), R=[pba.b, dtb.b], W=[r(0)])
        S.op("act", lambda e: e.activation(SM(4), pbv[:, :, 0:16], AF.Exp, scale=-1.0), R=[pba.b], W=[r(4)])
        S.op("act", lambda e: e.activation(SM(1), SM(0), AF.Abs), R=[r(0)], W=[r(1)])
        S.op("act", lambda e: e.activation(SM(1), SM(1), AF.Exp, scale=-1.0), R=[r(1)], W=[r(1)])
        S.op("act", lambda e: e.activation(SM(1), SM(1), AF.Ln, bias=ONE), R=[r(1), cst.b], W=[r(1)])
        S.op("dve", lambda e: e.tensor_scalar(SM(4), SM(4), 1.0, None, ALU.add), R=[r(4)], W=[r(4)])
        S.op("dve", lambda e: e.reciprocal(SM(5), SM(4)), R=[r(4)], W=[r(5)])
        S.op("dve", lambda e: e.scalar_tensor_tensor(SM(2), SM(0), 0.0, SM(1), ALU.max, ALU.add), R=[r(0), r(1)], W=[r(2)])
        S.op("dve", lambda e: e.tensor_tensor(SM(3), SM(2), bc(negA), ALU.mult), R=[r(2), negA.b], W=[r(3)])
        S.op("dve", lambda e: e.tensor_copy(SB(0), SM(3)), R=[r(3)], W=[rb(0)])
        S.op("dve", lambda e: e.tensor_copy(SM(6), SB(0)), R=[rb(0)], W=[r(6)])
        S.op("dve", lambda e: e.tensor_tensor(SB(1), SM(3), SM(6), ALU.subtract), R=[r(3), r(6)], W=[rb(1)])
        pc = pb()
        for i in range(NT):
            for k in range(2):
                S.op("pe", lambda e, i=i, k=k: e.matmul(pc[:, i * 32:i * 32 + 16], triub[:], smb[:, k, i * 16:(i + 1) * 16], start=(k == 0), stop=(k == 1)),
                     R=[triub.b, rb(k)], W=[pc.b])
            for k in range(2):
                S.op("pe", lambda e, i=i, k=k: e.matmul(pc[:, i * 32 + 16:i * 32 + 32], onesb[:], smb[:, k, i * 16:(i + 1) * 16], start=(k == 0), stop=(k == 1)),
                     R=[onesb.b, rb(k)], W=[pc.b])
        pcv = pc[:, 0:NT * 32].rearrange("p (i c) -> p i c", c=32)
        S.op("dve", lambda e: e.tensor_copy(SM(7), pcv[:, :, 0:16]), R=[pc.b], W=[r(7)])
        S.op("dve", lambda e: e.tensor_copy(SM(8), pcv[:, :, 16:32]), R=[pc.b], W=[r(8)])
        S.op("act", lambda e: e.activation(SM(9), SM(8), AF.Exp), R=[r(8)], W=[r(9)])
        S.op("dve", lambda e: e.tensor_tensor(SM(10), SM(8), SM(7), ALU.subtract), R=[r(7), r(8)], W=[r(10)])
        S.op("act", lambda e: e.activation(SM(15), SM(7), AF.Exp), R=[r(7)], W=[r(15)])
        S.op("act", lambda e: e.activation(SM(10), SM(10), AF.Exp), R=[r(10)], W=[r(10)])
        S.op("dve", lambda e: e.tensor_scalar(SM(12), SM(7), -1.0, None, ALU.mult), R=[r(7)], W=[r(12)])
        S.op("dve", lambda e: e.tensor_copy(SB(2), SM(12)), R=[r(12)], W=[rb(2)])
        S.op("dve", lambda e: e.tensor_copy(SM(13), SB(2)), R=[rb(2)], W=[r(13)])
        S.op("dve", lambda e: e.tensor_tensor(SM(14), SM(12), SM(13), ALU.subtract), R=[r(12), r(13)], W=[r(14)])
        S.op("dve", lambda e: e.tensor_tensor(SM(11), SM(5), SM(15), ALU.mult), R=[r(5), r(15)], W=[r(11)])

    def gdn_tile(B, xnT, i, t, full, hbuf, sp):
        qkvT = B["qkvT"]
        WB = B["WB"]
        c0 = i * 128
        sm = B["sm"][sp % 2]
        o16 = i * 16
        SM = lambda k: sm[:, k, o16:o16 + 16]
        if full:
            zs = B["zs"]
            nzb = 2048 // WB
            for blk in range(nzb):
                wb, wv = wload(B, blk, Win_s[16 + blk], 8)
                S.dma("sp", wv, Win_s[16 + blk], R=[scrB[id(Win_s)]], W=[wb.b])
                p = pb()
                for c in range(8):
                    S.op("pe", lambda e, c=c, wv=wv, p=p: e.matmul(p[:, 0:WB], xnT[:, c, c0:c0 + 128], wv[:, c, :],
                                                                   start=(c == 0), stop=(c == 7)),
                         R=[wb.b, xnT.b], W=[p.b])
                S.op("act", lambda e, p=p, blk=blk: e.activation(zs[:, blk * WB:(blk + 1) * WB], p[:, 0:WB], AF.Silu),
                     R=[p.b], W=[zs.b])
        ktm, kbg, kd, vb = B["ktm"], B["kbg"], B["kd"], B["vb"]
        p = pb()
        pv = p[:].bitcast(BF16)
        for kh in range(8):
            S.op("pe", lambda e, kh=kh, pv=pv: e.transpose(pv[:, kh * 128:(kh + 1) * 128], qkvT[:, 8 + kh, c0:c0 + 128], identb[:]),
                 R=[qkvT.bs[8 + kh], identb.b], W=[p.b])
        S.op("act", lambda e, pv=pv: e.copy(ktm[:], pv.rearrange("p (k d) -> p k d", k=8)), R=[p.b], W=[ktm.b])
        if cut < 3.2:
            return
        k4 = ktm[:].unsqueeze(2).to_broadcast([128, 8, 2, 128])
        S.op("pool", lambda e: e.tensor_tensor(kbg[:].rearrange("p (k r) d -> p k r d", r=2), k4,
                                               SM(11).rearrange("p (k r) -> p k r", r=2).unsqueeze(3).to_broadcast([128, 8, 2, 128]),
                                               ALU.mult), R=[ktm.b, sm.bs[11]], W=[kbg.b])
        S.op("pool", lambda e: e.tensor_tensor(kd[:].rearrange("p (k r) d -> p k r d", r=2), k4,
                                               SM(10).rearrange("p (k r) -> p k r", r=2).unsqueeze(3).to_broadcast([128, 8, 2, 128]),
                                               ALU.mult), R=[ktm.b, sm.bs[10]], W=[kd.b])
        if cut < 3.4:
            return
        for half in range(2):
            p = pb()
            pv = p[:].bitcast(BF16)
            for j in range(8):
                h_ = half * 8 + j
                S.op("pe", lambda e, j=j, h_=h_, pv=pv: e.transpose(pv[:, j * 128:(j + 1) * 128], qkvT[:, 16 + h_, c0:c0 + 128], identb[:]),
                     R=[qkvT.bs[16 + h_], identb.b], W=[p.b])
            S.op("dve", lambda e, half=half, pv=pv: e.tensor_tensor(
                vb[:, half * 8:(half + 1) * 8, :], pv.rearrange("p (k d) -> p k d", k=8),
                sm[:, 5, o16 + half * 8:o16 + (half + 1) * 8].unsqueeze(2).to_broadcast([128, 8, 128]), ALU.mult),
                R=[p.b, sm.bs[5]], W=[vb.b])
        if cut < 3.6:
            return
        Asb = B["Asb"]
        pA = [pb(), pb()]
        for kh in range(8):
            S.op("pe", lambda e, kh=kh: e.matmul(pA[kh // 4][:, (kh % 4) * 128:(kh % 4 + 1) * 128], qkvT[:, 8 + kh, c0:c0 + 128],
                                                 qkvT[:, 8 + kh, c0:c0 + 128], start=True, stop=True),
                 R=[qkvT.bs[8 + kh]], W=[pA[kh // 4].b])
        for j in range(2):
            S.op("act", lambda e, j=j: e.copy(Asb[:, j * 4:(j + 1) * 4, :], pA[j][:].rearrange("p (k s) -> p k s", k=4)),
                 R=[pA[j].b], W=[Asb.b])
        if cut < 3.8:
            return
        if full:
            KQsb = B["KQsb"]
            pK = [pb(), pb()]
            for kh in range(8):
                S.op("pe", lambda e, kh=kh: e.matmul(pK[kh // 4][:, (kh % 4) * 128:(kh % 4 + 1) * 128], qkvT[:, 8 + kh, c0:c0 + 128],
                                                     qkvT[:, kh, c0:c0 + 128], start=True, stop=True),
                     R=[qkvT.bs[8 + kh], qkvT.bs[kh]], W=[pK[kh // 4].b])
            for j in range(2):
                S.op("dve", lambda e, j=j: e.tensor_copy(KQsb[:, j * 4:(j + 1) * 4, :], pK[j][:].rearrange("p (k s) -> p k s", k=4)),
                     R=[pK[j].b], W=[KQsb.b])
        if cut < 4:
            return
        G = B["G"]
        b4 = lambda ap2: ap2.unsqueeze(1).to_broadcast([128, 4, 128])
        col4 = lambda k, h0: sm[:, k, o16 + h0:o16 + h0 + 4].unsqueeze(2).to_broadcast([128, 4, 128])
        flat = lambda t_: t_[:].rearrange("p j s -> p (j s)")
        kr = lambda ap3: ap3.rearrange("p (k r) s -> p k r s", r=2)
        NG = len(G)
        for gp in range(4 // NG):
            grp = list(range(NG * gp, NG * gp + NG))
            GB = {g: G[g % NG] for g in grp}
            H0 = {g: g * 4 for g in grp}
            for g in grp:
                gb, h0 = GB[g], H0[g]
                S.op("dve", lambda e, gb=gb, h0=h0: e.tensor_tensor(gb["dgh"][:], b4(identf[:]), col4(13, h0), ALU.mult),
                     R=[identf.b, sm.bs[13]], W=[gb["dgh"].b])
                S.op("pool", lambda e, gb=gb, h0=h0: e.tensor_tensor(gb["dgl"][:], b4(identf[:]), col4(14, h0), ALU.mult),
                     R=[identf.b, sm.bs[14]], W=[gb["dgl"].b])
            PR = {}
            for g in grp:
                gb = GB[g]
                pR = pb()
                PR[g] = pR
                S.op("pe", lambda e, pR=pR, gb=gb: e.matmul(pR[:], onesb[:], flat(gb["dgh"]), start=True, stop=False),
                     R=[onesb.b, gb["dgh"].b], W=[pR.b])
                S.op("pe", lambda e, pR=pR, gb=gb: e.matmul(pR[:], onesb[:], flat(gb["dgl"]), start=False, stop=True),
                     R=[onesb.b, gb["dgl"].b], W=[pR.b])
            for g in grp:
                gb, h0, pR = GB[g], H0[g], PR[g]
                S.op("dve", lambda e, pR=pR, gb=gb, h0=h0: e.tensor_tensor(gb["Z"][:], pR[:].rearrange("p (j s) -> p j s", j=4), col4(7, h0), ALU.add),
                     R=[pR.b, sm.bs[7]], W=[gb["Z"].b])
            if full:
                for g in grp:
                    gb, pR = GB[g], PR[g]
                    S.op("act", lambda e, pR=pR, gb=gb: e.activation(flat(gb["ER"]), pR[:], AF.Exp, scale=-1.0), R=[pR.b], W=[gb["ER"].b])
                for g in grp:
                    gb = GB[g]
                    S.op("dve", lambda e, gb=gb: e.scalar_tensor_tensor(gb["DT"][:], gb["Z"][:], -1.0, b4(maskUt[:]), ALU.mult, ALU.add),
                         R=[gb["Z"].b, maskUt.b], W=[gb["DT"].b])
            for g in grp:
                gb = GB[g]
                S.op("pool", lambda e, gb=gb: e.tensor_tensor(gb["Z"][:], gb["Z"][:], b4(maskLt[:]), ALU.add), R=[gb["Z"].b, maskLt.b], W=[gb["Z"].b])
            if full:
                for g in grp:
                    gb = GB[g]
                    S.op("act", lambda e, gb=gb: e.activation(gb["DT"][:], gb["DT"][:], AF.Exp), R=[gb["DT"].b], W=[gb["DT"].b])
            for g in grp:
                gb = GB[g]
                S.op("act", lambda e, gb=gb: e.activation(gb["Z"][:], gb["Z"][:], AF.Exp), R=[gb["Z"].b], W=[gb["Z"].b])
            for g in grp:
                gb, h0 = GB[g], H0[g]
                S.op("pool", lambda e, gb=gb, h0=h0: e.tensor_tensor(gb["Z"][:], gb["Z"][:], col4(5, h0), ALU.mult), R=[gb["Z"].b, sm.bs[5]], W=[gb["Z"].b])
            if full:
                for g in grp:
                    gb, kh0 = GB[g], H0[g] // 2
                    S.op("pool", lambda e, gb=gb, kh0=kh0: e.tensor_tensor(
                        kr(gb["aT"][:]), kr(gb["DT"][:]), KQsb[:, kh0:kh0 + 2, :].unsqueeze(2).to_broadcast([128, 2, 2, 128]), ALU.mult),
                        R=[gb["DT"].b, KQsb.b], W=[gb["aT"].b])
                    S.op("pool", lambda e, gb=gb, kh0=kh0: e.tensor_tensor(
                        kr(gb["qdT"][:]), kr(gb["ER"][:]), qkvT[:, kh0:kh0 + 2, c0:c0 + 128].unsqueeze(2).to_broadcast([128, 2, 2, 128]), ALU.mult),
                        R=[gb["ER"].b, qkvT.bs[kh0], qkvT.bs[kh0 + 1]], W=[gb["qdT"].b])
            for g in grp:
                gb, kh0 = GB[g], H0[g] // 2
                S.op("dve", lambda e, gb=gb, kh0=kh0: e.tensor_tensor(
                    kr(gb["Z"][:]), kr(gb["Z"][:]), Asb[:, kh0:kh0 + 2, :].unsqueeze(2).to_broadcast([128, 2, 2, 128]), ALU.mult),
                    R=[gb["Z"].b, Asb.b], W=[gb["Z"].b])
            for g in grp:
                gb = GB[g]
                S.op("act", lambda e, gb=gb: e.copy(gb["Mk"][:], gb["Z"][:]), R=[gb["Z"].b], W=[gb["Mk"].b])
            for g in grp:
                gb = GB[g]
                S.op("pool", lambda e, gb=gb: e.tensor_copy(gb["Mh"][:], gb["Mk"][:]), R=[gb["Mk"].b], W=[gb["Mh"].b])
                S.op("pool", lambda e, gb=gb: e.tensor_tensor(gb["Ml"][:], gb["Z"][:], gb["Mh"][:], ALU.subtract),
                     R=[gb["Z"].b, gb["Mh"].b], W=[gb["Ml"].b])
            PT = {}
            for g in grp:
                gb = GB[g]
                p = pb()
                PT[g] = p
                pv = p[:].bitcast(BF16)
                for j in range(4):
                    S.op("pe", lambda e, j=j, pv=pv, gb=gb: e.transpose(pv[:, j * 128:(j + 1) * 128], gb["Mk"][:, j, :], identb[:]),
                         R=[gb["Mk"].b, identb.b], W=[p.b])
            for g in grp:
                gb, p = GB[g], PT[g]
                pv = p[:].bitcast(BF16)
                S.op("act", lambda e, gb=gb, pv=pv: e.copy(flat(gb["Nk"]), pv[:, 0:512]), R=[p.b], W=[gb["Nk"].b])
                S.op("dve", lambda e, gb=gb, pv=pv: e.scalar_tensor_tensor(
                    gb["V"][:], pv[:, 0:512].rearrange("p (j s) -> p j s", j=4), -1.0, b4(identb[:]), ALU.mult, ALU.add),
                    R=[p.b, identb.b], W=[gb["V"].b])
            for r in range(1, 7):
                for g in grp:
                    gb = GB[g]
                    Mk, Nk, V = gb["Mk"], gb["Nk"], gb["V"]
                    pM = pN = pV = None
                    if r <= 5:
                        pM = pb()
                        for j in range(4):
                            S.op("pe", lambda e, j=j, pM=pM, Mk=Mk, Nk=Nk: e.matmul(
                                pM[:, j * 128:(j + 1) * 128], Nk[:, j, :], Mk[:, j, :], start=True, stop=True),
                                R=[Nk.b, Mk.b], W=[pM.b])
                    if r <= 5:
                        pN = pb()
                        for j in range(4):
                            S.op("pe", lambda e, j=j, pN=pN, Mk=Mk, Nk=Nk: e.matmul(
                                pN[:, j * 128:(j + 1) * 128], Mk[:, j, :], Nk[:, j, :], start=True, stop=True),
                                R=[Nk.b, Mk.b], W=[pN.b])
                    if r >= 2:
                        pV = pb()
                        for j in range(4):
                            S.op("pe", lambda e, j=j, pV=pV, Mk=Mk, V=V: e.matmul(
                                pV[:, j * 128:(j + 1) * 128], Mk[:, j, :], V[:, j, :], start=True, stop=True),
                                R=[Mk.b, V.b], W=[pV.b])
                        S.op("dve", lambda e, pV=pV, V=V: e.tensor_tensor(flat(V), pV[:], flat(V), ALU.add),
                             R=[pV.b, V.b], W=[V.b])
                    if pM is not None:
                        S.op("act", lambda e, pM=pM, Mk=Mk: e.copy(flat(Mk), pM[:]), R=[pM.b], W=[Mk.b])
                    if pN is not None:
                        S.op("act", lambda e, pN=pN, Nk=Nk: e.copy(flat(Nk), pN[:]), R=[pN.b], W=[Nk.b])
            PNV, PT0 = {}, {}
            for g in grp:
                gb = GB[g]
                V, Mh, Ml = gb["V"], gb["Mh"], gb["Ml"]
                pNV = pb()
                PNV[g] = pNV
                for j in range(4):
                    S.op("pe", lambda e, j=j, pNV=pNV, Mh=Mh, V=V: e.matmul(pNV[:, j * 128:(j + 1) * 128], Mh[:, j, :], V[:, j, :], start=True, stop=False),
                         R=[Mh.b, V.b], W=[pNV.b])
                    S.op("pe", lambda e, j=j, pNV=pNV, Ml=Ml, V=V: e.matmul(pNV[:, j * 128:(j + 1) * 128], Ml[:, j, :], V[:, j, :], start=False, stop=True),
                         R=[Ml.b, V.b], W=[pNV.b])
                pT0 = pb()
                PT0[g] = pT0
                pT0v = pT0[:].bitcast(BF16)
                for j in range(4):
                    S.op("pe", lambda e, j=j, pT0v=pT0v, V=V: e.transpose(pT0v[:, j * 128:(j + 1) * 128], V[:, j, :], identb[:]),
                         R=[V.b, identb.b], W=[pT0.b])
            for g in grp:
                gb = GB[g]
                V, Z, Rv, T0 = gb["V"], gb["Z"], gb["dgh"], gb["Nk"]
                pNV, pT0 = PNV[g], PT0[g]
                pT0v = pT0[:].bitcast(BF16)
                S.op("dve", lambda e, pNV=pNV, Z=Z, V=V: e.scalar_tensor_tensor(flat(Z), pNV[:], -1.0, flat(V), ALU.mult, ALU.subtract),
                     R=[pNV.b, V.b], W=[Z.b])
                S.op("pool", lambda e, Z=Z, Rv=Rv: e.tensor_tensor(Rv[:], Z[:], b4(identf[:]), ALU.add), R=[Z.b, identf.b], W=[Rv.b])
                S.op("act", lambda e, pT0v=pT0v, T0=T0: e.copy(flat(T0), pT0v[:, 0:512]), R=[pT0.b], W=[T0.b])
            PVR = {}
            for g in grp:
                gb = GB[g]
                Rv, T0 = gb["dgh"], gb["Nk"]
                pVR = pb()
                PVR[g] = pVR
                for j in range(4):
                    S.op("pe", lambda e, j=j, pVR=pVR, T0=T0, Rv=Rv: e.matmul(pVR[:, j * 128:(j + 1) * 128], T0[:, j, :], Rv[:, j, :], start=True, stop=True),
                         R=[T0.b, Rv.b], W=[pVR.b])
            for g in grp:
                gb = GB[g]
                V, Z, Vl, pVR = gb["V"], gb["Z"], gb["Mk"], PVR[g]
                S.op("dve", lambda e, pVR=pVR, Z=Z, V=V: e.tensor_tensor(flat(Z), pVR[:], flat(V), ALU.add), R=[pVR.b, V.b], W=[Z.b])
                S.op("act", lambda e, Z=Z, V=V: e.copy(V[:], Z[:]), R=[Z.b], W=[V.b])
                S.op("pool", lambda e, Z=Z, V=V, Vl=Vl: e.tensor_tensor(Vl[:], Z[:], V[:], ALU.subtract), R=[Z.b, V.b], W=[Vl.b])
            for g in grp:
                gb, h0 = GB[g], H0[g]
                V, Vl = gb["V"], gb["Mk"]
                pU = pb()
                pW = pb()
                for j in range(4):
                    S.op("pe", lambda e, j=j, pU=pU, V=V, h0=h0: e.matmul(pU[:, j * 128:(j + 1) * 128], V[:, j, :], vb[:, h0 + j, :], start=True, stop=False),
                         R=[V.b, vb.b], W=[pU.b])
                    S.op("pe", lambda e, j=j, pU=pU, Vl=Vl, h0=h0: e.matmul(pU[:, j * 128:(j + 1) * 128], Vl[:, j, :], vb[:, h0 + j, :], start=False, stop=True),
                         R=[Vl.b, vb.b], W=[pU.b])
                for j in range(4):
                    S.op("pe", lambda e, j=j, pW=pW, V=V, h0=h0: e.matmul(pW[:, j * 128:(j + 1) * 128], kbg[:, h0 + j, :], V[:, j, :], start=True, stop=False),
                         R=[V.b, kbg.b], W=[pW.b])
                    S.op("pe", lambda e, j=j, pW=pW, Vl=Vl, h0=h0: e.matmul(pW[:, j * 128:(j + 1) * 128], kbg[:, h0 + j, :], Vl[:, j, :], start=False, stop=True),
                         R=[Vl.b, kbg.b], W=[pW.b])
                S.op("act", lambda e, gb=gb, pU=pU: e.copy(flat(gb["u"]), pU[:]), R=[pU.b], W=[gb["u"].b])
                S.op("dve", lambda e, gb=gb, pW=pW: e.tensor_copy(flat(gb["wT"]), pW[:]), R=[pW.b], W=[gb["wT"].b])
            PWS = {}
            for g in grp:
                gb, h0 = GB[g], H0[g]
                pWS = pb()
                PWS[g] = pWS
                for j in range(4):
                    S.op("pe", lambda e, j=j, pWS=pWS, gb=gb, h0=h0: e.matmul(pWS[:, j * 128:(j + 1) * 128], gb["wT"][:, j, :], Sbf[:, h0 + j, :], start=True, stop=True),
                         R=[gb["wT"].b, Sbf.bs[g]], W=[pWS.b])
            for g in grp:
                gb, pWS = GB[g], PWS[g]
                S.op("dve", lambda e, gb=gb, pWS=pWS: e.tensor_tensor(flat(gb["vn"]), flat(gb["u"]), pWS[:], ALU.subtract),
                     R=[pWS.b, gb["u"].b], W=[gb["vn"].b])
            PO, PS = {}, {}
            for g in grp:
                gb, h0 = GB[g], H0[g]
                if full:
                    pO = pb()
                    PO[g] = pO
                    for j in range(4):
                        S.op("pe", lambda e, j=j, pO=pO, gb=gb, h0=h0: e.matmul(pO[:, j * 128:(j + 1) * 128], gb["qdT"][:, j, :], Sbf[:, h0 + j, :], start=True, stop=False),
                             R=[gb["qdT"].b, Sbf.bs[g]], W=[pO.b])
                        S.op("pe", lambda e, j=j, pO=pO, gb=gb: e.matmul(pO[:, j * 128:(j + 1) * 128], gb["aT"][:, j, :], gb["vn"][:, j, :], start=False, stop=True),
                             R=[gb["aT"].b, gb["vn"].b], W=[pO.b])
                pS = pb()
                PS[g] = pS
                for j in range(4):
                    S.op("pe", lambda e, j=j, pS=pS, gb=gb, h0=h0: e.matmul(pS[:, j * 128:(j + 1) * 128], kd[:, h0 + j, :], gb["vn"][:, j, :], start=True, stop=True),
                         R=[kd.b, gb["vn"].b], W=[pS.b])
            for g in grp:
                h0 = H0[g]
                S.op("pool", lambda e, h0=h0: e.tensor_tensor(S32[:, h0:h0 + 4, :], S32[:, h0:h0 + 4, :], col4(9, h0), ALU.mult),
                     R=[sm.bs[9], S32.bs[g]], W=[S32.bs[g]])
            for g in grp:
                h0, pS = H0[g], PS[g]
                S.op("dve", lambda e, h0=h0, pS=pS: e.tensor_tensor(S32[:, h0:h0 + 4, :], S32[:, h0:h0 + 4, :], pS[:].rearrange("p (j s) -> p j s", j=4), ALU.add),
                     R=[pS.b, S32.bs[g]], W=[S32.bs[g]])
            for g in grp:
                h0 = H0[g]
                S.op("act", lambda e, h0=h0: e.copy(Sbf[:, h0:h0 + 4, :], S32[:, h0:h0 + 4, :]), R=[S32.bs[g]], W=[Sbf.bs[g]])
            if full:
                for g in grp:
                    h0, pO = H0[g], PO[g]
                    oss, og, on, zs = B["oss"], B["og"][g % 2], B["on"], B["zs"]
                    for j in range(4):
                        S.op("act", lambda e, j=j, pO=pO, h0=h0: e.activation(B["junk"][:, 0:128], pO[:, j * 128:(j + 1) * 128], AF.Square,
                                                                             scale=float(128.0 ** -0.5), accum_out=oss[:, h0 + j:h0 + j + 1]),
                             R=[pO.b], W=[B["junk"].b, oss.b])
                    S.op("act", lambda e, h0=h0: e.activation(oss[:, 16 + h0:20 + h0], oss[:, h0:h0 + 4], AF.Ln, bias=EPS), R=[oss.b, cst.b], W=[oss.b])
                    S.op("act", lambda e, h0=h0: e.activation(oss[:, 16 + h0:20 + h0], oss[:, 16 + h0:20 + h0], AF.Exp, scale=-0.5), R=[oss.b], W=[oss.b])
                    S.op("dve", lambda e, pO=pO, og=og, h0=h0: e.tensor_tensor(og[:], pO[:].rearrange("p (j s) -> p j s", j=4),
                                                                               oss[:, 16 + h0:20 + h0].unsqueeze(2).to_broadcast([128, 4, 128]), ALU.mult),
                         R=[pO.b, oss.b], W=[og.b])
                    S.op("pool", lambda e, og=og, h0=h0: e.tensor_tensor(on[:, h0:h0 + 4, :], og[:], zs[:, h0 * 128:(h0 + 4) * 128].rearrange("p (h e) -> p h e", h=4), ALU.mult),
                         R=[og.b, zs.b], W=[on.b])
        if not full:
            return
        on, onT = B["on"], B["onT"]
        for half in range(2):
            p = pb()
            pv = p[:].bitcast(BF16)
            for j in range(8):
                S.op("pe", lambda e, j=j, pv=pv, half=half: e.transpose(pv[:, j * 128:(j + 1) * 128], on[:, half * 8 + j, :], identb[:]),
                     R=[on.b, identb.b], W=[p.b])
            if half == 0:
                S.op("act", lambda e, pv=pv, half=half: e.copy(onT[:, half * 8:(half + 1) * 8, :], pv.rearrange("p (k d) -> p k d", k=8)),
                     R=[p.b], W=[onT.b])
            else:
                S.op("dve", lambda e, pv=pv, half=half: e.tensor_copy(onT[:, half * 8:(half + 1) * 8, :], pv.rearrange("p (k d) -> p k d", k=8)),
                     R=[p.b], W=[onT.b])
        ft = t - nhist
        py = [pb(), pb()]
        HPB = WB // 256
        for hb in range(16 // HPB):
            wb = B["wblk"][hb % len(B["wblk"])]
            wv = wb[:, 0:HPB * 1024].rearrange("p (h n) -> p h n", h=HPB)
            S.dma("sp", wv, Wout_s[:, hb * HPB:(hb + 1) * HPB, :], R=[scrB[id(Wout_s)]], W=[wb.b])
            for hl in range(HPB):
                h_ = hb * HPB + hl
                for half in range(2):
                    S.op("pe", lambda e, hl=hl, h_=h_, half=half, wv=wv: e.matmul(
                        py[half][:], onT[:, h_, :], wv[:, hl, half * 512:(half + 1) * 512],
                        start=(h_ == 0), stop=(h_ == 15)), R=[wb.b, onT.b], W=[py[half].b])
        for half in range(2):
            S.op("dve", lambda e, half=half: e.tensor_tensor(
                hbuf[:, ft, half * 512:(half + 1) * 512], hbuf[:, ft, half * 512:(half + 1) * 512],
                py[half][:], ALU.add), R=[py[half].b, hbuf.bs[ft]], W=[hbuf.bs[ft]])

    def gdn_bufs(ph, N, full):
        B = {}
        B["WB"] = 256
        B["pref"] = {}
        B["xnT"] = [sb(ph, f"xnT{i}", [128, 8, N], BF16) for i in range(2)]
        B["qkvT"] = sb(ph, "qkvT", [128, 32, N], BF16, nslots=32)
        B["xs"] = [sb(ph, f"xs{i}", [128, D]) for i in range(2)] if not full else None
        B["xn"] = [sb(ph, f"xn{i}", [128, D], BF16) for i in range(2)]
        B["junk"] = sb(ph, "junk", [128, D], BF16)
        B["ms"] = sb(ph, "ms", [128, 4])
        B["wblk"] = [sb(ph, f"wblk{i}", [128, 8 * B["WB"]], BF16) for i in range(2 if full else 4)]
        B["u"] = [sb(ph, f"u{i}", [128, N + 3]) for i in range(2)]
        B["acc"] = [sb(ph, f"acc{i}", [128, N]) for i in range(2)]
        B["sq"] = [sb(ph, f"sq{i}", [128, N], BF16) for i in range(2)]
        B["sm"] = [sb(ph, f"sm{i}", [128, 16, (N // 128) * 16], F32, nslots=16) for i in range(2)]
        B["smb"] = [sb(ph, f"smb{i}", [128, 4, (N // 128) * 16], BF16, nslots=4) for i in range(2)]
        B["ktm"] = sb(ph, "ktm", [128, 8, 128], BF16)
        B["kbg"] = sb(ph, "kbg", [128, 16, 128], BF16)
        B["kd"] = sb(ph, "kd", [128, 16, 128], BF16)
        B["vb"] = sb(ph, "vb", [128, 16, 128], BF16)
        B["Asb"] = sb(ph, "Asb", [128, 8, 128], BF16)
        if full:
            B["KQsb"] = sb(ph, "KQsb", [128, 8, 128], BF16)
        G = []
        for g in range(2 if full else 4):
            gb = {}
            gb["dgh"] = sb(ph, f"dgh{g}", [128, 4, 128], BF16)
            gb["dgl"] = sb(ph, f"dgl{g}", [128, 4, 128], BF16)
            gb["Z"] = sb(ph, f"Z{g}", [128, 4, 128])
            gb["Mk"] = sb(ph, f"Mk{g}", [128, 4, 128], BF16)
            gb["Nk"] = sb(ph, f"Nk{g}", [128, 4, 128], BF16)
            gb["V"] = sb(ph, f"V{g}", [128, 4, 128], BF16)
            gb["Mh"] = sb(ph, f"Mh{g}", [128, 4, 128], BF16)
            gb["Ml"] = sb(ph, f"Ml{g}", [128, 4, 128], BF16)
            gb["u"] = sb(ph, f"ug{g}", [128, 4, 128])
            gb["wT"] = sb(ph, f"wT{g}", [128, 4, 128], BF16)
            gb["vn"] = sb(ph, f"vn{g}", [128, 4, 128], BF16)
            if full:
                gb["ER"] = sb(ph, f"ER{g}", [128, 4, 128], BF16)
                gb["DT"] = sb(ph, f"DT{g}", [128, 4, 128])
                gb["aT"] = sb(ph, f"aT{g}", [128, 4, 128], BF16)
                gb["qdT"] = sb(ph, f"qdT{g}", [128, 4, 128], BF16)
            G.append(gb)
        B["G"] = G
        if full:
            B["zs"] = sb(ph, "zs", [128, 2048], BF16)
            B["og"] = [sb(ph, f"og{i}", [128, 4, 128]) for i in range(2)]
            B["oss"] = sb(ph, "oss", [128, 32])
            B["on"] = sb(ph, "on", [128, 16, 128], BF16)
            B["onT"] = sb(ph, "onT", [128, 16, 128], BF16)
        return B

    if nhist > 0:
        with contextlib.ExitStack() as ph:
            B = gdn_bufs(ph, 512, False)
            sts = [list(range(t, min(t + 4, nhist))) for t in range(0, nhist, 4)]
            gdn_st_norm(B, sts[0], False, None, 0)
            for k, tl in enumerate(sts):
                if k + 1 < len(sts):
                    gdn_st_norm(B, sts[k + 1], False, None, k + 1)
                gdn_st_rest(B, tl, False, None, k, k + 1 < len(sts))
                run_bg(8)
            S.barrier()
    run_bg(len(bgq))
    S.barrier()
    ph0.close()

    hbuf = sb(root, "h", [128, NFULL, D], F32, nslots=NFULL)

    with contextlib.ExitStack() as ph:
        B = gdn_bufs(ph, 256, True)
        sts = [[nhist + 2 * st, nhist + 2 * st + 1] for st in range(NFULL // 2)]
        gdn_st_norm(B, sts[0], True, hbuf, 0)
        for k, tl in enumerate(sts):
            if k + 1 < len(sts):
                gdn_st_norm(B, sts[k + 1], True, hbuf, k + 1)
            gdn_st_rest(B, tl, True, hbuf, k)
        S.barrier()

    def ffn_phase(l):
        with contextlib.ExitStack() as ph:
            xnT = sb(ph, "f_xnT", [128, 8, 512], BF16)
            xn = [sb(ph, f"f_xn{i}", [128, D], BF16) for i in range(2)]
            junk = sb(ph, "f_junk", [128, D], BF16)
            ms = sb(ph, "f_ms", [128, 4])
            wblk = [sb(ph, f"f_w{i}", [128, 4096], BF16) for i in range(3)]
            u = [sb(ph, f"f_u{i}", [128, 514]) for i in range(4)]
            acc = [sb(ph, f"f_acc{i}", [128, 512]) for i in range(4)]
            gs = [sb(ph, f"f_gs{i}", [128, 512]) for i in range(2)]
            act = sb(ph, "f_act", [128, 22, 512], BF16, nslots=22)
            fcarry = sb(ph, "f_carry", [128, 44, 2])
            cw = sb(ph, "f_cw", [128, 44, 3])
            cb = sb(ph, "f_cb", [128, 44])
            S.op("pool", lambda e: e.memset(fcarry[:], 0.0), W=[fcarry.b])
            S.dma("sp", cw[:], f_cw_d[l][:, :, :], W=[cw.b])
            S.dma("sp", cb[:], f_cb_d[l][:, :], W=[cb.b])
            sts = [list(range(s, min(s + 4, NFULL))) for s in range(0, NFULL, 4)]

            def ffn_supertile(tiles):
                NT = len(tiles)
                N = NT * 128
                for i, ft in enumerate(tiles):
                    rmsnorm_T((junk, ms, xn[i % 2]), hbuf[:, ft, :], [hbuf.bs[ft]], xnT, i * 128)
                for b in range(11):
                    wb = wblk[b % 3]
                    wv = wb[:].rearrange("p (c n) -> p c n", c=8)
                    S.dma("sp", wv, Wup_s[l][b], R=[scrB[id(Wup_s[l])]], W=[wb.b])
                    for jj in range(2):
                        res = []
                        for which in range(2):
                            f = which * 22 + b * 2 + jj
                            col = which * 256 + jj * 128
                            p = pb()
                            for c in range(8):
                                S.op("pe", lambda e, c=c, p=p, wv=wv, col=col: e.matmul(p[:, 0:N], wv[:, c, col:col + 128], xnT[:, c, 0:N],
                                                                                      start=(c == 0), stop=(c == 7)),
                                     R=[wb.b, xnT.b], W=[p.b])
                            k = (jj * 2 + which)
                            uu, aa = u[k], acc[k]
                            S.op("pool", lambda e, uu=uu, f=f: e.tensor_copy(uu[:, 0:2], fcarry[:, f, :]), R=[fcarry.b], W=[uu.b])
                            S.op("act", lambda e, uu=uu, p=p: e.copy(uu[:, 2:2 + N], p[:, 0:N]), R=[p.b], W=[uu.b])
                            S.op("pool", lambda e, uu=uu, f=f: e.tensor_copy(fcarry[:, f, :], uu[:, N:N + 2]), R=[uu.b], W=[fcarry.b])
                            S.op("dve", lambda e, uu=uu, aa=aa, f=f: e.tensor_scalar(aa[:, 0:N], uu[:, 2:2 + N], cw[:, f, 2:3], cb[:, f:f + 1], ALU.mult, ALU.add),
                                 R=[uu.b, cw.b, cb.b], W=[aa.b])
                            for j in (1, 0):
                                S.op("dve", lambda e, uu=uu, aa=aa, f=f, j=j: e.scalar_tensor_tensor(
                                    aa[:, 0:N], uu[:, j:j + N], cw[:, f, j:j + 1], aa[:, 0:N], ALU.mult, ALU.add),
                                    R=[uu.b, cw.b, aa.b], W=[aa.b])
                            res.append(aa)
                        g_ = gs[jj]
                        S.op("act", lambda e, g_=g_, a0=res[0]: e.activation(g_[:, 0:N], a0[:, 0:N], AF.Silu), R=[res[0].b], W=[g_.b])
                        S.op("pool", lambda e, g_=g_, a1=res[1], b=b, jj=jj: e.tensor_tensor(act[:, b * 2 + jj, 0:N], g_[:, 0:N], a1[:, 0:N], ALU.mult),
                             R=[g_.b, res[1].b], W=[act.bs[b * 2 + jj]])
                pys = [[pb(), pb()] for _ in range(NT)]
                for jb in range(6):
                    nj = 4 if jb < 5 else 2
                    wb = wblk[jb % 3]
                    wv = wb[:].rearrange("p (j n) -> p j n", j=4)
                    S.dma("sp", wv[:, 0:nj, :], Wdn_s[l][:, jb * 4:jb * 4 + nj, :], R=[scrB[id(Wdn_s[l])]], W=[wb.b])
                    for jl in range(nj):
                        j = jb * 4 + jl
                        for i in range(NT):
                            for half in range(2):
                                S.op("pe", lambda e, i=i, j=j, jl=jl, half=half, wv=wv: e.matmul(
                                    pys[i][half][:], act[:, j, i * 128:(i + 1) * 128], wv[:, jl, half * 512:(half + 1) * 512],
                                    start=(j == 0), stop=(j == 21)), R=[wb.b, act.bs[j]], W=[pys[i][half].b])
                for i, ft in enumerate(tiles):
                    for half in range(2):
                        S.op("dve", lambda e, i=i, ft=ft, half=half: e.scalar_tensor_tensor(
                            hbuf[:, ft, half * 512:(half + 1) * 512], pys[i][half][:], validt[:, ft:ft + 1],
                            hbuf[:, ft, half * 512:(half + 1) * 512], ALU.mult, ALU.add),
                            R=[pys[i][half].b, hbuf.bs[ft], validt.b], W=[hbuf.bs[ft]])

            for tiles in sts:
                ffn_supertile(tiles)
            S.barrier()

    if stage >= 2:
        ffn_phase(0)

    def attn_phase():
        with contextlib.ExitStack() as ph:
            NK = NFULL + 1
            xnT = sb(ph, "a_xnT", [128, 8, 512], BF16)
            xn = [sb(ph, f"a_xn{i}", [128, D], BF16) for i in range(2)]
            ms = sb(ph, "a_ms", [128, 4])
            wblk = [sb(ph, f"a_w{i}", [128, 8, 512], BF16) for i in range(3)]
            wk = [0]

            def wnext(src3, n):
                wb = wblk[wk[0] % 3]
                wk[0] += 1
                S.dma("sp", wb[:, :, 0:n], src3, R=[scrB[id(Wkv_s)], scrB[id(Wq_s)], scrB[id(Wo_s)]], W=[wb.b])
                return wb

            KT = sb(ph, "a_KT", [128, 4, NK * 128], BF16)
            Vt = sb(ph, "a_V", [128, NK, 256], BF16)
            QT = sb(ph, "a_QT", [128, 8, 512], BF16)
            BMf = sb(ph, "a_BMf", [128, 4, 256])
            BM = sb(ph, "a_BM", [128, 16, 256], BF16)
            am = sb(ph, "a_am", [128, 256])
            kbf = sb(ph, "a_kbf", [1, 512])
            kbb = sb(ph, "a_kbb", [1, NK * 128], BF16)
            sinkb = sb(ph, "a_sink", [128, 16])
            sc = [sb(ph, f"a_sc{i}", [128, 2, 256]) for i in range(3)]
            pr = [sb(ph, f"a_pr{i}", [128, 2, 256], BF16) for i in range(3)]
            pT = [sb(ph, f"a_pT{i}", [128, 4, 128], BF16) for i in range(3)]
            st_ = sb(ph, "a_st", [128, 6, 16])
            obf = sb(ph, "a_obf", [128, D], BF16)
            junk = obf
            oT = sb(ph, "a_oT", [128, 8, 128], BF16)
            S.dma("sp", am[:], amask_d[:, :], W=[am.b])
            S.dma("sp", sinkb[:], sink_d[:, :], W=[sinkb.b])
            for q4 in range(4):
                S.dma("sp", BMf[:], band_d[:, q4 * 4:(q4 + 1) * 4, :], W=[BMf.b])
                S.op("dve", lambda e, q4=q4: e.tensor_tensor(BM[:, q4 * 4:(q4 + 1) * 4, :], BMf[:], am[:].unsqueeze(1).to_broadcast([128, 4, 256]), ALU.add),
                     R=[BMf.b, am.b], W=[BM.b])
            for k0_ in range(0, NK * 128, 512):
                kn = min(512, NK * 128 - k0_)
                S.dma("sp", kbf[:, 0:kn], kbias_d[:, k0_:k0_ + kn], W=[kbf.b])
                S.op("dve", lambda e, k0_=k0_, kn=kn: e.tensor_copy(kbb[:, k0_:k0_ + kn], kbf[:, 0:kn]), R=[kbf.b], W=[kbb.b])
            S.op("pool", lambda e: e.memset(KT[:, :, 0:128], 0.0), W=[KT.b])
            S.op("pool", lambda e: e.memset(Vt[:, 0, :], 0.0), W=[Vt.b])
            sts = [list(range(s, min(s + 4, NFULL))) for s in range(0, NFULL, 4)]

            def attn_supertile(tiles):
                NT = len(tiles)
                N = NT * 128
                for i, ft in enumerate(tiles):
                    rmsnorm_T((junk, ms, xn[i % 2]), hbuf[:, ft, :], [hbuf.bs[ft]], xnT, i * 128)
                kc0 = (tiles[0] + 1) * 128
                wkK = wnext(Wkv_s[0], 512)
                for j in range(4):
                    p = pb()
                    for c in range(8):
                        S.op("pe", lambda e, c=c, p=p, j=j, wkK=wkK: e.matmul(p[:, 0:N], wkK[:, c, j * 128:(j + 1) * 128], xnT[:, c, 0:N],
                                                                     start=(c == 0), stop=(c == 7)), R=[wkK.b, xnT.b], W=[p.b])
                    S.op("act", lambda e, p=p, j=j: e.copy(KT[:, j, kc0:kc0 + N], p[:, 0:N]), R=[p.b], W=[KT.b])
                wkV = wnext(Wkv_s[1, :, :, 0:256], 256)
                for i, ft in enumerate(tiles):
                    p = pb()
                    for c in range(8):
                        S.op("pe", lambda e, c=c, p=p, i=i, wkV=wkV: e.matmul(p[:, 0:256], xnT[:, c, i * 128:(i + 1) * 128], wkV[:, c, 0:256],
                                                                     start=(c == 0), stop=(c == 7)), R=[wkV.b, xnT.b], W=[p.b])
                    S.op("dve", lambda e, p=p, ft=ft: e.tensor_copy(Vt[:, ft + 1, :], p[:, 0:256]), R=[p.b], W=[Vt.b])
                for f in range(8):
                    if f % 4 == 0:
                        wqb = wnext(Wq_s[f // 4], 512)
                    p = pb()
                    for c in range(8):
                        S.op("pe", lambda e, c=c, p=p, f=f, wqb=wqb: e.matmul(p[:, 0:N], wqb[:, c, (f % 4) * 128:(f % 4 + 1) * 128], xnT[:, c, 0:N],
                                                                     start=(c == 0), stop=(c == 7)), R=[wqb.b, xnT.b], W=[p.b])
                    S.op("act", lambda e, p=p, f=f: e.activation(QT[:, f, 0:N], p[:, 0:N], AF.Copy, scale=0.125), R=[p.b], W=[QT.b])
                for i, ft in enumerate(tiles):
                    attn_tile(i, ft)

            def attn_tile(i, ft):
                if True:
                    k0 = ft * 128
                    pO = [PB[6], PB[7]]
                    reserved.update((6, 7))
                    for f in range(8):
                        kv = f // 2
                        ps = pb()
                        for hh in range(2):
                            lo = hh * 64
                            S.op("pe", lambda e, ps=ps, hh=hh, lo=lo, f=f, kv=kv, i=i, k0=k0: e.matmul(
                                ps[:, hh * 256:(hh + 1) * 256], QT[lo:lo + 64, f, i * 128:(i + 1) * 128], KT[lo:lo + 64, kv, k0:k0 + 256],
                                start=True, stop=False), R=[QT.b, KT.b], W=[ps.b])
                            S.op("pe", lambda e, ps=ps, hh=hh, k0=k0: e.matmul(
                                ps[:, hh * 256:(hh + 1) * 256], onesb[0:1, 0:128], kbb[0:1, k0:k0 + 256], start=False, stop=True),
                                R=[onesb.b, kbb.b], W=[ps.b])
                        s_, p_, t_ = sc[f % 3], pr[f % 3], pT[f % 3]
                        S.op("dve", lambda e, ps=ps, s_=s_, f=f: e.tensor_tensor(s_[:], ps[:].rearrange("p (h k) -> p h k", h=2), BM[:, 2 * f:2 * f + 2, :], ALU.add),
                             R=[ps.b, BM.b], W=[s_.b])
                        S.op("dve", lambda e, s_=s_, f=f: e.tensor_reduce(st_[:, 0, 2 * f:2 * f + 2], s_[:], AX.X, ALU.max), R=[s_.b], W=[st_.b])
                        S.op("dve", lambda e, f=f: e.tensor_tensor(st_[:, 0, 2 * f:2 * f + 2], st_[:, 0, 2 * f:2 * f + 2], sinkb[:, 2 * f:2 * f + 2], ALU.max),
                             R=[st_.b, sinkb.b], W=[st_.b])
                        S.op("dve", lambda e, f=f: e.tensor_scalar(st_[:, 1, 2 * f:2 * f + 2], st_[:, 0, 2 * f:2 * f + 2], -1.0, None, ALU.mult),
                             R=[st_.b], W=[st_.b])
                        for hh in range(2):
                            h_ = 2 * f + hh
                            S.op("act", lambda e, s_=s_, p_=p_, hh=hh, h_=h_: e.activation(p_[:, hh, :], s_[:, hh, :], AF.Exp, bias=st_[:, 1, h_:h_ + 1],
                                                                                           accum_out=st_[:, 2, h_:h_ + 1]),
                                 R=[s_.b, st_.b], W=[p_.b, st_.b])
                        pt = pb()
                        ptv = pt[:].bitcast(BF16)
                        for hh in range(2):
                            for kb in range(2):
                                S.op("pe", lambda e, ptv=ptv, pt=pt, p_=p_, hh=hh, kb=kb: e.transpose(
                                    ptv[:, (hh * 2 + kb) * 128:(hh * 2 + kb + 1) * 128], p_[:, hh, kb * 128:(kb + 1) * 128], identb[:]),
                                    R=[p_.b, identb.b], W=[pt.b])
                        if f % 2 == 0:
                            S.op("act", lambda e, ptv=ptv, t_=t_: e.copy(t_[:].rearrange("p a b -> p (a b)"), ptv[:, 0:512]), R=[pt.b], W=[t_.b])
                        else:
                            S.op("dve", lambda e, ptv=ptv, t_=t_: e.tensor_copy(t_[:].rearrange("p a b -> p (a b)"), ptv[:, 0:512]), R=[pt.b], W=[t_.b])
                        for hh in range(2):
                            h_ = 2 * f + hh
                            for kb in range(2):
                                S.op("pe", lambda e, t_=t_, hh=hh, kb=kb, h_=h_, kv=kv, ft=ft: e.matmul(
                                    pO[h_ // 8][:, (h_ % 8) * 64:(h_ % 8 + 1) * 64], t_[:, hh * 2 + kb, :], Vt[:, ft + kb, kv * 64:(kv + 1) * 64],
                                    start=(kb == 0), stop=(kb == 1)), R=[t_.b, Vt.b], W=[pO[h_ // 8].b])
                    reserved.clear()
                    S.op("dve", lambda e: e.tensor_tensor(st_[:, 3, :], sinkb[:], st_[:, 1, :], ALU.add), R=[st_.b, sinkb.b], W=[st_.b])
                    S.op("act", lambda e: e.activation(st_[:, 3, :], st_[:, 3, :], AF.Exp), R=[st_.b], W=[st_.b])
                    S.op("dve", lambda e: e.tensor_tensor(st_[:, 3, :], st_[:, 3, :], st_[:, 2, :], ALU.add), R=[st_.b], W=[st_.b])
                    S.op("dve", lambda e: e.reciprocal(st_[:, 4, :], st_[:, 3, :]), R=[st_.b], W=[st_.b])
                    for half in range(2):
                        S.op("dve", lambda e, half=half: e.tensor_tensor(
                            obf[:, half * 512:(half + 1) * 512].rearrange("p (h d) -> p h d", h=8), pO[half][:].rearrange("p (h d) -> p h d", h=8),
                            st_[:, 4, half * 8:(half + 1) * 8].unsqueeze(2).to_broadcast([128, 8, 64]), ALU.mult),
                            R=[pO[half].b, st_.b], W=[obf.b])
                    p = pb()
                    pv = p[:].bitcast(BF16)
                    for c in range(8):
                        S.op("pe", lambda e, c=c, pv=pv, p=p: e.transpose(pv[:, c * 128:(c + 1) * 128], obf[:, c * 128:(c + 1) * 128], identb[:]),
                             R=[obf.b, identb.b], W=[p.b])
                    S.op("act", lambda e, pv=pv: e.copy(oT[:].rearrange("p c t -> p (c t)"), pv), R=[p.b], W=[oT.b])
                    py = [pb(), pb()]
                    for half in range(2):
                        wob = wnext(Wo_s[half], 512)
                        for c in range(8):
                            S.op("pe", lambda e, c=c, half=half, wob=wob: e.matmul(py[half][:], oT[:, c, :], wob[:, c, :],
                                                                           start=(c == 0), stop=(c == 7)), R=[oT.b, wob.b], W=[py[half].b])
                        S.op("dve", lambda e, half=half, ft=ft: e.scalar_tensor_tensor(
                            hbuf[:, ft, half * 512:(half + 1) * 512], py[half][:], validt[:, ft:ft + 1],
                            hbuf[:, ft, half * 512:(half + 1) * 512], ALU.mult, ALU.add),
                            R=[py[half].b, hbuf.bs[ft], validt.b], W=[hbuf.bs[ft]])

            for tiles in sts:
                attn_supertile(tiles)
            S.barrier()

    if stage >= 3:
        attn_phase()
    if stage >= 4:
        ffn_phase(1)

    toks = []
    with contextlib.ExitStack() as ph:
        fnw = sb(ph, "fnw", [128, D])
        junk = sb(ph, "o_junk", [128, D], BF16)
        ms = sb(ph, "o_ms", [128, 4])
        ob = [sb(ph, f"o_b{i}", [128, D]) for i in range(2)]
        S.dma("sp", fnw[:], fin_w_d[:, :], W=[fnw.b])
        for t in range(NOWN):
            ft = t + NHALO
            o_ = ob[t % 2]
            if stage >= 5:
                S.op("act", lambda e, ft=ft: e.activation(junk[:], hbuf[:, ft, :], AF.Square, scale=1.0 / 32.0, accum_out=ms[:, 0:1]),
                     R=[hbuf.bs[ft]], W=[junk.b, ms.b])
                S.op("act", lambda e: e.activation(ms[:, 1:2], ms[:, 0:1], AF.Ln, bias=EPS), R=[ms.b, cst.b], W=[ms.b])
                S.op("act", lambda e: e.activation(ms[:, 2:3], ms[:, 1:2], AF.Exp, scale=-0.5), R=[ms.b], W=[ms.b])
                S.op("dve", lambda e, ft=ft, o_=o_: e.scalar_tensor_tensor(o_[:], hbuf[:, ft, :], ms[:, 2:3], fnw[:], ALU.mult, ALU.mult),
                     R=[hbuf.bs[ft], ms.b, fnw.b], W=[o_.b])
            else:
                S.op("dve", lambda e, ft=ft, o_=o_: e.tensor_copy(o_[:], hbuf[:, ft, :]), R=[hbuf.bs[ft]], W=[o_.b])
            toks.append(S.dma("sp", out_d[t * 128:(t + 1) * 128, :], o_[:], R=[o_.b]))
        S.finish(toks)
    root.close()
    return nc


def _t5_bucket(dist):
    n = np.maximum(dist, 0)
    nf = np.maximum(n, 1).astype(np.float32)
    large = 16 + (np.log(nf / 16) / np.log(128 / 16) * 16).astype(np.int32)
    large = np.minimum(large, 31)
    return np.where(n < 16, n, large)


def make_inputs(inp, nhist, cores):
    f32 = np.float32
    A = lambda v: np.ascontiguousarray(np.asarray(v), dtype=f32)
    x = A(inp["x"])

    def pc(v):
        return np.ascontiguousarray(A(v).reshape(8, 128).T)

    common = {
        "a_w_in": A(inp["a_w_in"][0]),
        "a_w_out": A(inp["a_w_out"][0]),
        "ffn_w_up0": A(inp["ffn_w_up"][0]), "ffn_w_up1": A(inp["ffn_w_up"][1]),
        "ffn_w_down0": A(inp["ffn_w_down"][0]), "ffn_w_down1": A(inp["ffn_w_down"][1]),
        "b_w_q": A(inp["b_w_q"][0]), "b_w_o": A(inp["b_w_o"][0]),
        "a_norm_wT": pc(inp["a_norm_w"][0]),
        "ffn_norm_wT0": pc(inp["ffn_norm_w"][0]), "ffn_norm_wT1": pc(inp["ffn_norm_w"][1]),
        "kv_norm_wT": pc(inp["kv_norm_w"]), "b_norm_wT": pc(inp["b_norm_w"][0]),
        "out_norm_wT": A(inp["a_out_norm_w"][0]).reshape(128, 1),
        "final_norm_wb": np.ascontiguousarray(np.broadcast_to(A(inp["final_norm_w"])[None, :], (128, D))),
        "a_conv_wT": np.ascontiguousarray(A(inp["a_conv_w"][0]).T.reshape(32, 128, 4).transpose(1, 0, 2)),
        "a_log_b": np.ascontiguousarray(np.broadcast_to(A(inp["a_a_log"][0])[None, :], (128, 16))),
        "dt_bias_b": np.ascontiguousarray(np.broadcast_to(A(inp["a_dt_bias"][0])[None, :], (128, 16))),
        "sinks_b": np.ascontiguousarray(np.broadcast_to(A(inp["b_sinks"][0])[None, :], (128, 16))),
    }
    for l in range(2):
        common[f"ffn_conv_wT{l}"] = np.ascontiguousarray(A(inp["ffn_conv_w"][l]).T.reshape(44, 128, 3).transpose(1, 0, 2))
        common[f"ffn_conv_bT{l}"] = np.ascontiguousarray(A(inp["ffn_conv_b"][l]).reshape(44, 128).T)
    wkv = A(inp["w_kv"])
    cols = []
    for j in range(4):
        cols += [wkv[:, j * 64:(j + 1) * 64], wkv[:, j * 64:(j + 1) * 64]]
    cols.append(wkv[:, 256:512])
    common["w_kv_dup"] = np.ascontiguousarray(np.concatenate(cols, axis=1))
    qi = np.arange(128)[:, None]
    ki = np.arange(256)[None, :]
    dist = qi + 128 - ki
    bucket = _t5_bucket(dist)
    tab = A(inp["rel_bias_table"])
    common["biasband"] = np.ascontiguousarray(tab[bucket].transpose(0, 2, 1))
    inwin = (dist >= 0) & (dist < 128)
    common["attnmask"] = np.where(inwin, 0.0, NEG).astype(f32)
    common["ident"] = np.eye(128, dtype=f32)
    p_ = np.arange(128)[:, None]
    j_ = np.arange(128)[None, :]
    common["triu"] = (p_ <= j_).astype(f32)
    common["maskL"] = np.where(p_ > j_, 0.0, NEG).astype(f32)
    common["maskU"] = np.where(j_ >= p_, 0.0, NEG).astype(f32)
    NT_ALL = nhist + NFULL
    maps = []
    for c in cores:
        b, j = c // 4, c % 4
        end = 2048 * (j + 1)
        start = end - NT_ALL * 128
        xe = np.zeros((NT_ALL * 128, D), f32)
        s0 = max(start, 0)
        xe[s0 - start:] = x[b, s0:end]
        pos_full = np.arange(end - NFULL * 128, end)
        valid = (pos_full >= 0).astype(f32).reshape(NFULL, 128).T
        kb = np.concatenate([np.full(128, NEG, f32), np.where(pos_full >= 0, 0.0, NEG).astype(f32)])[None, :]
        m = dict(common)
        m["x_ext"] = xe
        m["valid"] = np.ascontiguousarray(valid)
        m["kbias"] = np.ascontiguousarray(kb)
        maps.append(m)
    return maps


_NHIST = 46


def kernel(**inputs):
    nc = build_program(_NHIST)
    maps = make_inputs(inputs, _NHIST, list(range(8)))
    res = run_bass_kernel_spmd(nc, maps, core_ids=list(range(8)))
    out = np.empty((2, 8192, D), np.float32)
    for c in range(8):
        b, j = c // 4, c % 4
        out[b, 2048 * j:2048 * (j + 1)] = res.results[c]["out"]
    return out
```

```python
import contextlib
import numpy as np
import concourse.bass as bass
import concourse.mybir as mybir
from concourse.bass_utils import run_bass_kernel_spmd

F32 = mybir.dt.float32
BF16 = mybir.dt.bfloat16
AF = mybir.ActivationFunctionType
ALU = mybir.AluOpType
AX = mybir.AxisListType

D = 1024
NFULL = 18
NHALO = 2
NOWN = 16
NEG = -1.0e30
EPOCH = 30000


class Buf:
    __slots__ = ("name", "w", "r", "excl")

    def __init__(self, name="", excl=False):
        self.name = name
        self.w = None
        self.r = []
        self.excl = excl


class Sched:
    ENGS = ("pe", "act", "dve", "pool", "sp")

    def __init__(self, nc, n_dma_sems=10):
        self.nc = nc
        self.prog = {e: [] for e in self.ENGS}
        self.cnt = {e: 0 for e in self.ENGS}
        self.sems = {}
        self.seen = {e: {} for e in self.ENGS}
        self.dma_sems = {}
        self.n_dma_sems = n_dma_sems
        self.dma_rr = {e: 0 for e in self.ENGS}
        self._semctx = []
        self.last_tok = {}

    def _new_sem(self, name):
        ctx = self.nc.semaphore(name)
        h = ctx.__enter__()
        self._semctx.append(ctx)
        return h

    def _eng_sem(self, eng, idx):
        key = (eng, idx // EPOCH)
        if key not in self.sems:
            self.sems[key] = self._new_sem(f"s_{eng}_{idx // EPOCH}")
        return self.sems[key], (idx % EPOCH) + 1

    def _wait(self, eng, tok):
        teng, sem, val = tok
        if teng == eng and eng == "pe":
            return
        seen = self.seen[eng]
        if seen.get(sem.name, 0) >= val:
            return
        seen[sem.name] = val
        self.prog[eng].append(lambda e, sem=sem, val=val: e.wait_ge(sem, val))

    def _deps(self, eng, reads, writes):
        for b in reads:
            if b.w is not None:
                self._wait(eng, b.w)
            if b.excl:
                for t in b.r:
                    if t[0] != eng:
                        self._wait(eng, t)
        for b in writes:
            if b.w is not None and b.w[0] != eng:
                self._wait(eng, b.w)
            for t in b.r:
                if t[0] != eng:
                    self._wait(eng, t)

    def _commit(self, tok, reads, writes):
        for b in reads:
            b.r.append(tok)
        for b in writes:
            b.w = tok
            b.r = []

    def op(self, eng, fn, R=(), W=()):
        self._deps(eng, R, W)
        idx = self.cnt[eng]
        self.cnt[eng] += 1
        sem, val = self._eng_sem(eng, idx)
        self.prog[eng].append(lambda e, fn=fn, sem=sem: fn(e).then_inc(sem, 1))
        tok = (eng, sem, val)
        self.last_tok[eng] = tok
        self._commit(tok, R, W)
        return tok

    def dma(self, eng, out, in_, R=(), W=()):
        k = self.dma_rr[eng]
        self.dma_rr[eng] = (k + 1) % self.n_dma_sems
        key = (eng, k)
        if key not in self.dma_sems:
            self.dma_sems[key] = [self._new_sem(f"d_{eng}_{k}"), 0]
        ent = self.dma_sems[key]
        sem, tot = ent
        if tot > 0:
            self._wait(eng, ("dma", sem, tot))
        self._deps(eng, R, W)
        ent[1] = tot + 16
        self.prog[eng].append(
            lambda e, out=out, in_=in_, sem=sem: e.dma_start(out=out, in_=in_).then_inc(sem, 16))
        tok = ("dma", sem, tot + 16)
        self._commit(tok, R, W)
        return tok

    def barrier(self):
        toks = list(self.last_tok.values())
        for (eng, k), (sem, tot) in self.dma_sems.items():
            if tot > 0:
                toks.append(("dma", sem, tot))
        for e in self.ENGS:
            for t in toks:
                if t[0] != e or e != "pe":
                    self._wait(e, t)

    def finish(self, final_toks):
        for t in final_toks:
            self._wait("sp", t)
        nc = self.nc
        with nc.Block() as block:
            @block.tensor
            def _(e):
                for f in self.prog["pe"]:
                    f(e)

            @block.scalar
            def _(e):
                for f in self.prog["act"]:
                    f(e)

            @block.vector
            def _(e):
                for f in self.prog["dve"]:
                    f(e)

            @block.gpsimd
            def _(e):
                for f in self.prog["pool"]:
                    f(e)

            @block.sync
            def _(e):
                for f in self.prog["sp"]:
                    f(e)
        for ctx in reversed(self._semctx):
            ctx.__exit__(None, None, None)


class T:
    def __init__(self, t, name, nslots=0):
        self.t = t
        self.b = Buf(name)
        self.bs = [Buf(f"{name}{i}") for i in range(nslots)]

    def __getitem__(self, k):
        return self.t[k]


def build_program(nhist, stage=99, cut=99):
    nc = bass.Bass("TRN2", target_bir_lowering=False)
    S = Sched(nc)
    NT_ALL = nhist + NFULL

    def din(name, shape, dt=F32):
        return nc.dram_tensor(name, list(shape), dt, kind="ExternalInput").ap()

    x_ext = din("x_ext", [NT_ALL * 128, D])
    valid_d = din("valid", [128, NFULL])
    kbias_d = din("kbias", [1, (NFULL + 1) * 128])
    w_in_d = din("a_w_in", [D, 6176])
    w_out_d = din("a_w_out", [2048, D])
    w_up_d = [din(f"ffn_w_up{l}", [D, 5632]) for l in range(2)]
    w_dn_d = [din(f"ffn_w_down{l}", [2816, D]) for l in range(2)]
    w_kv_d = din("w_kv_dup", [D, 768])
    w_q_d = din("b_w_q", [D, D])
    w_o_d = din("b_w_o", [D, D])
    a_nw_d = din("a_norm_wT", [128, 8])
    f_nw_d = [din(f"ffn_norm_wT{l}", [128, 8]) for l in range(2)]
    kv_nw_d = din("kv_norm_wT", [128, 8])
    b_nw_d = din("b_norm_wT", [128, 8])
    on_w_d = din("out_norm_wT", [128, 1])
    fin_w_d = din("final_norm_wb", [128, D])
    a_cw_d = din("a_conv_wT", [128, 32, 4])
    f_cw_d = [din(f"ffn_conv_wT{l}", [128, 44, 3]) for l in range(2)]
    f_cb_d = [din(f"ffn_conv_bT{l}", [128, 44]) for l in range(2)]
    alog_d = din("a_log_b", [128, 16])
    dtb_d = din("dt_bias_b", [128, 16])
    sink_d = din("sinks_b", [128, 16])
    band_d = din("biasband", [128, 16, 256])
    amask_d = din("attnmask", [128, 256])
    ident_d = din("ident", [128, 128])
    triu_d = din("triu", [128, 128])
    maskL_d = din("maskL", [128, 128])
    maskU_d = din("maskU", [128, 128])
    out_d = nc.dram_tensor("out", [NOWN * 128, D], F32, kind="ExternalOutput").ap()

    def dscr(name, shape):
        return nc.dram_tensor(name, list(shape), BF16).ap()

    Win_s = dscr("Win_b", [25, 128, 8, 256])
    Wout_s = dscr("Wout_s", [128, 16, D])
    Wup_s = [dscr(f"Wup_b{l}", [11, 128, 8, 512]) for l in range(2)]
    Wdn_s = [dscr(f"Wdn_s{l}", [128, 22, D]) for l in range(2)]
    Wkv_s = dscr("Wkv_b", [2, 128, 8, 512])
    Wq_s = dscr("Wq_b", [2, 128, 8, 512])
    Wo_s = dscr("Wo_b", [2, 128, 8, 512])
    scrB = {id(a): Buf("scr") for a in [Win_s, Wout_s, Wkv_s, Wq_s, Wo_s] + Wup_s + Wdn_s}

    uid = [0]

    def sb(stk, name, shape, dt=F32, nslots=0):
        uid[0] += 1
        t = stk.enter_context(nc.sbuf_tensor(f"{name}_{uid[0]}", list(shape), dt))
        return T(t, name, nslots)

    root = contextlib.ExitStack()
    PB = [T(root.enter_context(nc.psum_tensor(f"pb{i}", [128, 512], F32)), f"pb{i}") for i in range(8)]
    for p_ in PB:
        p_.b.excl = True
    pbi = [0]

    reserved = set()

    def pb():
        while True:
            k = pbi[0] % 8
            pbi[0] += 1
            if k not in reserved:
                return PB[k]

    identf = sb(root, "identf", [128, 128])
    identb = sb(root, "identb", [128, 128], BF16)
    onesb = sb(root, "onesb", [128, 128], BF16)
    triub = sb(root, "triub", [128, 128], BF16)
    maskLt = sb(root, "maskLt", [128, 128])
    maskUt = sb(root, "maskUt", [128, 128])
    cst = sb(root, "cst", [128, 8])
    validt = sb(root, "validt", [128, NFULL])
    tmpc = sb(root, "tmpc", [128, 128])

    S.dma("sp", identf[:], ident_d[:, :], W=[identf.b])
    S.op("dve", lambda e: e.tensor_copy(identb[:], identf[:]), R=[identf.b], W=[identb.b])
    S.op("pool", lambda e: e.memset(onesb[:], 1.0), W=[onesb.b])
    S.dma("sp", tmpc[:], triu_d[:, :], W=[tmpc.b])
    S.op("dve", lambda e: e.tensor_copy(triub[:], tmpc[:]), R=[tmpc.b], W=[triub.b])
    S.dma("sp", maskLt[:], maskL_d[:, :], W=[maskLt.b])
    S.dma("sp", maskUt[:], maskU_d[:, :], W=[maskUt.b])
    S.op("pool", lambda e: e.memset(cst[:, 0:1], 1e-6), W=[cst.b])
    S.op("pool", lambda e: e.memset(cst[:, 1:2], 1.0), W=[cst.b])
    S.op("pool", lambda e: e.memset(cst[:, 2:3], float(np.log(128.0 ** -0.5))), W=[cst.b])
    S.op("pool", lambda e: e.memset(cst[:, 3:4], 0.0), W=[cst.b])
    S.dma("sp", validt[:], valid_d[:, :], W=[validt.b])
    EPS = cst[:, 0:1]
    ONE = cst[:, 1:2]

    gdn = root
    S32 = sb(gdn, "S32", [128, 16, 128], F32, nslots=4)
    Sbf = sb(gdn, "Sbf", [128, 16, 128], BF16, nslots=4)
    carry = sb(gdn, "carry", [128, 32, 3])
    convw = sb(gdn, "convw", [128, 32, 4])
    wba = sb(gdn, "wba", [128, 8, 32], BF16)
    negA = sb(gdn, "negA", [128, 16])
    dtb = sb(gdn, "dtb", [128, 16])
    S.op("pool", lambda e: e.memset(S32[:], 0.0), W=[S32.b] + S32.bs)
    S.op("pool", lambda e: e.memset(Sbf[:], 0.0), W=[Sbf.b] + Sbf.bs)
    S.op("pool", lambda e: e.memset(carry[:], 0.0), W=[carry.b])
    S.dma("sp", convw[:], a_cw_d[:, :, :], W=[convw.b])
    S.dma("sp", negA[:], alog_d[:, :], W=[negA.b])
    S.dma("sp", dtb[:], dtb_d[:, :], W=[dtb.b])
    S.op("act", lambda e: e.activation(negA[:], negA[:], AF.Exp), R=[negA.b], W=[negA.b])
    S.op("dve", lambda e: e.tensor_scalar(negA[:], negA[:], -1.0, None, ALU.mult), R=[negA.b], W=[negA.b])

    ph0 = contextlib.ExitStack()
    bgq = []
    if True:
        ph = ph0
        stg = [sb(ph, f"stg{i}", [128, 2048]) for i in range(2)]
        cvt = [sb(ph, f"cvt{i}", [128, 2048], BF16) for i in range(2)]
        nws = sb(ph, "nws", [128, 8 * 5 + 1])
        nw_ap = {}
        for i, (nm, d_) in enumerate([("a", a_nw_d), ("f0", f_nw_d[0]), ("f1", f_nw_d[1]), ("kv", kv_nw_d), ("b", b_nw_d)]):
            S.dma("sp", nws[:, i * 8:(i + 1) * 8], d_[:, :], W=[nws.b])
            nw_ap[nm] = (i * 8)
        S.dma("sp", nws[:, 40:41], on_w_d[:, :], W=[nws.b])
        blk = [0]

        def convert(src3, dst3, C, N, scale=None, dstfn=None, ng=512, defer=False):
            ng = min(N, ng)
            cg = max(1, min(C, 2048 // ng))

            def emit(c0, cc, n0, nn):
                i = blk[0] % 2
                blk[0] += 1
                st, cv = stg[i], cvt[i]
                sv = st[:, 0:cc * nn].rearrange("p (c n) -> p c n", c=cc)
                cvv = cv[:, 0:cc * nn].rearrange("p (c n) -> p c n", c=cc)
                S.dma("sp", sv, src3[:, c0:c0 + cc, n0:n0 + nn], W=[st.b])
                eng = "dve" if (blk[0] % 2 == 0) else "pool"
                if scale is None:
                    S.op(eng, lambda e: e.tensor_copy(cvv, sv), R=[st.b], W=[cv.b])
                elif scale[0] == "pc":
                    g = nws[:, scale[1] + c0:scale[1] + c0 + cc].unsqueeze(2).to_broadcast([128, cc, nn])
                    S.op(eng, lambda e: e.tensor_tensor(cvv, sv, g, ALU.mult), R=[st.b, nws.b], W=[cv.b])
                else:
                    g = nws[:, scale[1]:scale[1] + 1].unsqueeze(2).to_broadcast([128, cc, nn])
                    S.op(eng, lambda e: e.tensor_tensor(cvv, sv, g, ALU.mult), R=[st.b, nws.b], W=[cv.b])
                dap = dst3[:, c0:c0 + cc, n0:n0 + nn] if dstfn is None else dstfn(c0, cc, n0, nn)
                S.dma("pool", dap, cvv, R=[cv.b], W=[scrB[id(dst3)]])

            for c0 in range(0, C, cg):
                cc = min(cg, C - c0)
                for n0 in range(0, N, ng):
                    nn = min(ng, N - n0)
                    if defer:
                        bgq.append(lambda c0=c0, cc=cc, n0=n0, nn=nn: emit(c0, cc, n0, nn))
                    else:
                        emit(c0, cc, n0, nn)

        def pcn(ap):
            return ap.rearrange("(c p) n -> p c n", p=128)

        convert(pcn(w_in_d), Win_s, 8, 6176, ("pc", nw_ap["a"]), ng=256,
                dstfn=lambda c0, cc, n0, nn: Win_s[n0 // 256, :, c0:c0 + cc, 0:nn])
        convert(pcn(w_out_d), Wout_s, 16, D, ("p", 40))
        if stage >= 2:
            for l in range(2):
                convert(pcn(w_up_d[l]), Wup_s[l], 8, 5632, ("pc", nw_ap[f"f{l}"]), ng=256,
                        dstfn=lambda c0, cc, n0, nn, l=l: (Wup_s[l][n0 // 256, :, c0:c0 + cc, 0:256] if n0 < 2816
                                                           else Wup_s[l][(n0 - 2816) // 256, :, c0:c0 + cc, 256:512]), defer=True)
                convert(pcn(w_dn_d[l]), Wdn_s[l], 22, D, None, defer=True)
        if stage >= 3:
            convert(pcn(w_kv_d), Wkv_s, 8, 768, ("pc", nw_ap["kv"]), ng=256,
                    dstfn=lambda c0, cc, n0, nn: (Wkv_s[0, :, c0:c0 + cc, n0:n0 + nn] if n0 < 512 else Wkv_s[1, :, c0:c0 + cc, 0:nn]), defer=True)
            convert(pcn(w_q_d), Wq_s, 8, D, ("pc", nw_ap["b"]), ng=512,
                    dstfn=lambda c0, cc, n0, nn: Wq_s[n0 // 512, :, c0:c0 + cc, 0:nn], defer=True)
            convert(pcn(w_o_d), Wo_s, 8, D, None, ng=512,
                    dstfn=lambda c0, cc, n0, nn: Wo_s[n0 // 512, :, c0:c0 + cc, 0:nn], defer=True)
        S.dma("sp", wba[:], Win_s[24, :, :, 0:32], R=[scrB[id(Win_s)]], W=[wba.b])

    def run_bg(n):
        for _ in range(min(n, len(bgq))):
            bgq.pop(0)()


    def rmsnorm_T(ph_bufs, src_ap, src_bufs, xnT, col0):
        junk, ms, xn = ph_bufs
        S.op("act", lambda e: e.activation(junk[:], src_ap, AF.Square, scale=1.0 / 32.0, accum_out=ms[:, 0:1]),
             R=src_bufs, W=[junk.b, ms.b])
        S.op("act", lambda e: e.activation(ms[:, 1:2], ms[:, 0:1], AF.Ln, bias=EPS), R=[ms.b, cst.b], W=[ms.b])
        S.op("act", lambda e: e.activation(ms[:, 2:3], ms[:, 1:2], AF.Exp, scale=-0.5), R=[ms.b], W=[ms.b])
        S.op("dve", lambda e: e.tensor_scalar(xn[:], src_ap, ms[:, 2:3], None, ALU.mult), R=src_bufs + [ms.b], W=[xn.b])
        p = pb()
        pv = p[:].bitcast(BF16)
        for c in range(8):
            S.op("pe", lambda e, c=c: e.transpose(pv[:, c * 128:(c + 1) * 128], xn[:, c * 128:(c + 1) * 128], identb[:]),
                 R=[xn.b, identb.b], W=[p.b])
        S.op("act", lambda e: e.copy(xnT[:, :, col0:col0 + 128], pv.rearrange("p (c t) -> p c t", c=8)),
             R=[p.b], W=[xnT.b])

    def wload(B, k, src3, nparts):
        wb = B["wblk"][k % len(B["wblk"])]
        n = src3.shape[2]
        wv = wb[:, 0:nparts * n].rearrange("p (c n) -> p c n", c=nparts)
        return wb, wv

    def gdn_st_norm(B, tiles, full, hbuf, sp):
        xnT = B["xnT"][sp % 2]
        for i, t in enumerate(tiles):
            if full:
                ft = t - nhist
                src, sbufs = hbuf[:, ft, :], [hbuf.bs[ft]]
            else:
                xs = B["xs"][t % 2]
                src, sbufs = xs[:], [xs.b]
            S.dma("sp", src, x_ext[t * 128:(t + 1) * 128, :], W=sbufs)
            rmsnorm_T((B["junk"], B["ms"], B["xn"][i % 2]), src, sbufs, xnT, i * 128)

    def gdn_st_rest(B, tiles, full, hbuf, sp, has_next=False):
        NT = len(tiles)
        N = NT * 128
        WB = B["WB"]
        CPB = WB // 128
        xnT, qkvT = B["xnT"][sp % 2], B["qkvT"]
        gdn_st_small(B, xnT, NT, sp)
        f0 = 0 if full else 8
        wb = None
        for f in range(f0, 32):
            if f % CPB == 0 or wb is None:
                blk = f // CPB
                if blk in B["pref"]:
                    wb, wv = B["pref"].pop(blk)
                else:
                    wb, wv = wload(B, blk, Win_s[blk], 8)
                    S.dma("sp", wv, Win_s[blk], R=[scrB[id(Win_s)]], W=[wb.b])
            p = pb()
            for c in range(8):
                S.op("pe", lambda e, c=c, wv=wv, f=f, p=p: e.matmul(p[:, 0:N], wv[:, c, (f % CPB) * 128:(f % CPB + 1) * 128],
                                                                    xnT[:, c, 0:N], start=(c == 0), stop=(c == 7)),
                     R=[wb.b, xnT.b], W=[p.b])
            u = B["u"][f % 2]
            acc = B["acc"][f % 2]
            S.op("pool", lambda e, u=u, f=f: e.tensor_copy(u[:, 0:3], carry[:, f, :]), R=[carry.b], W=[u.b])
            S.op("act", lambda e, u=u, p=p: e.copy(u[:, 3:3 + N], p[:, 0:N]), R=[p.b], W=[u.b])
            S.op("pool", lambda e, u=u, f=f: e.tensor_copy(carry[:, f, :], u[:, N:N + 3]), R=[u.b], W=[carry.b])
            S.op("dve", lambda e, u=u, acc=acc, f=f: e.tensor_scalar(acc[:, 0:N], u[:, 3:3 + N], convw[:, f, 3:4], None, ALU.mult),
                 R=[u.b, convw.b], W=[acc.b])
            for j in (2, 1, 0):
                S.op("dve", lambda e, u=u, acc=acc, f=f, j=j: e.scalar_tensor_tensor(
                    acc[:, 0:N], u[:, j:j + N], convw[:, f, j:j + 1], acc[:, 0:N], ALU.mult, ALU.add),
                    R=[u.b, convw.b, acc.b], W=[acc.b])
            S.op("act", lambda e, acc=acc, f=f: e.activation(qkvT[:, f, 0:N], acc[:, 0:N], AF.Silu),
                 R=[acc.b], W=[qkvT.bs[f]])
        for f in range(f0, 16):
            sq = B["sq"][f % 2]
            S.op("pool", lambda e, sq=sq, f=f: e.tensor_tensor(sq[:, 0:N], qkvT[:, f, 0:N], qkvT[:, f, 0:N], ALU.mult),
                 R=[qkvT.bs[f]], W=[sq.b])
            p = pb()
            S.op("pe", lambda e, p=p, sq=sq: e.matmul(p[:, 0:N], onesb[:], sq[:, 0:N], start=True, stop=True),
                 R=[onesb.b, sq.b], W=[p.b])
            rn = B["acc"][f % 2]
            S.op("act", lambda e, p=p, rn=rn: e.activation(rn[:, 0:N], p[:, 0:N], AF.Ln, bias=EPS), R=[p.b, cst.b], W=[rn.b])
            bias = cst[:, 2:3] if f < 8 else cst[:, 3:4]
            S.op("act", lambda e, rn=rn, bias=bias: e.activation(rn[:, 0:N], rn[:, 0:N], AF.Exp, scale=-0.5, bias=bias),
                 R=[rn.b, cst.b], W=[rn.b])
            S.op("dve", lambda e, rn=rn, f=f: e.tensor_tensor(qkvT[:, f, 0:N], qkvT[:, f, 0:N], rn[:, 0:N], ALU.mult),
                 R=[rn.b, qkvT.bs[f]], W=[qkvT.bs[f]])
        if cut < 2:
            return
        if has_next and not full:
            for blk in range(f0 // CPB, f0 // CPB + len(B["wblk"])):
                wb, wv = wload(B, blk, Win_s[blk], 8)
                S.dma("sp", wv, Win_s[blk], R=[scrB[id(Win_s)]], W=[wb.b])
                B["pref"][blk] = (wb, wv)
        for i, t in enumerate(tiles):
            gdn_tile(B, xnT, i, t, full, hbuf, sp)

    def gdn_st_small(B, xnT, NT, sp):
        sm = B["sm"][sp % 2]
        smb = B["smb"][sp % 2]
        W_ = NT * 16
        SM = lambda k: sm[:, k, 0:W_].rearrange("p (i h) -> p i h", h=16)
        SB = lambda k: smb[:, k, 0:W_].rearrange("p (i h) -> p i h", h=16)
        r = lambda k: sm.bs[k]
        rb = lambda k: smb.bs[k]
        bc = lambda t_: t_[:].unsqueeze(1).to_broadcast([128, NT, 16])
        pba = pb()
        for i in range(NT):
            for c in range(8):
                S.op("pe", lambda e, c=c, i=i: e.matmul(pba[:, i * 32:(i + 1) * 32], xnT[:, c, i * 128:(i + 1) * 128], wba[:, c, :],
                                                        start=(c == 0), stop=(c == 7)), R=[xnT.b, wba.b], W=[pba.b])
        pbv = pba[:, 0:NT * 32].rearrange("p (i c) -> p i c", c=32)
        S.op("dve", lambda e: e.tensor_tensor(SM(0), pbv[:, :, 16:32], bc(dtb), ALU.add), R=[pba.b, dtb.b], W=[r(0)])
        S.op("act", lambda e: e.activation(SM(4), pbv[:, :, 0:16], AF.Exp, scale=-1.0), R=[pba.b], W=[r(4)])
        S.op("act", lambda e: e.activation(SM(1), SM(0), AF.Abs), R=[r(0)], W=[r(1)])
        S.op("act", lambda e: e.activation(SM(1), SM(1), AF.Exp, scale=-1.0), R=[r(1)], W=[r(1)])
        S.op("act", lambda e: e.activation(SM(1), SM(1), AF.Ln, bias=ONE), R=[r(1), cst.b], W=[r(1)])
        S.op("dve", lambda e: e.tensor_scalar(SM(4), SM(4), 1.0, None, ALU.add), R=[r(4)], W=[r(4)])
        S.op("dve", lambda e: e.reciprocal(SM(5), SM(4)), R=[r(4)], W=[r(5)])
        S.op("dve", lambda e: e.scalar_tensor_tensor(SM(2), SM(0), 0.0, SM(1), ALU.max, ALU.add), R=[r(0), r(1)], W=[r(2)])
        S.op("dve", lambda e: e.tensor_tensor(SM(3), SM(2), bc(negA), ALU.mult), R=[r(2), negA.b], W=[r(3)])
        S.op("dve", lambda e: e.tensor_copy(SB(0), SM(3)), R=[r(3)], W=[rb(0)])
        S.op("dve", lambda e: e.tensor_copy(SM(6), SB(0)), R=[rb(0)], W=[r(6)])
        S.op("dve", lambda e: e.tensor_tensor(SB(1), SM(3), SM(6), ALU.subtract), R=[r(3), r(6)], W=[rb(1)])
        pc = pb()
        for i in range(NT):
            for k in range(2):
                S.op("pe", lambda e, i=i, k=k: e.matmul(pc[:, i * 32:i * 32 + 16], triub[:], smb[:, k, i * 16:(i + 1) * 16], start=(k == 0), stop=(k == 1)),
                     R=[triub.b, rb(k)], W=[pc.b])
            for k in range(2):
                S.op("pe", lambda e, i=i, k=k: e.matmul(pc[:, i * 32 + 16:i * 32 + 32], onesb[:], smb[:, k, i * 16:(i + 1) * 16], start=(k == 0), stop=(k == 1)),
                     R=[onesb.b, rb(k)], W=[pc.b])
        pcv = pc[:, 0:NT * 32].rearrange("p (i c) -> p i c", c=32)
        S.op("dve", lambda e: e.tensor_copy(SM(7), pcv[:, :, 0:16]), R=[pc.b], W=[r(7)])
        S.op("dve", lambda e: e.tensor_copy(SM(8), pcv[:, :, 16:32]), R=[pc.b], W=[r(8)])
        S.op("act", lambda e: e.activation(SM(9), SM(8), AF.Exp), R=[r(8)], W=[r(9)])
        S.op("dve", lambda e: e.tensor_tensor(SM(10), SM(8), SM(7), ALU.subtract), R=[r(7), r(8)], W=[r(10)])
        S.op("act", lambda e: e.activation(SM(15), SM(7), AF.Exp), R=[r(7)], W=[r(15)])
        S.op("act", lambda e: e.activation(SM(10), SM(10), AF.Exp), R=[r(10)], W=[r(10)])
        S.op("dve", lambda e: e.tensor_scalar(SM(12), SM(7), -1.0, None, ALU.mult), R=[r(7)], W=[r(12)])
        S.op("dve", lambda e: e.tensor_copy(SB(2), SM(12)), R=[r(12)], W=[rb(2)])
        S.op("dve", lambda e: e.tensor_copy(SM(13), SB(2)), R=[rb(2)], W=[r(13)])
        S.op("dve", lambda e: e.tensor_tensor(SM(14), SM(12), SM(13), ALU.subtract), R=[r(12), r(13)], W=[r(14)])
        S.op("dve", lambda e: e.tensor_tensor(SM(11), SM(5), SM(15), ALU.mult), R=[r(5), r(15)], W=[r(11)])

    def gdn_tile(B, xnT, i, t, full, hbuf, sp):
        qkvT = B["qkvT"]
        WB = B["WB"]
        c0 = i * 128
        sm = B["sm"][sp % 2]
        o16 = i * 16
        SM = lambda k: sm[:, k, o16:o16 + 16]
        if full:
            zs = B["zs"]
            nzb = 2048 // WB
            for blk in range(nzb):
                wb, wv = wload(B, blk, Win_s[16 + blk], 8)
                S.dma("sp", wv, Win_s[16 + blk], R=[scrB[id(Win_s)]], W=[wb.b])
                p = pb()
                for c in range(8):
                    S.op("pe", lambda e, c=c, wv=wv, p=p: e.matmul(p[:, 0:WB], xnT[:, c, c0:c0 + 128], wv[:, c, :],
                                                                   start=(c == 0), stop=(c == 7)),
                         R=[wb.b, xnT.b], W=[p.b])
                S.op("act", lambda e, p=p, blk=blk: e.activation(zs[:, blk * WB:(blk + 1) * WB], p[:, 0:WB], AF.Silu),
                     R=[p.b], W=[zs.b])
        ktm, kbg, kd, vb = B["ktm"], B["kbg"], B["kd"], B["vb"]
        p = pb()
        pv = p[:].bitcast(BF16)
        for kh in range(8):
            S.op("pe", lambda e, kh=kh, pv=pv: e.transpose(pv[:, kh * 128:(kh + 1) * 128], qkvT[:, 8 + kh, c0:c0 + 128], identb[:]),
                 R=[qkvT.bs[8 + kh], identb.b], W=[p.b])
        S.op("act", lambda e, pv=pv: e.copy(ktm[:], pv.rearrange("p (k d) -> p k d", k=8)), R=[p.b], W=[ktm.b])
        if cut < 3.2:
            return
        k4 = ktm[:].unsqueeze(2).to_broadcast([128, 8, 2, 128])
        S.op("pool", lambda e: e.tensor_tensor(kbg[:].rearrange("p (k r) d -> p k r d", r=2), k4,
                                               SM(11).rearrange("p (k r) -> p k r", r=2).unsqueeze(3).to_broadcast([128, 8, 2, 128]),
                                               ALU.mult), R=[ktm.b, sm.bs[11]], W=[kbg.b])
        S.op("pool", lambda e: e.tensor_tensor(kd[:].rearrange("p (k r) d -> p k r d", r=2), k4,
                                               SM(10).rearrange("p (k r) -> p k r", r=2).unsqueeze(3).to_broadcast([128, 8, 2, 128]),
                                               ALU.mult), R=[ktm.b, sm.bs[10]], W=[kd.b])
        if cut < 3.4:
            return
        for half in range(2):
            p = pb()
            pv = p[:].bitcast(BF16)
            for j in range(8):
                h_ = half * 8 + j
                S.op("pe", lambda e, j=j, h_=h_, pv=pv: e.transpose(pv[:, j * 128:(j + 1) * 128], qkvT[:, 16 + h_, c0:c0 + 128], identb[:]),
                     R=[qkvT.bs[16 + h_], identb.b], W=[p.b])
            S.op("dve", lambda e, half=half, pv=pv: e.tensor_tensor(
                vb[:, half * 8:(half + 1) * 8, :], pv.rearrange("p (k d) -> p k d", k=8),
                sm[:, 5, o16 + half * 8:o16 + (half + 1) * 8].unsqueeze(2).to_broadcast([128, 8, 128]), ALU.mult),
                R=[p.b, sm.bs[5]], W=[vb.b])
        if cut < 3.6:
            return
        Asb = B["Asb"]
        pA = [pb(), pb()]
        for kh in range(8):
            S.op("pe", lambda e, kh=kh: e.matmul(pA[kh // 4][:, (kh % 4) * 128:(kh % 4 + 1) * 128], qkvT[:, 8 + kh, c0:c0 + 128],
                                                 qkvT[:, 8 + kh, c0:c0 + 128], start=True, stop=True),
                 R=[qkvT.bs[8 + kh]], W=[pA[kh // 4].b])
        for j in range(2):
            S.op("act", lambda e, j=j: e.copy(Asb[:, j * 4:(j + 1) * 4, :], pA[j][:].rearrange("p (k s) -> p k s", k=4)),
                 R=[pA[j].b], W=[Asb.b])
        if cut < 3.8:
            return
        if full:
            KQsb = B["KQsb"]
            pK = [pb(), pb()]
            for kh in range(8):
                S.op("pe", lambda e, kh=kh: e.matmul(pK[kh // 4][:, (kh % 4) * 128:(kh % 4 + 1) * 128], qkvT[:, 8 + kh, c0:c0 + 128],
                                                     qkvT[:, kh, c0:c0 + 128], start=True, stop=True),
                     R=[qkvT.bs[8 + kh], qkvT.bs[kh]], W=[pK[kh // 4].b])
            for j in range(2):
                S.op("dve", lambda e, j=j: e.tensor_copy(KQsb[:, j * 4:(j + 1) * 4, :], pK[j][:].rearrange("p (k s) -> p k s", k=4)),
                     R=[pK[j].b], W=[KQsb.b])
        if cut < 4:
            return
        G = B["G"]
        b4 = lambda ap2: ap2.unsqueeze(1).to_broadcast([128, 4, 128])
        col4 = lambda k, h0: sm[:, k, o16 + h0:o16 + h0 + 4].unsqueeze(2).to_broadcast([128, 4, 128])
        flat = lambda t_: t_[:].rearrange("p j s -> p (j s)")
        kr = lambda ap3: ap3.rearrange("p (k r) s -> p k r s", r=2)
        NG = len(G)
        for gp in range(4 // NG):
            grp = list(range(NG * gp, NG * gp + NG))
            GB = {g: G[g % NG] for g in grp}
            H0 = {g: g * 4 for g in grp}
            for g in grp:
                gb, h0 = GB[g], H0[g]
                S.op("dve", lambda e, gb=gb, h0=h0: e.tensor_tensor(gb["dgh"][:], b4(identf[:]), col4(13, h0), ALU.mult),
                     R=[identf.b, sm.bs[13]], W=[gb["dgh"].b])
                S.op("pool", lambda e, gb=gb, h0=h0: e.tensor_tensor(gb["dgl"][:], b4(identf[:]), col4(14, h0), ALU.mult),
                     R=[identf.b, sm.bs[14]], W=[gb["dgl"].b])
            PR = {}
            for g in grp:
                gb = GB[g]
                pR = pb()
                PR[g] = pR
                S.op("pe", lambda e, pR=pR, gb=gb: e.matmul(pR[:], onesb[:], flat(gb["dgh"]), start=True, stop=False),
                     R=[onesb.b, gb["dgh"].b], W=[pR.b])
                S.op("pe", lambda e, pR=pR, gb=gb: e.matmul(pR[:], onesb[:], flat(gb["dgl"]), start=False, stop=True),
                     R=[onesb.b, gb["dgl"].b], W=[pR.b])
            for g in grp:
                gb, h0, pR = GB[g], H0[g], PR[g]
                S.op("dve", lambda e, pR=pR, gb=gb, h0=h0: e.tensor_tensor(gb["Z"][:], pR[:].rearrange("p (j s) -> p j s", j=4), col4(7, h0), ALU.add),
                     R=[pR.b, sm.bs[7]], W=[gb["Z"].b])
            if full:
                for g in grp:
                    gb, pR = GB[g], PR[g]
                    S.op("act", lambda e, pR=pR, gb=gb: e.activation(flat(gb["ER"]), pR[:], AF.Exp, scale=-1.0), R=[pR.b], W=[gb["ER"].b])
                for g in grp:
                    gb = GB[g]
                    S.op("dve", lambda e, gb=gb: e.scalar_tensor_tensor(gb["DT"][:], gb["Z"][:], -1.0, b4(maskUt[:]), ALU.mult, ALU.add),
                         R=[gb["Z"].b, maskUt.b], W=[gb["DT"].b])
            for g in grp:
                gb = GB[g]
                S.op("pool", lambda e, gb=gb: e.tensor_tensor(gb["Z"][:], gb["Z"][:], b4(maskLt[:]), ALU.add), R=[gb["Z"].b, maskLt.b], W=[gb["Z"].b])
            if full:
                for g in grp:
                    gb = GB[g]
                    S.op("act", lambda e, gb=gb: e.activation(gb["DT"][:], gb["DT"][:], AF.Exp), R=[gb["DT"].b], W=[gb["DT"].b])
            for g in grp:
                gb = GB[g]
                S.op("act", lambda e, gb=gb: e.activation(gb["Z"][:], gb["Z"][:], AF.Exp), R=[gb["Z"].b], W=[gb["Z"].b])
            for g in grp:
                gb, h0 = GB[g], H0[g]
                S.op("pool", lambda e, gb=gb, h0=h0: e.tensor_tensor(gb["Z"][:], gb["Z"][:], col4(5, h0), ALU.mult), R=[gb["Z"].b, sm.bs[5]], W=[gb["Z"].b])
            if full:
                for g in grp:
                    gb, kh0 = GB[g], H0[g] // 2
                    S.op("pool", lambda e, gb=gb, kh0=kh0: e.tensor_tensor(
                        kr(gb["aT"][:]), kr(gb["DT"][:]), KQsb[:, kh0:kh0 + 2, :].unsqueeze(2).to_broadcast([128, 2, 2, 128]), ALU.mult),
                        R=[gb["DT"].b, KQsb.b], W=[gb["aT"].b])
                    S.op("pool", lambda e, gb=gb, kh0=kh0: e.tensor_tensor(
                        kr(gb["qdT"][:]), kr(gb["ER"][:]), qkvT[:, kh0:kh0 + 2, c0:c0 + 128].unsqueeze(2).to_broadcast([128, 2, 2, 128]), ALU.mult),
                        R=[gb["ER"].b, qkvT.bs[kh0], qkvT.bs[kh0 + 1]], W=[gb["qdT"].b])
            for g in grp:
                gb, kh0 = GB[g], H0[g] // 2
                S.op("dve", lambda e, gb=gb, kh0=kh0: e.tensor_tensor(
                    kr(gb["Z"][:]), kr(gb["Z"][:]), Asb[:, kh0:kh0 + 2, :].unsqueeze(2).to_broadcast([128, 2, 2, 128]), ALU.mult),
                    R=[gb["Z"].b, Asb.b], W=[gb["Z"].b])
            for g in grp:
                gb = GB[g]
                S.op("act", lambda e, gb=gb: e.copy(gb["Mk"][:], gb["Z"][:]), R=[gb["Z"].b], W=[gb["Mk"].b])
            for g in grp:
                gb = GB[g]
                S.op("pool", lambda e, gb=gb: e.tensor_copy(gb["Mh"][:], gb["Mk"][:]), R=[gb["Mk"].b], W=[gb["Mh"].b])
                S.op("pool", lambda e, gb=gb: e.tensor_tensor(gb["Ml"][:], gb["Z"][:], gb["Mh"][:], ALU.subtract),
                     R=[gb["Z"].b, gb["Mh"].b], W=[gb["Ml"].b])
            PT = {}
            for g in grp:
                gb = GB[g]
                p = pb()
                PT[g] = p
                pv = p[:].bitcast(BF16)
                for j in range(4):
                    S.op("pe", lambda e, j=j, pv=pv, gb=gb: e.transpose(pv[:, j * 128:(j + 1) * 128], gb["Mk"][:, j, :], identb[:]),
                         R=[gb["Mk"].b, identb.b], W=[p.b])
            for g in grp:
                gb, p = GB[g], PT[g]
                pv = p[:].bitcast(BF16)
                S.op("act", lambda e, gb=gb, pv=pv: e.copy(flat(gb["Nk"]), pv[:, 0:512]), R=[p.b], W=[gb["Nk"].b])
                S.op("dve", lambda e, gb=gb, pv=pv: e.scalar_tensor_tensor(
                    gb["V"][:], pv[:, 0:512].rearrange("p (j s) -> p j s", j=4), -1.0, b4(identb[:]), ALU.mult, ALU.add),
                    R=[p.b, identb.b], W=[gb["V"].b])
            for r in range(1, 7):
                for g in grp:
                    gb = GB[g]
                    Mk, Nk, V = gb["Mk"], gb["Nk"], gb["V"]
                    pM = pN = pV = None
                    if r <= 5:
                        pM = pb()
                        for j in range(4):
                            S.op("pe", lambda e, j=j, pM=pM, Mk=Mk, Nk=Nk: e.matmul(
                                pM[:, j * 128:(j + 1) * 128], Nk[:, j, :], Mk[:, j, :], start=True, stop=True),
                                R=[Nk.b, Mk.b], W=[pM.b])
                    if r <= 5:
                        pN = pb()
                        for j in range(4):
                            S.op("pe", lambda e, j=j, pN=pN, Mk=Mk, Nk=Nk: e.matmul(
                                pN[:, j * 128:(j + 1) * 128], Mk[:, j, :], Nk[:, j, :], start=True, stop=True),
                                R=[Nk.b, Mk.b], W=[pN.b])
                    if r >= 2:
                        pV = pb()
                        for j in range(4):
                            S.op("pe", lambda e, j=j, pV=pV, Mk=Mk, V=V: e.matmul(
                                pV[:, j * 128:(j + 1) * 128], Mk[:, j, :], V[:, j, :], start=True, stop=True),
                                R=[Mk.b, V.b], W=[pV.b])
                        S.op("dve", lambda e, pV=pV, V=V: e.tensor_tensor(flat(V), pV[:], flat(V), ALU.add),
                             R=[pV.b, V.b], W=[V.b])
                    if pM is not None:
                        S.op("act", lambda e, pM=pM, Mk=Mk: e.copy(flat(Mk), pM[:]), R=[pM.b], W=[Mk.b])
                    if pN is not None:
                        S.op("act", lambda e, pN=pN, Nk=Nk: e.copy(flat(Nk), pN[:]), R=[pN.b], W=[Nk.b])
            PNV, PT0 = {}, {}
            for g in grp:
                gb = GB[g]
                V, Mh, Ml = gb["V"], gb["Mh"], gb["Ml"]
                pNV = pb()
                PNV[g] = pNV
                for j in range(4):
                    S.op("pe", lambda e, j=j, pNV=pNV, Mh=Mh, V=V: e.matmul(pNV[:, j * 128:(j + 1) * 128], Mh[:, j, :], V[:, j, :], start=True, stop=False),
                         R=[Mh.b, V.b], W=[pNV.b])
                    S.op("pe", lambda e, j=j, pNV=pNV, Ml=Ml, V=V: e.matmul(pNV[:, j * 128:(j + 1) * 128], Ml[:, j, :], V[:, j, :], start=False, stop=True),
                         R=[Ml.b, V.b], W=[pNV.b])
                pT0 = pb()
                PT0[g] = pT0
                pT0v = pT0[:].bitcast(BF16)
                for j in range(4):
                    S.op("pe", lambda e, j=j, pT0v=pT0v, V=V: e.transpose(pT0v[:, j * 128:(j + 1) * 128], V[:, j, :], identb[:]),
                         R=[V.b, identb.b], W=[pT0.b])
            for g in grp:
                gb = GB[g]
                V, Z, Rv, T0 = gb["V"], gb["Z"], gb["dgh"], gb["Nk"]
                pNV, pT0 = PNV[g], PT0[g]
                pT0v = pT0[:].bitcast(BF16)
                S.op("dve", lambda e, pNV=pNV, Z=Z, V=V: e.scalar_tensor_tensor(flat(Z), pNV[:], -1.0, flat(V), ALU.mult, ALU.subtract),
                     R=[pNV.b, V.b], W=[Z.b])
                S.op("pool", lambda e, Z=Z, Rv=Rv: e.tensor_tensor(Rv[:], Z[:], b4(identf[:]), ALU.add), R=[Z.b, identf.b], W=[Rv.b])
                S.op("act", lambda e, pT0v=pT0v, T0=T0: e.copy(flat(T0), pT0v[:, 0:512]), R=[pT0.b], W=[T0.b])
            PVR = {}
            for g in grp:
                gb = GB[g]
                Rv, T0 = gb["dgh"], gb["Nk"]
                pVR = pb()
                PVR[g] = pVR
                for j in range(4):
                    S.op("pe", lambda e, j=j, pVR=pVR, T0=T0, Rv=Rv: e.matmul(pVR[:, j * 128:(j + 1) * 128], T0[:, j, :], Rv[:, j, :], start=True, stop=True),
                         R=[T0.b, Rv.b], W=[pVR.b])
            for g in grp:
                gb = GB[g]
                V, Z, Vl, pVR = gb["V"], gb["Z"], gb["Mk"], PVR[g]
                S.op("dve", lambda e, pVR=pVR, Z=Z, V=V: e.tensor_tensor(flat(Z), pVR[:], flat(V), ALU.add), R=[pVR.b, V.b], W=[Z.b])
                S.op("act", lambda e, Z=Z, V=V: e.copy(V[:], Z[:]), R=[Z.b], W=[V.b])
                S.op("pool", lambda e, Z=Z, V=V, Vl=Vl: e.tensor_tensor(Vl[:], Z[:], V[:], ALU.subtract), R=[Z.b, V.b], W=[Vl.b])
            for g in grp:
                gb, h0 = GB[g], H0[g]
                V, Vl = gb["V"], gb["Mk"]
                pU = pb()
                pW = pb()
                for j in range(4):
                    S.op("pe", lambda e, j=j, pU=pU, V=V, h0=h0: e.matmul(pU[:, j * 128:(j + 1) * 128], V[:, j, :], vb[:, h0 + j, :], start=True, stop=False),
                         R=[V.b, vb.b], W=[pU.b])
                    S.op("pe", lambda e, j=j, pU=pU, Vl=Vl, h0=h0: e.matmul(pU[:, j * 128:(j + 1) * 128], Vl[:, j, :], vb[:, h0 + j, :], start=False, stop=True),
                         R=[Vl.b, vb.b], W=[pU.b])
                for j in range(4):
                    S.op("pe", lambda e, j=j, pW=pW, V=V, h0=h0: e.matmul(pW[:, j * 128:(j + 1) * 128], kbg[:, h0 + j, :], V[:, j, :], start=True, stop=False),
                         R=[V.b, kbg.b], W=[pW.b])
                    S.op("pe", lambda e, j=j, pW=pW, Vl=Vl, h0=h0: e.matmul(pW[:, j * 128:(j + 1) * 128], kbg[:, h0 + j, :], Vl[:, j, :], start=False, stop=True),
                         R=[Vl.b, kbg.b], W=[pW.b])
                S.op("act", lambda e, gb=gb, pU=pU: e.copy(flat(gb["u"]), pU[:]), R=[pU.b], W=[gb["u"].b])
                S.op("dve", lambda e, gb=gb, pW=pW: e.tensor_copy(flat(gb["wT"]), pW[:]), R=[pW.b], W=[gb["wT"].b])
            PWS = {}
            for g in grp:
                gb, h0 = GB[g], H0[g]
                pWS = pb()
                PWS[g] = pWS
                for j in range(4):
                    S.op("pe", lambda e, j=j, pWS=pWS, gb=gb, h0=h0: e.matmul(pWS[:, j * 128:(j + 1) * 128], gb["wT"][:, j, :], Sbf[:, h0 + j, :], start=True, stop=True),
                         R=[gb["wT"].b, Sbf.bs[g]], W=[pWS.b])
            for g in grp:
                gb, pWS = GB[g], PWS[g]
                S.op("dve", lambda e, gb=gb, pWS=pWS: e.tensor_tensor(flat(gb["vn"]), flat(gb["u"]), pWS[:], ALU.subtract),
                     R=[pWS.b, gb["u"].b], W=[gb["vn"].b])
            PO, PS = {}, {}
            for g in grp:
                gb, h0 = GB[g], H0[g]
                if full:
                    pO = pb()
                    PO[g] = pO
                    for j in range(4):
                        S.op("pe", lambda e, j=j, pO=pO, gb=gb, h0=h0: e.matmul(pO[:, j * 128:(j + 1) * 128], gb["qdT"][:, j, :], Sbf[:, h0 + j, :], start=True, stop=False),
                             R=[gb["qdT"].b, Sbf.bs[g]], W=[pO.b])
                        S.op("pe", lambda e, j=j, pO=pO, gb=gb: e.matmul(pO[:, j * 128:(j + 1) * 128], gb["aT"][:, j, :], gb["vn"][:, j, :], start=False, stop=True),
                             R=[gb["aT"].b, gb["vn"].b], W=[pO.b])
                pS = pb()
                PS[g] = pS
                for j in range(4):
                    S.op("pe", lambda e, j=j, pS=pS, gb=gb, h0=h0: e.matmul(pS[:, j * 128:(j + 1) * 128], kd[:, h0 + j, :], gb["vn"][:, j, :], start=True, stop=True),
                         R=[kd.b, gb["vn"].b], W=[pS.b])
            for g in grp:
                h0 = H0[g]
                S.op("pool", lambda e, h0=h0: e.tensor_tensor(S32[:, h0:h0 + 4, :], S32[:, h0:h0 + 4, :], col4(9, h0), ALU.mult),
                     R=[sm.bs[9], S32.bs[g]], W=[S32.bs[g]])
            for g in grp:
                h0, pS = H0[g], PS[g]
                S.op("dve", lambda e, h0=h0, pS=pS: e.tensor_tensor(S32[:, h0:h0 + 4, :], S32[:, h0:h0 + 4, :], pS[:].rearrange("p (j s) -> p j s", j=4), ALU.add),
                     R=[pS.b, S32.bs[g]], W=[S32.bs[g]])
            for g in grp:
                h0 = H0[g]
                S.op("act", lambda e, h0=h0: e.copy(Sbf[:, h0:h0 + 4, :], S32[:, h0:h0 + 4, :]), R=[S32.bs[g]], W=[Sbf.bs[g]])
            if full:
                for g in grp:
                    h0, pO = H0[g], PO[g]
                    oss, og, on, zs = B["oss"], B["og"][g % 2], B["on"], B["zs"]
                    for j in range(4):
                        S.op("act", lambda e, j=j, pO=pO, h0=h0: e.activation(B["junk"][:, 0:128], pO[:, j * 128:(j + 1) * 128], AF.Square,
                                                                             scale=float(128.0 ** -0.5), accum_out=oss[:, h0 + j:h0 + j + 1]),
                             R=[pO.b], W=[B["junk"].b, oss.b])
                    S.op("act", lambda e, h0=h0: e.activation(oss[:, 16 + h0:20 + h0], oss[:, h0:h0 + 4], AF.Ln, bias=EPS), R=[oss.b, cst.b], W=[oss.b])
                    S.op("act", lambda e, h0=h0: e.activation(oss[:, 16 + h0:20 + h0], oss[:, 16 + h0:20 + h0], AF.Exp, scale=-0.5), R=[oss.b], W=[oss.b])
                    S.op("dve", lambda e, pO=pO, og=og, h0=h0: e.tensor_tensor(og[:], pO[:].rearrange("p (j s) -> p j s", j=4),
                                                                               oss[:, 16 + h0:20 + h0].unsqueeze(2).to_broadcast([128, 4, 128]), ALU.mult),
                         R=[pO.b, oss.b], W=[og.b])
                    S.op("pool", lambda e, og=og, h0=h0: e.tensor_tensor(on[:, h0:h0 + 4, :], og[:], zs[:, h0 * 128:(h0 + 4) * 128].rearrange("p (h e) -> p h e", h=4), ALU.mult),
                         R=[og.b, zs.b], W=[on.b])
        if not full:
            return
        on, onT = B["on"], B["onT"]
        for half in range(2):
            p = pb()
            pv = p[:].bitcast(BF16)
            for j in range(8):
                S.op("pe", lambda e, j=j, pv=pv, half=half: e.transpose(pv[:, j * 128:(j + 1) * 128], on[:, half * 8 + j, :], identb[:]),
                     R=[on.b, identb.b], W=[p.b])
            if half == 0:
                S.op("act", lambda e, pv=pv, half=half: e.copy(onT[:, half * 8:(half + 1) * 8, :], pv.rearrange("p (k d) -> p k d", k=8)),
                     R=[p.b], W=[onT.b])
            else:
                S.op("dve", lambda e, pv=pv, half=half: e.tensor_copy(onT[:, half * 8:(half + 1) * 8, :], pv.rearrange("p (k d) -> p k d", k=8)),
                     R=[p.b], W=[onT.b])
        ft = t - nhist
        py = [pb(), pb()]
        HPB = WB // 256
        for hb in range(16 // HPB):
            wb = B["wblk"][hb % len(B["wblk"])]
            wv = wb[:, 0:HPB * 1024].rearrange("p (h n) -> p h n", h=HPB)
            S.dma("sp", wv, Wout_s[:, hb * HPB:(hb + 1) * HPB, :], R=[scrB[id(Wout_s)]], W=[wb.b])
            for hl in range(HPB):
                h_ = hb * HPB + hl
                for half in range(2):
                    S.op("pe", lambda e, hl=hl, h_=h_, half=half, wv=wv: e.matmul(
                        py[half][:], onT[:, h_, :], wv[:, hl, half * 512:(half + 1) * 512],
                        start=(h_ == 0), stop=(h_ == 15)), R=[wb.b, onT.b], W=[py[half].b])
        for half in range(2):
            S.op("dve", lambda e, half=half: e.tensor_tensor(
                hbuf[:, ft, half * 512:(half + 1) * 512], hbuf[:, ft, half * 512:(half + 1) * 512],
                py[half][:], ALU.add), R=[py[half].b, hbuf.bs[ft]], W=[hbuf.bs[ft]])

    def gdn_bufs(ph, N, full):
        B = {}
        B["WB"] = 256
        B["pref"] = {}
        B["xnT"] = [sb(ph, f"xnT{i}", [128, 8, N], BF16) for i in range(2)]
        B["qkvT"] = sb(ph, "qkvT", [128, 32, N], BF16, nslots=32)
        B["xs"] = [sb(ph, f"xs{i}", [128, D]) for i in range(2)] if not full else None
        B["xn"] = [sb(ph, f"xn{i}", [128, D], BF16) for i in range(2)]
        B["junk"] = sb(ph, "junk", [128, D], BF16)
        B["ms"] = sb(ph, "ms", [128, 4])
        B["wblk"] = [sb(ph, f"wblk{i}", [128, 8 * B["WB"]], BF16) for i in range(2 if full else 4)]
        B["u"] = [sb(ph, f"u{i}", [128, N + 3]) for i in range(2)]
        B["acc"] = [sb(ph, f"acc{i}", [128, N]) for i in range(2)]
        B["sq"] = [sb(ph, f"sq{i}", [128, N], BF16) for i in range(2)]
        B["sm"] = [sb(ph, f"sm{i}", [128, 16, (N // 128) * 16], F32, nslots=16) for i in range(2)]
        B["smb"] = [sb(ph, f"smb{i}", [128, 4, (N // 128) * 16], BF16, nslots=4) for i in range(2)]
        B["ktm"] = sb(ph, "ktm", [128, 8, 128], BF16)
        B["kbg"] = sb(ph, "kbg", [128, 16, 128], BF16)
        B["kd"] = sb(ph, "kd", [128, 16, 128], BF16)
        B["vb"] = sb(ph, "vb", [128, 16, 128], BF16)
        B["Asb"] = sb(ph, "Asb", [128, 8, 128], BF16)
        if full:
            B["KQsb"] = sb(ph, "KQsb", [128, 8, 128], BF16)
        G = []
        for g in range(2 if full else 4):
            gb = {}
            gb["dgh"] = sb(ph, f"dgh{g}", [128, 4, 128], BF16)
            gb["dgl"] = sb(ph, f"dgl{g}", [128, 4, 128], BF16)
            gb["Z"] = sb(ph, f"Z{g}", [128, 4, 128])
            gb["Mk"] = sb(ph, f"Mk{g}", [128, 4, 128], BF16)
            gb["Nk"] = sb(ph, f"Nk{g}", [128, 4, 128], BF16)
            gb["V"] = sb(ph, f"V{g}", [128, 4, 128], BF16)
            gb["Mh"] = sb(ph, f"Mh{g}", [128, 4, 128], BF16)
            gb["Ml"] = sb(ph, f"Ml{g}", [128, 4, 128], BF16)
            gb["u"] = sb(ph, f"ug{g}", [128, 4, 128])
            gb["wT"] = sb(ph, f"wT{g}", [128, 4, 128], BF16)
            gb["vn"] = sb(ph, f"vn{g}", [128, 4, 128], BF16)
            if full:
                gb["ER"] = sb(ph, f"ER{g}", [128, 4, 128], BF16)
                gb["DT"] = sb(ph, f"DT{g}", [128, 4, 128])
                gb["aT"] = sb(ph, f"aT{g}", [128, 4, 128], BF16)
                gb["qdT"] = sb(ph, f"qdT{g}", [128, 4, 128], BF16)
            G.append(gb)
        B["G"] = G
        if full:
            B["zs"] = sb(ph, "zs", [128, 2048], BF16)
            B["og"] = [sb(ph, f"og{i}", [128, 4, 128]) for i in range(2)]
            B["oss"] = sb(ph, "oss", [128, 32])
            B["on"] = sb(ph, "on", [128, 16, 128], BF16)
            B["onT"] = sb(ph, "onT", [128, 16, 128], BF16)
        return B

    if nhist > 0:
        with contextlib.ExitStack() as ph:
            B = gdn_bufs(ph, 512, False)
            sts = [list(range(t, min(t + 4, nhist))) for t in range(0, nhist, 4)]
            gdn_st_norm(B, sts[0], False, None, 0)
            for k, tl in enumerate(sts):
                if k + 1 < len(sts):
                    gdn_st_norm(B, sts[k + 1], False, None, k + 1)
                gdn_st_rest(B, tl, False, None, k, k + 1 < len(sts))
                run_bg(8)
            S.barrier()
    run_bg(len(bgq))
    S.barrier()
    ph0.close()

    hbuf = sb(root, "h", [128, NFULL, D], F32, nslots=NFULL)

    with contextlib.ExitStack() as ph:
        B = gdn_bufs(ph, 256, True)
        sts = [[nhist + 2 * st, nhist + 2 * st + 1] for st in range(NFULL // 2)]
        gdn_st_norm(B, sts[0], True, hbuf, 0)
        for k, tl in enumerate(sts):
            if k + 1 < len(sts):
                gdn_st_norm(B, sts[k + 1], True, hbuf, k + 1)
            gdn_st_rest(B, tl, True, hbuf, k)
        S.barrier()

    def ffn_phase(l):
        with contextlib.ExitStack() as ph:
            xnT = sb(ph, "f_xnT", [128, 8, 512], BF16)
            xn = [sb(ph, f"f_xn{i}", [128, D], BF16) for i in range(2)]
            junk = sb(ph, "f_junk", [128, D], BF16)
            ms = sb(ph, "f_ms", [128, 4])
            wblk = [sb(ph, f"f_w{i}", [128, 4096], BF16) for i in range(3)]
            u = [sb(ph, f"f_u{i}", [128, 514]) for i in range(4)]
            acc = [sb(ph, f"f_acc{i}", [128, 512]) for i in range(4)]
            gs = [sb(ph, f"f_gs{i}", [128, 512]) for i in range(2)]
            act = sb(ph, "f_act", [128, 22, 512], BF16, nslots=22)
            fcarry = sb(ph, "f_carry", [128, 44, 2])
            cw = sb(ph, "f_cw", [128, 44, 3])
            cb = sb(ph, "f_cb", [128, 44])
            S.op("pool", lambda e: e.memset(fcarry[:], 0.0), W=[fcarry.b])
            S.dma("sp", cw[:], f_cw_d[l][:, :, :], W=[cw.b])
            S.dma("sp", cb[:], f_cb_d[l][:, :], W=[cb.b])
            sts = [list(range(s, min(s + 4, NFULL))) for s in range(0, NFULL, 4)]

            def ffn_supertile(tiles):
                NT = len(tiles)
                N = NT * 128
                for i, ft in enumerate(tiles):
                    rmsnorm_T((junk, ms, xn[i % 2]), hbuf[:, ft, :], [hbuf.bs[ft]], xnT, i * 128)
                for b in range(11):
                    wb = wblk[b % 3]
                    wv = wb[:].rearrange("p (c n) -> p c n", c=8)
                    S.dma("sp", wv, Wup_s[l][b], R=[scrB[id(Wup_s[l])]], W=[wb.b])
                    for jj in range(2):
                        res = []
                        for which in range(2):
                            f = which * 22 + b * 2 + jj
                            col = which * 256 + jj * 128
                            p = pb()
                            for c in range(8):
                                S.op("pe", lambda e, c=c, p=p, wv=wv, col=col: e.matmul(p[:, 0:N], wv[:, c, col:col + 128], xnT[:, c, 0:N],
                                                                                      start=(c == 0), stop=(c == 7)),
                                     R=[wb.b, xnT.b], W=[p.b])
                            k = (jj * 2 + which)
                            uu, aa = u[k], acc[k]
                            S.op("pool", lambda e, uu=uu, f=f: e.tensor_copy(uu[:, 0:2], fcarry[:, f, :]), R=[fcarry.b], W=[uu.b])
                            S.op("act", lambda e, uu=uu, p=p: e.copy(uu[:, 2:2 + N], p[:, 0:N]), R=[p.b], W=[uu.b])
                            S.op("pool", lambda e, uu=uu, f=f: e.tensor_copy(fcarry[:, f, :], uu[:, N:N + 2]), R=[uu.b], W=[fcarry.b])
                            S.op("dve", lambda e, uu=uu, aa=aa, f=f: e.tensor_scalar(aa[:, 0:N], uu[:, 2:2 + N], cw[:, f, 2:3], cb[:, f:f + 1], ALU.mult, ALU.add),
                                 R=[uu.b, cw.b, cb.b], W=[aa.b])
                            for j in (1, 0):
                                S.op("dve", lambda e, uu=uu, aa=aa, f=f, j=j: e.scalar_tensor_tensor(
                                    aa[:, 0:N], uu[:, j:j + N], cw[:, f, j:j + 1], aa[:, 0:N], ALU.mult, ALU.add),
                                    R=[uu.b, cw.b, aa.b], W=[aa.b])
                            res.append(aa)
                        g_ = gs[jj]
                        S.op("act", lambda e, g_=g_, a0=res[0]: e.activation(g_[:, 0:N], a0[:, 0:N], AF.Silu), R=[res[0].b], W=[g_.b])
                        S.op("pool", lambda e, g_=g_, a1=res[1], b=b, jj=jj: e.tensor_tensor(act[:, b * 2 + jj, 0:N], g_[:, 0:N], a1[:, 0:N], ALU.mult),
                             R=[g_.b, res[1].b], W=[act.bs[b * 2 + jj]])
                pys = [[pb(), pb()] for _ in range(NT)]
                for jb in range(6):
                    nj = 4 if jb < 5 else 2
                    wb = wblk[jb % 3]
                    wv = wb[:].rearrange("p (j n) -> p j n", j=4)
                    S.dma("sp", wv[:, 0:nj, :], Wdn_s[l][:, jb * 4:jb * 4 + nj, :], R=[scrB[id(Wdn_s[l])]], W=[wb.b])
                    for jl in range(nj):
                        j = jb * 4 + jl
                        for i in range(NT):
                            for half in range(2):
                                S.op("pe", lambda e, i=i, j=j, jl=jl, half=half, wv=wv: e.matmul(
                                    pys[i][half][:], act[:, j, i * 128:(i + 1) * 128], wv[:, jl, half * 512:(half + 1) * 512],
                                    start=(j == 0), stop=(j == 21)), R=[wb.b, act.bs[j]], W=[pys[i][half].b])
                for i, ft in enumerate(tiles):
                    for half in range(2):
                        S.op("dve", lambda e, i=i, ft=ft, half=half: e.scalar_tensor_tensor(
                            hbuf[:, ft, half * 512:(half + 1) * 512], pys[i][half][:], validt[:, ft:ft + 1],
                            hbuf[:, ft, half * 512:(half + 1) * 512], ALU.mult, ALU.add),
                            R=[pys[i][half].b, hbuf.bs[ft], validt.b], W=[hbuf.bs[ft]])

            for tiles in sts:
                ffn_supertile(tiles)
            S.barrier()

    if stage >= 2:
        ffn_phase(0)

    def attn_phase():
        with contextlib.ExitStack() as ph:
            NK = NFULL + 1
            xnT = sb(ph, "a_xnT", [128, 8, 512], BF16)
            xn = [sb(ph, f"a_xn{i}", [128, D], BF16) for i in range(2)]
            ms = sb(ph, "a_ms", [128, 4])
            wblk = [sb(ph, f"a_w{i}", [128, 8, 512], BF16) for i in range(3)]
            wk = [0]

            def wnext(src3, n):
                wb = wblk[wk[0] % 3]
                wk[0] += 1
                S.dma("sp", wb[:, :, 0:n], src3, R=[scrB[id(Wkv_s)], scrB[id(Wq_s)], scrB[id(Wo_s)]], W=[wb.b])
                return wb

            KT = sb(ph, "a_KT", [128, 4, NK * 128], BF16)
            Vt = sb(ph, "a_V", [128, NK, 256], BF16)
            QT = sb(ph, "a_QT", [128, 8, 512], BF16)
            BMf = sb(ph, "a_BMf", [128, 4, 256])
            BM = sb(ph, "a_BM", [128, 16, 256], BF16)
            am = sb(ph, "a_am", [128, 256])
            kbf = sb(ph, "a_kbf", [1, 512])
            kbb = sb(ph, "a_kbb", [1, NK * 128], BF16)
            sinkb = sb(ph, "a_sink", [128, 16])
            sc = [sb(ph, f"a_sc{i}", [128, 2, 256]) for i in range(3)]
            pr = [sb(ph, f"a_pr{i}", [128, 2, 256], BF16) for i in range(3)]
            pT = [sb(ph, f"a_pT{i}", [128, 4, 128], BF16) for i in range(3)]
            st_ = sb(ph, "a_st", [128, 6, 16], F32, nslots=8)
            obf = sb(ph, "a_obf", [128, D], BF16)
            junk = obf
            oT = sb(ph, "a_oT", [128, 8, 128], BF16)
            S.dma("sp", am[:], amask_d[:, :], W=[am.b])
            S.dma("sp", sinkb[:], sink_d[:, :], W=[sinkb.b])
            for q4 in range(4):
                S.dma("sp", BMf[:], band_d[:, q4 * 4:(q4 + 1) * 4, :], W=[BMf.b])
                S.op("dve", lambda e, q4=q4: e.tensor_tensor(BM[:, q4 * 4:(q4 + 1) * 4, :], BMf[:], am[:].unsqueeze(1).to_broadcast([128, 4, 256]), ALU.add),
                     R=[BMf.b, am.b], W=[BM.b])
            for k0_ in range(0, NK * 128, 512):
                kn = min(512, NK * 128 - k0_)
                S.dma("sp", kbf[:, 0:kn], kbias_d[:, k0_:k0_ + kn], W=[kbf.b])
                S.op("dve", lambda e, k0_=k0_, kn=kn: e.tensor_copy(kbb[:, k0_:k0_ + kn], kbf[:, 0:kn]), R=[kbf.b], W=[kbb.b])
            S.op("pool", lambda e: e.memset(KT[:, :, 0:128], 0.0), W=[KT.b])
            S.op("pool", lambda e: e.memset(Vt[:, 0, :], 0.0), W=[Vt.b])
            sts = [list(range(s, min(s + 4, NFULL))) for s in range(0, NFULL, 4)]

            def attn_supertile(tiles):
                NT = len(tiles)
                N = NT * 128
                for i, ft in enumerate(tiles):
                    rmsnorm_T((junk, ms, xn[i % 2]), hbuf[:, ft, :], [hbuf.bs[ft]], xnT, i * 128)
                kc0 = (tiles[0] + 1) * 128
                wkK = wnext(Wkv_s[0], 512)
                for j in range(4):
                    p = pb()
                    for c in range(8):
                        S.op("pe", lambda e, c=c, p=p, j=j, wkK=wkK: e.matmul(p[:, 0:N], wkK[:, c, j * 128:(j + 1) * 128], xnT[:, c, 0:N],
                                                                     start=(c == 0), stop=(c == 7)), R=[wkK.b, xnT.b], W=[p.b])
                    S.op("act", lambda e, p=p, j=j: e.copy(KT[:, j, kc0:kc0 + N], p[:, 0:N]), R=[p.b], W=[KT.b])
                wkV = wnext(Wkv_s[1, :, :, 0:256], 256)
                for i, ft in enumerate(tiles):
                    p = pb()
                    for c in range(8):
                        S.op("pe", lambda e, c=c, p=p, i=i, wkV=wkV: e.matmul(p[:, 0:256], xnT[:, c, i * 128:(i + 1) * 128], wkV[:, c, 0:256],
                                                                     start=(c == 0), stop=(c == 7)), R=[wkV.b, xnT.b], W=[p.b])
                    S.op("dve", lambda e, p=p, ft=ft: e.tensor_copy(Vt[:, ft + 1, :], p[:, 0:256]), R=[p.b], W=[Vt.b])
                for f in range(8):
                    if f % 4 == 0:
                        wqb = wnext(Wq_s[f // 4], 512)
                    p = pb()
                    for c in range(8):
                        S.op("pe", lambda e, c=c, p=p, f=f, wqb=wqb: e.matmul(p[:, 0:N], wqb[:, c, (f % 4) * 128:(f % 4 + 1) * 128], xnT[:, c, 0:N],
                                                                     start=(c == 0), stop=(c == 7)), R=[wqb.b, xnT.b], W=[p.b])
                    S.op("act", lambda e, p=p, f=f: e.activation(QT[:, f, 0:N], p[:, 0:N], AF.Copy, scale=0.125), R=[p.b], W=[QT.b])
                for i, ft in enumerate(tiles):
                    attn_tile(i, ft)

            def attn_tile(i, ft):
                if True:
                    k0 = ft * 128
                    pO = [PB[6], PB[7]]
                    reserved.update((6, 7))
                    PS, PTT = {}, {}

                    def stage_a(f):
                        kv = f // 2
                        ps = pb()
                        for hh in range(2):
                            lo = hh * 64
                            S.op("pe", lambda e, ps=ps, hh=hh, lo=lo, f=f, kv=kv: e.matmul(
                                ps[:, hh * 256:(hh + 1) * 256], QT[lo:lo + 64, f, i * 128:(i + 1) * 128], KT[lo:lo + 64, kv, k0:k0 + 256],
                                start=True, stop=False), R=[QT.b, KT.b], W=[ps.b])
                            S.op("pe", lambda e, ps=ps, hh=hh: e.matmul(
                                ps[:, hh * 256:(hh + 1) * 256], onesb[0:1, 0:128], kbb[0:1, k0:k0 + 256], start=False, stop=True),
                                R=[onesb.b, kbb.b], W=[ps.b])
                        s_, p_ = sc[f % 3], pr[f % 3]
                        sb_ = st_.bs[f]
                        S.op("dve", lambda e, ps=ps, s_=s_, f=f: e.tensor_tensor(s_[:], ps[:].rearrange("p (h k) -> p h k", h=2), BM[:, 2 * f:2 * f + 2, :], ALU.add),
                             R=[ps.b, BM.b], W=[s_.b])
                        S.op("dve", lambda e, s_=s_, f=f: e.tensor_reduce(st_[:, 0, 2 * f:2 * f + 2], s_[:], AX.X, ALU.max), R=[s_.b], W=[sb_])
                        S.op("dve", lambda e, f=f: e.tensor_tensor(st_[:, 0, 2 * f:2 * f + 2], st_[:, 0, 2 * f:2 * f + 2], sinkb[:, 2 * f:2 * f + 2], ALU.max),
                             R=[sb_, sinkb.b], W=[sb_])
                        S.op("dve", lambda e, f=f: e.tensor_scalar(st_[:, 1, 2 * f:2 * f + 2], st_[:, 0, 2 * f:2 * f + 2], -1.0, None, ALU.mult),
                             R=[sb_], W=[sb_])
                        for hh in range(2):
                            h_ = 2 * f + hh
                            S.op("act", lambda e, s_=s_, p_=p_, hh=hh, h_=h_: e.activation(p_[:, hh, :], s_[:, hh, :], AF.Exp, bias=st_[:, 1, h_:h_ + 1],
                                                                                           accum_out=st_[:, 2, h_:h_ + 1]),
                                 R=[s_.b, sb_], W=[p_.b, sb_])

                    def stage_b(f):
                        kv = f // 2
                        p_, t_ = pr[f % 3], pT[f % 3]
                        pt = pb()
                        ptv = pt[:].bitcast(BF16)
                        for hh in range(2):
                            for kb in range(2):
                                S.op("pe", lambda e, ptv=ptv, p_=p_, hh=hh, kb=kb: e.transpose(
                                    ptv[:, (hh * 2 + kb) * 128:(hh * 2 + kb + 1) * 128], p_[:, hh, kb * 128:(kb + 1) * 128], identb[:]),
                                    R=[p_.b, identb.b], W=[pt.b])
                        if f % 2 == 0:
                            S.op("act", lambda e, ptv=ptv, t_=t_: e.copy(t_[:].rearrange("p a b -> p (a b)"), ptv[:, 0:512]), R=[pt.b], W=[t_.b])
                        else:
                            S.op("dve", lambda e, ptv=ptv, t_=t_: e.tensor_copy(t_[:].rearrange("p a b -> p (a b)"), ptv[:, 0:512]), R=[pt.b], W=[t_.b])
                        for hh in range(2):
                            h_ = 2 * f + hh
                            for kb in range(2):
                                S.op("pe", lambda e, t_=t_, hh=hh, kb=kb, h_=h_, kv=kv: e.matmul(
                                    pO[h_ // 8][:, (h_ % 8) * 64:(h_ % 8 + 1) * 64], t_[:, hh * 2 + kb, :], Vt[:, ft + kb, kv * 64:(kv + 1) * 64],
                                    start=(kb == 0), stop=(kb == 1)), R=[t_.b, Vt.b], W=[pO[h_ // 8].b])

                    for f in range(9):
                        if f < 8:
                            stage_a(f)
                        if f >= 1:
                            stage_b(f - 1)
                    reserved.clear()
                    S.op("dve", lambda e: e.tensor_tensor(st_[:, 3, :], sinkb[:], st_[:, 1, :], ALU.add), R=[st_.b, sinkb.b] + st_.bs, W=[st_.b])
                    S.op("act", lambda e: e.activation(st_[:, 3, :], st_[:, 3, :], AF.Exp), R=[st_.b], W=[st_.b])
                    S.op("dve", lambda e: e.tensor_tensor(st_[:, 3, :], st_[:, 3, :], st_[:, 2, :], ALU.add), R=[st_.b], W=[st_.b])
                    S.op("dve", lambda e: e.reciprocal(st_[:, 4, :], st_[:, 3, :]), R=[st_.b], W=[st_.b])
                    for half in range(2):
                        S.op("dve", lambda e, half=half: e.tensor_tensor(
                            obf[:, half * 512:(half + 1) * 512].rearrange("p (h d) -> p h d", h=8), pO[half][:].rearrange("p (h d) -> p h d", h=8),
                            st_[:, 4, half * 8:(half + 1) * 8].unsqueeze(2).to_broadcast([128, 8, 64]), ALU.mult),
                            R=[pO[half].b, st_.b], W=[obf.b])
                    p = pb()
                    pv = p[:].bitcast(BF16)
                    for c in range(8):
                        S.op("pe", lambda e, c=c, pv=pv, p=p: e.transpose(pv[:, c * 128:(c + 1) * 128], obf[:, c * 128:(c + 1) * 128], identb[:]),
                             R=[obf.b, identb.b], W=[p.b])
                    S.op("act", lambda e, pv=pv: e.copy(oT[:].rearrange("p c t -> p (c t)"), pv), R=[p.b], W=[oT.b])
                    py = [pb(), pb()]
                    for half in range(2):
                        wob = wnext(Wo_s[half], 512)
                        for c in range(8):
                            S.op("pe", lambda e, c=c, half=half, wob=wob: e.matmul(py[half][:], oT[:, c, :], wob[:, c, :],
                                                                           start=(c == 0), stop=(c == 7)), R=[oT.b, wob.b], W=[py[half].b])
                        S.op("dve", lambda e, half=half, ft=ft: e.scalar_tensor_tensor(
                            hbuf[:, ft, half * 512:(half + 1) * 512], py[half][:], validt[:, ft:ft + 1],
                            hbuf[:, ft, half * 512:(half + 1) * 512], ALU.mult, ALU.add),
                            R=[py[half].b, hbuf.bs[ft], validt.b], W=[hbuf.bs[ft]])

            for tiles in sts:
                attn_supertile(tiles)
            S.barrier()

    if stage >= 3:
        attn_phase()
    if stage >= 4:
        ffn_phase(1)

    toks = []
    with contextlib.ExitStack() as ph:
        fnw = sb(ph, "fnw", [128, D])
        junk = sb(ph, "o_junk", [128, D], BF16)
        ms = sb(ph, "o_ms", [128, 4])
        ob = [sb(ph, f"o_b{i}", [128, D]) for i in range(2)]
        S.dma("sp", fnw[:], fin_w_d[:, :], W=[fnw.b])
        for t in range(NOWN):
            ft = t + NHALO
            o_ = ob[t % 2]
            if stage >= 5:
                S.op("act", lambda e, ft=ft: e.activation(junk[:], hbuf[:, ft, :], AF.Square, scale=1.0 / 32.0, accum_out=ms[:, 0:1]),
                     R=[hbuf.bs[ft]], W=[junk.b, ms.b])
                S.op("act", lambda e: e.activation(ms[:, 1:2], ms[:, 0:1], AF.Ln, bias=EPS), R=[ms.b, cst.b], W=[ms.b])
                S.op("act", lambda e: e.activation(ms[:, 2:3], ms[:, 1:2], AF.Exp, scale=-0.5), R=[ms.b], W=[ms.b])
                S.op("dve", lambda e, ft=ft, o_=o_: e.scalar_tensor_tensor(o_[:], hbuf[:, ft, :], ms[:, 2:3], fnw[:], ALU.mult, ALU.mult),
                     R=[hbuf.bs[ft], ms.b, fnw.b], W=[o_.b])
            else:
                S.op("dve", lambda e, ft=ft, o_=o_: e.tensor_copy(o_[:], hbuf[:, ft, :]), R=[hbuf.bs[ft]], W=[o_.b])
            toks.append(S.dma("sp", out_d[t * 128:(t + 1) * 128, :], o_[:], R=[o_.b]))
        S.finish(toks)
    root.close()
    return nc


def _t5_bucket(dist):
    n = np.maximum(dist, 0)
    nf = np.maximum(n, 1).astype(np.float32)
    large = 16 + (np.log(nf / 16) / np.log(128 / 16) * 16).astype(np.int32)
    large = np.minimum(large, 31)
    return np.where(n < 16, n, large)


def make_inputs(inp, nhist, cores):
    f32 = np.float32
    A = lambda v: np.ascontiguousarray(np.asarray(v), dtype=f32)
    x = A(inp["x"])

    def pc(v):
        return np.ascontiguousarray(A(v).reshape(8, 128).T)

    common = {
        "a_w_in": A(inp["a_w_in"][0]),
        "a_w_out": A(inp["a_w_out"][0]),
        "ffn_w_up0": A(inp["ffn_w_up"][0]), "ffn_w_up1": A(inp["ffn_w_up"][1]),
        "ffn_w_down0": A(inp["ffn_w_down"][0]), "ffn_w_down1": A(inp["ffn_w_down"][1]),
        "b_w_q": A(inp["b_w_q"][0]), "b_w_o": A(inp["b_w_o"][0]),
        "a_norm_wT": pc(inp["a_norm_w"][0]),
        "ffn_norm_wT0": pc(inp["ffn_norm_w"][0]), "ffn_norm_wT1": pc(inp["ffn_norm_w"][1]),
        "kv_norm_wT": pc(inp["kv_norm_w"]), "b_norm_wT": pc(inp["b_norm_w"][0]),
        "out_norm_wT": A(inp["a_out_norm_w"][0]).reshape(128, 1),
        "final_norm_wb": np.ascontiguousarray(np.broadcast_to(A(inp["final_norm_w"])[None, :], (128, D))),
        "a_conv_wT": np.ascontiguousarray(A(inp["a_conv_w"][0]).T.reshape(32, 128, 4).transpose(1, 0, 2)),
        "a_log_b": np.ascontiguousarray(np.broadcast_to(A(inp["a_a_log"][0])[None, :], (128, 16))),
        "dt_bias_b": np.ascontiguousarray(np.broadcast_to(A(inp["a_dt_bias"][0])[None, :], (128, 16))),
        "sinks_b": np.ascontiguousarray(np.broadcast_to(A(inp["b_sinks"][0])[None, :], (128, 16))),
    }
    for l in range(2):
        common[f"ffn_conv_wT{l}"] = np.ascontiguousarray(A(inp["ffn_conv_w"][l]).T.reshape(44, 128, 3).transpose(1, 0, 2))
        common[f"ffn_conv_bT{l}"] = np.ascontiguousarray(A(inp["ffn_conv_b"][l]).reshape(44, 128).T)
    wkv = A(inp["w_kv"])
    cols = []
    for j in range(4):
        cols += [wkv[:, j * 64:(j + 1) * 64], wkv[:, j * 64:(j + 1) * 64]]
    cols.append(wkv[:, 256:512])
    common["w_kv_dup"] = np.ascontiguousarray(np.concatenate(cols, axis=1))
    qi = np.arange(128)[:, None]
    ki = np.arange(256)[None, :]
    dist = qi + 128 - ki
    bucket = _t5_bucket(dist)
    tab = A(inp["rel_bias_table"])
    common["biasband"] = np.ascontiguousarray(tab[bucket].transpose(0, 2, 1))
    inwin = (dist >= 0) & (dist < 128)
    common["attnmask"] = np.where(inwin, 0.0, NEG).astype(f32)
    common["ident"] = np.eye(128, dtype=f32)
    p_ = np.arange(128)[:, None]
    j_ = np.arange(128)[None, :]
    common["triu"] = (p_ <= j_).astype(f32)
    common["maskL"] = np.where(p_ > j_, 0.0, NEG).astype(f32)
    common["maskU"] = np.where(j_ >= p_, 0.0, NEG).astype(f32)
    NT_ALL = nhist + NFULL
    maps = []
    for c in cores:
        b, j = c // 4, c % 4
        end = 2048 * (j + 1)
        start = end - NT_ALL * 128
        xe = np.zeros((NT_ALL * 128, D), f32)
        s0 = max(start, 0)
        xe[s0 - start:] = x[b, s0:end]
        pos_full = np.arange(end - NFULL * 128, end)
        valid = (pos_full >= 0).astype(f32).reshape(NFULL, 128).T
        kb = np.concatenate([np.full(128, NEG, f32), np.where(pos_full >= 0, 0.0, NEG).astype(f32)])[None, :]
        m = dict(common)
        m["x_ext"] = xe
        m["valid"] = np.ascontiguousarray(valid)
        m["kbias"] = np.ascontiguousarray(kb)
        maps.append(m)
    return maps


_NHIST = 46


def kernel(**inputs):
    nc = build_program(_NHIST)
    maps = make_inputs(inputs, _NHIST, list(range(8)))
    res = run_bass_kernel_spmd(nc, maps, core_ids=list(range(8)))
    out = np.empty((2, 8192, D), np.float32)
    for c in range(8):
        b, j = c // 4, c % 4
        out[b, 2048 * j:2048 * (j + 1)] = res.results[c]["out"]
    return out
```

```python
import contextlib
import numpy as np
import concourse.bass as bass
import concourse.mybir as mybir
from concourse.bass_utils import run_bass_kernel_spmd

F32 = mybir.dt.float32
BF16 = mybir.dt.bfloat16
AF = mybir.ActivationFunctionType
ALU = mybir.AluOpType
AX = mybir.AxisListType

D = 1024
NFULL = 18
NHALO = 2
NOWN = 16
NEG = -1.0e30
EPOCH = 30000


class Buf:
    __slots__ = ("name", "w", "r", "excl")

    def __init__(self, name="", excl=False):
        self.name = name
        self.w = None
        self.r = []
        self.excl = excl


class Sched:
    ENGS = ("pe", "act", "dve", "pool", "sp")

    def __init__(self, nc, n_dma_sems=10):
        self.nc = nc
        self.prog = {e: [] for e in self.ENGS}
        self.cnt = {e: 0 for e in self.ENGS}
        self.sems = {}
        self.seen = {e: {} for e in self.ENGS}
        self.dma_sems = {}
        self.n_dma_sems = n_dma_sems
        self.dma_rr = {e: 0 for e in self.ENGS}
        self._semctx = []
        self.last_tok = {}

    def _new_sem(self, name):
        ctx = self.nc.semaphore(name)
        h = ctx.__enter__()
        self._semctx.append(ctx)
        return h

    def _eng_sem(self, eng, idx):
        key = (eng, idx // EPOCH)
        if key not in self.sems:
            self.sems[key] = self._new_sem(f"s_{eng}_{idx // EPOCH}")
        return self.sems[key], (idx % EPOCH) + 1

    def _wait(self, eng, tok):
        teng, sem, val = tok
        if teng == eng and eng == "pe":
            return
        seen = self.seen[eng]
        if seen.get(sem.name, 0) >= val:
            return
        seen[sem.name] = val
        self.prog[eng].append(lambda e, sem=sem, val=val: e.wait_ge(sem, val))

    def _deps(self, eng, reads, writes):
        for b in reads:
            if b.w is not None:
                self._wait(eng, b.w)
            if b.excl:
                for t in b.r:
                    if t[0] != eng:
                        self._wait(eng, t)
        for b in writes:
            if b.w is not None and b.w[0] != eng:
                self._wait(eng, b.w)
            for t in b.r:
                if t[0] != eng:
                    self._wait(eng, t)

    def _commit(self, tok, reads, writes):
        for b in reads:
            b.r.append(tok)
        for b in writes:
            b.w = tok
            b.r = []

    def op(self, eng, fn, R=(), W=()):
        self._deps(eng, R, W)
        idx = self.cnt[eng]
        self.cnt[eng] += 1
        sem, val = self._eng_sem(eng, idx)
        self.prog[eng].append(lambda e, fn=fn, sem=sem: fn(e).then_inc(sem, 1))
        tok = (eng, sem, val)
        self.last_tok[eng] = tok
        self._commit(tok, R, W)
        return tok

    def dma(self, eng, out, in_, R=(), W=()):
        k = self.dma_rr[eng]
        self.dma_rr[eng] = (k + 1) % self.n_dma_sems
        key = (eng, k)
        if key not in self.dma_sems:
            self.dma_sems[key] = [self._new_sem(f"d_{eng}_{k}"), 0]
        ent = self.dma_sems[key]
        sem, tot = ent
        if tot > 0:
            self._wait(eng, ("dma", sem, tot))
        self._deps(eng, R, W)
        ent[1] = tot + 16
        self.prog[eng].append(
            lambda e, out=out, in_=in_, sem=sem: e.dma_start(out=out, in_=in_).then_inc(sem, 16))
        tok = ("dma", sem, tot + 16)
        self._commit(tok, R, W)
        return tok

    def barrier(self):
        toks = list(self.last_tok.values())
        for (eng, k), (sem, tot) in self.dma_sems.items():
            if tot > 0:
                toks.append(("dma", sem, tot))
        for e in self.ENGS:
            for t in toks:
                if t[0] != e or e != "pe":
                    self._wait(e, t)

    def finish(self, final_toks):
        for t in final_toks:
            self._wait("sp", t)
        nc = self.nc
        with nc.Block() as block:
            @block.tensor
            def _(e):
                for f in self.prog["pe"]:
                    f(e)

            @block.scalar
            def _(e):
                for f in self.prog["act"]:
                    f(e)

            @block.vector
            def _(e):
                for f in self.prog["dve"]:
                    f(e)

            @block.gpsimd
            def _(e):
                for f in self.prog["pool"]:
                    f(e)

            @block.sync
            def _(e):
                for f in self.prog["sp"]:
                    f(e)
        for ctx in reversed(self._semctx):
            ctx.__exit__(None, None, None)


class T:
    def __init__(self, t, name, nslots=0):
        self.t = t
        self.b = Buf(name)
        self.bs = [Buf(f"{name}{i}") for i in range(nslots)]

    def __getitem__(self, k):
        return self.t[k]


def build_program(nhist, stage=99, cut=99):
    nc = bass.Bass("TRN2", target_bir_lowering=False)
    S = Sched(nc)
    NT_ALL = nhist + NFULL

    def din(name, shape, dt=F32):
        return nc.dram_tensor(name, list(shape), dt, kind="ExternalInput").ap()

    x_ext = din("x_ext", [NT_ALL * 128, D])
    valid_d = din("valid", [128, NFULL])
    kbias_d = din("kbias", [1, (NFULL + 1) * 128])
    w_in_d = din("a_w_in", [D, 6176])
    w_out_d = din("a_w_out", [2048, D])
    w_up_d = [din(f"ffn_w_up{l}", [D, 5632]) for l in range(2)]
    w_dn_d = [din(f"ffn_w_down{l}", [2816, D]) for l in range(2)]
    w_kv_d = din("w_kv_dup", [D, 768])
    w_q_d = din("b_w_q", [D, D])
    w_o_d = din("b_w_o", [D, D])
    a_nw_d = din("a_norm_wT", [128, 8])
    f_nw_d = [din(f"ffn_norm_wT{l}", [128, 8]) for l in range(2)]
    kv_nw_d = din("kv_norm_wT", [128, 8])
    b_nw_d = din("b_norm_wT", [128, 8])
    on_w_d = din("out_norm_wT", [128, 1])
    fin_w_d = din("final_norm_wb", [128, D])
    a_cw_d = din("a_conv_wT", [128, 32, 4])
    f_cw_d = [din(f"ffn_conv_wT{l}", [128, 44, 3]) for l in range(2)]
    f_cb_d = [din(f"ffn_conv_bT{l}", [128, 44]) for l in range(2)]
    alog_d = din("a_log_b", [128, 16])
    dtb_d = din("dt_bias_b", [128, 16])
    sink_d = din("sinks_b", [128, 16])
    band_d = din("biasband", [128, 16, 256])
    amask_d = din("attnmask", [128, 256])
    ident_d = din("ident", [128, 128])
    triu_d = din("triu", [128, 128])
    maskL_d = din("maskL", [128, 128])
    maskU_d = din("maskU", [128, 128])
    out_d = nc.dram_tensor("out", [NOWN * 128, D], F32, kind="ExternalOutput").ap()

    def dscr(name, shape):
        return nc.dram_tensor(name, list(shape), BF16).ap()

    Win_s = dscr("Win_b", [25, 128, 8, 256])
    Wout_s = dscr("Wout_s", [128, 16, D])
    Wup_s = [dscr(f"Wup_b{l}", [11, 128, 8, 512]) for l in range(2)]
    Wdn_s = [dscr(f"Wdn_s{l}", [128, 22, D]) for l in range(2)]
    Wkv_s = dscr("Wkv_b", [2, 128, 8, 512])
    Wq_s = dscr("Wq_b", [2, 128, 8, 512])
    Wo_s = dscr("Wo_b", [2, 128, 8, 512])
    scrB = {id(a): Buf("scr") for a in [Win_s, Wout_s, Wkv_s, Wq_s, Wo_s] + Wup_s + Wdn_s}

    uid = [0]

    def sb(stk, name, shape, dt=F32, nslots=0):
        uid[0] += 1
        t = stk.enter_context(nc.sbuf_tensor(f"{name}_{uid[0]}", list(shape), dt))
        return T(t, name, nslots)

    root = contextlib.ExitStack()
    PB = [T(root.enter_context(nc.psum_tensor(f"pb{i}", [128, 512], F32)), f"pb{i}") for i in range(8)]
    for p_ in PB:
        p_.b.excl = True
    pbi = [0]

    reserved = set()

    def pb():
        while True:
            k = pbi[0] % 8
            pbi[0] += 1
            if k not in reserved:
                return PB[k]

    identf = sb(root, "identf", [128, 128])
    identb = sb(root, "identb", [128, 128], BF16)
    onesb = sb(root, "onesb", [128, 128], BF16)
    triub = sb(root, "triub", [128, 128], BF16)
    maskLt = sb(root, "maskLt", [128, 128])
    maskUt = sb(root, "maskUt", [128, 128])
    cst = sb(root, "cst", [128, 8])
    validt = sb(root, "validt", [128, NFULL])
    tmpc = sb(root, "tmpc", [128, 128])

    S.dma("sp", identf[:], ident_d[:, :], W=[identf.b])
    S.op("dve", lambda e: e.tensor_copy(identb[:], identf[:]), R=[identf.b], W=[identb.b])
    S.op("pool", lambda e: e.memset(onesb[:], 1.0), W=[onesb.b])
    S.dma("sp", tmpc[:], triu_d[:, :], W=[tmpc.b])
    S.op("dve", lambda e: e.tensor_copy(triub[:], tmpc[:]), R=[tmpc.b], W=[triub.b])
    S.dma("sp", maskLt[:], maskL_d[:, :], W=[maskLt.b])
    S.dma("sp", maskUt[:], maskU_d[:, :], W=[maskUt.b])
    S.op("pool", lambda e: e.memset(cst[:, 0:1], 1e-6), W=[cst.b])
    S.op("pool", lambda e: e.memset(cst[:, 1:2], 1.0), W=[cst.b])
    S.op("pool", lambda e: e.memset(cst[:, 2:3], float(np.log(128.0 ** -0.5))), W=[cst.b])
    S.op("pool", lambda e: e.memset(cst[:, 3:4], 0.0), W=[cst.b])
    S.dma("sp", validt[:], valid_d[:, :], W=[validt.b])
    EPS = cst[:, 0:1]
    ONE = cst[:, 1:2]

    gdn = root
    S32 = sb(gdn, "S32", [128, 16, 128], F32, nslots=4)
    Sbf = sb(gdn, "Sbf", [128, 16, 128], BF16, nslots=4)
    carry = sb(gdn, "carry", [128, 32, 3])
    convw = sb(gdn, "convw", [128, 32, 4])
    wba = sb(gdn, "wba", [128, 8, 32], BF16)
    negA = sb(gdn, "negA", [128, 16])
    dtb = sb(gdn, "dtb", [128, 16])
    S.op("pool", lambda e: e.memset(S32[:], 0.0), W=[S32.b] + S32.bs)
    S.op("pool", lambda e: e.memset(Sbf[:], 0.0), W=[Sbf.b] + Sbf.bs)
    S.op("pool", lambda e: e.memset(carry[:], 0.0), W=[carry.b])
    S.dma("sp", convw[:], a_cw_d[:, :, :], W=[convw.b])
    S.dma("sp", negA[:], alog_d[:, :], W=[negA.b])
    S.dma("sp", dtb[:], dtb_d[:, :], W=[dtb.b])
    S.op("act", lambda e: e.activation(negA[:], negA[:], AF.Exp), R=[negA.b], W=[negA.b])
    S.op("dve", lambda e: e.tensor_scalar(negA[:], negA[:], -1.0, None, ALU.mult), R=[negA.b], W=[negA.b])

    ph0 = contextlib.ExitStack()
    bgq = []
    if True:
        ph = ph0
        stg = [sb(ph, f"stg{i}", [128, 2048]) for i in range(2)]
        cvt = [sb(ph, f"cvt{i}", [128, 2048], BF16) for i in range(2)]
        nws = sb(ph, "nws", [128, 8 * 5 + 1])
        nw_ap = {}
        for i, (nm, d_) in enumerate([("a", a_nw_d), ("f0", f_nw_d[0]), ("f1", f_nw_d[1]), ("kv", kv_nw_d), ("b", b_nw_d)]):
            S.dma("sp", nws[:, i * 8:(i + 1) * 8], d_[:, :], W=[nws.b])
            nw_ap[nm] = (i * 8)
        S.dma("sp", nws[:, 40:41], on_w_d[:, :], W=[nws.b])
        blk = [0]

        def convert(src3, dst3, C, N, scale=None, dstfn=None, ng=512, defer=False):
            ng = min(N, ng)
            cg = max(1, min(C, 2048 // ng))

            def emit(c0, cc, n0, nn):
                i = blk[0] % 2
                blk[0] += 1
                st, cv = stg[i], cvt[i]
                sv = st[:, 0:cc * nn].rearrange("p (c n) -> p c n", c=cc)
                cvv = cv[:, 0:cc * nn].rearrange("p (c n) -> p c n", c=cc)
                S.dma("sp", sv, src3[:, c0:c0 + cc, n0:n0 + nn], W=[st.b])
                eng = "dve" if (blk[0] % 2 == 0) else "pool"
                if scale is None:
                    S.op(eng, lambda e: e.tensor_copy(cvv, sv), R=[st.b], W=[cv.b])
                elif scale[0] == "pc":
                    g = nws[:, scale[1] + c0:scale[1] + c0 + cc].unsqueeze(2).to_broadcast([128, cc, nn])
                    S.op(eng, lambda e: e.tensor_tensor(cvv, sv, g, ALU.mult), R=[st.b, nws.b], W=[cv.b])
                else:
                    g = nws[:, scale[1]:scale[1] + 1].unsqueeze(2).to_broadcast([128, cc, nn])
                    S.op(eng, lambda e: e.tensor_tensor(cvv, sv, g, ALU.mult), R=[st.b, nws.b], W=[cv.b])
                dap = dst3[:, c0:c0 + cc, n0:n0 + nn] if dstfn is None else dstfn(c0, cc, n0, nn)
                S.dma("pool", dap, cvv, R=[cv.b], W=[scrB[id(dst3)]])

            for c0 in range(0, C, cg):
                cc = min(cg, C - c0)
                for n0 in range(0, N, ng):
                    nn = min(ng, N - n0)
                    if defer:
                        bgq.append(lambda c0=c0, cc=cc, n0=n0, nn=nn: emit(c0, cc, n0, nn))
                    else:
                        emit(c0, cc, n0, nn)

        def pcn(ap):
            return ap.rearrange("(c p) n -> p c n", p=128)

        convert(pcn(w_in_d), Win_s, 8, 6176, ("pc", nw_ap["a"]), ng=256,
                dstfn=lambda c0, cc, n0, nn: Win_s[n0 // 256, :, c0:c0 + cc, 0:nn])
        convert(pcn(w_out_d), Wout_s, 16, D, ("p", 40))
        if stage >= 2:
            for l in range(2):
                convert(pcn(w_up_d[l]), Wup_s[l], 8, 5632, ("pc", nw_ap[f"f{l}"]), ng=256,
                        dstfn=lambda c0, cc, n0, nn, l=l: (Wup_s[l][n0 // 256, :, c0:c0 + cc, 0:256] if n0 < 2816
                                                           else Wup_s[l][(n0 - 2816) // 256, :, c0:c0 + cc, 256:512]), defer=True)
                convert(pcn(w_dn_d[l]), Wdn_s[l], 22, D, None, defer=True)
        if stage >= 3:
            convert(pcn(w_kv_d), Wkv_s, 8, 768, ("pc", nw_ap["kv"]), ng=256,
                    dstfn=lambda c0, cc, n0, nn: (Wkv_s[0, :, c0:c0 + cc, n0:n0 + nn] if n0 < 512 else Wkv_s[1, :, c0:c0 + cc, 0:nn]), defer=True)
            convert(pcn(w_q_d), Wq_s, 8, D, ("pc", nw_ap["b"]), ng=512,
                    dstfn=lambda c0, cc, n0, nn: Wq_s[n0 // 512, :, c0:c0 + cc, 0:nn], defer=True)
            convert(pcn(w_o_d), Wo_s, 8, D, None, ng=512,
                    dstfn=lambda c0, cc, n0, nn: Wo_s[n0 // 512, :, c0:c0 + cc, 0:nn], defer=True)
        S.dma("sp", wba[:], Win_s[24, :, :, 0:32], R=[scrB[id(Win_s)]], W=[wba.b])

    def run_bg(n):
        for _ in range(min(n, len(bgq))):
            bgq.pop(0)()


    def rmsnorm_T(ph_bufs, src_ap, src_bufs, xnT, col0):
        junk, ms, xn = ph_bufs
        S.op("act", lambda e: e.activation(junk[:], src_ap, AF.Square, scale=1.0 / 32.0, accum_out=ms[:, 0:1]),
             R=src_bufs, W=[junk.b, ms.b])
        S.op("act", lambda e: e.activation(ms[:, 1:2], ms[:, 0:1], AF.Ln, bias=EPS), R=[ms.b, cst.b], W=[ms.b])
        S.op("act", lambda e: e.activation(ms[:, 2:3], ms[:, 1:2], AF.Exp, scale=-0.5), R=[ms.b], W=[ms.b])
        S.op("dve", lambda e: e.tensor_scalar(xn[:], src_ap, ms[:, 2:3], None, ALU.mult), R=src_bufs + [ms.b], W=[xn.b])
        p = pb()
        pv = p[:].bitcast(BF16)
        for c in range(8):
            S.op("pe", lambda e, c=c: e.transpose(pv[:, c * 128:(c + 1) * 128], xn[:, c * 128:(c + 1) * 128], identb[:]),
                 R=[xn.b, identb.b], W=[p.b])
        S.op("act", lambda e: e.copy(xnT[:, :, col0:col0 + 128], pv.rearrange("p (c t) -> p c t", c=8)),
             R=[p.b], W=[xnT.b])

    def wload(B, k, src3, nparts):
        wb = B["wblk"][k % len(B["wblk"])]
        n = src3.shape[2]
        wv = wb[:, 0:nparts * n].rearrange("p (c n) -> p c n", c=nparts)
        return wb, wv

    def gdn_st_norm(B, tiles, full, hbuf, sp):
        xnT = B["xnT"][sp % 2]
        for i, t in enumerate(tiles):
            if full:
                ft = t - nhist
                src, sbufs = hbuf[:, ft, :], [hbuf.bs[ft]]
            else:
                xs = B["xs"][t % 2]
                src, sbufs = xs[:], [xs.b]
            S.dma("sp", src, x_ext[t * 128:(t + 1) * 128, :], W=sbufs)
            rmsnorm_T((B["junk"], B["ms"], B["xn"][i % 2]), src, sbufs, xnT, i * 128)

    def gdn_st_rest(B, tiles, full, hbuf, sp, has_next=False):
        NT = len(tiles)
        N = NT * 128
        WB = B["WB"]
        CPB = WB // 128
        xnT, qkvT = B["xnT"][sp % 2], B["qkvT"]
        gdn_st_small(B, xnT, NT, sp)
        f0 = 0 if full else 8
        wb = None
        for f in range(f0, 32):
            if f % CPB == 0 or wb is None:
                blk = f // CPB
                if blk in B["pref"]:
                    wb, wv = B["pref"].pop(blk)
                else:
                    wb, wv = wload(B, blk, Win_s[blk], 8)
                    S.dma("sp", wv, Win_s[blk], R=[scrB[id(Win_s)]], W=[wb.b])
            p = pb()
            for c in range(8):
                S.op("pe", lambda e, c=c, wv=wv, f=f, p=p: e.matmul(p[:, 0:N], wv[:, c, (f % CPB) * 128:(f % CPB + 1) * 128],
                                                                    xnT[:, c, 0:N], start=(c == 0), stop=(c == 7)),
                     R=[wb.b, xnT.b], W=[p.b])
            u = B["u"][f % 2]
            acc = B["acc"][f % 2]
            S.op("pool", lambda e, u=u, f=f: e.tensor_copy(u[:, 0:3], carry[:, f, :]), R=[carry.b], W=[u.b])
            S.op("act", lambda e, u=u, p=p: e.copy(u[:, 3:3 + N], p[:, 0:N]), R=[p.b], W=[u.b])
            S.op("pool", lambda e, u=u, f=f: e.tensor_copy(carry[:, f, :], u[:, N:N + 3]), R=[u.b], W=[carry.b])
            S.op("dve", lambda e, u=u, acc=acc, f=f: e.tensor_scalar(acc[:, 0:N], u[:, 3:3 + N], convw[:, f, 3:4], None, ALU.mult),
                 R=[u.b, convw.b], W=[acc.b])
            for j in (2, 1, 0):
                S.op("dve", lambda e, u=u, acc=acc, f=f, j=j: e.scalar_tensor_tensor(
                    acc[:, 0:N], u[:, j:j + N], convw[:, f, j:j + 1], acc[:, 0:N], ALU.mult, ALU.add),
                    R=[u.b, convw.b, acc.b], W=[acc.b])
            S.op("act", lambda e, acc=acc, f=f: e.activation(qkvT[:, f, 0:N], acc[:, 0:N], AF.Silu),
                 R=[acc.b], W=[qkvT.bs[f]])
        for f in range(f0, 16):
            sq = B["sq"][f % 2]
            S.op("pool", lambda e, sq=sq, f=f: e.tensor_tensor(sq[:, 0:N], qkvT[:, f, 0:N], qkvT[:, f, 0:N], ALU.mult),
                 R=[qkvT.bs[f]], W=[sq.b])
            p = pb()
            S.op("pe", lambda e, p=p, sq=sq: e.matmul(p[:, 0:N], onesb[:], sq[:, 0:N], start=True, stop=True),
                 R=[onesb.b, sq.b], W=[p.b])
            rn = B["acc"][f % 2]
            S.op("act", lambda e, p=p, rn=rn: e.activation(rn[:, 0:N], p[:, 0:N], AF.Ln, bias=EPS), R=[p.b, cst.b], W=[rn.b])
            bias = cst[:, 2:3] if f < 8 else cst[:, 3:4]
            S.op("act", lambda e, rn=rn, bias=bias: e.activation(rn[:, 0:N], rn[:, 0:N], AF.Exp, scale=-0.5, bias=bias),
                 R=[rn.b, cst.b], W=[rn.b])
            S.op("dve", lambda e, rn=rn, f=f: e.tensor_tensor(qkvT[:, f, 0:N], qkvT[:, f, 0:N], rn[:, 0:N], ALU.mult),
                 R=[rn.b, qkvT.bs[f]], W=[qkvT.bs[f]])
        if cut < 2:
            return
        if has_next and not full:
            for blk in range(f0 // CPB, f0 // CPB + len(B["wblk"])):
                wb, wv = wload(B, blk, Win_s[blk], 8)
                S.dma("sp", wv, Win_s[blk], R=[scrB[id(Win_s)]], W=[wb.b])
                B["pref"][blk] = (wb, wv)
        for i, t in enumerate(tiles):
            gdn_tile(B, xnT, i, t, full, hbuf, sp)

    def gdn_st_small(B, xnT, NT, sp):
        sm = B["sm"][sp % 2]
        smb = B["smb"][sp % 2]
        W_ = NT * 16
        SM = lambda k: sm[:, k, 0:W_].rearrange("p (i h) -> p i h", h=16)
        SB = lambda k: smb[:, k, 0:W_].rearrange("p (i h) -> p i h", h=16)
        r = lambda k: sm.bs[k]
        rb = lambda k: smb.bs[k]
        bc = lambda t_: t_[:].unsqueeze(1).to_broadcast([128, NT, 16])
        pba = pb()
        for i in range(NT):
            for c in range(8):
                S.op("pe", lambda e, c=c, i=i: e.matmul(pba[:, i * 32:(i + 1) * 32], xnT[:, c, i * 128:(i + 1) * 128], wba[:, c, :],
                                                        start=(c == 0), stop=(c == 7)), R=[xnT.b, wba.b], W=[pba.b])
        pbv = pba[:, 0:NT * 32].rearrange("p (i c) -> p i c", c=32)
        S.op("dve", lambda e: e.tensor_tensor(SM(0), pbv[:, :, 16:32], bc(dtb), ALU.add), R=[pba.b, dtb.b], W=[r(0)])
        S.op("act", lambda e: e.activation(SM(4), pbv[:, :, 0:16], AF.Exp, scale=-1.0), R=[pba.b], W=[r(4)])
        S.op("act", lambda e: e.activation(SM(1), SM(0), AF.Abs), R=[r(0)], W=[r(1)])
        S.op("act", lambda e: e.activation(SM(1), SM(1), AF.Exp, scale=-1.0), R=[r(1)], W=[r(1)])
        S.op("act", lambda e: e.activation(SM(1), SM(1), AF.Ln, bias=ONE), R=[r(1), cst.b], W=[r(1)])
        S.op("dve", lambda e: e.tensor_scalar(SM(4), SM(4), 1.0, None, ALU.add), R=[r(4)], W=[r(4)])
        S.op("dve", lambda e: e.reciprocal(SM(5), SM(4)), R=[r(4)], W=[r(5)])
        S.op("dve", lambda e: e.scalar_tensor_tensor(SM(2), SM(0), 0.0, SM(1), ALU.max, ALU.add), R=[r(0), r(1)], W=[r(2)])
        S.op("dve", lambda e: e.tensor_tensor(SM(3), SM(2), bc(negA), ALU.mult), R=[r(2), negA.b], W=[r(3)])
        S.op("dve", lambda e: e.tensor_copy(SB(0), SM(3)), R=[r(3)], W=[rb(0)])
        S.op("dve", lambda e: e.tensor_copy(SM(6), SB(0)), R=[rb(0)], W=[r(6)])
        S.op("dve", lambda e: e.tensor_tensor(SB(1), SM(3), SM(6), ALU.subtract), R=[r(3), r(6)], W=[rb(1)])
        pc = pb()
        for i in range(NT):
            for k in range(2):
                S.op("pe", lambda e, i=i, k=k: e.matmul(pc[:, i * 32:i * 32 + 16], triub[:], smb[:, k, i * 16:(i + 1) * 16], start=(k == 0), stop=(k == 1)),
                     R=[triub.b, rb(k)], W=[pc.b])
            for k in range(2):
                S.op("pe", lambda e, i=i, k=k: e.matmul(pc[:, i * 32 + 16:i * 32 + 32], onesb[:], smb[:, k, i * 16:(i + 1) * 16], start=(k == 0), stop=(k == 1)),
                     R=[onesb.b, rb(k)], W=[pc.b])
        pcv = pc[:, 0:NT * 32].rearrange("p (i c) -> p i c", c=32)
        S.op("dve", lambda e: e.tensor_copy(SM(7), pcv[:, :, 0:16]), R=[pc.b], W=[r(7)])
        S.op("dve", lambda e: e.tensor_copy(SM(8), pcv[:, :, 16:32]), R=[pc.b], W=[r(8)])
        S.op("act", lambda e: e.activation(SM(9), SM(8), AF.Exp), R=[r(8)], W=[r(9)])
        S.op("dve", lambda e: e.tensor_tensor(SM(10), SM(8), SM(7), ALU.subtract), R=[r(7), r(8)], W=[r(10)])
        S.op("act", lambda e: e.activation(SM(15), SM(7), AF.Exp), R=[r(7)], W=[r(15)])
        S.op("act", lambda e: e.activation(SM(10), SM(10), AF.Exp), R=[r(10)], W=[r(10)])
        S.op("dve", lambda e: e.tensor_scalar(SM(12), SM(7), -1.0, None, ALU.mult), R=[r(7)], W=[r(12)])
        S.op("dve", lambda e: e.tensor_copy(SB(2), SM(12)), R=[r(12)], W=[rb(2)])
        S.op("dve", lambda e: e.tensor_copy(SM(13), SB(2)), R=[rb(2)], W=[r(13)])
        S.op("dve", lambda e: e.tensor_tensor(SM(14), SM(12), SM(13), ALU.subtract), R=[r(12), r(13)], W=[r(14)])
        S.op("dve", lambda e: e.tensor_tensor(SM(11), SM(5), SM(15), ALU.mult), R=[r(5), r(15)], W=[r(11)])

    def gdn_tile(B, xnT, i, t, full, hbuf, sp):
        qkvT = B["qkvT"]
        WB = B["WB"]
        c0 = i * 128
        sm = B["sm"][sp % 2]
        o16 = i * 16
        SM = lambda k: sm[:, k, o16:o16 + 16]
        if full:
            zs = B["zs"]
            nzb = 2048 // WB
            for blk in range(nzb):
                wb, wv = wload(B, blk, Win_s[16 + blk], 8)
                S.dma("sp", wv, Win_s[16 + blk], R=[scrB[id(Win_s)]], W=[wb.b])
                p = pb()
                for c in range(8):
                    S.op("pe", lambda e, c=c, wv=wv, p=p: e.matmul(p[:, 0:WB], xnT[:, c, c0:c0 + 128], wv[:, c, :],
                                                                   start=(c == 0), stop=(c == 7)),
                         R=[wb.b, xnT.b], W=[p.b])
                S.op("act", lambda e, p=p, blk=blk: e.activation(zs[:, blk * WB:(blk + 1) * WB], p[:, 0:WB], AF.Silu),
                     R=[p.b], W=[zs.b])
        ktm, kbg, kd, vb = B["ktm"], B["kbg"], B["kd"], B["vb"]
        p = pb()
        pv = p[:].bitcast(BF16)
        for kh in range(8):
            S.op("pe", lambda e, kh=kh, pv=pv: e.transpose(pv[:, kh * 128:(kh + 1) * 128], qkvT[:, 8 + kh, c0:c0 + 128], identb[:]),
                 R=[qkvT.bs[8 + kh], identb.b], W=[p.b])
        S.op("act", lambda e, pv=pv: e.copy(ktm[:], pv.rearrange("p (k d) -> p k d", k=8)), R=[p.b], W=[ktm.b])
        if cut < 3.2:
            return
        k4 = ktm[:].unsqueeze(2).to_broadcast([128, 8, 2, 128])
        S.op("pool", lambda e: e.tensor_tensor(kbg[:].rearrange("p (k r) d -> p k r d", r=2), k4,
                                               SM(11).rearrange("p (k r) -> p k r", r=2).unsqueeze(3).to_broadcast([128, 8, 2, 128]),
                                               ALU.mult), R=[ktm.b, sm.bs[11]], W=[kbg.b])
        S.op("pool", lambda e: e.tensor_tensor(kd[:].rearrange("p (k r) d -> p k r d", r=2), k4,
                                               SM(10).rearrange("p (k r) -> p k r", r=2).unsqueeze(3).to_broadcast([128, 8, 2, 128]),
                                               ALU.mult), R=[ktm.b, sm.bs[10]], W=[kd.b])
        if cut < 3.4:
            return
        for half in range(2):
            p = pb()
            pv = p[:].bitcast(BF16)
            for j in range(8):
                h_ = half * 8 + j
                S.op("pe", lambda e, j=j, h_=h_, pv=pv: e.transpose(pv[:, j * 128:(j + 1) * 128], qkvT[:, 16 + h_, c0:c0 + 128], identb[:]),
                     R=[qkvT.bs[16 + h_], identb.b], W=[p.b])
            S.op("dve", lambda e, half=half, pv=pv: e.tensor_tensor(
                vb[:, half * 8:(half + 1) * 8, :], pv.rearrange("p (k d) -> p k d", k=8),
                sm[:, 5, o16 + half * 8:o16 + (half + 1) * 8].unsqueeze(2).to_broadcast([128, 8, 128]), ALU.mult),
                R=[p.b, sm.bs[5]], W=[vb.b])
        if cut < 3.6:
            return
        Asb = B["Asb"]
        pA = [pb(), pb()]
        for kh in range(8):
            S.op("pe", lambda e, kh=kh: e.matmul(pA[kh // 4][:, (kh % 4) * 128:(kh % 4 + 1) * 128], qkvT[:, 8 + kh, c0:c0 + 128],
                                                 qkvT[:, 8 + kh, c0:c0 + 128], start=True, stop=True),
                 R=[qkvT.bs[8 + kh]], W=[pA[kh // 4].b])
        for j in range(2):
            S.op("act", lambda e, j=j: e.copy(Asb[:, j * 4:(j + 1) * 4, :], pA[j][:].rearrange("p (k s) -> p k s", k=4)),
                 R=[pA[j].b], W=[Asb.b])
        if cut < 3.8:
            return
        if full:
            KQsb = B["KQsb"]
            pK = [pb(), pb()]
            for kh in range(8):
                S.op("pe", lambda e, kh=kh: e.matmul(pK[kh // 4][:, (kh % 4) * 128:(kh % 4 + 1) * 128], qkvT[:, 8 + kh, c0:c0 + 128],
                                                     qkvT[:, kh, c0:c0 + 128], start=True, stop=True),
                     R=[qkvT.bs[8 + kh], qkvT.bs[kh]], W=[pK[kh // 4].b])
            for j in range(2):
                S.op("dve", lambda e, j=j: e.tensor_copy(KQsb[:, j * 4:(j + 1) * 4, :], pK[j][:].rearrange("p (k s) -> p k s", k=4)),
                     R=[pK[j].b], W=[KQsb.b])
        if cut < 4:
            return
        G = B["G"]
        b4 = lambda ap2: ap2.unsqueeze(1).to_broadcast([128, 4, 128])
        col4 = lambda k, h0: sm[:, k, o16 + h0:o16 + h0 + 4].unsqueeze(2).to_broadcast([128, 4, 128])
        flat = lambda t_: t_[:].rearrange("p j s -> p (j s)")
        kr = lambda ap3: ap3.rearrange("p (k r) s -> p k r s", r=2)
        NG = len(G)
        for gp in range(4 // NG):
            grp = list(range(NG * gp, NG * gp + NG))
            GB = {g: G[g % NG] for g in grp}
            H0 = {g: g * 4 for g in grp}
            for g in grp:
                gb, h0 = GB[g], H0[g]
                S.op("dve", lambda e, gb=gb, h0=h0: e.tensor_tensor(gb["dgh"][:], b4(identf[:]), col4(13, h0), ALU.mult),
                     R=[identf.b, sm.bs[13]], W=[gb["dgh"].b])
                S.op("pool", lambda e, gb=gb, h0=h0: e.tensor_tensor(gb["dgl"][:], b4(identf[:]), col4(14, h0), ALU.mult),
                     R=[identf.b, sm.bs[14]], W=[gb["dgl"].b])
            PR = {}
            for g in grp:
                gb = GB[g]
                pR = pb()
                PR[g] = pR
                S.op("pe", lambda e, pR=pR, gb=gb: e.matmul(pR[:], onesb[:], flat(gb["dgh"]), start=True, stop=False),
                     R=[onesb.b, gb["dgh"].b], W=[pR.b])
                S.op("pe", lambda e, pR=pR, gb=gb: e.matmul(pR[:], onesb[:], flat(gb["dgl"]), start=False, stop=True),
                     R=[onesb.b, gb["dgl"].b], W=[pR.b])
            for g in grp:
                gb, h0, pR = GB[g], H0[g], PR[g]
                S.op("dve", lambda e, pR=pR, gb=gb, h0=h0: e.tensor_tensor(gb["Z"][:], pR[:].rearrange("p (j s) -> p j s", j=4), col4(7, h0), ALU.add),
                     R=[pR.b, sm.bs[7]], W=[gb["Z"].b])
            if full:
                for g in grp:
                    gb, pR = GB[g], PR[g]
                    S.op("act", lambda e, pR=pR, gb=gb: e.activation(flat(gb["ER"]), pR[:], AF.Exp, scale=-1.0), R=[pR.b], W=[gb["ER"].b])
                for g in grp:
                    gb = GB[g]
                    S.op("dve", lambda e, gb=gb: e.scalar_tensor_tensor(gb["DT"][:], gb["Z"][:], -1.0, b4(maskUt[:]), ALU.mult, ALU.add),
                         R=[gb["Z"].b, maskUt.b], W=[gb["DT"].b])
            for g in grp:
                gb = GB[g]
                S.op("pool", lambda e, gb=gb: e.tensor_tensor(gb["Z"][:], gb["Z"][:], b4(maskLt[:]), ALU.add), R=[gb["Z"].b, maskLt.b], W=[gb["Z"].b])
            if full:
                for g in grp:
                    gb = GB[g]
                    S.op("act", lambda e, gb=gb: e.activation(gb["DT"][:], gb["DT"][:], AF.Exp), R=[gb["DT"].b], W=[gb["DT"].b])
            for g in grp:
                gb = GB[g]
                S.op("act", lambda e, gb=gb: e.activation(gb["Z"][:], gb["Z"][:], AF.Exp), R=[gb["Z"].b], W=[gb["Z"].b])
            for g in grp:
                gb, h0 = GB[g], H0[g]
                S.op("pool", lambda e, gb=gb, h0=h0: e.tensor_tensor(gb["Z"][:], gb["Z"][:], col4(5, h0), ALU.mult), R=[gb["Z"].b, sm.bs[5]], W=[gb["Z"].b])
            if full:
                for g in grp:
                    gb, kh0 = GB[g], H0[g] // 2
                    S.op("pool", lambda e, gb=gb, kh0=kh0: e.tensor_tensor(
                        kr(gb["aT"][:]), kr(gb["DT"][:]), KQsb[:, kh0:kh0 + 2, :].unsqueeze(2).to_broadcast([128, 2, 2, 128]), ALU.mult),
                        R=[gb["DT"].b, KQsb.b], W=[gb["aT"].b])
                    S.op("pool", lambda e, gb=gb, kh0=kh0: e.tensor_tensor(
                        kr(gb["qdT"][:]), kr(gb["ER"][:]), qkvT[:, kh0:kh0 + 2, c0:c0 + 128].unsqueeze(2).to_broadcast([128, 2, 2, 128]), ALU.mult),
                        R=[gb["ER"].b, qkvT.bs[kh0], qkvT.bs[kh0 + 1]], W=[gb["qdT"].b])
            for g in grp:
                gb, kh0 = GB[g], H0[g] // 2
                S.op("dve", lambda e, gb=gb, kh0=kh0: e.tensor_tensor(
                    kr(gb["Mk"][:]), kr(gb["Z"][:]), Asb[:, kh0:kh0 + 2, :].unsqueeze(2).to_broadcast([128, 2, 2, 128]), ALU.mult),
                    R=[gb["Z"].b, Asb.b], W=[gb["Mk"].b])
            for g in grp:
                gb = GB[g]
                S.op("pool", lambda e, gb=gb: e.tensor_copy(gb["Mh"][:], gb["Mk"][:]), R=[gb["Mk"].b], W=[gb["Mh"].b])
            PT = {}
            for g in grp:
                gb = GB[g]
                p = pb()
                PT[g] = p
                pv = p[:].bitcast(BF16)
                for j in range(4):
                    S.op("pe", lambda e, j=j, pv=pv, gb=gb: e.transpose(pv[:, j * 128:(j + 1) * 128], gb["Mk"][:, j, :], identb[:]),
                         R=[gb["Mk"].b, identb.b], W=[p.b])
            for g in grp:
                gb, p = GB[g], PT[g]
                pv = p[:].bitcast(BF16)
                S.op("act", lambda e, gb=gb, pv=pv: e.copy(flat(gb["Nk"]), pv[:, 0:512]), R=[p.b], W=[gb["Nk"].b])
                S.op("dve", lambda e, gb=gb, pv=pv: e.scalar_tensor_tensor(
                    gb["V"][:], pv[:, 0:512].rearrange("p (j s) -> p j s", j=4), -1.0, b4(identb[:]), ALU.mult, ALU.add),
                    R=[p.b, identb.b], W=[gb["V"].b])
            for r in range(1, 7):
                for g in grp:
                    gb = GB[g]
                    Mk, Nk, V = gb["Mk"], gb["Nk"], gb["V"]
                    pM = pN = pV = None
                    if r <= 5:
                        pM = pb()
                        for j in range(4):
                            S.op("pe", lambda e, j=j, pM=pM, Mk=Mk, Nk=Nk: e.matmul(
                                pM[:, j * 128:(j + 1) * 128], Nk[:, j, :], Mk[:, j, :], start=True, stop=True),
                                R=[Nk.b, Mk.b], W=[pM.b])
                    if r <= 5:
                        pN = pb()
                        for j in range(4):
                            S.op("pe", lambda e, j=j, pN=pN, Mk=Mk, Nk=Nk: e.matmul(
                                pN[:, j * 128:(j + 1) * 128], Mk[:, j, :], Nk[:, j, :], start=True, stop=True),
                                R=[Nk.b, Mk.b], W=[pN.b])
                    if r >= 2:
                        pV = pb()
                        for j in range(4):
                            S.op("pe", lambda e, j=j, pV=pV, Mk=Mk, V=V: e.matmul(
                                pV[:, j * 128:(j + 1) * 128], Mk[:, j, :], V[:, j, :], start=True, stop=True),
                                R=[Mk.b, V.b], W=[pV.b])
                        S.op("dve", lambda e, pV=pV, V=V: e.tensor_tensor(flat(V), pV[:], flat(V), ALU.add),
                             R=[pV.b, V.b], W=[V.b])
                    if pM is not None:
                        S.op("act", lambda e, pM=pM, Mk=Mk: e.copy(flat(Mk), pM[:]), R=[pM.b], W=[Mk.b])
                    if pN is not None:
                        S.op("act", lambda e, pN=pN, Nk=Nk: e.copy(flat(Nk), pN[:]), R=[pN.b], W=[Nk.b])
            PNV, PT0 = {}, {}
            for g in grp:
                gb = GB[g]
                V, Mh = gb["V"], gb["Mh"]
                pNV = pb()
                PNV[g] = pNV
                for j in range(4):
                    S.op("pe", lambda e, j=j, pNV=pNV, Mh=Mh, V=V: e.matmul(pNV[:, j * 128:(j + 1) * 128], Mh[:, j, :], V[:, j, :], start=True, stop=True),
                         R=[Mh.b, V.b], W=[pNV.b])
                pT0 = pb()
                PT0[g] = pT0
                pT0v = pT0[:].bitcast(BF16)
                for j in range(4):
                    S.op("pe", lambda e, j=j, pT0v=pT0v, V=V: e.transpose(pT0v[:, j * 128:(j + 1) * 128], V[:, j, :], identb[:]),
                         R=[V.b, identb.b], W=[pT0.b])
            for g in grp:
                gb = GB[g]
                V, Z, Rv, T0 = gb["V"], gb["Z"], gb["dgh"], gb["Nk"]
                pNV, pT0 = PNV[g], PT0[g]
                pT0v = pT0[:].bitcast(BF16)
                S.op("dve", lambda e, pNV=pNV, Z=Z, V=V: e.scalar_tensor_tensor(flat(Z), pNV[:], -1.0, flat(V), ALU.mult, ALU.subtract),
                     R=[pNV.b, V.b], W=[Z.b])
                S.op("pool", lambda e, Z=Z, Rv=Rv: e.tensor_tensor(Rv[:], Z[:], b4(identf[:]), ALU.add), R=[Z.b, identf.b], W=[Rv.b])
                S.op("act", lambda e, pT0v=pT0v, T0=T0: e.copy(flat(T0), pT0v[:, 0:512]), R=[pT0.b], W=[T0.b])
            PVR = {}
            for g in grp:
                gb = GB[g]
                Rv, T0 = gb["dgh"], gb["Nk"]
                pVR = pb()
                PVR[g] = pVR
                for j in range(4):
                    S.op("pe", lambda e, j=j, pVR=pVR, T0=T0, Rv=Rv: e.matmul(pVR[:, j * 128:(j + 1) * 128], T0[:, j, :], Rv[:, j, :], start=True, stop=True),
                         R=[T0.b, Rv.b], W=[pVR.b])
            for g in grp:
                gb = GB[g]
                V, Z, Vl, pVR = gb["V"], gb["Z"], gb["Mk"], PVR[g]
                S.op("dve", lambda e, pVR=pVR, Z=Z, V=V: e.tensor_tensor(flat(Z), pVR[:], flat(V), ALU.add), R=[pVR.b, V.b], W=[Z.b])
                S.op("act", lambda e, Z=Z, V=V: e.copy(V[:], Z[:]), R=[Z.b], W=[V.b])
            for g in grp:
                gb, h0 = GB[g], H0[g]
                V, Vl = gb["V"], gb["Mk"]
                pU = pb()
                pW = pb()
                for j in range(4):
                    S.op("pe", lambda e, j=j, pU=pU, V=V, h0=h0: e.matmul(pU[:, j * 128:(j + 1) * 128], V[:, j, :], vb[:, h0 + j, :], start=True, stop=True),
                         R=[V.b, vb.b], W=[pU.b])
                for j in range(4):
                    S.op("pe", lambda e, j=j, pW=pW, V=V, h0=h0: e.matmul(pW[:, j * 128:(j + 1) * 128], kbg[:, h0 + j, :], V[:, j, :], start=True, stop=True),
                         R=[V.b, kbg.b], W=[pW.b])
                S.op("act", lambda e, gb=gb, pU=pU: e.copy(flat(gb["u"]), pU[:]), R=[pU.b], W=[gb["u"].b])
                S.op("dve", lambda e, gb=gb, pW=pW: e.tensor_copy(flat(gb["wT"]), pW[:]), R=[pW.b], W=[gb["wT"].b])
            PWS = {}
            for g in grp:
                gb, h0 = GB[g], H0[g]
                pWS = pb()
                PWS[g] = pWS
                for j in range(4):
                    S.op("pe", lambda e, j=j, pWS=pWS, gb=gb, h0=h0: e.matmul(pWS[:, j * 128:(j + 1) * 128], gb["wT"][:, j, :], Sbf[:, h0 + j, :], start=True, stop=True),
                         R=[gb["wT"].b, Sbf.bs[g]], W=[pWS.b])
            for g in grp:
                gb, pWS = GB[g], PWS[g]
                S.op("dve", lambda e, gb=gb, pWS=pWS: e.tensor_tensor(flat(gb["vn"]), flat(gb["u"]), pWS[:], ALU.subtract),
                     R=[pWS.b, gb["u"].b], W=[gb["vn"].b])
            PO, PS = {}, {}
            for g in grp:
                gb, h0 = GB[g], H0[g]
                if full:
                    pO = pb()
                    PO[g] = pO
                    for j in range(4):
                        S.op("pe", lambda e, j=j, pO=pO, gb=gb, h0=h0: e.matmul(pO[:, j * 128:(j + 1) * 128], gb["qdT"][:, j, :], Sbf[:, h0 + j, :], start=True, stop=False),
                             R=[gb["qdT"].b, Sbf.bs[g]], W=[pO.b])
                        S.op("pe", lambda e, j=j, pO=pO, gb=gb: e.matmul(pO[:, j * 128:(j + 1) * 128], gb["aT"][:, j, :], gb["vn"][:, j, :], start=False, stop=True),
                             R=[gb["aT"].b, gb["vn"].b], W=[pO.b])
                pS = pb()
                PS[g] = pS
                for j in range(4):
                    S.op("pe", lambda e, j=j, pS=pS, gb=gb, h0=h0: e.matmul(pS[:, j * 128:(j + 1) * 128], kd[:, h0 + j, :], gb["vn"][:, j, :], start=True, stop=True),
                         R=[kd.b, gb["vn"].b], W=[pS.b])
            for g in grp:
                h0 = H0[g]
                S.op("pool", lambda e, h0=h0: e.tensor_tensor(S32[:, h0:h0 + 4, :], S32[:, h0:h0 + 4, :], col4(9, h0), ALU.mult),
                     R=[sm.bs[9], S32.bs[g]], W=[S32.bs[g]])
            for g in grp:
                h0, pS = H0[g], PS[g]
                S.op("dve", lambda e, h0=h0, pS=pS: e.tensor_tensor(S32[:, h0:h0 + 4, :], S32[:, h0:h0 + 4, :], pS[:].rearrange("p (j s) -> p j s", j=4), ALU.add),
                     R=[pS.b, S32.bs[g]], W=[S32.bs[g]])
            for g in grp:
                h0 = H0[g]
                S.op("act", lambda e, h0=h0: e.copy(Sbf[:, h0:h0 + 4, :], S32[:, h0:h0 + 4, :]), R=[S32.bs[g]], W=[Sbf.bs[g]])
            if full:
                for g in grp:
                    h0, pO = H0[g], PO[g]
                    oss, og, on, zs = B["oss"], B["og"][g % 2], B["on"], B["zs"]
                    for j in range(4):
                        S.op("act", lambda e, j=j, pO=pO, h0=h0: e.activation(B["junk"][:, 0:128], pO[:, j * 128:(j + 1) * 128], AF.Square,
                                                                             scale=float(128.0 ** -0.5), accum_out=oss[:, h0 + j:h0 + j + 1]),
                             R=[pO.b], W=[B["junk"].b, oss.b])
                    S.op("act", lambda e, h0=h0: e.activation(oss[:, 16 + h0:20 + h0], oss[:, h0:h0 + 4], AF.Ln, bias=EPS), R=[oss.b, cst.b], W=[oss.b])
                    S.op("act", lambda e, h0=h0: e.activation(oss[:, 16 + h0:20 + h0], oss[:, 16 + h0:20 + h0], AF.Exp, scale=-0.5), R=[oss.b], W=[oss.b])
                    S.op("dve", lambda e, pO=pO, og=og, h0=h0: e.tensor_tensor(og[:], pO[:].rearrange("p (j s) -> p j s", j=4),
                                                                               oss[:, 16 + h0:20 + h0].unsqueeze(2).to_broadcast([128, 4, 128]), ALU.mult),
                         R=[pO.b, oss.b], W=[og.b])
                    S.op("pool", lambda e, og=og, h0=h0: e.tensor_tensor(on[:, h0:h0 + 4, :], og[:], zs[:, h0 * 128:(h0 + 4) * 128].rearrange("p (h e) -> p h e", h=4), ALU.mult),
                         R=[og.b, zs.b], W=[on.b])
        if not full:
            return
        on, onT = B["on"], B["onT"]
        for half in range(2):
            p = pb()
            pv = p[:].bitcast(BF16)
            for j in range(8):
                S.op("pe", lambda e, j=j, pv=pv, half=half: e.transpose(pv[:, j * 128:(j + 1) * 128], on[:, half * 8 + j, :], identb[:]),
                     R=[on.b, identb.b], W=[p.b])
            if half == 0:
                S.op("act", lambda e, pv=pv, half=half: e.copy(onT[:, half * 8:(half + 1) * 8, :], pv.rearrange("p (k d) -> p k d", k=8)),
                     R=[p.b], W=[onT.b])
            else:
                S.op("dve", lambda e, pv=pv, half=half: e.tensor_copy(onT[:, half * 8:(half + 1) * 8, :], pv.rearrange("p (k d) -> p k d", k=8)),
                     R=[p.b], W=[onT.b])
        ft = t - nhist
        py = [pb(), pb()]
        HPB = WB // 256
        for hb in range(16 // HPB):
            wb = B["wblk"][hb % len(B["wblk"])]
            wv = wb[:, 0:HPB * 1024].rearrange("p (h n) -> p h n", h=HPB)
            S.dma("sp", wv, Wout_s[:, hb * HPB:(hb + 1) * HPB, :], R=[scrB[id(Wout_s)]], W=[wb.b])
            for hl in range(HPB):
                h_ = hb * HPB + hl
                for half in range(2):
                    S.op("pe", lambda e, hl=hl, h_=h_, half=half, wv=wv: e.matmul(
                        py[half][:], onT[:, h_, :], wv[:, hl, half * 512:(half + 1) * 512],
                        start=(h_ == 0), stop=(h_ == 15)), R=[wb.b, onT.b], W=[py[half].b])
        for half in range(2):
            S.op("dve", lambda e, half=half: e.tensor_tensor(
                hbuf[:, ft, half * 512:(half + 1) * 512], hbuf[:, ft, half * 512:(half + 1) * 512],
                py[half][:], ALU.add), R=[py[half].b, hbuf.bs[ft]], W=[hbuf.bs[ft]])

    def gdn_bufs(ph, N, full):
        B = {}
        B["WB"] = 256
        B["pref"] = {}
        B["xnT"] = [sb(ph, f"xnT{i}", [128, 8, N], BF16) for i in range(2)]
        B["qkvT"] = sb(ph, "qkvT", [128, 32, N], BF16, nslots=32)
        B["xs"] = [sb(ph, f"xs{i}", [128, D]) for i in range(2)] if not full else None
        B["xn"] = [sb(ph, f"xn{i}", [128, D], BF16) for i in range(2)]
        B["junk"] = sb(ph, "junk", [128, D], BF16)
        B["ms"] = sb(ph, "ms", [128, 4])
        B["wblk"] = [sb(ph, f"wblk{i}", [128, 8 * B["WB"]], BF16) for i in range(2 if full else 4)]
        B["u"] = [sb(ph, f"u{i}", [128, N + 3]) for i in range(2)]
        B["acc"] = [sb(ph, f"acc{i}", [128, N]) for i in range(2)]
        B["sq"] = [sb(ph, f"sq{i}", [128, N], BF16) for i in range(2)]
        B["sm"] = [sb(ph, f"sm{i}", [128, 16, (N // 128) * 16], F32, nslots=16) for i in range(2)]
        B["smb"] = [sb(ph, f"smb{i}", [128, 4, (N // 128) * 16], BF16, nslots=4) for i in range(2)]
        B["ktm"] = sb(ph, "ktm", [128, 8, 128], BF16)
        B["kbg"] = sb(ph, "kbg", [128, 16, 128], BF16)
        B["kd"] = sb(ph, "kd", [128, 16, 128], BF16)
        B["vb"] = sb(ph, "vb", [128, 16, 128], BF16)
        B["Asb"] = sb(ph, "Asb", [128, 8, 128], BF16)
        if full:
            B["KQsb"] = sb(ph, "KQsb", [128, 8, 128], BF16)
        G = []
        for g in range(2 if full else 4):
            gb = {}
            gb["dgh"] = sb(ph, f"dgh{g}", [128, 4, 128], BF16)
            gb["dgl"] = sb(ph, f"dgl{g}", [128, 4, 128], BF16)
            gb["Z"] = sb(ph, f"Z{g}", [128, 4, 128])
            gb["Mk"] = sb(ph, f"Mk{g}", [128, 4, 128], BF16)
            gb["Nk"] = sb(ph, f"Nk{g}", [128, 4, 128], BF16)
            gb["V"] = sb(ph, f"V{g}", [128, 4, 128], BF16)
            gb["Mh"] = sb(ph, f"Mh{g}", [128, 4, 128], BF16)
            gb["u"] = sb(ph, f"ug{g}", [128, 4, 128])
            gb["wT"] = sb(ph, f"wT{g}", [128, 4, 128], BF16)
            gb["vn"] = sb(ph, f"vn{g}", [128, 4, 128], BF16)
            if full:
                gb["ER"] = sb(ph, f"ER{g}", [128, 4, 128], BF16)
                gb["DT"] = sb(ph, f"DT{g}", [128, 4, 128])
                gb["aT"] = sb(ph, f"aT{g}", [128, 4, 128], BF16)
                gb["qdT"] = sb(ph, f"qdT{g}", [128, 4, 128], BF16)
            G.append(gb)
        B["G"] = G
        if full:
            B["zs"] = sb(ph, "zs", [128, 2048], BF16)
            B["og"] = [sb(ph, f"og{i}", [128, 4, 128]) for i in range(2)]
            B["oss"] = sb(ph, "oss", [128, 32])
            B["on"] = sb(ph, "on", [128, 16, 128], BF16)
            B["onT"] = sb(ph, "onT", [128, 16, 128], BF16)
        return B

    if nhist > 0:
        with contextlib.ExitStack() as ph:
            B = gdn_bufs(ph, 512, False)
            sts = [list(range(t, min(t + 4, nhist))) for t in range(0, nhist, 4)]
            gdn_st_norm(B, sts[0], False, None, 0)
            for k, tl in enumerate(sts):
                if k + 1 < len(sts):
                    gdn_st_norm(B, sts[k + 1], False, None, k + 1)
                gdn_st_rest(B, tl, False, None, k, k + 1 < len(sts))
                run_bg(8)
            S.barrier()
    run_bg(len(bgq))
    S.barrier()
    ph0.close()

    hbuf = sb(root, "h", [128, NFULL, D], F32, nslots=NFULL)

    with contextlib.ExitStack() as ph:
        B = gdn_bufs(ph, 256, True)
        sts = [[nhist + 2 * st, nhist + 2 * st + 1] for st in range(NFULL // 2)]
        gdn_st_norm(B, sts[0], True, hbuf, 0)
        for k, tl in enumerate(sts):
            if k + 1 < len(sts):
                gdn_st_norm(B, sts[k + 1], True, hbuf, k + 1)
            gdn_st_rest(B, tl, True, hbuf, k)
        S.barrier()

    def ffn_phase(l):
        with contextlib.ExitStack() as ph:
            xnT = sb(ph, "f_xnT", [128, 8, 512], BF16)
            xn = [sb(ph, f"f_xn{i}", [128, D], BF16) for i in range(2)]
            junk = sb(ph, "f_junk", [128, D], BF16)
            ms = sb(ph, "f_ms", [128, 4])
            wblk = [sb(ph, f"f_w{i}", [128, 4096], BF16) for i in range(3)]
            u = [sb(ph, f"f_u{i}", [128, 514]) for i in range(4)]
            acc = [sb(ph, f"f_acc{i}", [128, 512]) for i in range(4)]
            gs = [sb(ph, f"f_gs{i}", [128, 512]) for i in range(2)]
            act = sb(ph, "f_act", [128, 22, 512], BF16, nslots=22)
            fcarry = sb(ph, "f_carry", [128, 44, 2])
            cw = sb(ph, "f_cw", [128, 44, 3])
            cb = sb(ph, "f_cb", [128, 44])
            S.op("pool", lambda e: e.memset(fcarry[:], 0.0), W=[fcarry.b])
            S.dma("sp", cw[:], f_cw_d[l][:, :, :], W=[cw.b])
            S.dma("sp", cb[:], f_cb_d[l][:, :], W=[cb.b])
            sts = [list(range(s, min(s + 4, NFULL))) for s in range(0, NFULL, 4)]

            def ffn_supertile(tiles):
                NT = len(tiles)
                N = NT * 128
                for i, ft in enumerate(tiles):
                    rmsnorm_T((junk, ms, xn[i % 2]), hbuf[:, ft, :], [hbuf.bs[ft]], xnT, i * 128)
                for b in range(11):
                    wb = wblk[b % 3]
                    wv = wb[:].rearrange("p (c n) -> p c n", c=8)
                    S.dma("sp", wv, Wup_s[l][b], R=[scrB[id(Wup_s[l])]], W=[wb.b])
                    for jj in range(2):
                        res = []
                        for which in range(2):
                            f = which * 22 + b * 2 + jj
                            col = which * 256 + jj * 128
                            p = pb()
                            for c in range(8):
                                S.op("pe", lambda e, c=c, p=p, wv=wv, col=col: e.matmul(p[:, 0:N], wv[:, c, col:col + 128], xnT[:, c, 0:N],
                                                                                      start=(c == 0), stop=(c == 7)),
                                     R=[wb.b, xnT.b], W=[p.b])
                            k = (jj * 2 + which)
                            uu, aa = u[k], acc[k]
                            S.op("pool", lambda e, uu=uu, f=f: e.tensor_copy(uu[:, 0:2], fcarry[:, f, :]), R=[fcarry.b], W=[uu.b])
                            S.op("act", lambda e, uu=uu, p=p: e.copy(uu[:, 2:2 + N], p[:, 0:N]), R=[p.b], W=[uu.b])
                            S.op("pool", lambda e, uu=uu, f=f: e.tensor_copy(fcarry[:, f, :], uu[:, N:N + 2]), R=[uu.b], W=[fcarry.b])
                            S.op("dve", lambda e, uu=uu, aa=aa, f=f: e.tensor_scalar(aa[:, 0:N], uu[:, 2:2 + N], cw[:, f, 2:3], cb[:, f:f + 1], ALU.mult, ALU.add),
                                 R=[uu.b, cw.b, cb.b], W=[aa.b])
                            for j in (1, 0):
                                S.op("dve", lambda e, uu=uu, aa=aa, f=f, j=j: e.scalar_tensor_tensor(
                                    aa[:, 0:N], uu[:, j:j + N], cw[:, f, j:j + 1], aa[:, 0:N], ALU.mult, ALU.add),
                                    R=[uu.b, cw.b, aa.b], W=[aa.b])
                            res.append(aa)
                        g_ = gs[jj]
                        S.op("act", lambda e, g_=g_, a0=res[0]: e.activation(g_[:, 0:N], a0[:, 0:N], AF.Silu), R=[res[0].b], W=[g_.b])
                        S.op("pool", lambda e, g_=g_, a1=res[1], b=b, jj=jj: e.tensor_tensor(act[:, b * 2 + jj, 0:N], g_[:, 0:N], a1[:, 0:N], ALU.mult),
                             R=[g_.b, res[1].b], W=[act.bs[b * 2 + jj]])
                pys = [[pb(), pb()] for _ in range(NT)]
                for jb in range(6):
                    nj = 4 if jb < 5 else 2
                    wb = wblk[jb % 3]
                    wv = wb[:].rearrange("p (j n) -> p j n", j=4)
                    S.dma("sp", wv[:, 0:nj, :], Wdn_s[l][:, jb * 4:jb * 4 + nj, :], R=[scrB[id(Wdn_s[l])]], W=[wb.b])
                    for jl in range(nj):
                        j = jb * 4 + jl
                        for i in range(NT):
                            for half in range(2):
                                S.op("pe", lambda e, i=i, j=j, jl=jl, half=half, wv=wv: e.matmul(
                                    pys[i][half][:], act[:, j, i * 128:(i + 1) * 128], wv[:, jl, half * 512:(half + 1) * 512],
                                    start=(j == 0), stop=(j == 21)), R=[wb.b, act.bs[j]], W=[pys[i][half].b])
                for i, ft in enumerate(tiles):
                    for half in range(2):
                        S.op("dve", lambda e, i=i, ft=ft, half=half: e.scalar_tensor_tensor(
                            hbuf[:, ft, half * 512:(half + 1) * 512], pys[i][half][:], validt[:, ft:ft + 1],
                            hbuf[:, ft, half * 512:(half + 1) * 512], ALU.mult, ALU.add),
                            R=[pys[i][half].b, hbuf.bs[ft], validt.b], W=[hbuf.bs[ft]])

            for tiles in sts:
                ffn_supertile(tiles)
            S.barrier()

    if stage >= 2:
        ffn_phase(0)

    def attn_phase():
        with contextlib.ExitStack() as ph:
            NK = NFULL + 1
            xnT = sb(ph, "a_xnT", [128, 8, 512], BF16)
            xn = [sb(ph, f"a_xn{i}", [128, D], BF16) for i in range(2)]
            ms = sb(ph, "a_ms", [128, 4])
            wblk = [sb(ph, f"a_w{i}", [128, 8, 512], BF16) for i in range(3)]
            wk = [0]

            def wnext(src3, n):
                wb = wblk[wk[0] % 3]
                wk[0] += 1
                S.dma("sp", wb[:, :, 0:n], src3, R=[scrB[id(Wkv_s)], scrB[id(Wq_s)], scrB[id(Wo_s)]], W=[wb.b])
                return wb

            KT = sb(ph, "a_KT", [128, 4, NK * 128], BF16)
            Vt = sb(ph, "a_V", [128, NK, 256], BF16)
            QT = sb(ph, "a_QT", [128, 8, 512], BF16)
            BMf = sb(ph, "a_BMf", [128, 4, 256])
            BM = sb(ph, "a_BM", [128, 16, 256], BF16)
            am = sb(ph, "a_am", [128, 256])
            kbf = sb(ph, "a_kbf", [1, 512])
            kbb = sb(ph, "a_kbb", [1, NK * 128], BF16)
            sinkb = sb(ph, "a_sink", [128, 16])
            sc = [sb(ph, f"a_sc{i}", [128, 2, 256]) for i in range(3)]
            pr = [sb(ph, f"a_pr{i}", [128, 2, 256], BF16) for i in range(3)]
            pT = [sb(ph, f"a_pT{i}", [128, 4, 128], BF16) for i in range(3)]
            st_ = sb(ph, "a_st", [128, 6, 16], F32, nslots=8)
            obf = sb(ph, "a_obf", [128, D], BF16)
            junk = obf
            oT = sb(ph, "a_oT", [128, 8, 128], BF16)
            S.dma("sp", am[:], amask_d[:, :], W=[am.b])
            S.dma("sp", sinkb[:], sink_d[:, :], W=[sinkb.b])
            for q4 in range(4):
                S.dma("sp", BMf[:], band_d[:, q4 * 4:(q4 + 1) * 4, :], W=[BMf.b])
                S.op("dve", lambda e, q4=q4: e.tensor_tensor(BM[:, q4 * 4:(q4 + 1) * 4, :], BMf[:], am[:].unsqueeze(1).to_broadcast([128, 4, 256]), ALU.add),
                     R=[BMf.b, am.b], W=[BM.b])
            for k0_ in range(0, NK * 128, 512):
                kn = min(512, NK * 128 - k0_)
                S.dma("sp", kbf[:, 0:kn], kbias_d[:, k0_:k0_ + kn], W=[kbf.b])
                S.op("dve", lambda e, k0_=k0_, kn=kn: e.tensor_copy(kbb[:, k0_:k0_ + kn], kbf[:, 0:kn]), R=[kbf.b], W=[kbb.b])
            S.op("pool", lambda e: e.memset(KT[:, :, 0:128], 0.0), W=[KT.b])
            S.op("pool", lambda e: e.memset(Vt[:, 0, :], 0.0), W=[Vt.b])
            sts = [list(range(s, min(s + 4, NFULL))) for s in range(0, NFULL, 4)]

            def attn_supertile(tiles):
                NT = len(tiles)
                N = NT * 128
                for i, ft in enumerate(tiles):
                    rmsnorm_T((junk, ms, xn[i % 2]), hbuf[:, ft, :], [hbuf.bs[ft]], xnT, i * 128)
                kc0 = (tiles[0] + 1) * 128
                wkK = wnext(Wkv_s[0], 512)
                for j in range(4):
                    p = pb()
                    for c in range(8):
                        S.op("pe", lambda e, c=c, p=p, j=j, wkK=wkK: e.matmul(p[:, 0:N], wkK[:, c, j * 128:(j + 1) * 128], xnT[:, c, 0:N],
                                                                     start=(c == 0), stop=(c == 7)), R=[wkK.b, xnT.b], W=[p.b])
                    S.op("act", lambda e, p=p, j=j: e.copy(KT[:, j, kc0:kc0 + N], p[:, 0:N]), R=[p.b], W=[KT.b])
                wkV = wnext(Wkv_s[1, :, :, 0:256], 256)
                for i, ft in enumerate(tiles):
                    p = pb()
                    for c in range(8):
                        S.op("pe", lambda e, c=c, p=p, i=i, wkV=wkV: e.matmul(p[:, 0:256], xnT[:, c, i * 128:(i + 1) * 128], wkV[:, c, 0:256],
                                                                     start=(c == 0), stop=(c == 7)), R=[wkV.b, xnT.b], W=[p.b])
                    S.op("dve", lambda e, p=p, ft=ft: e.tensor_copy(Vt[:, ft + 1, :], p[:, 0:256]), R=[p.b], W=[Vt.b])
                for f in range(8):
                    if f % 4 == 0:
                        wqb = wnext(Wq_s[f // 4], 512)
                    p = pb()
                    for c in range(8):
                        S.op("pe", lambda e, c=c, p=p, f=f, wqb=wqb: e.matmul(p[:, 0:N], wqb[:, c, (f % 4) * 128:(f % 4 + 1) * 128], xnT[:, c, 0:N],
                                                                     start=(c == 0), stop=(c == 7)), R=[wqb.b, xnT.b], W=[p.b])
                    S.op("act", lambda e, p=p, f=f: e.activation(QT[:, f, 0:N], p[:, 0:N], AF.Copy, scale=0.125), R=[p.b], W=[QT.b])
                for i, ft in enumerate(tiles):
                    attn_tile(i, ft)

            def attn_tile(i, ft):
                if True:
                    k0 = ft * 128
                    pO = [PB[6], PB[7]]
                    reserved.update((6, 7))
                    PS, PTT = {}, {}

                    def stage_a(f):
                        kv = f // 2
                        ps = pb()
                        for hh in range(2):
                            lo = hh * 64
                            S.op("pe", lambda e, ps=ps, hh=hh, lo=lo, f=f, kv=kv: e.matmul(
                                ps[:, hh * 256:(hh + 1) * 256], QT[lo:lo + 64, f, i * 128:(i + 1) * 128], KT[lo:lo + 64, kv, k0:k0 + 256],
                                start=True, stop=False), R=[QT.b, KT.b], W=[ps.b])
                            S.op("pe", lambda e, ps=ps, hh=hh: e.matmul(
                                ps[:, hh * 256:(hh + 1) * 256], onesb[0:1, 0:128], kbb[0:1, k0:k0 + 256], start=False, stop=True),
                                R=[onesb.b, kbb.b], W=[ps.b])
                        s_, p_ = sc[f % 3], pr[f % 3]
                        sb_ = st_.bs[f]
                        S.op("dve", lambda e, ps=ps, s_=s_, f=f: e.tensor_tensor(s_[:], ps[:].rearrange("p (h k) -> p h k", h=2), BM[:, 2 * f:2 * f + 2, :], ALU.add),
                             R=[ps.b, BM.b], W=[s_.b])
                        S.op("dve", lambda e, s_=s_, f=f: e.tensor_reduce(st_[:, 0, 2 * f:2 * f + 2], s_[:], AX.X, ALU.max), R=[s_.b], W=[sb_])
                        S.op("dve", lambda e, f=f: e.tensor_tensor(st_[:, 0, 2 * f:2 * f + 2], st_[:, 0, 2 * f:2 * f + 2], sinkb[:, 2 * f:2 * f + 2], ALU.max),
                             R=[sb_, sinkb.b], W=[sb_])
                        S.op("dve", lambda e, f=f: e.tensor_scalar(st_[:, 1, 2 * f:2 * f + 2], st_[:, 0, 2 * f:2 * f + 2], -1.0, None, ALU.mult),
                             R=[sb_], W=[sb_])
                        for hh in range(2):
                            h_ = 2 * f + hh
                            S.op("act", lambda e, s_=s_, p_=p_, hh=hh, h_=h_: e.activation(p_[:, hh, :], s_[:, hh, :], AF.Exp, bias=st_[:, 1, h_:h_ + 1],
                                                                                           accum_out=st_[:, 2, h_:h_ + 1]),
                                 R=[s_.b, sb_], W=[p_.b, sb_])

                    def stage_b(f):
                        kv = f // 2
                        p_, t_ = pr[f % 3], pT[f % 3]
                        pt = pb()
                        ptv = pt[:].bitcast(BF16)
                        for hh in range(2):
                            for kb in range(2):
                                S.op("pe", lambda e, ptv=ptv, p_=p_, hh=hh, kb=kb: e.transpose(
                                    ptv[:, (hh * 2 + kb) * 128:(hh * 2 + kb + 1) * 128], p_[:, hh, kb * 128:(kb + 1) * 128], identb[:]),
                                    R=[p_.b, identb.b], W=[pt.b])
                        if f % 2 == 0:
                            S.op("act", lambda e, ptv=ptv, t_=t_: e.copy(t_[:].rearrange("p a b -> p (a b)"), ptv[:, 0:512]), R=[pt.b], W=[t_.b])
                        else:
                            S.op("dve", lambda e, ptv=ptv, t_=t_: e.tensor_copy(t_[:].rearrange("p a b -> p (a b)"), ptv[:, 0:512]), R=[pt.b], W=[t_.b])
                        for hh in range(2):
                            h_ = 2 * f + hh
                            for kb in range(2):
                                S.op("pe", lambda e, t_=t_, hh=hh, kb=kb, h_=h_, kv=kv: e.matmul(
                                    pO[h_ // 8][:, (h_ % 8) * 64:(h_ % 8 + 1) * 64], t_[:, hh * 2 + kb, :], Vt[:, ft + kb, kv * 64:(kv + 1) * 64],
                                    start=(kb == 0), stop=(kb == 1)), R=[t_.b, Vt.b], W=[pO[h_ // 8].b])

                    for f in range(9):
                        if f < 8:
                            stage_a(f)
                        if f >= 1:
                            stage_b(f - 1)
                    reserved.clear()
                    S.op("dve", lambda e: e.tensor_tensor(st_[:, 3, :], sinkb[:], st_[:, 1, :], ALU.add), R=[st_.b, sinkb.b] + st_.bs, W=[st_.b])
                    S.op("act", lambda e: e.activation(st_[:, 3, :], st_[:, 3, :], AF.Exp), R=[st_.b], W=[st_.b])
                    S.op("dve", lambda e: e.tensor_tensor(st_[:, 3, :], st_[:, 3, :], st_[:, 2, :], ALU.add), R=[st_.b], W=[st_.b])
                    S.op("dve", lambda e: e.reciprocal(st_[:, 4, :], st_[:, 3, :]), R=[st_.b], W=[st_.b])
                    for half in range(2):
                        S.op("dve", lambda e, half=half: e.tensor_tensor(
                            obf[:, half * 512:(half + 1) * 512].rearrange("p (h d) -> p h d", h=8), pO[half][:].rearrange("p (h d) -> p h d", h=8),
                            st_[:, 4, half * 8:(half + 1) * 8].unsqueeze(2).to_broadcast([128, 8, 64]), ALU.mult),
                            R=[pO[half].b, st_.b], W=[obf.b])
                    p = pb()
                    pv = p[:].bitcast(BF16)
                    for c in range(8):
                        S.op("pe", lambda e, c=c, pv=pv, p=p: e.transpose(pv[:, c * 128:(c + 1) * 128], obf[:, c * 128:(c + 1) * 128], identb[:]),
                             R=[obf.b, identb.b], W=[p.b])
                    S.op("act", lambda e, pv=pv: e.copy(oT[:].rearrange("p c t -> p (c t)"), pv), R=[p.b], W=[oT.b])
                    py = [pb(), pb()]
                    for half in range(2):
                        wob = wnext(Wo_s[half], 512)
                        for c in range(8):
                            S.op("pe", lambda e, c=c, half=half, wob=wob: e.matmul(py[half][:], oT[:, c, :], wob[:, c, :],
                                                                           start=(c == 0), stop=(c == 7)), R=[oT.b, wob.b], W=[py[half].b])
                        S.op("dve", lambda e, half=half, ft=ft: e.scalar_tensor_tensor(
                            hbuf[:, ft, half * 512:(half + 1) * 512], py[half][:], validt[:, ft:ft + 1],
                            hbuf[:, ft, half * 512:(half + 1) * 512], ALU.mult, ALU.add),
                            R=[py[half].b, hbuf.bs[ft], validt.b], W=[hbuf.bs[ft]])

            for tiles in sts:
                attn_supertile(tiles)
            S.barrier()

    if stage >= 3:
        attn_phase()
    if stage >= 4:
        ffn_phase(1)

    toks = []
    with contextlib.ExitStack() as ph:
        fnw = sb(ph, "fnw", [128, D])
        junk = sb(ph, "o_junk", [128, D], BF16)
        ms = sb(ph, "o_ms", [128, 4])
        ob = [sb(ph, f"o_b{i}", [128, D]) for i in range(2)]
        S.dma("sp", fnw[:], fin_w_d[:, :], W=[fnw.b])
        for t in range(NOWN):
            ft = t + NHALO
            o_ = ob[t % 2]
            if stage >= 5:
                S.op("act", lambda e, ft=ft: e.activation(junk[:], hbuf[:, ft, :], AF.Square, scale=1.0 / 32.0, accum_out=ms[:, 0:1]),
                     R=[hbuf.bs[ft]], W=[junk.b, ms.b])
                S.op("act", lambda e: e.activation(ms[:, 1:2], ms[:, 0:1], AF.Ln, bias=EPS), R=[ms.b, cst.b], W=[ms.b])
                S.op("act", lambda e: e.activation(ms[:, 2:3], ms[:, 1:2], AF.Exp, scale=-0.5), R=[ms.b], W=[ms.b])
                S.op("dve", lambda e, ft=ft, o_=o_: e.scalar_tensor_tensor(o_[:], hbuf[:, ft, :], ms[:, 2:3], fnw[:], ALU.mult, ALU.mult),
                     R=[hbuf.bs[ft], ms.b, fnw.b], W=[o_.b])
            else:
                S.op("dve", lambda e, ft=ft, o_=o_: e.tensor_copy(o_[:], hbuf[:, ft, :]), R=[hbuf.bs[ft]], W=[o_.b])
            toks.append(S.dma("sp", out_d[t * 128:(t + 1) * 128, :], o_[:], R=[o_.b]))
        S.finish(toks)
    root.close()
    return nc


def _t5_bucket(dist):
    n = np.maximum(dist, 0)
    nf = np.maximum(n, 1).astype(np.float32)
    large = 16 + (np.log(nf / 16) / np.log(128 / 16) * 16).astype(np.int32)
    large = np.minimum(large, 31)
    return np.where(n < 16, n, large)


def make_inputs(inp, nhist, cores):
    f32 = np.float32
    A = lambda v: np.ascontiguousarray(np.asarray(v), dtype=f32)
    x = A(inp["x"])

    def pc(v):
        return np.ascontiguousarray(A(v).reshape(8, 128).T)

    common = {
        "a_w_in": A(inp["a_w_in"][0]),
        "a_w_out": A(inp["a_w_out"][0]),
        "ffn_w_up0": A(inp["ffn_w_up"][0]), "ffn_w_up1": A(inp["ffn_w_up"][1]),
        "ffn_w_down0": A(inp["ffn_w_down"][0]), "ffn_w_down1": A(inp["ffn_w_down"][1]),
        "b_w_q": A(inp["b_w_q"][0]), "b_w_o": A(inp["b_w_o"][0]),
        "a_norm_wT": pc(inp["a_norm_w"][0]),
        "ffn_norm_wT0": pc(inp["ffn_norm_w"][0]), "ffn_norm_wT1": pc(inp["ffn_norm_w"][1]),
        "kv_norm_wT": pc(inp["kv_norm_w"]), "b_norm_wT": pc(inp["b_norm_w"][0]),
        "out_norm_wT": A(inp["a_out_norm_w"][0]).reshape(128, 1),
        "final_norm_wb": np.ascontiguousarray(np.broadcast_to(A(inp["final_norm_w"])[None, :], (128, D))),
        "a_conv_wT": np.ascontiguousarray(A(inp["a_conv_w"][0]).T.reshape(32, 128, 4).transpose(1, 0, 2)),
        "a_log_b": np.ascontiguousarray(np.broadcast_to(A(inp["a_a_log"][0])[None, :], (128, 16))),
        "dt_bias_b": np.ascontiguousarray(np.broadcast_to(A(inp["a_dt_bias"][0])[None, :], (128, 16))),
        "sinks_b": np.ascontiguousarray(np.broadcast_to(A(inp["b_sinks"][0])[None, :], (128, 16))),
    }
    for l in range(2):
        common[f"ffn_conv_wT{l}"] = np.ascontiguousarray(A(inp["ffn_conv_w"][l]).T.reshape(44, 128, 3).transpose(1, 0, 2))
        common[f"ffn_conv_bT{l}"] = np.ascontiguousarray(A(inp["ffn_conv_b"][l]).reshape(44, 128).T)
    wkv = A(inp["w_kv"])
    cols = []
    for j in range(4):
        cols += [wkv[:, j * 64:(j + 1) * 64], wkv[:, j * 64:(j + 1) * 64]]
    cols.append(wkv[:, 256:512])
    common["w_kv_dup"] = np.ascontiguousarray(np.concatenate(cols, axis=1))
    qi = np.arange(128)[:, None]
    ki = np.arange(256)[None, :]
    dist = qi + 128 - ki
    bucket = _t5_bucket(dist)
    tab = A(inp["rel_bias_table"])
    common["biasband"] = np.ascontiguousarray(tab[bucket].transpose(0, 2, 1))
    inwin = (dist >= 0) & (dist < 128)
    common["attnmask"] = np.where(inwin, 0.0, NEG).astype(f32)
    common["ident"] = np.eye(128, dtype=f32)
    p_ = np.arange(128)[:, None]
    j_ = np.arange(128)[None, :]
    common["triu"] = (p_ <= j_).astype(f32)
    common["maskL"] = np.where(p_ > j_, 0.0, NEG).astype(f32)
    common["maskU"] = np.where(j_ >= p_, 0.0, NEG).astype(f32)
    NT_ALL = nhist + NFULL
    maps = []
    for c in cores:
        b, j = c // 4, c % 4
        end = 2048 * (j + 1)
        start = end - NT_ALL * 128
        xe = np.zeros((NT_ALL * 128, D), f32)
        s0 = max(start, 0)
        xe[s0 - start:] = x[b, s0:end]
        pos_full = np.arange(end - NFULL * 128, end)
        valid = (pos_full >= 0).astype(f32).reshape(NFULL, 128).T
        kb = np.concatenate([np.full(128, NEG, f32), np.where(pos_full >= 0, 0.0, NEG).astype(f32)])[None, :]
        m = dict(common)
        m["x_ext"] = xe
        m["valid"] = np.ascontiguousarray(valid)
        m["kbias"] = np.ascontiguousarray(kb)
        maps.append(m)
    return maps


_NHIST = 46


def kernel(**inputs):
    nc = build_program(_NHIST)
    maps = make_inputs(inputs, _NHIST, list(range(8)))
    res = run_bass_kernel_spmd(nc, maps, core_ids=list(range(8)))
    out = np.empty((2, 8192, D), np.float32)
    for c in range(8):
        b, j = c // 4, c % 4
        out[b, 2048 * j:2048 * (j + 1)] = res.results[c]["out"]
    return out
```

```python
import contextlib
import numpy as np
import concourse.bass as bass
import concourse.mybir as mybir
from concourse.bass_utils import run_bass_kernel_spmd

F32 = mybir.dt.float32
BF16 = mybir.dt.bfloat16
AF = mybir.ActivationFunctionType
ALU = mybir.AluOpType
AX = mybir.AxisListType

D = 1024
NFULL = 18
NHALO = 2
NOWN = 16
NEG = -1.0e30
EPOCH = 30000


class Buf:
    __slots__ = ("name", "w", "r", "excl")

    def __init__(self, name="", excl=False):
        self.name = name
        self.w = None
        self.r = []
        self.excl = excl


class Sched:
    ENGS = ("pe", "act", "dve", "pool", "sp")

    def __init__(self, nc, n_dma_sems=10):
        self.nc = nc
        self.prog = {e: [] for e in self.ENGS}
        self.cnt = {e: 0 for e in self.ENGS}
        self.sems = {}
        self.seen = {e: {} for e in self.ENGS}
        self.dma_sems = {}
        self.n_dma_sems = n_dma_sems
        self.dma_rr = {e: 0 for e in self.ENGS}
        self._semctx = []
        self.last_tok = {}

    def _new_sem(self, name):
        ctx = self.nc.semaphore(name)
        h = ctx.__enter__()
        self._semctx.append(ctx)
        return h

    def _eng_sem(self, eng, idx):
        key = (eng, idx // EPOCH)
        if key not in self.sems:
            self.sems[key] = self._new_sem(f"s_{eng}_{idx // EPOCH}")
        return self.sems[key], (idx % EPOCH) + 1

    def _wait(self, eng, tok):
        teng, sem, val = tok
        if teng == eng and eng == "pe":
            return
        seen = self.seen[eng]
        if seen.get(sem.name, 0) >= val:
            return
        seen[sem.name] = val
        self.prog[eng].append(lambda e, sem=sem, val=val: e.wait_ge(sem, val))

    def _deps(self, eng, reads, writes):
        for b in reads:
            if b.w is not None:
                self._wait(eng, b.w)
            if b.excl:
                for t in b.r:
                    if t[0] != eng:
                        self._wait(eng, t)
        for b in writes:
            if b.w is not None and b.w[0] != eng:
                self._wait(eng, b.w)
            for t in b.r:
                if t[0] != eng:
                    self._wait(eng, t)

    def _commit(self, tok, reads, writes):
        for b in reads:
            b.r.append(tok)
        for b in writes:
            b.w = tok
            b.r = []

    def op(self, eng, fn, R=(), W=()):
        self._deps(eng, R, W)
        idx = self.cnt[eng]
        self.cnt[eng] += 1
        sem, val = self._eng_sem(eng, idx)
        self.prog[eng].append(lambda e, fn=fn, sem=sem: fn(e).then_inc(sem, 1))
        tok = (eng, sem, val)
        self.last_tok[eng] = tok
        self._commit(tok, R, W)
        return tok

    def dma(self, eng, out, in_, R=(), W=()):
        k = self.dma_rr[eng]
        self.dma_rr[eng] = (k + 1) % self.n_dma_sems
        key = (eng, k)
        if key not in self.dma_sems:
            self.dma_sems[key] = [self._new_sem(f"d_{eng}_{k}"), 0]
        ent = self.dma_sems[key]
        sem, tot = ent
        if tot > 0:
            self._wait(eng, ("dma", sem, tot))
        self._deps(eng, R, W)
        ent[1] = tot + 16
        self.prog[eng].append(
            lambda e, out=out, in_=in_, sem=sem: e.dma_start(out=out, in_=in_).then_inc(sem, 16))
        tok = ("dma", sem, tot + 16)
        self._commit(tok, R, W)
        return tok

    def barrier(self):
        toks = list(self.last_tok.values())
        for (eng, k), (sem, tot) in self.dma_sems.items():
            if tot > 0:
                toks.append(("dma", sem, tot))
        for e in self.ENGS:
            for t in toks:
                if t[0] != e or e != "pe":
                    self._wait(e, t)

    def finish(self, final_toks):
        for t in final_toks:
            self._wait("sp", t)
        nc = self.nc
        with nc.Block() as block:
            @block.tensor
            def _(e):
                for f in self.prog["pe"]:
                    f(e)

            @block.scalar
            def _(e):
                for f in self.prog["act"]:
                    f(e)

            @block.vector
            def _(e):
                for f in self.prog["dve"]:
                    f(e)

            @block.gpsimd
            def _(e):
                for f in self.prog["pool"]:
                    f(e)

            @block.sync
            def _(e):
                for f in self.prog["sp"]:
                    f(e)
        for ctx in reversed(self._semctx):
            ctx.__exit__(None, None, None)


class T:
    def __init__(self, t, name, nslots=0):
        self.t = t
        self.b = Buf(name)
        self.bs = [Buf(f"{name}{i}") for i in range(nslots)]

    def __getitem__(self, k):
        return self.t[k]


def build_program(nhist, stage=99, cut=99):
    nc = bass.Bass("TRN2", target_bir_lowering=False)
    S = Sched(nc)
    NT_ALL = nhist + NFULL

    def din(name, shape, dt=F32):
        return nc.dram_tensor(name, list(shape), dt, kind="ExternalInput").ap()

    x_ext = din("x_ext", [NT_ALL * 128, D])
    valid_d = din("valid", [128, NFULL])
    kbias_d = din("kbias", [1, (NFULL + 1) * 128])
    w_in_d = din("a_w_in", [D, 6176])
    w_out_d = din("a_w_out", [2048, D])
    w_up_d = [din(f"ffn_w_up{l}", [D, 5632]) for l in range(2)]
    w_dn_d = [din(f"ffn_w_down{l}", [2816, D]) for l in range(2)]
    w_kv_d = din("w_kv_dup", [D, 768])
    w_q_d = din("b_w_q", [D, D])
    w_o_d = din("b_w_o", [D, D])
    a_nw_d = din("a_norm_wT", [128, 8])
    f_nw_d = [din(f"ffn_norm_wT{l}", [128, 8]) for l in range(2)]
    kv_nw_d = din("kv_norm_wT", [128, 8])
    b_nw_d = din("b_norm_wT", [128, 8])
    on_w_d = din("out_norm_wT", [128, 1])
    fin_w_d = din("final_norm_wb", [128, D])
    a_cw_d = din("a_conv_wT", [128, 32, 4])
    f_cw_d = [din(f"ffn_conv_wT{l}", [128, 44, 3]) for l in range(2)]
    f_cb_d = [din(f"ffn_conv_bT{l}", [128, 44]) for l in range(2)]
    alog_d = din("a_log_b", [128, 16])
    dtb_d = din("dt_bias_b", [128, 16])
    sink_d = din("sinks_b", [128, 16])
    band_d = din("biasband", [128, 16, 256])
    amask_d = din("attnmask", [128, 256])
    ident_d = din("ident", [128, 128])
    triu_d = din("triu", [128, 128])
    maskL_d = din("maskL", [128, 128])
    maskU_d = din("maskU", [128, 128])
    out_d = nc.dram_tensor("out", [NOWN * 128, D], F32, kind="ExternalOutput").ap()

    def dscr(name, shape):
        return nc.dram_tensor(name, list(shape), BF16).ap()

    Win_s = dscr("Win_b", [25, 128, 8, 256])
    Wout_s = dscr("Wout_s", [128, 16, D])
    Wup_s = [dscr(f"Wup_b{l}", [11, 128, 8, 512]) for l in range(2)]
    Wdn_s = [dscr(f"Wdn_s{l}", [128, 22, D]) for l in range(2)]
    Wkv_s = dscr("Wkv_b", [2, 128, 8, 512])
    Wq_s = dscr("Wq_b", [2, 128, 8, 512])
    Wo_s = dscr("Wo_b", [2, 128, 8, 512])
    scrB = {id(a): Buf("scr") for a in [Win_s, Wout_s, Wkv_s, Wq_s, Wo_s] + Wup_s + Wdn_s}

    uid = [0]

    def sb(stk, name, shape, dt=F32, nslots=0):
        uid[0] += 1
        t = stk.enter_context(nc.sbuf_tensor(f"{name}_{uid[0]}", list(shape), dt))
        return T(t, name, nslots)

    root = contextlib.ExitStack()
    PB = [T(root.enter_context(nc.psum_tensor(f"pb{i}", [128, 512], F32)), f"pb{i}") for i in range(8)]
    for p_ in PB:
        p_.b.excl = True
    pbi = [0]

    reserved = set()

    def pb():
        while True:
            k = pbi[0] % 8
            pbi[0] += 1
            if k not in reserved:
                return PB[k]

    identf = sb(root, "identf", [128, 128])
    identb = sb(root, "identb", [128, 128], BF16)
    onesb = sb(root, "onesb", [128, 128], BF16)
    triub = sb(root, "triub", [128, 128], BF16)
    maskLt = sb(root, "maskLt", [128, 128])
    maskUt = sb(root, "maskUt", [128, 128])
    cst = sb(root, "cst", [128, 8])
    validt = sb(root, "validt", [128, NFULL])
    tmpc = sb(root, "tmpc", [128, 128])

    S.dma("sp", identf[:], ident_d[:, :], W=[identf.b])
    S.op("dve", lambda e: e.tensor_copy(identb[:], identf[:]), R=[identf.b], W=[identb.b])
    S.op("pool", lambda e: e.memset(onesb[:], 1.0), W=[onesb.b])
    S.dma("sp", tmpc[:], triu_d[:, :], W=[tmpc.b])
    S.op("dve", lambda e: e.tensor_copy(triub[:], tmpc[:]), R=[tmpc.b], W=[triub.b])
    S.dma("sp", maskLt[:], maskL_d[:, :], W=[maskLt.b])
    S.dma("sp", maskUt[:], maskU_d[:, :], W=[maskUt.b])
    S.op("pool", lambda e: e.memset(cst[:, 0:1], 1e-6), W=[cst.b])
    S.op("pool", lambda e: e.memset(cst[:, 1:2], 1.0), W=[cst.b])
    S.op("pool", lambda e: e.memset(cst[:, 2:3], float(np.log(128.0 ** -0.5))), W=[cst.b])
    S.op("pool", lambda e: e.memset(cst[:, 3:4], 0.0), W=[cst.b])
    S.dma("sp", validt[:], valid_d[:, :], W=[validt.b])
    EPS = cst[:, 0:1]
    ONE = cst[:, 1:2]

    gdn = root
    S32 = sb(gdn, "S32", [128, 16, 128], F32, nslots=4)
    Sbf = sb(gdn, "Sbf", [128, 16, 128], BF16, nslots=4)
    carry = sb(gdn, "carry", [128, 32, 3])
    convw = sb(gdn, "convw", [128, 32, 4])
    wba = sb(gdn, "wba", [128, 8, 32], BF16)
    negA = sb(gdn, "negA", [128, 16])
    dtb = sb(gdn, "dtb", [128, 16])
    S.op("pool", lambda e: e.memset(S32[:], 0.0), W=[S32.b] + S32.bs)
    S.op("pool", lambda e: e.memset(Sbf[:], 0.0), W=[Sbf.b] + Sbf.bs)
    S.op("pool", lambda e: e.memset(carry[:], 0.0), W=[carry.b])
    S.dma("sp", convw[:], a_cw_d[:, :, :], W=[convw.b])
    S.dma("sp", negA[:], alog_d[:, :], W=[negA.b])
    S.dma("sp", dtb[:], dtb_d[:, :], W=[dtb.b])
    S.op("act", lambda e: e.activation(negA[:], negA[:], AF.Exp), R=[negA.b], W=[negA.b])
    S.op("dve", lambda e: e.tensor_scalar(negA[:], negA[:], -1.0, None, ALU.mult), R=[negA.b], W=[negA.b])

    ph0 = contextlib.ExitStack()
    bgq = []
    if True:
        ph = ph0
        stg = [sb(ph, f"stg{i}", [128, 2048]) for i in range(2)]
        cvt = [sb(ph, f"cvt{i}", [128, 2048], BF16) for i in range(2)]
        nws = sb(ph, "nws", [128, 8 * 5 + 1])
        nw_ap = {}
        for i, (nm, d_) in enumerate([("a", a_nw_d), ("f0", f_nw_d[0]), ("f1", f_nw_d[1]), ("kv", kv_nw_d), ("b", b_nw_d)]):
            S.dma("sp", nws[:, i * 8:(i + 1) * 8], d_[:, :], W=[nws.b])
            nw_ap[nm] = (i * 8)
        S.dma("sp", nws[:, 40:41], on_w_d[:, :], W=[nws.b])
        blk = [0]

        def convert(src3, dst3, C, N, scale=None, dstfn=None, ng=512, defer=False):
            ng = min(N, ng)
            cg = max(1, min(C, 2048 // ng))

            def emit(c0, cc, n0, nn):
                i = blk[0] % 2
                blk[0] += 1
                st, cv = stg[i], cvt[i]
                sv = st[:, 0:cc * nn].rearrange("p (c n) -> p c n", c=cc)
                cvv = cv[:, 0:cc * nn].rearrange("p (c n) -> p c n", c=cc)
                S.dma("sp", sv, src3[:, c0:c0 + cc, n0:n0 + nn], W=[st.b])
                eng = "dve" if (blk[0] % 2 == 0) else "pool"
                if scale is None:
                    S.op(eng, lambda e: e.tensor_copy(cvv, sv), R=[st.b], W=[cv.b])
                elif scale[0] == "pc":
                    g = nws[:, scale[1] + c0:scale[1] + c0 + cc].unsqueeze(2).to_broadcast([128, cc, nn])
                    S.op(eng, lambda e: e.tensor_tensor(cvv, sv, g, ALU.mult), R=[st.b, nws.b], W=[cv.b])
                else:
                    g = nws[:, scale[1]:scale[1] + 1].unsqueeze(2).to_broadcast([128, cc, nn])
                    S.op(eng, lambda e: e.tensor_tensor(cvv, sv, g, ALU.mult), R=[st.b, nws.b], W=[cv.b])
                dap = dst3[:, c0:c0 + cc, n0:n0 + nn] if dstfn is None else dstfn(c0, cc, n0, nn)
                S.dma("pool", dap, cvv, R=[cv.b], W=[scrB[id(dst3)]])

            for c0 in range(0, C, cg):
                cc = min(cg, C - c0)
                for n0 in range(0, N, ng):
                    nn = min(ng, N - n0)
                    if defer:
                        bgq.append(lambda c0=c0, cc=cc, n0=n0, nn=nn: emit(c0, cc, n0, nn))
                    else:
                        emit(c0, cc, n0, nn)

        def pcn(ap):
            return ap.rearrange("(c p) n -> p c n", p=128)

        convert(pcn(w_in_d), Win_s, 8, 6176, ("pc", nw_ap["a"]), ng=256,
                dstfn=lambda c0, cc, n0, nn: Win_s[n0 // 256, :, c0:c0 + cc, 0:nn])
        convert(pcn(w_out_d), Wout_s, 16, D, ("p", 40))
        if stage >= 2:
            for l in range(2):
                convert(pcn(w_up_d[l]), Wup_s[l], 8, 5632, ("pc", nw_ap[f"f{l}"]), ng=256,
                        dstfn=lambda c0, cc, n0, nn, l=l: (Wup_s[l][n0 // 256, :, c0:c0 + cc, 0:256] if n0 < 2816
                                                           else Wup_s[l][(n0 - 2816) // 256, :, c0:c0 + cc, 256:512]), defer=True)
                convert(pcn(w_dn_d[l]), Wdn_s[l], 22, D, None, defer=True)
        if stage >= 3:
            convert(pcn(w_kv_d), Wkv_s, 8, 768, ("pc", nw_ap["kv"]), ng=256,
                    dstfn=lambda c0, cc, n0, nn: (Wkv_s[0, :, c0:c0 + cc, n0:n0 + nn] if n0 < 512 else Wkv_s[1, :, c0:c0 + cc, 0:nn]), defer=True)
            convert(pcn(w_q_d), Wq_s, 8, D, ("pc", nw_ap["b"]), ng=512,
                    dstfn=lambda c0, cc, n0, nn: Wq_s[n0 // 512, :, c0:c0 + cc, 0:nn], defer=True)
            convert(pcn(w_o_d), Wo_s, 8, D, None, ng=512,
                    dstfn=lambda c0, cc, n0, nn: Wo_s[n0 // 512, :, c0:c0 + cc, 0:nn], defer=True)
        S.dma("sp", wba[:], Win_s[24, :, :, 0:32], R=[scrB[id(Win_s)]], W=[wba.b])

    def run_bg(n):
        for _ in range(min(n, len(bgq))):
            bgq.pop(0)()


    def rmsnorm_T(ph_bufs, src_ap, src_bufs, xnT, col0):
        junk, ms, xn = ph_bufs
        S.op("act", lambda e: e.activation(junk[:], src_ap, AF.Square, scale=1.0 / 32.0, accum_out=ms[:, 0:1]),
             R=src_bufs, W=[junk.b, ms.b])
        S.op("act", lambda e: e.activation(ms[:, 1:2], ms[:, 0:1], AF.Ln, bias=EPS), R=[ms.b, cst.b], W=[ms.b])
        S.op("act", lambda e: e.activation(ms[:, 2:3], ms[:, 1:2], AF.Exp, scale=-0.5), R=[ms.b], W=[ms.b])
        S.op("dve", lambda e: e.tensor_scalar(xn[:], src_ap, ms[:, 2:3], None, ALU.mult), R=src_bufs + [ms.b], W=[xn.b])
        p = pb()
        pv = p[:].bitcast(BF16)
        for c in range(8):
            S.op("pe", lambda e, c=c: e.transpose(pv[:, c * 128:(c + 1) * 128], xn[:, c * 128:(c + 1) * 128], identb[:]),
                 R=[xn.b, identb.b], W=[p.b])
        S.op("act", lambda e: e.copy(xnT[:, :, col0:col0 + 128], pv.rearrange("p (c t) -> p c t", c=8)),
             R=[p.b], W=[xnT.b])

    def wload(B, k, src3, nparts):
        wb = B["wblk"][k % len(B["wblk"])]
        n = src3.shape[2]
        wv = wb[:, 0:nparts * n].rearrange("p (c n) -> p c n", c=nparts)
        return wb, wv

    def gdn_st_norm(B, tiles, full, hbuf, sp):
        xnT = B["xnT"][sp % 2]
        for i, t in enumerate(tiles):
            if full:
                ft = t - nhist
                src, sbufs = hbuf[:, ft, :], [hbuf.bs[ft]]
            else:
                xs = B["xs"][t % 2]
                src, sbufs = xs[:], [xs.b]
            S.dma("sp", src, x_ext[t * 128:(t + 1) * 128, :], W=sbufs)
            rmsnorm_T((B["junk"], B["ms"], B["xn"][i % 2]), src, sbufs, xnT, i * 128)

    def gdn_st_rest(B, tiles, full, hbuf, sp, has_next=False):
        NT = len(tiles)
        N = NT * 128
        WB = B["WB"]
        CPB = WB // 128
        xnT, qkvT = B["xnT"][sp % 2], B["qkvT"]
        gdn_st_small(B, xnT, NT, sp)
        f0 = 0 if full else 8
        wb = None
        for f in range(f0, 32):
            if f % CPB == 0 or wb is None:
                blk = f // CPB
                if blk in B["pref"]:
                    wb, wv = B["pref"].pop(blk)
                else:
                    wb, wv = wload(B, blk, Win_s[blk], 8)
                    S.dma("sp", wv, Win_s[blk], R=[scrB[id(Win_s)]], W=[wb.b])
            p = pb()
            for c in range(8):
                S.op("pe", lambda e, c=c, wv=wv, f=f, p=p: e.matmul(p[:, 0:N], wv[:, c, (f % CPB) * 128:(f % CPB + 1) * 128],
                                                                    xnT[:, c, 0:N], start=(c == 0), stop=(c == 7)),
                     R=[wb.b, xnT.b], W=[p.b])
            u = B["u"][f % 2]
            acc = B["acc"][f % 2]
            S.op("pool", lambda e, u=u, f=f: e.tensor_copy(u[:, 0:3], carry[:, f, :]), R=[carry.b], W=[u.b])
            S.op("act", lambda e, u=u, p=p: e.copy(u[:, 3:3 + N], p[:, 0:N]), R=[p.b], W=[u.b])
            S.op("pool", lambda e, u=u, f=f: e.tensor_copy(carry[:, f, :], u[:, N:N + 3]), R=[u.b], W=[carry.b])
            S.op("dve", lambda e, u=u, acc=acc, f=f: e.tensor_scalar(acc[:, 0:N], u[:, 3:3 + N], convw[:, f, 3:4], None, ALU.mult),
                 R=[u.b, convw.b], W=[acc.b])
            for j in (2, 1, 0):
                S.op("dve", lambda e, u=u, acc=acc, f=f, j=j: e.scalar_tensor_tensor(
                    acc[:, 0:N], u[:, j:j + N], convw[:, f, j:j + 1], acc[:, 0:N], ALU.mult, ALU.add),
                    R=[u.b, convw.b, acc.b], W=[acc.b])
            S.op("act", lambda e, acc=acc, f=f: e.activation(qkvT[:, f, 0:N], acc[:, 0:N], AF.Silu),
                 R=[acc.b], W=[qkvT.bs[f]])
        for f in range(f0, 16):
            sq = B["sq"][f % 2]
            S.op("pool", lambda e, sq=sq, f=f: e.tensor_tensor(sq[:, 0:N], qkvT[:, f, 0:N], qkvT[:, f, 0:N], ALU.mult),
                 R=[qkvT.bs[f]], W=[sq.b])
            p = pb()
            S.op("pe", lambda e, p=p, sq=sq: e.matmul(p[:, 0:N], onesb[:], sq[:, 0:N], start=True, stop=True),
                 R=[onesb.b, sq.b], W=[p.b])
            rn = B["acc"][f % 2]
            S.op("act", lambda e, p=p, rn=rn: e.activation(rn[:, 0:N], p[:, 0:N], AF.Ln, bias=EPS), R=[p.b, cst.b], W=[rn.b])
            bias = cst[:, 2:3] if f < 8 else cst[:, 3:4]
            S.op("act", lambda e, rn=rn, bias=bias: e.activation(rn[:, 0:N], rn[:, 0:N], AF.Exp, scale=-0.5, bias=bias),
                 R=[rn.b, cst.b], W=[rn.b])
            S.op("dve", lambda e, rn=rn, f=f: e.tensor_tensor(qkvT[:, f, 0:N], qkvT[:, f, 0:N], rn[:, 0:N], ALU.mult),
                 R=[rn.b, qkvT.bs[f]], W=[qkvT.bs[f]])
        if cut < 2:
            return
        if has_next and not full:
            for blk in range(f0 // CPB, f0 // CPB + len(B["wblk"])):
                wb, wv = wload(B, blk, Win_s[blk], 8)
                S.dma("sp", wv, Win_s[blk], R=[scrB[id(Win_s)]], W=[wb.b])
                B["pref"][blk] = (wb, wv)
        for i, t in enumerate(tiles):
            gdn_tile(B, xnT, i, t, full, hbuf, sp)

    def gdn_st_small(B, xnT, NT, sp):
        sm = B["sm"][sp % 2]
        smb = B["smb"][sp % 2]
        W_ = NT * 16
        SM = lambda k: sm[:, k, 0:W_].rearrange("p (i h) -> p i h", h=16)
        SB = lambda k: smb[:, k, 0:W_].rearrange("p (i h) -> p i h", h=16)
        r = lambda k: sm.bs[k]
        rb = lambda k: smb.bs[k]
        bc = lambda t_: t_[:].unsqueeze(1).to_broadcast([128, NT, 16])
        pba = pb()
        for i in range(NT):
            for c in range(8):
                S.op("pe", lambda e, c=c, i=i: e.matmul(pba[:, i * 32:(i + 1) * 32], xnT[:, c, i * 128:(i + 1) * 128], wba[:, c, :],
                                                        start=(c == 0), stop=(c == 7)), R=[xnT.b, wba.b], W=[pba.b])
        pbv = pba[:, 0:NT * 32].rearrange("p (i c) -> p i c", c=32)
        S.op("dve", lambda e: e.tensor_tensor(SM(0), pbv[:, :, 16:32], bc(dtb), ALU.add), R=[pba.b, dtb.b], W=[r(0)])
        S.op("act", lambda e: e.activation(SM(4), pbv[:, :, 0:16], AF.Exp, scale=-1.0), R=[pba.b], W=[r(4)])
        S.op("act", lambda e: e.activation(SM(1), SM(0), AF.Abs), R=[r(0)], W=[r(1)])
        S.op("act", lambda e: e.activation(SM(1), SM(1), AF.Exp, scale=-1.0), R=[r(1)], W=[r(1)])
        S.op("act", lambda e: e.activation(SM(1), SM(1), AF.Ln, bias=ONE), R=[r(1), cst.b], W=[r(1)])
        S.op("dve", lambda e: e.tensor_scalar(SM(4), SM(4), 1.0, None, ALU.add), R=[r(4)], W=[r(4)])
        S.op("dve", lambda e: e.reciprocal(SM(5), SM(4)), R=[r(4)], W=[r(5)])
        S.op("dve", lambda e: e.scalar_tensor_tensor(SM(2), SM(0), 0.0, SM(1), ALU.max, ALU.add), R=[r(0), r(1)], W=[r(2)])
        S.op("dve", lambda e: e.tensor_tensor(SM(3), SM(2), bc(negA), ALU.mult), R=[r(2), negA.b], W=[r(3)])
        S.op("dve", lambda e: e.tensor_copy(SB(0), SM(3)), R=[r(3)], W=[rb(0)])
        S.op("dve", lambda e: e.tensor_copy(SM(6), SB(0)), R=[rb(0)], W=[r(6)])
        S.op("dve", lambda e: e.tensor_tensor(SB(1), SM(3), SM(6), ALU.subtract), R=[r(3), r(6)], W=[rb(1)])
        pc = pb()
        for i in range(NT):
            for k in range(2):
                S.op("pe", lambda e, i=i, k=k: e.matmul(pc[:, i * 32:i * 32 + 16], triub[:], smb[:, k, i * 16:(i + 1) * 16], start=(k == 0), stop=(k == 1)),
                     R=[triub.b, rb(k)], W=[pc.b])
            for k in range(2):
                S.op("pe", lambda e, i=i, k=k: e.matmul(pc[:, i * 32 + 16:i * 32 + 32], onesb[:], smb[:, k, i * 16:(i + 1) * 16], start=(k == 0), stop=(k == 1)),
                     R=[onesb.b, rb(k)], W=[pc.b])
        pcv = pc[:, 0:NT * 32].rearrange("p (i c) -> p i c", c=32)
        S.op("dve", lambda e: e.tensor_copy(SM(7), pcv[:, :, 0:16]), R=[pc.b], W=[r(7)])
        S.op("dve", lambda e: e.tensor_copy(SM(8), pcv[:, :, 16:32]), R=[pc.b], W=[r(8)])
        S.op("act", lambda e: e.activation(SM(9), SM(8), AF.Exp), R=[r(8)], W=[r(9)])
        S.op("dve", lambda e: e.tensor_tensor(SM(10), SM(8), SM(7), ALU.subtract), R=[r(7), r(8)], W=[r(10)])
        S.op("act", lambda e: e.activation(SM(15), SM(7), AF.Exp), R=[r(7)], W=[r(15)])
        S.op("act", lambda e: e.activation(SM(10), SM(10), AF.Exp), R=[r(10)], W=[r(10)])
        S.op("dve", lambda e: e.tensor_scalar(SM(12), SM(7), -1.0, None, ALU.mult), R=[r(7)], W=[r(12)])
        S.op("dve", lambda e: e.tensor_copy(SB(2), SM(12)), R=[r(12)], W=[rb(2)])
        S.op("dve", lambda e: e.tensor_copy(SM(13), SB(2)), R=[rb(2)], W=[r(13)])
        S.op("dve", lambda e: e.tensor_tensor(SM(14), SM(12), SM(13), ALU.subtract), R=[r(12), r(13)], W=[r(14)])
        S.op("dve", lambda e: e.tensor_tensor(SM(11), SM(5), SM(15), ALU.mult), R=[r(5), r(15)], W=[r(11)])

    def gdn_tile(B, xnT, i, t, full, hbuf, sp):
        qkvT = B["qkvT"]
        WB = B["WB"]
        c0 = i * 128
        sm = B["sm"][sp % 2]
        o16 = i * 16
        SM = lambda k: sm[:, k, o16:o16 + 16]
        def emit_z(blks):
            zs = B["zs"]
            for blk in blks:
                wb, wv = wload(B, blk, Win_s[16 + blk], 8)
                S.dma("sp", wv, Win_s[16 + blk], R=[scrB[id(Win_s)]], W=[wb.b])
                p = pb()
                for c in range(8):
                    S.op("pe", lambda e, c=c, wv=wv, p=p: e.matmul(p[:, 0:WB], xnT[:, c, c0:c0 + 128], wv[:, c, :],
                                                                   start=(c == 0), stop=(c == 7)),
                         R=[wb.b, xnT.b], W=[p.b])
                S.op("act", lambda e, p=p, blk=blk: e.activation(zs[:, blk * WB:(blk + 1) * WB], p[:, 0:WB], AF.Silu),
                     R=[p.b], W=[zs.bs[blk // 4]])
        if full:
            emit_z(range(0, 4))
        ktm, kbg, kd, vb = B["ktm"], B["kbg"], B["kd"], B["vb"]
        p = pb()
        pv = p[:].bitcast(BF16)
        for kh in range(8):
            S.op("pe", lambda e, kh=kh, pv=pv: e.transpose(pv[:, kh * 128:(kh + 1) * 128], qkvT[:, 8 + kh, c0:c0 + 128], identb[:]),
                 R=[qkvT.bs[8 + kh], identb.b], W=[p.b])
        S.op("act", lambda e, pv=pv: e.copy(ktm[:], pv.rearrange("p (k d) -> p k d", k=8)), R=[p.b], W=[ktm.b])
        if cut < 3.2:
            return
        k4 = ktm[:].unsqueeze(2).to_broadcast([128, 8, 2, 128])
        S.op("pool", lambda e: e.tensor_tensor(kbg[:].rearrange("p (k r) d -> p k r d", r=2), k4,
                                               SM(11).rearrange("p (k r) -> p k r", r=2).unsqueeze(3).to_broadcast([128, 8, 2, 128]),
                                               ALU.mult), R=[ktm.b, sm.bs[11]], W=[kbg.b])
        S.op("pool", lambda e: e.tensor_tensor(kd[:].rearrange("p (k r) d -> p k r d", r=2), k4,
                                               SM(10).rearrange("p (k r) -> p k r", r=2).unsqueeze(3).to_broadcast([128, 8, 2, 128]),
                                               ALU.mult), R=[ktm.b, sm.bs[10]], W=[kd.b])
        if cut < 3.4:
            return
        for half in range(2):
            p = pb()
            pv = p[:].bitcast(BF16)
            for j in range(8):
                h_ = half * 8 + j
                S.op("pe", lambda e, j=j, h_=h_, pv=pv: e.transpose(pv[:, j * 128:(j + 1) * 128], qkvT[:, 16 + h_, c0:c0 + 128], identb[:]),
                     R=[qkvT.bs[16 + h_], identb.b], W=[p.b])
            S.op("dve", lambda e, half=half, pv=pv: e.tensor_tensor(
                vb[:, half * 8:(half + 1) * 8, :], pv.rearrange("p (k d) -> p k d", k=8),
                sm[:, 5, o16 + half * 8:o16 + (half + 1) * 8].unsqueeze(2).to_broadcast([128, 8, 128]), ALU.mult),
                R=[p.b, sm.bs[5]], W=[vb.b])
        if cut < 3.6:
            return
        Asb = B["Asb"]
        pA = [pb(), pb()]
        for kh in range(8):
            S.op("pe", lambda e, kh=kh: e.matmul(pA[kh // 4][:, (kh % 4) * 128:(kh % 4 + 1) * 128], qkvT[:, 8 + kh, c0:c0 + 128],
                                                 qkvT[:, 8 + kh, c0:c0 + 128], start=True, stop=True),
                 R=[qkvT.bs[8 + kh]], W=[pA[kh // 4].b])
        for j in range(2):
            S.op("act", lambda e, j=j: e.copy(Asb[:, j * 4:(j + 1) * 4, :], pA[j][:].rearrange("p (k s) -> p k s", k=4)),
                 R=[pA[j].b], W=[Asb.b])
        if cut < 3.8:
            return
        if full:
            KQsb = B["KQsb"]
            pK = [pb(), pb()]
            for kh in range(8):
                S.op("pe", lambda e, kh=kh: e.matmul(pK[kh // 4][:, (kh % 4) * 128:(kh % 4 + 1) * 128], qkvT[:, 8 + kh, c0:c0 + 128],
                                                     qkvT[:, kh, c0:c0 + 128], start=True, stop=True),
                     R=[qkvT.bs[8 + kh], qkvT.bs[kh]], W=[pK[kh // 4].b])
            for j in range(2):
                S.op("dve", lambda e, j=j: e.tensor_copy(KQsb[:, j * 4:(j + 1) * 4, :], pK[j][:].rearrange("p (k s) -> p k s", k=4)),
                     R=[pK[j].b], W=[KQsb.b])
        if cut < 4:
            return
        G = B["G"]
        b4 = lambda ap2: ap2.unsqueeze(1).to_broadcast([128, 4, 128])
        col4 = lambda k, h0: sm[:, k, o16 + h0:o16 + h0 + 4].unsqueeze(2).to_broadcast([128, 4, 128])
        flat = lambda t_: t_[:].rearrange("p j s -> p (j s)")
        kr = lambda ap3: ap3.rearrange("p (k r) s -> p k r s", r=2)
        NG = len(G)
        for gp in range(4 // NG):
            grp = list(range(NG * gp, NG * gp + NG))
            GB = {g: G[g % NG] for g in grp}
            H0 = {g: g * 4 for g in grp}
            for g in grp:
                gb, h0 = GB[g], H0[g]
                S.op("dve", lambda e, gb=gb, h0=h0: e.tensor_tensor(gb["dgh"][:], b4(identf[:]), col4(13, h0), ALU.mult),
                     R=[identf.b, sm.bs[13]], W=[gb["dgh"].b])
                S.op("pool", lambda e, gb=gb, h0=h0: e.tensor_tensor(gb["dgl"][:], b4(identf[:]), col4(14, h0), ALU.mult),
                     R=[identf.b, sm.bs[14]], W=[gb["dgl"].b])
            PR = {}
            for g in grp:
                gb = GB[g]
                pR = pb()
                PR[g] = pR
                S.op("pe", lambda e, pR=pR, gb=gb: e.matmul(pR[:], onesb[:], flat(gb["dgh"]), start=True, stop=False),
                     R=[onesb.b, gb["dgh"].b], W=[pR.b])
                S.op("pe", lambda e, pR=pR, gb=gb: e.matmul(pR[:], onesb[:], flat(gb["dgl"]), start=False, stop=True),
                     R=[onesb.b, gb["dgl"].b], W=[pR.b])
            for g in grp:
                gb, h0, pR = GB[g], H0[g], PR[g]
                S.op("dve", lambda e, pR=pR, gb=gb, h0=h0: e.tensor_tensor(gb["Z"][:], pR[:].rearrange("p (j s) -> p j s", j=4), col4(7, h0), ALU.add),
                     R=[pR.b, sm.bs[7]], W=[gb["Z"].b])
            if full:
                for g in grp:
                    gb, pR = GB[g], PR[g]
                    S.op("act", lambda e, pR=pR, gb=gb: e.activation(flat(gb["ER"]), pR[:], AF.Exp, scale=-1.0), R=[pR.b], W=[gb["ER"].b])
                for g in grp:
                    gb = GB[g]
                    S.op("dve", lambda e, gb=gb: e.scalar_tensor_tensor(gb["DT"][:], gb["Z"][:], -1.0, b4(maskUt[:]), ALU.mult, ALU.add),
                         R=[gb["Z"].b, maskUt.b], W=[gb["DT"].b])
            for g in grp:
                gb = GB[g]
                S.op("pool", lambda e, gb=gb: e.tensor_tensor(gb["Z"][:], gb["Z"][:], b4(maskLt[:]), ALU.add), R=[gb["Z"].b, maskLt.b], W=[gb["Z"].b])
            if full:
                for g in grp:
                    gb = GB[g]
                    S.op("act", lambda e, gb=gb: e.activation(gb["DT"][:], gb["DT"][:], AF.Exp), R=[gb["DT"].b], W=[gb["DT"].b])
            for g in grp:
                gb = GB[g]
                S.op("act", lambda e, gb=gb: e.activation(gb["Z"][:], gb["Z"][:], AF.Exp), R=[gb["Z"].b], W=[gb["Z"].b])
            for g in grp:
                gb, h0 = GB[g], H0[g]
                S.op("pool", lambda e, gb=gb, h0=h0: e.tensor_tensor(gb["Z"][:], gb["Z"][:], col4(5, h0), ALU.mult), R=[gb["Z"].b, sm.bs[5]], W=[gb["Z"].b])
            if full:
                for g in grp:
                    gb, kh0 = GB[g], H0[g] // 2
                    S.op("pool", lambda e, gb=gb, kh0=kh0: e.tensor_tensor(
                        kr(gb["aT"][:]), kr(gb["DT"][:]), KQsb[:, kh0:kh0 + 2, :].unsqueeze(2).to_broadcast([128, 2, 2, 128]), ALU.mult),
                        R=[gb["DT"].b, KQsb.b], W=[gb["aT"].b])
                    S.op("pool", lambda e, gb=gb, kh0=kh0: e.tensor_tensor(
                        kr(gb["qdT"][:]), kr(gb["ER"][:]), qkvT[:, kh0:kh0 + 2, c0:c0 + 128].unsqueeze(2).to_broadcast([128, 2, 2, 128]), ALU.mult),
                        R=[gb["ER"].b, qkvT.bs[kh0], qkvT.bs[kh0 + 1]], W=[gb["qdT"].b])
            for g in grp:
                gb, kh0 = GB[g], H0[g] // 2
                S.op("dve", lambda e, gb=gb, kh0=kh0: e.tensor_tensor(
                    kr(gb["Mk"][:]), kr(gb["Z"][:]), Asb[:, kh0:kh0 + 2, :].unsqueeze(2).to_broadcast([128, 2, 2, 128]), ALU.mult),
                    R=[gb["Z"].b, Asb.b], W=[gb["Mk"].b])
            for g in grp:
                gb = GB[g]
                S.op("pool", lambda e, gb=gb: e.tensor_copy(gb["Mh"][:], gb["Mk"][:]), R=[gb["Mk"].b], W=[gb["Mh"].b])
            PT = {}
            for g in grp:
                gb = GB[g]
                p = pb()
                PT[g] = p
                pv = p[:].bitcast(BF16)
                for j in range(4):
                    S.op("pe", lambda e, j=j, pv=pv, gb=gb: e.transpose(pv[:, j * 128:(j + 1) * 128], gb["Mk"][:, j, :], identb[:]),
                         R=[gb["Mk"].b, identb.b], W=[p.b])
            for g in grp:
                gb, p = GB[g], PT[g]
                pv = p[:].bitcast(BF16)
                S.op("act", lambda e, gb=gb, pv=pv: e.copy(flat(gb["Nk"]), pv[:, 0:512]), R=[p.b], W=[gb["Nk"].b])
                S.op("dve", lambda e, gb=gb, pv=pv: e.scalar_tensor_tensor(
                    gb["V"][:], pv[:, 0:512].rearrange("p (j s) -> p j s", j=4), -1.0, b4(identb[:]), ALU.mult, ALU.add),
                    R=[p.b, identb.b], W=[gb["V"].b])
            for r in range(1, 7):
                for g in grp:
                    gb = GB[g]
                    Mk, Nk, V = gb["Mk"], gb["Nk"], gb["V"]
                    pM = pN = pV = None
                    if r <= 5:
                        pM = pb()
                        for j in range(4):
                            S.op("pe", lambda e, j=j, pM=pM, Mk=Mk, Nk=Nk: e.matmul(
                                pM[:, j * 128:(j + 1) * 128], Nk[:, j, :], Mk[:, j, :], start=True, stop=True),
                                R=[Nk.b, Mk.b], W=[pM.b])
                    if r <= 5:
                        pN = pb()
                        for j in range(4):
                            S.op("pe", lambda e, j=j, pN=pN, Mk=Mk, Nk=Nk: e.matmul(
                                pN[:, j * 128:(j + 1) * 128], Mk[:, j, :], Nk[:, j, :], start=True, stop=True),
                                R=[Nk.b, Mk.b], W=[pN.b])
                    if r >= 2:
                        pV = pb()
                        for j in range(4):
                            S.op("pe", lambda e, j=j, pV=pV, Mk=Mk, V=V: e.matmul(
                                pV[:, j * 128:(j + 1) * 128], Mk[:, j, :], V[:, j, :], start=True, stop=True),
                                R=[Mk.b, V.b], W=[pV.b])
                        S.op("dve", lambda e, pV=pV, V=V: e.tensor_tensor(flat(V), pV[:], flat(V), ALU.add),
                             R=[pV.b, V.b], W=[V.b])
                    if pM is not None:
                        S.op("act", lambda e, pM=pM, Mk=Mk: e.copy(flat(Mk), pM[:]), R=[pM.b], W=[Mk.b])
                    if pN is not None:
                        S.op("act", lambda e, pN=pN, Nk=Nk: e.copy(flat(Nk), pN[:]), R=[pN.b], W=[Nk.b])
            PNV, PT0 = {}, {}
            for g in grp:
                gb = GB[g]
                V, Mh = gb["V"], gb["Mh"]
                pNV = pb()
                PNV[g] = pNV
                for j in range(4):
                    S.op("pe", lambda e, j=j, pNV=pNV, Mh=Mh, V=V: e.matmul(pNV[:, j * 128:(j + 1) * 128], Mh[:, j, :], V[:, j, :], start=True, stop=True),
                         R=[Mh.b, V.b], W=[pNV.b])
                pT0 = pb()
                PT0[g] = pT0
                pT0v = pT0[:].bitcast(BF16)
                for j in range(4):
                    S.op("pe", lambda e, j=j, pT0v=pT0v, V=V: e.transpose(pT0v[:, j * 128:(j + 1) * 128], V[:, j, :], identb[:]),
                         R=[V.b, identb.b], W=[pT0.b])
            for g in grp:
                gb = GB[g]
                V, Z, Rv, T0 = gb["V"], gb["Z"], gb["dgh"], gb["Nk"]
                pNV, pT0 = PNV[g], PT0[g]
                pT0v = pT0[:].bitcast(BF16)
                S.op("dve", lambda e, pNV=pNV, Z=Z, V=V: e.scalar_tensor_tensor(flat(Z), pNV[:], -1.0, flat(V), ALU.mult, ALU.subtract),
                     R=[pNV.b, V.b], W=[Z.b])
                S.op("pool", lambda e, Z=Z, Rv=Rv: e.tensor_tensor(Rv[:], Z[:], b4(identf[:]), ALU.add), R=[Z.b, identf.b], W=[Rv.b])
                S.op("act", lambda e, pT0v=pT0v, T0=T0: e.copy(flat(T0), pT0v[:, 0:512]), R=[pT0.b], W=[T0.b])
            PVR = {}
            for g in grp:
                gb = GB[g]
                Rv, T0 = gb["dgh"], gb["Nk"]
                pVR = pb()
                PVR[g] = pVR
                for j in range(4):
                    S.op("pe", lambda e, j=j, pVR=pVR, T0=T0, Rv=Rv: e.matmul(pVR[:, j * 128:(j + 1) * 128], T0[:, j, :], Rv[:, j, :], start=True, stop=True),
                         R=[T0.b, Rv.b], W=[pVR.b])
            for g in grp:
                gb = GB[g]
                V, Z, Vl, pVR = gb["V"], gb["Z"], gb["Mk"], PVR[g]
                S.op("dve", lambda e, pVR=pVR, Z=Z, V=V: e.tensor_tensor(flat(Z), pVR[:], flat(V), ALU.add), R=[pVR.b, V.b], W=[Z.b])
                S.op("act", lambda e, Z=Z, V=V: e.copy(V[:], Z[:]), R=[Z.b], W=[V.b])
            for g in grp:
                gb, h0 = GB[g], H0[g]
                V, Vl = gb["V"], gb["Mk"]
                pU = pb()
                pW = pb()
                for j in range(4):
                    S.op("pe", lambda e, j=j, pU=pU, V=V, h0=h0: e.matmul(pU[:, j * 128:(j + 1) * 128], V[:, j, :], vb[:, h0 + j, :], start=True, stop=True),
                         R=[V.b, vb.b], W=[pU.b])
                for j in range(4):
                    S.op("pe", lambda e, j=j, pW=pW, V=V, h0=h0: e.matmul(pW[:, j * 128:(j + 1) * 128], kbg[:, h0 + j, :], V[:, j, :], start=True, stop=True),
                         R=[V.b, kbg.b], W=[pW.b])
                S.op("act", lambda e, gb=gb, pU=pU: e.copy(flat(gb["u"]), pU[:]), R=[pU.b], W=[gb["u"].b])
                S.op("dve", lambda e, gb=gb, pW=pW: e.tensor_copy(flat(gb["wT"]), pW[:]), R=[pW.b], W=[gb["wT"].b])
            PWS = {}
            for g in grp:
                gb, h0 = GB[g], H0[g]
                pWS = pb()
                PWS[g] = pWS
                for j in range(4):
                    S.op("pe", lambda e, j=j, pWS=pWS, gb=gb, h0=h0: e.matmul(pWS[:, j * 128:(j + 1) * 128], gb["wT"][:, j, :], Sbf[:, h0 + j, :], start=True, stop=True),
                         R=[gb["wT"].b, Sbf.bs[g]], W=[pWS.b])
            for g in grp:
                gb, pWS = GB[g], PWS[g]
                S.op("dve", lambda e, gb=gb, pWS=pWS: e.tensor_tensor(flat(gb["vn"]), flat(gb["u"]), pWS[:], ALU.subtract),
                     R=[pWS.b, gb["u"].b], W=[gb["vn"].b])
            PO, PS = {}, {}
            for g in grp:
                gb, h0 = GB[g], H0[g]
                if full:
                    pO = pb()
                    PO[g] = pO
                    for j in range(4):
                        S.op("pe", lambda e, j=j, pO=pO, gb=gb, h0=h0: e.matmul(pO[:, j * 128:(j + 1) * 128], gb["qdT"][:, j, :], Sbf[:, h0 + j, :], start=True, stop=False),
                             R=[gb["qdT"].b, Sbf.bs[g]], W=[pO.b])
                        S.op("pe", lambda e, j=j, pO=pO, gb=gb: e.matmul(pO[:, j * 128:(j + 1) * 128], gb["aT"][:, j, :], gb["vn"][:, j, :], start=False, stop=True),
                             R=[gb["aT"].b, gb["vn"].b], W=[pO.b])
                pS = pb()
                PS[g] = pS
                for j in range(4):
                    S.op("pe", lambda e, j=j, pS=pS, gb=gb, h0=h0: e.matmul(pS[:, j * 128:(j + 1) * 128], kd[:, h0 + j, :], gb["vn"][:, j, :], start=True, stop=True),
                         R=[kd.b, gb["vn"].b], W=[pS.b])
            for g in grp:
                h0 = H0[g]
                S.op("pool", lambda e, h0=h0: e.tensor_tensor(S32[:, h0:h0 + 4, :], S32[:, h0:h0 + 4, :], col4(9, h0), ALU.mult),
                     R=[sm.bs[9], S32.bs[g]], W=[S32.bs[g]])
            for g in grp:
                h0, pS = H0[g], PS[g]
                S.op("dve", lambda e, h0=h0, pS=pS: e.tensor_tensor(S32[:, h0:h0 + 4, :], S32[:, h0:h0 + 4, :], pS[:].rearrange("p (j s) -> p j s", j=4), ALU.add),
                     R=[pS.b, S32.bs[g]], W=[S32.bs[g]])
            for g in grp:
                h0 = H0[g]
                S.op("act", lambda e, h0=h0: e.copy(Sbf[:, h0:h0 + 4, :], S32[:, h0:h0 + 4, :]), R=[S32.bs[g]], W=[Sbf.bs[g]])
            if full:
                for g in grp:
                    h0, pO = H0[g], PO[g]
                    oss, og, on, zs = B["oss"], B["og"][g % 2], B["on"], B["zs"]
                    for j in range(4):
                        S.op("act", lambda e, j=j, pO=pO, h0=h0: e.activation(B["junk"][:, 0:128], pO[:, j * 128:(j + 1) * 128], AF.Square,
                                                                             scale=float(128.0 ** -0.5), accum_out=oss[:, h0 + j:h0 + j + 1]),
                             R=[pO.b], W=[B["junk"].b, oss.b])
                    S.op("act", lambda e, h0=h0: e.activation(oss[:, 16 + h0:20 + h0], oss[:, h0:h0 + 4], AF.Ln, bias=EPS), R=[oss.b, cst.b], W=[oss.b])
                    S.op("act", lambda e, h0=h0: e.activation(oss[:, 16 + h0:20 + h0], oss[:, 16 + h0:20 + h0], AF.Exp, scale=-0.5), R=[oss.b], W=[oss.b])
                    S.op("dve", lambda e, pO=pO, og=og, h0=h0: e.tensor_tensor(og[:], pO[:].rearrange("p (j s) -> p j s", j=4),
                                                                               oss[:, 16 + h0:20 + h0].unsqueeze(2).to_broadcast([128, 4, 128]), ALU.mult),
                         R=[pO.b, oss.b], W=[og.b])
                    S.op("pool", lambda e, og=og, h0=h0: e.tensor_tensor(on[:, h0:h0 + 4, :], og[:], zs[:, h0 * 128:(h0 + 4) * 128].rearrange("p (h e) -> p h e", h=4), ALU.mult),
                         R=[og.b, zs.bs[h0 // 8]], W=[on.b])
            if full and gp == 0:
                emit_z(range(4, 8))
        if not full:
            return
        on, onT = B["on"], B["onT"]
        for half in range(2):
            p = pb()
            pv = p[:].bitcast(BF16)
            for j in range(8):
                S.op("pe", lambda e, j=j, pv=pv, half=half: e.transpose(pv[:, j * 128:(j + 1) * 128], on[:, half * 8 + j, :], identb[:]),
                     R=[on.b, identb.b], W=[p.b])
            if half == 0:
                S.op("act", lambda e, pv=pv, half=half: e.copy(onT[:, half * 8:(half + 1) * 8, :], pv.rearrange("p (k d) -> p k d", k=8)),
                     R=[p.b], W=[onT.b])
            else:
                S.op("dve", lambda e, pv=pv, half=half: e.tensor_copy(onT[:, half * 8:(half + 1) * 8, :], pv.rearrange("p (k d) -> p k d", k=8)),
                     R=[p.b], W=[onT.b])
        ft = t - nhist
        py = [pb(), pb()]
        HPB = WB // 256
        for hb in range(16 // HPB):
            wb = B["wblk"][hb % len(B["wblk"])]
            wv = wb[:, 0:HPB * 1024].rearrange("p (h n) -> p h n", h=HPB)
            S.dma("sp", wv, Wout_s[:, hb * HPB:(hb + 1) * HPB, :], R=[scrB[id(Wout_s)]], W=[wb.b])
            for hl in range(HPB):
                h_ = hb * HPB + hl
                for half in range(2):
                    S.op("pe", lambda e, hl=hl, h_=h_, half=half, wv=wv: e.matmul(
                        py[half][:], onT[:, h_, :], wv[:, hl, half * 512:(half + 1) * 512],
                        start=(h_ == 0), stop=(h_ == 15)), R=[wb.b, onT.b], W=[py[half].b])
        for half in range(2):
            S.op("dve", lambda e, half=half: e.tensor_tensor(
                hbuf[:, ft, half * 512:(half + 1) * 512], hbuf[:, ft, half * 512:(half + 1) * 512],
                py[half][:], ALU.add), R=[py[half].b, hbuf.bs[ft]], W=[hbuf.bs[ft]])

    def gdn_bufs(ph, N, full):
        B = {}
        B["WB"] = 256
        B["pref"] = {}
        B["xnT"] = [sb(ph, f"xnT{i}", [128, 8, N], BF16) for i in range(2)]
        B["qkvT"] = sb(ph, "qkvT", [128, 32, N], BF16, nslots=32)
        B["xs"] = [sb(ph, f"xs{i}", [128, D]) for i in range(2)] if not full else None
        B["xn"] = [sb(ph, f"xn{i}", [128, D], BF16) for i in range(2)]
        B["junk"] = sb(ph, "junk", [128, D], BF16)
        B["ms"] = sb(ph, "ms", [128, 4])
        B["wblk"] = [sb(ph, f"wblk{i}", [128, 8 * B["WB"]], BF16) for i in range(2 if full else 4)]
        B["u"] = [sb(ph, f"u{i}", [128, N + 3]) for i in range(2)]
        B["acc"] = [sb(ph, f"acc{i}", [128, N]) for i in range(2)]
        B["sq"] = [sb(ph, f"sq{i}", [128, N], BF16) for i in range(2)]
        B["sm"] = [sb(ph, f"sm{i}", [128, 16, (N // 128) * 16], F32, nslots=16) for i in range(2)]
        B["smb"] = [sb(ph, f"smb{i}", [128, 4, (N // 128) * 16], BF16, nslots=4) for i in range(2)]
        B["ktm"] = sb(ph, "ktm", [128, 8, 128], BF16)
        B["kbg"] = sb(ph, "kbg", [128, 16, 128], BF16)
        B["kd"] = sb(ph, "kd", [128, 16, 128], BF16)
        B["vb"] = sb(ph, "vb", [128, 16, 128], BF16)
        B["Asb"] = sb(ph, "Asb", [128, 8, 128], BF16)
        if full:
            B["KQsb"] = sb(ph, "KQsb", [128, 8, 128], BF16)
        G = []
        for g in range(2 if full else 4):
            gb = {}
            gb["dgh"] = sb(ph, f"dgh{g}", [128, 4, 128], BF16)
            gb["dgl"] = sb(ph, f"dgl{g}", [128, 4, 128], BF16)
            gb["Z"] = sb(ph, f"Z{g}", [128, 4, 128])
            gb["Mk"] = sb(ph, f"Mk{g}", [128, 4, 128], BF16)
            gb["Nk"] = sb(ph, f"Nk{g}", [128, 4, 128], BF16)
            gb["V"] = sb(ph, f"V{g}", [128, 4, 128], BF16)
            gb["Mh"] = sb(ph, f"Mh{g}", [128, 4, 128], BF16)
            gb["u"] = sb(ph, f"ug{g}", [128, 4, 128])
            gb["wT"] = sb(ph, f"wT{g}", [128, 4, 128], BF16)
            gb["vn"] = sb(ph, f"vn{g}", [128, 4, 128], BF16)
            if full:
                gb["ER"] = sb(ph, f"ER{g}", [128, 4, 128], BF16)
                gb["DT"] = sb(ph, f"DT{g}", [128, 4, 128])
                gb["aT"] = sb(ph, f"aT{g}", [128, 4, 128], BF16)
                gb["qdT"] = sb(ph, f"qdT{g}", [128, 4, 128], BF16)
            G.append(gb)
        B["G"] = G
        if full:
            B["zs"] = sb(ph, "zs", [128, 2048], BF16, nslots=2)
            B["og"] = [sb(ph, f"og{i}", [128, 4, 128]) for i in range(2)]
            B["oss"] = sb(ph, "oss", [128, 32])
            B["on"] = sb(ph, "on", [128, 16, 128], BF16)
            B["onT"] = sb(ph, "onT", [128, 16, 128], BF16)
        return B

    if nhist > 0:
        with contextlib.ExitStack() as ph:
            B = gdn_bufs(ph, 512, False)
            sts = [list(range(t, min(t + 4, nhist))) for t in range(0, nhist, 4)]
            gdn_st_norm(B, sts[0], False, None, 0)
            for k, tl in enumerate(sts):
                if k + 1 < len(sts):
                    gdn_st_norm(B, sts[k + 1], False, None, k + 1)
                gdn_st_rest(B, tl, False, None, k, k + 1 < len(sts))
                run_bg(8)
            S.barrier()
    run_bg(len(bgq))
    S.barrier()
    ph0.close()

    hbuf = sb(root, "h", [128, NFULL, D], F32, nslots=NFULL)

    with contextlib.ExitStack() as ph:
        B = gdn_bufs(ph, 256, True)
        sts = [[nhist + 2 * st, nhist + 2 * st + 1] for st in range(NFULL // 2)]
        gdn_st_norm(B, sts[0], True, hbuf, 0)
        for k, tl in enumerate(sts):
            if k + 1 < len(sts):
                gdn_st_norm(B, sts[k + 1], True, hbuf, k + 1)
            gdn_st_rest(B, tl, True, hbuf, k)
        S.barrier()

    def ffn_phase(l):
        with contextlib.ExitStack() as ph:
            xnT = sb(ph, "f_xnT", [128, 8, 512], BF16)
            xn = [sb(ph, f"f_xn{i}", [128, D], BF16) for i in range(2)]
            junk = sb(ph, "f_junk", [128, D], BF16)
            ms = sb(ph, "f_ms", [128, 4])
            wblk = [sb(ph, f"f_w{i}", [128, 4096], BF16) for i in range(3)]
            u = [sb(ph, f"f_u{i}", [128, 514]) for i in range(4)]
            acc = [sb(ph, f"f_acc{i}", [128, 512]) for i in range(4)]
            gs = [sb(ph, f"f_gs{i}", [128, 512]) for i in range(2)]
            act = sb(ph, "f_act", [128, 22, 512], BF16, nslots=22)
            fcarry = sb(ph, "f_carry", [128, 44, 2])
            cw = sb(ph, "f_cw", [128, 44, 3])
            cb = sb(ph, "f_cb", [128, 44])
            S.op("pool", lambda e: e.memset(fcarry[:], 0.0), W=[fcarry.b])
            S.dma("sp", cw[:], f_cw_d[l][:, :, :], W=[cw.b])
            S.dma("sp", cb[:], f_cb_d[l][:, :], W=[cb.b])
            sts = [list(range(s, min(s + 4, NFULL))) for s in range(0, NFULL, 4)]

            def ffn_supertile(tiles):
                NT = len(tiles)
                N = NT * 128
                for i, ft in enumerate(tiles):
                    rmsnorm_T((junk, ms, xn[i % 2]), hbuf[:, ft, :], [hbuf.bs[ft]], xnT, i * 128)
                for b in range(11):
                    wb = wblk[b % 3]
                    wv = wb[:].rearrange("p (c n) -> p c n", c=8)
                    S.dma("sp", wv, Wup_s[l][b], R=[scrB[id(Wup_s[l])]], W=[wb.b])
                    for jj in range(2):
                        res = []
                        for which in range(2):
                            f = which * 22 + b * 2 + jj
                            col = which * 256 + jj * 128
                            p = pb()
                            for c in range(8):
                                S.op("pe", lambda e, c=c, p=p, wv=wv, col=col: e.matmul(p[:, 0:N], wv[:, c, col:col + 128], xnT[:, c, 0:N],
                                                                                      start=(c == 0), stop=(c == 7)),
                                     R=[wb.b, xnT.b], W=[p.b])
                            k = (jj * 2 + which)
                            uu, aa = u[k], acc[k]
                            S.op("pool", lambda e, uu=uu, f=f: e.tensor_copy(uu[:, 0:2], fcarry[:, f, :]), R=[fcarry.b], W=[uu.b])
                            S.op("act", lambda e, uu=uu, p=p: e.copy(uu[:, 2:2 + N], p[:, 0:N]), R=[p.b], W=[uu.b])
                            S.op("pool", lambda e, uu=uu, f=f: e.tensor_copy(fcarry[:, f, :], uu[:, N:N + 2]), R=[uu.b], W=[fcarry.b])
                            S.op("dve", lambda e, uu=uu, aa=aa, f=f: e.tensor_scalar(aa[:, 0:N], uu[:, 2:2 + N], cw[:, f, 2:3], cb[:, f:f + 1], ALU.mult, ALU.add),
                                 R=[uu.b, cw.b, cb.b], W=[aa.b])
                            for j in (1, 0):
                                S.op("dve", lambda e, uu=uu, aa=aa, f=f, j=j: e.scalar_tensor_tensor(
                                    aa[:, 0:N], uu[:, j:j + N], cw[:, f, j:j + 1], aa[:, 0:N], ALU.mult, ALU.add),
                                    R=[uu.b, cw.b, aa.b], W=[aa.b])
                            res.append(aa)
                        g_ = gs[jj]
                        S.op("act", lambda e, g_=g_, a0=res[0]: e.activation(g_[:, 0:N], a0[:, 0:N], AF.Silu), R=[res[0].b], W=[g_.b])
                        S.op("pool", lambda e, g_=g_, a1=res[1], b=b, jj=jj: e.tensor_tensor(act[:, b * 2 + jj, 0:N], g_[:, 0:N], a1[:, 0:N], ALU.mult),
                             R=[g_.b, res[1].b], W=[act.bs[b * 2 + jj]])
                pys = [[pb(), pb()] for _ in range(NT)]
                for jb in range(6):
                    nj = 4 if jb < 5 else 2
                    wb = wblk[jb % 3]
                    wv = wb[:].rearrange("p (j n) -> p j n", j=4)
                    S.dma("sp", wv[:, 0:nj, :], Wdn_s[l][:, jb * 4:jb * 4 + nj, :], R=[scrB[id(Wdn_s[l])]], W=[wb.b])
                    for jl in range(nj):
                        j = jb * 4 + jl
                        for i in range(NT):
                            for half in range(2):
                                S.op("pe", lambda e, i=i, j=j, jl=jl, half=half, wv=wv: e.matmul(
                                    pys[i][half][:], act[:, j, i * 128:(i + 1) * 128], wv[:, jl, half * 512:(half + 1) * 512],
                                    start=(j == 0), stop=(j == 21)), R=[wb.b, act.bs[j]], W=[pys[i][half].b])
                for i, ft in enumerate(tiles):
                    for half in range(2):
                        S.op("dve", lambda e, i=i, ft=ft, half=half: e.scalar_tensor_tensor(
                            hbuf[:, ft, half * 512:(half + 1) * 512], pys[i][half][:], validt[:, ft:ft + 1],
                            hbuf[:, ft, half * 512:(half + 1) * 512], ALU.mult, ALU.add),
                            R=[pys[i][half].b, hbuf.bs[ft], validt.b], W=[hbuf.bs[ft]])

            for tiles in sts:
                ffn_supertile(tiles)
            S.barrier()

    if stage >= 2:
        ffn_phase(0)

    def attn_phase():
        with contextlib.ExitStack() as ph:
            NK = NFULL + 1
            xnT = sb(ph, "a_xnT", [128, 8, 512], BF16)
            xn = [sb(ph, f"a_xn{i}", [128, D], BF16) for i in range(2)]
            ms = sb(ph, "a_ms", [128, 4])
            wblk = [sb(ph, f"a_w{i}", [128, 8, 512], BF16) for i in range(3)]
            wk = [0]

            def wnext(src3, n):
                wb = wblk[wk[0] % 3]
                wk[0] += 1
                S.dma("sp", wb[:, :, 0:n], src3, R=[scrB[id(Wkv_s)], scrB[id(Wq_s)], scrB[id(Wo_s)]], W=[wb.b])
                return wb

            KT = sb(ph, "a_KT", [128, 4, NK * 128], BF16)
            Vt = sb(ph, "a_V", [128, NK, 256], BF16)
            QT = sb(ph, "a_QT", [128, 8, 512], BF16)
            BMf = sb(ph, "a_BMf", [128, 4, 256])
            BM = sb(ph, "a_BM", [128, 16, 256], BF16)
            am = sb(ph, "a_am", [128, 256])
            kbf = sb(ph, "a_kbf", [1, 512])
            kbb = sb(ph, "a_kbb", [1, NK * 128], BF16)
            sinkb = sb(ph, "a_sink", [128, 16])
            sc = [sb(ph, f"a_sc{i}", [128, 2, 256]) for i in range(3)]
            pr = [sb(ph, f"a_pr{i}", [128, 2, 256], BF16) for i in range(3)]
            pT = [sb(ph, f"a_pT{i}", [128, 4, 128], BF16) for i in range(3)]
            st_ = sb(ph, "a_st", [128, 6, 16], F32, nslots=8)
            obf = sb(ph, "a_obf", [128, D], BF16)
            junk = obf
            oT = sb(ph, "a_oT", [128, 8, 128], BF16)
            S.dma("sp", am[:], amask_d[:, :], W=[am.b])
            S.dma("sp", sinkb[:], sink_d[:, :], W=[sinkb.b])
            for q4 in range(4):
                S.dma("sp", BMf[:], band_d[:, q4 * 4:(q4 + 1) * 4, :], W=[BMf.b])
                S.op("dve", lambda e, q4=q4: e.tensor_tensor(BM[:, q4 * 4:(q4 + 1) * 4, :], BMf[:], am[:].unsqueeze(1).to_broadcast([128, 4, 256]), ALU.add),
                     R=[BMf.b, am.b], W=[BM.b])
            for k0_ in range(0, NK * 128, 512):
                kn = min(512, NK * 128 - k0_)
                S.dma("sp", kbf[:, 0:kn], kbias_d[:, k0_:k0_ + kn], W=[kbf.b])
                S.op("dve", lambda e, k0_=k0_, kn=kn: e.tensor_copy(kbb[:, k0_:k0_ + kn], kbf[:, 0:kn]), R=[kbf.b], W=[kbb.b])
            S.op("pool", lambda e: e.memset(KT[:, :, 0:128], 0.0), W=[KT.b])
            S.op("pool", lambda e: e.memset(Vt[:, 0, :], 0.0), W=[Vt.b])
            sts = [list(range(s, min(s + 4, NFULL))) for s in range(0, NFULL, 4)]

            def attn_supertile(tiles):
                NT = len(tiles)
                N = NT * 128
                for i, ft in enumerate(tiles):
                    rmsnorm_T((junk, ms, xn[i % 2]), hbuf[:, ft, :], [hbuf.bs[ft]], xnT, i * 128)
                kc0 = (tiles[0] + 1) * 128
                wkK = wnext(Wkv_s[0], 512)
                for j in range(4):
                    p = pb()
                    for c in range(8):
                        S.op("pe", lambda e, c=c, p=p, j=j, wkK=wkK: e.matmul(p[:, 0:N], wkK[:, c, j * 128:(j + 1) * 128], xnT[:, c, 0:N],
                                                                     start=(c == 0), stop=(c == 7)), R=[wkK.b, xnT.b], W=[p.b])
                    S.op("act", lambda e, p=p, j=j: e.copy(KT[:, j, kc0:kc0 + N], p[:, 0:N]), R=[p.b], W=[KT.b])
                wkV = wnext(Wkv_s[1, :, :, 0:256], 256)
                for i, ft in enumerate(tiles):
                    p = pb()
                    for c in range(8):
                        S.op("pe", lambda e, c=c, p=p, i=i, wkV=wkV: e.matmul(p[:, 0:256], xnT[:, c, i * 128:(i + 1) * 128], wkV[:, c, 0:256],
                                                                     start=(c == 0), stop=(c == 7)), R=[wkV.b, xnT.b], W=[p.b])
                    S.op("dve", lambda e, p=p, ft=ft: e.tensor_copy(Vt[:, ft + 1, :], p[:, 0:256]), R=[p.b], W=[Vt.b])
                for f in range(8):
                    if f % 4 == 0:
                        wqb = wnext(Wq_s[f // 4], 512)
                    p = pb()
                    for c in range(8):
                        S.op("pe", lambda e, c=c, p=p, f=f, wqb=wqb: e.matmul(p[:, 0:N], wqb[:, c, (f % 4) * 128:(f % 4 + 1) * 128], xnT[:, c, 0:N],
                                                                     start=(c == 0), stop=(c == 7)), R=[wqb.b, xnT.b], W=[p.b])
                    S.op("act", lambda e, p=p, f=f: e.activation(QT[:, f, 0:N], p[:, 0:N], AF.Copy, scale=0.125), R=[p.b], W=[QT.b])
                for i, ft in enumerate(tiles):
                    attn_tile(i, ft)

            def attn_tile(i, ft):
                if True:
                    k0 = ft * 128
                    pO = [PB[6], PB[7]]
                    reserved.update((6, 7))
                    PS, PTT = {}, {}

                    def stage_a(f):
                        kv = f // 2
                        ps = pb()
                        for hh in range(2):
                            lo = hh * 64
                            S.op("pe", lambda e, ps=ps, hh=hh, lo=lo, f=f, kv=kv: e.matmul(
                                ps[:, hh * 256:(hh + 1) * 256], QT[lo:lo + 64, f, i * 128:(i + 1) * 128], KT[lo:lo + 64, kv, k0:k0 + 256],
                                start=True, stop=False), R=[QT.b, KT.b], W=[ps.b])
                            S.op("pe", lambda e, ps=ps, hh=hh: e.matmul(
                                ps[:, hh * 256:(hh + 1) * 256], onesb[0:1, 0:128], kbb[0:1, k0:k0 + 256], start=False, stop=True),
                                R=[onesb.b, kbb.b], W=[ps.b])
                        s_, p_ = sc[f % 3], pr[f % 3]
                        sb_ = st_.bs[f]
                        S.op("dve", lambda e, ps=ps, s_=s_, f=f: e.tensor_tensor(s_[:], ps[:].rearrange("p (h k) -> p h k", h=2), BM[:, 2 * f:2 * f + 2, :], ALU.add),
                             R=[ps.b, BM.b], W=[s_.b])
                        S.op("dve", lambda e, s_=s_, f=f: e.tensor_reduce(st_[:, 0, 2 * f:2 * f + 2], s_[:], AX.X, ALU.max), R=[s_.b], W=[sb_])
                        S.op("dve", lambda e, f=f: e.tensor_tensor(st_[:, 0, 2 * f:2 * f + 2], st_[:, 0, 2 * f:2 * f + 2], sinkb[:, 2 * f:2 * f + 2], ALU.max),
                             R=[sb_, sinkb.b], W=[sb_])
                        S.op("dve", lambda e, f=f: e.tensor_scalar(st_[:, 1, 2 * f:2 * f + 2], st_[:, 0, 2 * f:2 * f + 2], -1.0, None, ALU.mult),
                             R=[sb_], W=[sb_])
                        for hh in range(2):
                            h_ = 2 * f + hh
                            S.op("act", lambda e, s_=s_, p_=p_, hh=hh, h_=h_: e.activation(p_[:, hh, :], s_[:, hh, :], AF.Exp, bias=st_[:, 1, h_:h_ + 1],
                                                                                           accum_out=st_[:, 2, h_:h_ + 1]),
                                 R=[s_.b, sb_], W=[p_.b, sb_])

                    def stage_b(f):
                        kv = f // 2
                        p_, t_ = pr[f % 3], pT[f % 3]
                        pt = pb()
                        ptv = pt[:].bitcast(BF16)
                        for hh in range(2):
                            for kb in range(2):
                                S.op("pe", lambda e, ptv=ptv, p_=p_, hh=hh, kb=kb: e.transpose(
                                    ptv[:, (hh * 2 + kb) * 128:(hh * 2 + kb + 1) * 128], p_[:, hh, kb * 128:(kb + 1) * 128], identb[:]),
                                    R=[p_.b, identb.b], W=[pt.b])
                        if f % 2 == 0:
                            S.op("act", lambda e, ptv=ptv, t_=t_: e.copy(t_[:].rearrange("p a b -> p (a b)"), ptv[:, 0:512]), R=[pt.b], W=[t_.b])
                        else:
                            S.op("dve", lambda e, ptv=ptv, t_=t_: e.tensor_copy(t_[:].rearrange("p a b -> p (a b)"), ptv[:, 0:512]), R=[pt.b], W=[t_.b])
                        for hh in range(2):
                            h_ = 2 * f + hh
                            for kb in range(2):
                                S.op("pe", lambda e, t_=t_, hh=hh, kb=kb, h_=h_, kv=kv: e.matmul(
                                    pO[h_ // 8][:, (h_ % 8) * 64:(h_ % 8 + 1) * 64], t_[:, hh * 2 + kb, :], Vt[:, ft + kb, kv * 64:(kv + 1) * 64],
                                    start=(kb == 0), stop=(kb == 1)), R=[t_.b, Vt.b], W=[pO[h_ // 8].b])

                    for f in range(9):
                        if f < 8:
                            stage_a(f)
                        if f >= 1:
                            stage_b(f - 1)
                    reserved.clear()
                    S.op("dve", lambda e: e.tensor_tensor(st_[:, 3, :], sinkb[:], st_[:, 1, :], ALU.add), R=[st_.b, sinkb.b] + st_.bs, W=[st_.b])
                    S.op("act", lambda e: e.activation(st_[:, 3, :], st_[:, 3, :], AF.Exp), R=[st_.b], W=[st_.b])
                    S.op("dve", lambda e: e.tensor_tensor(st_[:, 3, :], st_[:, 3, :], st_[:, 2, :], ALU.add), R=[st_.b], W=[st_.b])
                    S.op("dve", lambda e: e.reciprocal(st_[:, 4, :], st_[:, 3, :]), R=[st_.b], W=[st_.b])
                    for half in range(2):
                        S.op("dve", lambda e, half=half: e.tensor_tensor(
                            obf[:, half * 512:(half + 1) * 512].rearrange("p (h d) -> p h d", h=8), pO[half][:].rearrange("p (h d) -> p h d", h=8),
                            st_[:, 4, half * 8:(half + 1) * 8].unsqueeze(2).to_broadcast([128, 8, 64]), ALU.mult),
                            R=[pO[half].b, st_.b], W=[obf.b])
                    p = pb()
                    pv = p[:].bitcast(BF16)
                    for c in range(8):
                        S.op("pe", lambda e, c=c, pv=pv, p=p: e.transpose(pv[:, c * 128:(c + 1) * 128], obf[:, c * 128:(c + 1) * 128], identb[:]),
                             R=[obf.b, identb.b], W=[p.b])
                    S.op("act", lambda e, pv=pv: e.copy(oT[:].rearrange("p c t -> p (c t)"), pv), R=[p.b], W=[oT.b])
                    py = [pb(), pb()]
                    for half in range(2):
                        wob = wnext(Wo_s[half], 512)
                        for c in range(8):
                            S.op("pe", lambda e, c=c, half=half, wob=wob: e.matmul(py[half][:], oT[:, c, :], wob[:, c, :],
                                                                           start=(c == 0), stop=(c == 7)), R=[oT.b, wob.b], W=[py[half].b])
                        S.op("dve", lambda e, half=half, ft=ft: e.scalar_tensor_tensor(
                            hbuf[:, ft, half * 512:(half + 1) * 512], py[half][:], validt[:, ft:ft + 1],
                            hbuf[:, ft, half * 512:(half + 1) * 512], ALU.mult, ALU.add),
                            R=[py[half].b, hbuf.bs[ft], validt.b], W=[hbuf.bs[ft]])

            for tiles in sts:
                attn_supertile(tiles)
            S.barrier()

    if stage >= 3:
        attn_phase()
    if stage >= 4:
        ffn_phase(1)

    toks = []
    with contextlib.ExitStack() as ph:
        fnw = sb(ph, "fnw", [128, D])
        junk = sb(ph, "o_junk", [128, D], BF16)
        ms = sb(ph, "o_ms", [128, 4])
        ob = [sb(ph, f"o_b{i}", [128, D]) for i in range(2)]
        S.dma("sp", fnw[:], fin_w_d[:, :], W=[fnw.b])
        for t in range(NOWN):
            ft = t + NHALO
            o_ = ob[t % 2]
            if stage >= 5:
                S.op("act", lambda e, ft=ft: e.activation(junk[:], hbuf[:, ft, :], AF.Square, scale=1.0 / 32.0, accum_out=ms[:, 0:1]),
                     R=[hbuf.bs[ft]], W=[junk.b, ms.b])
                S.op("act", lambda e: e.activation(ms[:, 1:2], ms[:, 0:1], AF.Ln, bias=EPS), R=[ms.b, cst.b], W=[ms.b])
                S.op("act", lambda e: e.activation(ms[:, 2:3], ms[:, 1:2], AF.Exp, scale=-0.5), R=[ms.b], W=[ms.b])
                S.op("dve", lambda e, ft=ft, o_=o_: e.scalar_tensor_tensor(o_[:], hbuf[:, ft, :], ms[:, 2:3], fnw[:], ALU.mult, ALU.mult),
                     R=[hbuf.bs[ft], ms.b, fnw.b], W=[o_.b])
            else:
                S.op("dve", lambda e, ft=ft, o_=o_: e.tensor_copy(o_[:], hbuf[:, ft, :]), R=[hbuf.bs[ft]], W=[o_.b])
            toks.append(S.dma("sp", out_d[t * 128:(t + 1) * 128, :], o_[:], R=[o_.b]))
        S.finish(toks)
    root.close()
    return nc


def _t5_bucket(dist):
    n = np.maximum(dist, 0)
    nf = np.maximum(n, 1).astype(np.float32)
    large = 16 + (np.log(nf / 16) / np.log(128 / 16) * 16).astype(np.int32)
    large = np.minimum(large, 31)
    return np.where(n < 16, n, large)


def make_inputs(inp, nhist, cores):
    f32 = np.float32
    A = lambda v: np.ascontiguousarray(np.asarray(v), dtype=f32)
    x = A(inp["x"])

    def pc(v):
        return np.ascontiguousarray(A(v).reshape(8, 128).T)

    common = {
        "a_w_in": A(inp["a_w_in"][0]),
        "a_w_out": A(inp["a_w_out"][0]),
        "ffn_w_up0": A(inp["ffn_w_up"][0]), "ffn_w_up1": A(inp["ffn_w_up"][1]),
        "ffn_w_down0": A(inp["ffn_w_down"][0]), "ffn_w_down1": A(inp["ffn_w_down"][1]),
        "b_w_q": A(inp["b_w_q"][0]), "b_w_o": A(inp["b_w_o"][0]),
        "a_norm_wT": pc(inp["a_norm_w"][0]),
        "ffn_norm_wT0": pc(inp["ffn_norm_w"][0]), "ffn_norm_wT1": pc(inp["ffn_norm_w"][1]),
        "kv_norm_wT": pc(inp["kv_norm_w"]), "b_norm_wT": pc(inp["b_norm_w"][0]),
        "out_norm_wT": A(inp["a_out_norm_w"][0]).reshape(128, 1),
        "final_norm_wb": np.ascontiguousarray(np.broadcast_to(A(inp["final_norm_w"])[None, :], (128, D))),
        "a_conv_wT": np.ascontiguousarray(A(inp["a_conv_w"][0]).T.reshape(32, 128, 4).transpose(1, 0, 2)),
        "a_log_b": np.ascontiguousarray(np.broadcast_to(A(inp["a_a_log"][0])[None, :], (128, 16))),
        "dt_bias_b": np.ascontiguousarray(np.broadcast_to(A(inp["a_dt_bias"][0])[None, :], (128, 16))),
        "sinks_b": np.ascontiguousarray(np.broadcast_to(A(inp["b_sinks"][0])[None, :], (128, 16))),
    }
    for l in range(2):
        common[f"ffn_conv_wT{l}"] = np.ascontiguousarray(A(inp["ffn_conv_w"][l]).T.reshape(44, 128, 3).transpose(1, 0, 2))
        common[f"ffn_conv_bT{l}"] = np.ascontiguousarray(A(inp["ffn_conv_b"][l]).reshape(44, 128).T)
    wkv = A(inp["w_kv"])
    cols = []
    for j in range(4):
        cols += [wkv[:, j * 64:(j + 1) * 64], wkv[:, j * 64:(j + 1) * 64]]
    cols.append(wkv[:, 256:512])
    common["w_kv_dup"] = np.ascontiguousarray(np.concatenate(cols, axis=1))
    qi = np.arange(128)[:, None]
    ki = np.arange(256)[None, :]
    dist = qi + 128 - ki
    bucket = _t5_bucket(dist)
    tab = A(inp["rel_bias_table"])
    common["biasband"] = np.ascontiguousarray(tab[bucket].transpose(0, 2, 1))
    inwin = (dist >= 0) & (dist < 128)
    common["attnmask"] = np.where(inwin, 0.0, NEG).astype(f32)
    common["ident"] = np.eye(128, dtype=f32)
    p_ = np.arange(128)[:, None]
    j_ = np.arange(128)[None, :]
    common["triu"] = (p_ <= j_).astype(f32)
    common["maskL"] = np.where(p_ > j_, 0.0, NEG).astype(f32)
    common["maskU"] = np.where(j_ >= p_, 0.0, NEG).astype(f32)
    NT_ALL = nhist + NFULL
    maps = []
    for c in cores:
        b, j = c // 4, c % 4
        end = 2048 * (j + 1)
        start = end - NT_ALL * 128
        xe = np.zeros((NT_ALL * 128, D), f32)
        s0 = max(start, 0)
        xe[s0 - start:] = x[b, s0:end]
        pos_full = np.arange(end - NFULL * 128, end)
        valid = (pos_full >= 0).astype(f32).reshape(NFULL, 128).T
        kb = np.concatenate([np.full(128, NEG, f32), np.where(pos_full >= 0, 0.0, NEG).astype(f32)])[None, :]
        m = dict(common)
        m["x_ext"] = xe
        m["valid"] = np.ascontiguousarray(valid)
        m["kbias"] = np.ascontiguousarray(kb)
        maps.append(m)
    return maps


_NHIST = 46


def kernel(**inputs):
    nc = build_program(_NHIST)
    maps = make_inputs(inputs, _NHIST, list(range(8)))
    res = run_bass_kernel_spmd(nc, maps, core_ids=list(range(8)))
    out = np.empty((2, 8192, D), np.float32)
    for c in range(8):
        b, j = c // 4, c % 4
        out[b, 2048 * j:2048 * (j + 1)] = res.results[c]["out"]
    return out
```

```python
import contextlib
import numpy as np
import concourse.bass as bass
import concourse.mybir as mybir
from concourse.bass_utils import run_bass_kernel_spmd

F32 = mybir.dt.float32
BF16 = mybir.dt.bfloat16
AF = mybir.ActivationFunctionType
ALU = mybir.AluOpType
AX = mybir.AxisListType

D = 1024
NFULL = 18
NHALO = 2
NOWN = 16
NEG = -1.0e30
EPOCH = 30000


class Buf:
    __slots__ = ("name", "w", "r", "excl")

    def __init__(self, name="", excl=False):
        self.name = name
        self.w = None
        self.r = []
        self.excl = excl


class Sched:
    ENGS = ("pe", "act", "dve", "pool", "sp")

    def __init__(self, nc, n_dma_sems=10):
        self.nc = nc
        self.prog = {e: [] for e in self.ENGS}
        self.cnt = {e: 0 for e in self.ENGS}
        self.sems = {}
        self.seen = {e: {} for e in self.ENGS}
        self.dma_sems = {}
        self.n_dma_sems = n_dma_sems
        self.dma_rr = {e: 0 for e in self.ENGS}
        self._semctx = []
        self.last_tok = {}

    def _new_sem(self, name):
        ctx = self.nc.semaphore(name)
        h = ctx.__enter__()
        self._semctx.append(ctx)
        return h

    def _eng_sem(self, eng, idx):
        key = (eng, idx // EPOCH)
        if key not in self.sems:
            self.sems[key] = self._new_sem(f"s_{eng}_{idx // EPOCH}")
        return self.sems[key], (idx % EPOCH) + 1

    def _wait(self, eng, tok):
        teng, sem, val = tok
        if teng == eng and eng == "pe":
            return
        seen = self.seen[eng]
        if seen.get(sem.name, 0) >= val:
            return
        seen[sem.name] = val
        self.prog[eng].append(lambda e, sem=sem, val=val: e.wait_ge(sem, val))

    def _deps(self, eng, reads, writes):
        for b in reads:
            if b.w is not None:
                self._wait(eng, b.w)
            if b.excl:
                for t in b.r:
                    if t[0] != eng:
                        self._wait(eng, t)
        for b in writes:
            if b.w is not None and b.w[0] != eng:
                self._wait(eng, b.w)
            for t in b.r:
                if t[0] != eng:
                    self._wait(eng, t)

    def _commit(self, tok, reads, writes):
        for b in reads:
            b.r.append(tok)
        for b in writes:
            b.w = tok
            b.r = []

    def op(self, eng, fn, R=(), W=()):
        self._deps(eng, R, W)
        idx = self.cnt[eng]
        self.cnt[eng] += 1
        sem, val = self._eng_sem(eng, idx)
        self.prog[eng].append(lambda e, fn=fn, sem=sem: fn(e).then_inc(sem, 1))
        tok = (eng, sem, val)
        self.last_tok[eng] = tok
        self._commit(tok, R, W)
        return tok

    def dma(self, eng, out, in_, R=(), W=()):
        k = self.dma_rr[eng]
        self.dma_rr[eng] = (k + 1) % self.n_dma_sems
        key = (eng, k)
        if key not in self.dma_sems:
            self.dma_sems[key] = [self._new_sem(f"d_{eng}_{k}"), 0]
        ent = self.dma_sems[key]
        sem, tot = ent
        if tot > 0:
            self._wait(eng, ("dma", sem, tot))
        self._deps(eng, R, W)
        ent[1] = tot + 16
        self.prog[eng].append(
            lambda e, out=out, in_=in_, sem=sem: e.dma_start(out=out, in_=in_).then_inc(sem, 16))
        tok = ("dma", sem, tot + 16)
        self._commit(tok, R, W)
        return tok

    def barrier(self):
        toks = list(self.last_tok.values())
        for (eng, k), (sem, tot) in self.dma_sems.items():
            if tot > 0:
                toks.append(("dma", sem, tot))
        for e in self.ENGS:
            for t in toks:
                if t[0] != e or e != "pe":
                    self._wait(e, t)

    def finish(self, final_toks):
        for t in final_toks:
            self._wait("sp", t)
        nc = self.nc
        with nc.Block() as block:
            @block.tensor
            def _(e):
                for f in self.prog["pe"]:
                    f(e)

            @block.scalar
            def _(e):
                for f in self.prog["act"]:
                    f(e)

            @block.vector
            def _(e):
                for f in self.prog["dve"]:
                    f(e)

            @block.gpsimd
            def _(e):
                for f in self.prog["pool"]:
                    f(e)

            @block.sync
            def _(e):
                for f in self.prog["sp"]:
                    f(e)
        for ctx in reversed(self._semctx):
            ctx.__exit__(None, None, None)


class T:
    def __init__(self, t, name, nslots=0):
        self.t = t
        self.b = Buf(name)
        self.bs = [Buf(f"{name}{i}") for i in range(nslots)]

    def __getitem__(self, k):
        return self.t[k]


def build_program(nhist, stage=99, cut=99):
    nc = bass.Bass("TRN2", target_bir_lowering=False)
    S = Sched(nc)
    NT_ALL = nhist + NFULL

    def din(name, shape, dt=F32):
        return nc.dram_tensor(name, list(shape), dt, kind="ExternalInput").ap()

    x_ext = din("x_ext", [NT_ALL * 128, D])
    valid_d = din("valid", [128, NFULL])
    kbias_d = din("kbias", [1, (NFULL + 1) * 128])
    w_in_d = din("a_w_in", [D, 6176])
    w_out_d = din("a_w_out", [2048, D])
    w_up_d = [din(f"ffn_w_up{l}", [D, 5632]) for l in range(2)]
    w_dn_d = [din(f"ffn_w_down{l}", [2816, D]) for l in range(2)]
    w_kv_d = din("w_kv_dup", [D, 768])
    w_q_d = din("b_w_q", [D, D])
    w_o_d = din("b_w_o", [D, D])
    a_nw_d = din("a_norm_wT", [128, 8])
    f_nw_d = [din(f"ffn_norm_wT{l}", [128, 8]) for l in range(2)]
    kv_nw_d = din("kv_norm_wT", [128, 8])
    b_nw_d = din("b_norm_wT", [128, 8])
    on_w_d = din("out_norm_wT", [128, 1])
    fin_w_d = din("final_norm_wb", [128, D])
    a_cw_d = din("a_conv_wT", [128, 32, 4])
    f_cw_d = [din(f"ffn_conv_wT{l}", [128, 44, 3]) for l in range(2)]
    f_cb_d = [din(f"ffn_conv_bT{l}", [128, 44]) for l in range(2)]
    alog_d = din("a_log_b", [128, 16])
    dtb_d = din("dt_bias_b", [128, 16])
    sink_d = din("sinks_b", [128, 16])
    band_d = din("biasband", [128, 16, 256])
    amask_d = din("attnmask", [128, 256])
    ident_d = din("ident", [128, 128])
    triu_d = din("triu", [128, 128])
    maskL_d = din("maskL", [128, 128])
    maskU_d = din("maskU", [128, 128])
    out_d = nc.dram_tensor("out", [NOWN * 128, D], F32, kind="ExternalOutput").ap()

    def dscr(name, shape):
        return nc.dram_tensor(name, list(shape), BF16).ap()

    Win_s = dscr("Win_b", [25, 128, 8, 256])
    Wout_s = dscr("Wout_s", [128, 16, D])
    Wup_s = [dscr(f"Wup_b{l}", [11, 128, 8, 512]) for l in range(2)]
    Wdn_s = [dscr(f"Wdn_s{l}", [128, 22, D]) for l in range(2)]
    Wkv_s = dscr("Wkv_b", [2, 128, 8, 512])
    Wq_s = dscr("Wq_b", [2, 128, 8, 512])
    Wo_s = dscr("Wo_b", [2, 128, 8, 512])
    scrB = {id(a): Buf("scr") for a in [Win_s, Wout_s, Wkv_s, Wq_s, Wo_s] + Wup_s + Wdn_s}

    uid = [0]

    def sb(stk, name, shape, dt=F32, nslots=0):
        uid[0] += 1
        t = stk.enter_context(nc.sbuf_tensor(f"{name}_{uid[0]}", list(shape), dt))
        return T(t, name, nslots)

    root = contextlib.ExitStack()
    PB = [T(root.enter_context(nc.psum_tensor(f"pb{i}", [128, 512], F32)), f"pb{i}") for i in range(8)]
    for p_ in PB:
        p_.b.excl = True
    pbi = [0]

    reserved = set()

    def pb():
        while True:
            k = pbi[0] % 8
            pbi[0] += 1
            if k not in reserved:
                return PB[k]

    identf = sb(root, "identf", [128, 128])
    identb = sb(root, "identb", [128, 128], BF16)
    onesb = sb(root, "onesb", [128, 128], BF16)
    triub = sb(root, "triub", [128, 128], BF16)
    maskLt = sb(root, "maskLt", [128, 128])
    maskUt = sb(root, "maskUt", [128, 128])
    cst = sb(root, "cst", [128, 8])
    validt = sb(root, "validt", [128, NFULL])
    tmpc = sb(root, "tmpc", [128, 128])

    S.dma("sp", identf[:], ident_d[:, :], W=[identf.b])
    S.op("dve", lambda e: e.tensor_copy(identb[:], identf[:]), R=[identf.b], W=[identb.b])
    S.op("pool", lambda e: e.memset(onesb[:], 1.0), W=[onesb.b])
    S.dma("sp", tmpc[:], triu_d[:, :], W=[tmpc.b])
    S.op("dve", lambda e: e.tensor_copy(triub[:], tmpc[:]), R=[tmpc.b], W=[triub.b])
    S.dma("sp", maskLt[:], maskL_d[:, :], W=[maskLt.b])
    S.dma("sp", maskUt[:], maskU_d[:, :], W=[maskUt.b])
    S.op("pool", lambda e: e.memset(cst[:, 0:1], 1e-6), W=[cst.b])
    S.op("pool", lambda e: e.memset(cst[:, 1:2], 1.0), W=[cst.b])
    S.op("pool", lambda e: e.memset(cst[:, 2:3], float(np.log(128.0 ** -0.5))), W=[cst.b])
    S.op("pool", lambda e: e.memset(cst[:, 3:4], 0.0), W=[cst.b])
    S.dma("sp", validt[:], valid_d[:, :], W=[validt.b])
    EPS = cst[:, 0:1]
    ONE = cst[:, 1:2]

    gdn = root
    S32 = sb(gdn, "S32", [128, 16, 128], F32, nslots=4)
    Sbf = sb(gdn, "Sbf", [128, 16, 128], BF16, nslots=4)
    carry = sb(gdn, "carry", [128, 32, 3])
    convw = sb(gdn, "convw", [128, 32, 4])
    wba = sb(gdn, "wba", [128, 8, 32], BF16)
    negA = sb(gdn, "negA", [128, 16])
    dtb = sb(gdn, "dtb", [128, 16])
    S.op("pool", lambda e: e.memset(S32[:], 0.0), W=[S32.b] + S32.bs)
    S.op("pool", lambda e: e.memset(Sbf[:], 0.0), W=[Sbf.b] + Sbf.bs)
    S.op("pool", lambda e: e.memset(carry[:], 0.0), W=[carry.b])
    S.dma("sp", convw[:], a_cw_d[:, :, :], W=[convw.b])
    S.dma("sp", negA[:], alog_d[:, :], W=[negA.b])
    S.dma("sp", dtb[:], dtb_d[:, :], W=[dtb.b])
    S.op("act", lambda e: e.activation(negA[:], negA[:], AF.Exp), R=[negA.b], W=[negA.b])
    S.op("dve", lambda e: e.tensor_scalar(negA[:], negA[:], -1.0, None, ALU.mult), R=[negA.b], W=[negA.b])

    ph0 = contextlib.ExitStack()
    bgq = []
    if True:
        ph = ph0
        stg = [sb(ph, f"stg{i}", [128, 2048]) for i in range(2)]
        cvt = [sb(ph, f"cvt{i}", [128, 2048], BF16) for i in range(2)]
        nws = sb(ph, "nws", [128, 8 * 5 + 1])
        nw_ap = {}
        for i, (nm, d_) in enumerate([("a", a_nw_d), ("f0", f_nw_d[0]), ("f1", f_nw_d[1]), ("kv", kv_nw_d), ("b", b_nw_d)]):
            S.dma("sp", nws[:, i * 8:(i + 1) * 8], d_[:, :], W=[nws.b])
            nw_ap[nm] = (i * 8)
        S.dma("sp", nws[:, 40:41], on_w_d[:, :], W=[nws.b])
        blk = [0]

        def convert(src3, dst3, C, N, scale=None, dstfn=None, ng=512, defer=False):
            ng = min(N, ng)
            cg = max(1, min(C, 2048 // ng))

            def emit(c0, cc, n0, nn):
                i = blk[0] % 2
                blk[0] += 1
                st, cv = stg[i], cvt[i]
                sv = st[:, 0:cc * nn].rearrange("p (c n) -> p c n", c=cc)
                cvv = cv[:, 0:cc * nn].rearrange("p (c n) -> p c n", c=cc)
                S.dma("sp", sv, src3[:, c0:c0 + cc, n0:n0 + nn], W=[st.b])
                eng = "dve" if (blk[0] % 2 == 0) else "pool"
                if scale is None:
                    S.op(eng, lambda e: e.tensor_copy(cvv, sv), R=[st.b], W=[cv.b])
                elif scale[0] == "pc":
                    g = nws[:, scale[1] + c0:scale[1] + c0 + cc].unsqueeze(2).to_broadcast([128, cc, nn])
                    S.op(eng, lambda e: e.tensor_tensor(cvv, sv, g, ALU.mult), R=[st.b, nws.b], W=[cv.b])
                else:
                    g = nws[:, scale[1]:scale[1] + 1].unsqueeze(2).to_broadcast([128, cc, nn])
                    S.op(eng, lambda e: e.tensor_tensor(cvv, sv, g, ALU.mult), R=[st.b, nws.b], W=[cv.b])
                dap = dst3[:, c0:c0 + cc, n0:n0 + nn] if dstfn is None else dstfn(c0, cc, n0, nn)
                S.dma("pool", dap, cvv, R=[cv.b], W=[scrB[id(dst3)]])

            for c0 in range(0, C, cg):
                cc = min(cg, C - c0)
                for n0 in range(0, N, ng):
                    nn = min(ng, N - n0)
                    if defer:
                        bgq.append(lambda c0=c0, cc=cc, n0=n0, nn=nn: emit(c0, cc, n0, nn))
                    else:
                        emit(c0, cc, n0, nn)

        def pcn(ap):
            return ap.rearrange("(c p) n -> p c n", p=128)

        convert(pcn(w_in_d), Win_s, 8, 6176, ("pc", nw_ap["a"]), ng=256,
                dstfn=lambda c0, cc, n0, nn: Win_s[n0 // 256, :, c0:c0 + cc, 0:nn])
        convert(pcn(w_out_d), Wout_s, 16, D, ("p", 40))
        if stage >= 2:
            for l in range(2):
                convert(pcn(w_up_d[l]), Wup_s[l], 8, 5632, ("pc", nw_ap[f"f{l}"]), ng=256,
                        dstfn=lambda c0, cc, n0, nn, l=l: (Wup_s[l][n0 // 256, :, c0:c0 + cc, 0:256] if n0 < 2816
                                                           else Wup_s[l][(n0 - 2816) // 256, :, c0:c0 + cc, 256:512]), defer=True)
                convert(pcn(w_dn_d[l]), Wdn_s[l], 22, D, None, defer=True)
        if stage >= 3:
            convert(pcn(w_kv_d), Wkv_s, 8, 768, ("pc", nw_ap["kv"]), ng=256,
                    dstfn=lambda c0, cc, n0, nn: (Wkv_s[0, :, c0:c0 + cc, n0:n0 + nn] if n0 < 512 else Wkv_s[1, :, c0:c0 + cc, 0:nn]), defer=True)
            convert(pcn(w_q_d), Wq_s, 8, D, ("pc", nw_ap["b"]), ng=512,
                    dstfn=lambda c0, cc, n0, nn: Wq_s[n0 // 512, :, c0:c0 + cc, 0:nn], defer=True)
            convert(pcn(w_o_d), Wo_s, 8, D, None, ng=512,
                    dstfn=lambda c0, cc, n0, nn: Wo_s[n0 // 512, :, c0:c0 + cc, 0:nn], defer=True)
        S.dma("sp", wba[:], Win_s[24, :, :, 0:32], R=[scrB[id(Win_s)]], W=[wba.b])

    def run_bg(n):
        for _ in range(min(n, len(bgq))):
            bgq.pop(0)()


    def rmsnorm_T(ph_bufs, src_ap, src_bufs, xnT, col0):
        junk, ms, xn = ph_bufs
        S.op("act", lambda e: e.activation(junk[:], src_ap, AF.Square, scale=1.0 / 32.0, accum_out=ms[:, 0:1]),
             R=src_bufs, W=[junk.b, ms.b])
        S.op("act", lambda e: e.activation(ms[:, 1:2], ms[:, 0:1], AF.Ln, bias=EPS), R=[ms.b, cst.b], W=[ms.b])
        S.op("act", lambda e: e.activation(ms[:, 2:3], ms[:, 1:2], AF.Exp, scale=-0.5), R=[ms.b], W=[ms.b])
        S.op("dve", lambda e: e.tensor_scalar(xn[:], src_ap, ms[:, 2:3], None, ALU.mult), R=src_bufs + [ms.b], W=[xn.b])
        p = pb()
        pv = p[:].bitcast(BF16)
        for c in range(8):
            S.op("pe", lambda e, c=c: e.transpose(pv[:, c * 128:(c + 1) * 128], xn[:, c * 128:(c + 1) * 128], identb[:]),
                 R=[xn.b, identb.b], W=[p.b])
        S.op("act", lambda e: e.copy(xnT[:, :, col0:col0 + 128], pv.rearrange("p (c t) -> p c t", c=8)),
             R=[p.b], W=[xnT.b])

    def wload(B, k, src3, nparts):
        wb = B["wblk"][k % len(B["wblk"])]
        n = src3.shape[2]
        wv = wb[:, 0:nparts * n].rearrange("p (c n) -> p c n", c=nparts)
        return wb, wv

    def gdn_st_norm(B, tiles, full, hbuf, sp):
        xnT = B["xnT"][sp % 2]
        for i, t in enumerate(tiles):
            if full:
                ft = t - nhist
                src, sbufs = hbuf[:, ft, :], [hbuf.bs[ft]]
            else:
                xs = B["xs"][t % 2]
                src, sbufs = xs[:], [xs.b]
            S.dma("sp", src, x_ext[t * 128:(t + 1) * 128, :], W=sbufs)
            rmsnorm_T((B["junk"], B["ms"], B["xn"][i % 2]), src, sbufs, xnT, i * 128)

    def gdn_st_rest(B, tiles, full, hbuf, sp, has_next=False):
        NT = len(tiles)
        N = NT * 128
        WB = B["WB"]
        CPB = WB // 128
        xnT, qkvT = B["xnT"][sp % 2], B["qkvT"]
        gdn_st_small(B, xnT, NT, sp)
        f0 = 0 if full else 8
        wb = None
        for f in range(f0, 32):
            if f % CPB == 0 or wb is None:
                blk = f // CPB
                if blk in B["pref"]:
                    wb, wv = B["pref"].pop(blk)
                else:
                    wb, wv = wload(B, blk, Win_s[blk], 8)
                    S.dma("sp", wv, Win_s[blk], R=[scrB[id(Win_s)]], W=[wb.b])
            p = pb()
            for c in range(8):
                S.op("pe", lambda e, c=c, wv=wv, f=f, p=p: e.matmul(p[:, 0:N], wv[:, c, (f % CPB) * 128:(f % CPB + 1) * 128],
                                                                    xnT[:, c, 0:N], start=(c == 0), stop=(c == 7)),
                     R=[wb.b, xnT.b], W=[p.b])
            u = B["u"][f % 2]
            acc = B["acc"][f % 2]
            S.op("pool", lambda e, u=u, f=f: e.tensor_copy(u[:, 0:3], carry[:, f, :]), R=[carry.b], W=[u.b])
            S.op("act", lambda e, u=u, p=p: e.copy(u[:, 3:3 + N], p[:, 0:N]), R=[p.b], W=[u.b])
            S.op("pool", lambda e, u=u, f=f: e.tensor_copy(carry[:, f, :], u[:, N:N + 3]), R=[u.b], W=[carry.b])
            S.op("dve", lambda e, u=u, acc=acc, f=f: e.tensor_scalar(acc[:, 0:N], u[:, 3:3 + N], convw[:, f, 3:4], None, ALU.mult),
                 R=[u.b, convw.b], W=[acc.b])
            for j in (2, 1, 0):
                S.op("dve", lambda e, u=u, acc=acc, f=f, j=j: e.scalar_tensor_tensor(
                    acc[:, 0:N], u[:, j:j + N], convw[:, f, j:j + 1], acc[:, 0:N], ALU.mult, ALU.add),
                    R=[u.b, convw.b, acc.b], W=[acc.b])
            S.op("act", lambda e, acc=acc, f=f: e.activation(qkvT[:, f, 0:N], acc[:, 0:N], AF.Silu),
                 R=[acc.b], W=[qkvT.bs[f]])
        for f in range(f0, 16):
            sq = B["sq"][f % 2]
            S.op("pool", lambda e, sq=sq, f=f: e.tensor_tensor(sq[:, 0:N], qkvT[:, f, 0:N], qkvT[:, f, 0:N], ALU.mult),
                 R=[qkvT.bs[f]], W=[sq.b])
            p = pb()
            S.op("pe", lambda e, p=p, sq=sq: e.matmul(p[:, 0:N], onesb[:], sq[:, 0:N], start=True, stop=True),
                 R=[onesb.b, sq.b], W=[p.b])
            rn = B["acc"][f % 2]
            S.op("act", lambda e, p=p, rn=rn: e.activation(rn[:, 0:N], p[:, 0:N], AF.Ln, bias=EPS), R=[p.b, cst.b], W=[rn.b])
            bias = cst[:, 2:3] if f < 8 else cst[:, 3:4]
            S.op("act", lambda e, rn=rn, bias=bias: e.activation(rn[:, 0:N], rn[:, 0:N], AF.Exp, scale=-0.5, bias=bias),
                 R=[rn.b, cst.b], W=[rn.b])
            S.op("dve", lambda e, rn=rn, f=f: e.tensor_tensor(qkvT[:, f, 0:N], qkvT[:, f, 0:N], rn[:, 0:N], ALU.mult),
                 R=[rn.b, qkvT.bs[f]], W=[qkvT.bs[f]])
        if cut < 2:
            return
        if has_next and not full:
            for blk in range(f0 // CPB, f0 // CPB + len(B["wblk"])):
                wb, wv = wload(B, blk, Win_s[blk], 8)
                S.dma("sp", wv, Win_s[blk], R=[scrB[id(Win_s)]], W=[wb.b])
                B["pref"][blk] = (wb, wv)
        for i, t in enumerate(tiles):
            gdn_tile(B, xnT, i, t, full, hbuf, sp)

    def gdn_st_small(B, xnT, NT, sp):
        sm = B["sm"][sp % 2]
        smb = B["smb"][sp % 2]
        W_ = NT * 16
        SM = lambda k: sm[:, k, 0:W_].rearrange("p (i h) -> p i h", h=16)
        SB = lambda k: smb[:, k, 0:W_].rearrange("p (i h) -> p i h", h=16)
        r = lambda k: sm.bs[k]
        rb = lambda k: smb.bs[k]
        bc = lambda t_: t_[:].unsqueeze(1).to_broadcast([128, NT, 16])
        pba = pb()
        for i in range(NT):
            for c in range(8):
                S.op("pe", lambda e, c=c, i=i: e.matmul(pba[:, i * 32:(i + 1) * 32], xnT[:, c, i * 128:(i + 1) * 128], wba[:, c, :],
                                                        start=(c == 0), stop=(c == 7)), R=[xnT.b, wba.b], W=[pba.b])
        pbv = pba[:, 0:NT * 32].rearrange("p (i c) -> p i c", c=32)
        S.op("dve", lambda e: e.tensor_tensor(SM(0), pbv[:, :, 16:32], bc(dtb), ALU.add), R=[pba.b, dtb.b], W=[r(0)])
        S.op("act", lambda e: e.activation(SM(4), pbv[:, :, 0:16], AF.Exp, scale=-1.0), R=[pba.b], W=[r(4)])
        S.op("act", lambda e: e.activation(SM(1), SM(0), AF.Abs), R=[r(0)], W=[r(1)])
        S.op("act", lambda e: e.activation(SM(1), SM(1), AF.Exp, scale=-1.0), R=[r(1)], W=[r(1)])
        S.op("act", lambda e: e.activation(SM(1), SM(1), AF.Ln, bias=ONE), R=[r(1), cst.b], W=[r(1)])
        S.op("dve", lambda e: e.tensor_scalar(SM(4), SM(4), 1.0, None, ALU.add), R=[r(4)], W=[r(4)])
        S.op("dve", lambda e: e.reciprocal(SM(5), SM(4)), R=[r(4)], W=[r(5)])
        S.op("dve", lambda e: e.scalar_tensor_tensor(SM(2), SM(0), 0.0, SM(1), ALU.max, ALU.add), R=[r(0), r(1)], W=[r(2)])
        S.op("dve", lambda e: e.tensor_tensor(SM(3), SM(2), bc(negA), ALU.mult), R=[r(2), negA.b], W=[r(3)])
        S.op("dve", lambda e: e.tensor_copy(SB(0), SM(3)), R=[r(3)], W=[rb(0)])
        S.op("dve", lambda e: e.tensor_copy(SM(6), SB(0)), R=[rb(0)], W=[r(6)])
        S.op("dve", lambda e: e.tensor_tensor(SB(1), SM(3), SM(6), ALU.subtract), R=[r(3), r(6)], W=[rb(1)])
        pc = pb()
        for i in range(NT):
            for k in range(2):
                S.op("pe", lambda e, i=i, k=k: e.matmul(pc[:, i * 32:i * 32 + 16], triub[:], smb[:, k, i * 16:(i + 1) * 16], start=(k == 0), stop=(k == 1)),
                     R=[triub.b, rb(k)], W=[pc.b])
            for k in range(2):
                S.op("pe", lambda e, i=i, k=k: e.matmul(pc[:, i * 32 + 16:i * 32 + 32], onesb[:], smb[:, k, i * 16:(i + 1) * 16], start=(k == 0), stop=(k == 1)),
                     R=[onesb.b, rb(k)], W=[pc.b])
        pcv = pc[:, 0:NT * 32].rearrange("p (i c) -> p i c", c=32)
        S.op("dve", lambda e: e.tensor_copy(SM(7), pcv[:, :, 0:16]), R=[pc.b], W=[r(7)])
        S.op("dve", lambda e: e.tensor_copy(SM(8), pcv[:, :, 16:32]), R=[pc.b], W=[r(8)])
        S.op("act", lambda e: e.activation(SM(9), SM(8), AF.Exp), R=[r(8)], W=[r(9)])
        S.op("dve", lambda e: e.tensor_tensor(SM(10), SM(8), SM(7), ALU.subtract), R=[r(7), r(8)], W=[r(10)])
        S.op("act", lambda e: e.activation(SM(15), SM(7), AF.Exp), R=[r(7)], W=[r(15)])
        S.op("act", lambda e: e.activation(SM(10), SM(10), AF.Exp), R=[r(10)], W=[r(10)])
        S.op("dve", lambda e: e.tensor_scalar(SM(12), SM(7), -1.0, None, ALU.mult), R=[r(7)], W=[r(12)])
        S.op("dve", lambda e: e.tensor_copy(SB(2), SM(12)), R=[r(12)], W=[rb(2)])
        S.op("dve", lambda e: e.tensor_copy(SM(13), SB(2)), R=[rb(2)], W=[r(13)])
        S.op("dve", lambda e: e.tensor_tensor(SM(14), SM(12), SM(13), ALU.subtract), R=[r(12), r(13)], W=[r(14)])
        S.op("dve", lambda e: e.tensor_tensor(SM(11), SM(5), SM(15), ALU.mult), R=[r(5), r(15)], W=[r(11)])

    def gdn_tile(B, xnT, i, t, full, hbuf, sp):
        qkvT = B["qkvT"]
        WB = B["WB"]
        c0 = i * 128
        sm = B["sm"][sp % 2]
        o16 = i * 16
        SM = lambda k: sm[:, k, o16:o16 + 16]
        def emit_z(blks):
            zs = B["zs"]
            for blk in blks:
                wb, wv = wload(B, blk, Win_s[16 + blk], 8)
                S.dma("sp", wv, Win_s[16 + blk], R=[scrB[id(Win_s)]], W=[wb.b])
                p = pb()
                for c in range(8):
                    S.op("pe", lambda e, c=c, wv=wv, p=p: e.matmul(p[:, 0:WB], xnT[:, c, c0:c0 + 128], wv[:, c, :],
                                                                   start=(c == 0), stop=(c == 7)),
                         R=[wb.b, xnT.b], W=[p.b])
                S.op("act", lambda e, p=p, blk=blk: e.activation(zs[:, blk * WB:(blk + 1) * WB], p[:, 0:WB], AF.Silu),
                     R=[p.b], W=[zs.bs[blk // 4]])
        if full:
            emit_z(range(0, 4))
        ktm, kbg, kd, vb = B["ktm"], B["kbg"], B["kd"], B["vb"]
        p = pb()
        pv = p[:].bitcast(BF16)
        for kh in range(8):
            S.op("pe", lambda e, kh=kh, pv=pv: e.transpose(pv[:, kh * 128:(kh + 1) * 128], qkvT[:, 8 + kh, c0:c0 + 128], identb[:]),
                 R=[qkvT.bs[8 + kh], identb.b], W=[p.b])
        S.op("act", lambda e, pv=pv: e.copy(ktm[:], pv.rearrange("p (k d) -> p k d", k=8)), R=[p.b], W=[ktm.b])
        if cut < 3.2:
            return
        k4 = ktm[:].unsqueeze(2).to_broadcast([128, 8, 2, 128])
        S.op("pool", lambda e: e.tensor_tensor(kbg[:].rearrange("p (k r) d -> p k r d", r=2), k4,
                                               SM(11).rearrange("p (k r) -> p k r", r=2).unsqueeze(3).to_broadcast([128, 8, 2, 128]),
                                               ALU.mult), R=[ktm.b, sm.bs[11]], W=[kbg.b])
        S.op("pool", lambda e: e.tensor_tensor(kd[:].rearrange("p (k r) d -> p k r d", r=2), k4,
                                               SM(10).rearrange("p (k r) -> p k r", r=2).unsqueeze(3).to_broadcast([128, 8, 2, 128]),
                                               ALU.mult), R=[ktm.b, sm.bs[10]], W=[kd.b])
        if cut < 3.4:
            return
        for half in range(2):
            p = pb()
            pv = p[:].bitcast(BF16)
            for j in range(8):
                h_ = half * 8 + j
                S.op("pe", lambda e, j=j, h_=h_, pv=pv: e.transpose(pv[:, j * 128:(j + 1) * 128], qkvT[:, 16 + h_, c0:c0 + 128], identb[:]),
                     R=[qkvT.bs[16 + h_], identb.b], W=[p.b])
            S.op("dve", lambda e, half=half, pv=pv: e.tensor_tensor(
                vb[:, half * 8:(half + 1) * 8, :], pv.rearrange("p (k d) -> p k d", k=8),
                sm[:, 5, o16 + half * 8:o16 + (half + 1) * 8].unsqueeze(2).to_broadcast([128, 8, 128]), ALU.mult),
                R=[p.b, sm.bs[5]], W=[vb.b])
        if cut < 3.6:
            return
        Asb = B["Asb"]
        pA = [pb(), pb()]
        for kh in range(8):
            S.op("pe", lambda e, kh=kh: e.matmul(pA[kh // 4][:, (kh % 4) * 128:(kh % 4 + 1) * 128], qkvT[:, 8 + kh, c0:c0 + 128],
                                                 qkvT[:, 8 + kh, c0:c0 + 128], start=True, stop=True),
                 R=[qkvT.bs[8 + kh]], W=[pA[kh // 4].b])
        for j in range(2):
            S.op("act", lambda e, j=j: e.copy(Asb[:, j * 4:(j + 1) * 4, :], pA[j][:].rearrange("p (k s) -> p k s", k=4)),
                 R=[pA[j].b], W=[Asb.b])
        if cut < 3.8:
            return
        if full:
            KQsb = B["KQsb"]
            pK = [pb(), pb()]
            for kh in range(8):
                S.op("pe", lambda e, kh=kh: e.matmul(pK[kh // 4][:, (kh % 4) * 128:(kh % 4 + 1) * 128], qkvT[:, 8 + kh, c0:c0 + 128],
                                                     qkvT[:, kh, c0:c0 + 128], start=True, stop=True),
                     R=[qkvT.bs[8 + kh], qkvT.bs[kh]], W=[pK[kh // 4].b])
            for j in range(2):
                S.op("dve", lambda e, j=j: e.tensor_copy(KQsb[:, j * 4:(j + 1) * 4, :], pK[j][:].rearrange("p (k s) -> p k s", k=4)),
                     R=[pK[j].b], W=[KQsb.b])
        if cut < 4:
            return
        G = B["G"]
        b4 = lambda ap2: ap2.unsqueeze(1).to_broadcast([128, 4, 128])
        col4 = lambda k, h0: sm[:, k, o16 + h0:o16 + h0 + 4].unsqueeze(2).to_broadcast([128, 4, 128])
        flat = lambda t_: t_[:].rearrange("p j s -> p (j s)")
        kr = lambda ap3: ap3.rearrange("p (k r) s -> p k r s", r=2)
        NG = len(G)
        for gp in range(4 // NG):
            grp = list(range(NG * gp, NG * gp + NG))
            GB = {g: G[g % NG] for g in grp}
            H0 = {g: g * 4 for g in grp}
            for g in grp:
                gb, h0 = GB[g], H0[g]
                S.op("dve", lambda e, gb=gb, h0=h0: e.tensor_tensor(gb["dgh"][:], b4(identf[:]), col4(13, h0), ALU.mult),
                     R=[identf.b, sm.bs[13]], W=[gb["dgh"].b])
                S.op("pool", lambda e, gb=gb, h0=h0: e.tensor_tensor(gb["dgl"][:], b4(identf[:]), col4(14, h0), ALU.mult),
                     R=[identf.b, sm.bs[14]], W=[gb["dgl"].b])
            PR = {}
            for g in grp:
                gb = GB[g]
                pR = pb()
                PR[g] = pR
                S.op("pe", lambda e, pR=pR, gb=gb: e.matmul(pR[:], onesb[:], flat(gb["dgh"]), start=True, stop=False),
                     R=[onesb.b, gb["dgh"].b], W=[pR.b])
                S.op("pe", lambda e, pR=pR, gb=gb: e.matmul(pR[:], onesb[:], flat(gb["dgl"]), start=False, stop=True),
                     R=[onesb.b, gb["dgl"].b], W=[pR.b])
            for g in grp:
                gb, h0, pR = GB[g], H0[g], PR[g]
                S.op("dve", lambda e, pR=pR, gb=gb, h0=h0: e.tensor_tensor(gb["Z"][:], pR[:].rearrange("p (j s) -> p j s", j=4), col4(7, h0), ALU.add),
                     R=[pR.b, sm.bs[7]], W=[gb["Z"].b])
            if full:
                for g in grp:
                    gb, pR = GB[g], PR[g]
                    S.op("act", lambda e, pR=pR, gb=gb: e.activation(flat(gb["ER"]), pR[:], AF.Exp, scale=-1.0), R=[pR.b], W=[gb["ER"].b])
                for g in grp:
                    gb = GB[g]
                    S.op("dve", lambda e, gb=gb: e.scalar_tensor_tensor(gb["DT"][:], gb["Z"][:], -1.0, b4(maskUt[:]), ALU.mult, ALU.add),
                         R=[gb["Z"].b, maskUt.b], W=[gb["DT"].b])
            for g in grp:
                gb = GB[g]
                S.op("pool", lambda e, gb=gb: e.tensor_tensor(gb["Z"][:], gb["Z"][:], b4(maskLt[:]), ALU.add), R=[gb["Z"].b, maskLt.b], W=[gb["Z"].b])
            if full:
                for g in grp:
                    gb = GB[g]
                    S.op("act", lambda e, gb=gb: e.activation(gb["DT"][:], gb["DT"][:], AF.Exp), R=[gb["DT"].b], W=[gb["DT"].b])
            for g in grp:
                gb = GB[g]
                S.op("act", lambda e, gb=gb: e.activation(gb["Z"][:], gb["Z"][:], AF.Exp), R=[gb["Z"].b], W=[gb["Z"].b])
            for g in grp:
                gb, h0 = GB[g], H0[g]
                S.op("pool", lambda e, gb=gb, h0=h0: e.tensor_tensor(gb["Z"][:], gb["Z"][:], col4(5, h0), ALU.mult), R=[gb["Z"].b, sm.bs[5]], W=[gb["Z"].b])
            if full:
                for g in grp:
                    gb, kh0 = GB[g], H0[g] // 2
                    S.op("pool", lambda e, gb=gb, kh0=kh0: e.tensor_tensor(
                        kr(gb["aT"][:]), kr(gb["DT"][:]), KQsb[:, kh0:kh0 + 2, :].unsqueeze(2).to_broadcast([128, 2, 2, 128]), ALU.mult),
                        R=[gb["DT"].b, KQsb.b], W=[gb["aT"].b])
                    S.op("pool", lambda e, gb=gb, kh0=kh0: e.tensor_tensor(
                        kr(gb["qdT"][:]), kr(gb["ER"][:]), qkvT[:, kh0:kh0 + 2, c0:c0 + 128].unsqueeze(2).to_broadcast([128, 2, 2, 128]), ALU.mult),
                        R=[gb["ER"].b, qkvT.bs[kh0], qkvT.bs[kh0 + 1]], W=[gb["qdT"].b])
            for g in grp:
                gb, kh0 = GB[g], H0[g] // 2
                S.op("dve", lambda e, gb=gb, kh0=kh0: e.tensor_tensor(
                    kr(gb["Mk"][:]), kr(gb["Z"][:]), Asb[:, kh0:kh0 + 2, :].unsqueeze(2).to_broadcast([128, 2, 2, 128]), ALU.mult),
                    R=[gb["Z"].b, Asb.b], W=[gb["Mk"].b])
            for g in grp:
                gb = GB[g]
                S.op("pool", lambda e, gb=gb: e.tensor_copy(gb["Mh"][:], gb["Mk"][:]), R=[gb["Mk"].b], W=[gb["Mh"].b])
            PT = {}
            for g in grp:
                gb = GB[g]
                p = pb()
                PT[g] = p
                pv = p[:].bitcast(BF16)
                for j in range(4):
                    S.op("pe", lambda e, j=j, pv=pv, gb=gb: e.transpose(pv[:, j * 128:(j + 1) * 128], gb["Mk"][:, j, :], identb[:]),
                         R=[gb["Mk"].b, identb.b], W=[p.b])
            for g in grp:
                gb, p = GB[g], PT[g]
                pv = p[:].bitcast(BF16)
                S.op("act", lambda e, gb=gb, pv=pv: e.copy(flat(gb["Nk"]), pv[:, 0:512]), R=[p.b], W=[gb["Nk"].b])
                S.op("dve", lambda e, gb=gb, pv=pv: e.scalar_tensor_tensor(
                    gb["V"][:], pv[:, 0:512].rearrange("p (j s) -> p j s", j=4), -1.0, b4(identb[:]), ALU.mult, ALU.add),
                    R=[p.b, identb.b], W=[gb["V"].b])
            for r in range(1, 7):
                for g in grp:
                    gb = GB[g]
                    Mk, Nk, V = gb["Mk"], gb["Nk"], gb["V"]
                    pM = pN = pV = None
                    if r <= 5:
                        pM = pb()
                        for j in range(4):
                            S.op("pe", lambda e, j=j, pM=pM, Mk=Mk, Nk=Nk: e.matmul(
                                pM[:, j * 128:(j + 1) * 128], Nk[:, j, :], Mk[:, j, :], start=True, stop=True),
                                R=[Nk.b, Mk.b], W=[pM.b])
                    if r <= 5:
                        pN = pb()
                        for j in range(4):
                            S.op("pe", lambda e, j=j, pN=pN, Mk=Mk, Nk=Nk: e.matmul(
                                pN[:, j * 128:(j + 1) * 128], Mk[:, j, :], Nk[:, j, :], start=True, stop=True),
                                R=[Nk.b, Mk.b], W=[pN.b])
                    if r >= 2:
                        pV = pb()
                        for j in range(4):
                            S.op("pe", lambda e, j=j, pV=pV, Mk=Mk, V=V: e.matmul(
                                pV[:, j * 128:(j + 1) * 128], Mk[:, j, :], V[:, j, :], start=True, stop=True),
                                R=[Mk.b, V.b], W=[pV.b])
                        S.op("dve", lambda e, pV=pV, V=V: e.tensor_tensor(flat(V), pV[:], flat(V), ALU.add),
                             R=[pV.b, V.b], W=[V.b])
                    if pM is not None:
                        S.op("act", lambda e, pM=pM, Mk=Mk: e.copy(flat(Mk), pM[:]), R=[pM.b], W=[Mk.b])
                    if pN is not None:
                        S.op("act", lambda e, pN=pN, Nk=Nk: e.copy(flat(Nk), pN[:]), R=[pN.b], W=[Nk.b])
            PNV, PT0 = {}, {}
            for g in grp:
                gb = GB[g]
                V, Mh = gb["V"], gb["Mh"]
                pNV = pb()
                PNV[g] = pNV
                for j in range(4):
                    S.op("pe", lambda e, j=j, pNV=pNV, Mh=Mh, V=V: e.matmul(pNV[:, j * 128:(j + 1) * 128], Mh[:, j, :], V[:, j, :], start=True, stop=True),
                         R=[Mh.b, V.b], W=[pNV.b])
                pT0 = pb()
                PT0[g] = pT0
                pT0v = pT0[:].bitcast(BF16)
                for j in range(4):
                    S.op("pe", lambda e, j=j, pT0v=pT0v, V=V: e.transpose(pT0v[:, j * 128:(j + 1) * 128], V[:, j, :], identb[:]),
                         R=[V.b, identb.b], W=[pT0.b])
            for g in grp:
                gb = GB[g]
                V, Z, Rv, T0 = gb["V"], gb["Z"], gb["dgh"], gb["Nk"]
                pNV, pT0 = PNV[g], PT0[g]
                pT0v = pT0[:].bitcast(BF16)
                S.op("dve", lambda e, pNV=pNV, Z=Z, V=V: e.scalar_tensor_tensor(flat(Z), pNV[:], -1.0, flat(V), ALU.mult, ALU.subtract),
                     R=[pNV.b, V.b], W=[Z.b])
                S.op("pool", lambda e, Z=Z, Rv=Rv: e.tensor_tensor(Rv[:], Z[:], b4(identf[:]), ALU.add), R=[Z.b, identf.b], W=[Rv.b])
                S.op("act", lambda e, pT0v=pT0v, T0=T0: e.copy(flat(T0), pT0v[:, 0:512]), R=[pT0.b], W=[T0.b])
            PVR = {}
            for g in grp:
                gb = GB[g]
                Rv, T0 = gb["dgh"], gb["Nk"]
                pVR = pb()
                PVR[g] = pVR
                for j in range(4):
                    S.op("pe", lambda e, j=j, pVR=pVR, T0=T0, Rv=Rv: e.matmul(pVR[:, j * 128:(j + 1) * 128], T0[:, j, :], Rv[:, j, :], start=True, stop=True),
                         R=[T0.b, Rv.b], W=[pVR.b])
            for g in grp:
                gb = GB[g]
                V, Z, Vl, pVR = gb["V"], gb["Z"], gb["Mk"], PVR[g]
                S.op("dve", lambda e, pVR=pVR, Z=Z, V=V: e.tensor_tensor(flat(Z), pVR[:], flat(V), ALU.add), R=[pVR.b, V.b], W=[Z.b])
                S.op("act", lambda e, Z=Z, V=V: e.copy(V[:], Z[:]), R=[Z.b], W=[V.b])
            for g in grp:
                gb, h0 = GB[g], H0[g]
                V, Vl = gb["V"], gb["Mk"]
                pU = pb()
                pW = pb()
                for j in range(4):
                    S.op("pe", lambda e, j=j, pU=pU, V=V, h0=h0: e.matmul(pU[:, j * 128:(j + 1) * 128], V[:, j, :], vb[:, h0 + j, :], start=True, stop=True),
                         R=[V.b, vb.b], W=[pU.b])
                for j in range(4):
                    S.op("pe", lambda e, j=j, pW=pW, V=V, h0=h0: e.matmul(pW[:, j * 128:(j + 1) * 128], kbg[:, h0 + j, :], V[:, j, :], start=True, stop=True),
                         R=[V.b, kbg.b], W=[pW.b])
                S.op("act", lambda e, gb=gb, pU=pU: e.copy(flat(gb["u"]), pU[:]), R=[pU.b], W=[gb["u"].b])
                S.op("dve", lambda e, gb=gb, pW=pW: e.tensor_copy(flat(gb["wT"]), pW[:]), R=[pW.b], W=[gb["wT"].b])
            PWS = {}
            for g in grp:
                gb, h0 = GB[g], H0[g]
                pWS = pb()
                PWS[g] = pWS
                for j in range(4):
                    S.op("pe", lambda e, j=j, pWS=pWS, gb=gb, h0=h0: e.matmul(pWS[:, j * 128:(j + 1) * 128], gb["wT"][:, j, :], Sbf[:, h0 + j, :], start=True, stop=True),
                         R=[gb["wT"].b, Sbf.bs[g]], W=[pWS.b])
            for g in grp:
                gb, pWS = GB[g], PWS[g]
                S.op("dve", lambda e, gb=gb, pWS=pWS: e.tensor_tensor(flat(gb["vn"]), flat(gb["u"]), pWS[:], ALU.subtract),
                     R=[pWS.b, gb["u"].b], W=[gb["vn"].b])
            PO, PS = {}, {}
            for g in grp:
                gb, h0 = GB[g], H0[g]
                if full:
                    pO = pb()
                    PO[g] = pO
                    for j in range(4):
                        S.op("pe", lambda e, j=j, pO=pO, gb=gb, h0=h0: e.matmul(pO[:, j * 128:(j + 1) * 128], gb["qdT"][:, j, :], Sbf[:, h0 + j, :], start=True, stop=False),
                             R=[gb["qdT"].b, Sbf.bs[g]], W=[pO.b])
                        S.op("pe", lambda e, j=j, pO=pO, gb=gb: e.matmul(pO[:, j * 128:(j + 1) * 128], gb["aT"][:, j, :], gb["vn"][:, j, :], start=False, stop=True),
                             R=[gb["aT"].b, gb["vn"].b], W=[pO.b])
                pS = pb()
                PS[g] = pS
                for j in range(4):
                    S.op("pe", lambda e, j=j, pS=pS, gb=gb, h0=h0: e.matmul(pS[:, j * 128:(j + 1) * 128], kd[:, h0 + j, :], gb["vn"][:, j, :], start=True, stop=True),
                         R=[kd.b, gb["vn"].b], W=[pS.b])
            for g in grp:
                h0 = H0[g]
                S.op("pool", lambda e, h0=h0: e.tensor_tensor(S32[:, h0:h0 + 4, :], S32[:, h0:h0 + 4, :], col4(9, h0), ALU.mult),
                     R=[sm.bs[9], S32.bs[g]], W=[S32.bs[g]])
            for g in grp:
                h0, pS = H0[g], PS[g]
                S.op("dve", lambda e, h0=h0, pS=pS: e.tensor_tensor(S32[:, h0:h0 + 4, :], S32[:, h0:h0 + 4, :], pS[:].rearrange("p (j s) -> p j s", j=4), ALU.add),
                     R=[pS.b, S32.bs[g]], W=[S32.bs[g]])
            for g in grp:
                h0 = H0[g]
                S.op("act", lambda e, h0=h0: e.copy(Sbf[:, h0:h0 + 4, :], S32[:, h0:h0 + 4, :]), R=[S32.bs[g]], W=[Sbf.bs[g]])
            if full:
                for g in grp:
                    h0, pO = H0[g], PO[g]
                    oss, og, on, zs = B["oss"], B["og"][g % 2], B["on"], B["zs"]
                    for j in range(4):
                        S.op("act", lambda e, j=j, pO=pO, h0=h0: e.activation(B["junk"][:, 0:128], pO[:, j * 128:(j + 1) * 128], AF.Square,
                                                                             scale=float(128.0 ** -0.5), accum_out=oss[:, h0 + j:h0 + j + 1]),
                             R=[pO.b], W=[B["junk"].b, oss.b])
                    S.op("act", lambda e, h0=h0: e.activation(oss[:, 16 + h0:20 + h0], oss[:, h0:h0 + 4], AF.Ln, bias=EPS), R=[oss.b, cst.b], W=[oss.b])
                    S.op("act", lambda e, h0=h0: e.activation(oss[:, 16 + h0:20 + h0], oss[:, 16 + h0:20 + h0], AF.Exp, scale=-0.5), R=[oss.b], W=[oss.b])
                    S.op("dve", lambda e, pO=pO, og=og, h0=h0: e.tensor_tensor(og[:], pO[:].rearrange("p (j s) -> p j s", j=4),
                                                                               oss[:, 16 + h0:20 + h0].unsqueeze(2).to_broadcast([128, 4, 128]), ALU.mult),
                         R=[pO.b, oss.b], W=[og.b])
                    S.op("pool", lambda e, og=og, h0=h0: e.tensor_tensor(on[:, h0:h0 + 4, :], og[:], zs[:, h0 * 128:(h0 + 4) * 128].rearrange("p (h e) -> p h e", h=4), ALU.mult),
                         R=[og.b, zs.bs[h0 // 8]], W=[on.b])
            if full and gp == 0:
                emit_z(range(4, 8))
        if not full:
            return
        on, onT = B["on"], B["onT"]
        for half in range(2):
            p = pb()
            pv = p[:].bitcast(BF16)
            for j in range(8):
                S.op("pe", lambda e, j=j, pv=pv, half=half: e.transpose(pv[:, j * 128:(j + 1) * 128], on[:, half * 8 + j, :], identb[:]),
                     R=[on.b, identb.b], W=[p.b])
            if half == 0:
                S.op("act", lambda e, pv=pv, half=half: e.copy(onT[:, half * 8:(half + 1) * 8, :], pv.rearrange("p (k d) -> p k d", k=8)),
                     R=[p.b], W=[onT.b])
            else:
                S.op("dve", lambda e, pv=pv, half=half: e.tensor_copy(onT[:, half * 8:(half + 1) * 8, :], pv.rearrange("p (k d) -> p k d", k=8)),
                     R=[p.b], W=[onT.b])
        ft = t - nhist
        py = [pb(), pb()]
        HPB = WB // 256
        for hb in range(16 // HPB):
            wb = B["wblk"][hb % len(B["wblk"])]
            wv = wb[:, 0:HPB * 1024].rearrange("p (h n) -> p h n", h=HPB)
            S.dma("sp", wv, Wout_s[:, hb * HPB:(hb + 1) * HPB, :], R=[scrB[id(Wout_s)]], W=[wb.b])
            for hl in range(HPB):
                h_ = hb * HPB + hl
                for half in range(2):
                    S.op("pe", lambda e, hl=hl, h_=h_, half=half, wv=wv: e.matmul(
                        py[half][:], onT[:, h_, :], wv[:, hl, half * 512:(half + 1) * 512],
                        start=(h_ == 0), stop=(h_ == 15)), R=[wb.b, onT.b], W=[py[half].b])
        for half in range(2):
            S.op("dve", lambda e, half=half: e.tensor_tensor(
                hbuf[:, ft, half * 512:(half + 1) * 512], hbuf[:, ft, half * 512:(half + 1) * 512],
                py[half][:], ALU.add), R=[py[half].b, hbuf.bs[ft]], W=[hbuf.bs[ft]])

    def gdn_bufs(ph, N, full):
        B = {}
        B["WB"] = 256
        B["pref"] = {}
        B["xnT"] = [sb(ph, f"xnT{i}", [128, 8, N], BF16) for i in range(2)]
        B["qkvT"] = sb(ph, "qkvT", [128, 32, N], BF16, nslots=32)
        B["xs"] = [sb(ph, f"xs{i}", [128, D]) for i in range(2)] if not full else None
        B["xn"] = [sb(ph, f"xn{i}", [128, D], BF16) for i in range(2)]
        B["junk"] = sb(ph, "junk", [128, D], BF16)
        B["ms"] = sb(ph, "ms", [128, 4])
        B["wblk"] = [sb(ph, f"wblk{i}", [128, 8 * B["WB"]], BF16) for i in range(2 if full else 4)]
        B["u"] = [sb(ph, f"u{i}", [128, N + 3]) for i in range(2)]
        B["acc"] = [sb(ph, f"acc{i}", [128, N]) for i in range(2)]
        B["sq"] = [sb(ph, f"sq{i}", [128, N], BF16) for i in range(2)]
        B["sm"] = [sb(ph, f"sm{i}", [128, 16, (N // 128) * 16], F32, nslots=16) for i in range(2)]
        B["smb"] = [sb(ph, f"smb{i}", [128, 4, (N // 128) * 16], BF16, nslots=4) for i in range(2)]
        B["ktm"] = sb(ph, "ktm", [128, 8, 128], BF16)
        B["kbg"] = sb(ph, "kbg", [128, 16, 128], BF16)
        B["kd"] = sb(ph, "kd", [128, 16, 128], BF16)
        B["vb"] = sb(ph, "vb", [128, 16, 128], BF16)
        B["Asb"] = sb(ph, "Asb", [128, 8, 128], BF16)
        if full:
            B["KQsb"] = sb(ph, "KQsb", [128, 8, 128], BF16)
        G = []
        for g in range(2 if full else 4):
            gb = {}
            gb["dgh"] = sb(ph, f"dgh{g}", [128, 4, 128], BF16)
            gb["dgl"] = sb(ph, f"dgl{g}", [128, 4, 128], BF16)
            gb["Z"] = sb(ph, f"Z{g}", [128, 4, 128])
            gb["Mk"] = sb(ph, f"Mk{g}", [128, 4, 128], BF16)
            gb["Nk"] = sb(ph, f"Nk{g}", [128, 4, 128], BF16)
            gb["V"] = sb(ph, f"V{g}", [128, 4, 128], BF16)
            gb["Mh"] = sb(ph, f"Mh{g}", [128, 4, 128], BF16)
            gb["u"] = sb(ph, f"ug{g}", [128, 4, 128])
            gb["wT"] = sb(ph, f"wT{g}", [128, 4, 128], BF16)
            gb["vn"] = sb(ph, f"vn{g}", [128, 4, 128], BF16)
            if full:
                gb["ER"] = sb(ph, f"ER{g}", [128, 4, 128], BF16)
                gb["DT"] = sb(ph, f"DT{g}", [128, 4, 128])
                gb["aT"] = sb(ph, f"aT{g}", [128, 4, 128], BF16)
                gb["qdT"] = sb(ph, f"qdT{g}", [128, 4, 128], BF16)
            G.append(gb)
        B["G"] = G
        if full:
            B["zs"] = sb(ph, "zs", [128, 2048], BF16, nslots=2)
            B["og"] = [sb(ph, f"og{i}", [128, 4, 128]) for i in range(2)]
            B["oss"] = sb(ph, "oss", [128, 32])
            B["on"] = sb(ph, "on", [128, 16, 128], BF16)
            B["onT"] = sb(ph, "onT", [128, 16, 128], BF16)
        return B

    if nhist > 0:
        with contextlib.ExitStack() as ph:
            B = gdn_bufs(ph, 512, False)
            sts = [list(range(t, min(t + 4, nhist))) for t in range(0, nhist, 4)]
            gdn_st_norm(B, sts[0], False, None, 0)
            for k, tl in enumerate(sts):
                if k + 1 < len(sts):
                    gdn_st_norm(B, sts[k + 1], False, None, k + 1)
                gdn_st_rest(B, tl, False, None, k, k + 1 < len(sts))
                run_bg(8)
            S.barrier()
    run_bg(len(bgq))
    S.barrier()
    ph0.close()

    hbuf = sb(root, "h", [128, NFULL, D], F32, nslots=NFULL)

    with contextlib.ExitStack() as ph:
        B = gdn_bufs(ph, 256, True)
        sts = [[nhist + 2 * st, nhist + 2 * st + 1] for st in range(NFULL // 2)]
        gdn_st_norm(B, sts[0], True, hbuf, 0)
        for k, tl in enumerate(sts):
            if k + 1 < len(sts):
                gdn_st_norm(B, sts[k + 1], True, hbuf, k + 1)
            gdn_st_rest(B, tl, True, hbuf, k)
        S.barrier()

    def ffn_phase(l):
        with contextlib.ExitStack() as ph:
            xnT2 = [sb(ph, f"f_xnT{i}", [128, 8, 512], BF16) for i in range(2)]
            xn = [sb(ph, f"f_xn{i}", [128, D], BF16) for i in range(2)]
            junk = sb(ph, "f_junk", [128, D], BF16)
            ms = sb(ph, "f_ms", [128, 4])
            wblk = [sb(ph, f"f_w{i}", [128, 4096], BF16) for i in range(3)]
            u = [sb(ph, f"f_u{i}", [128, 514]) for i in range(4)]
            acc = [sb(ph, f"f_acc{i}", [128, 512]) for i in range(4)]
            gs = [sb(ph, f"f_gs{i}", [128, 512]) for i in range(2)]
            act = sb(ph, "f_act", [128, 22, 512], BF16, nslots=22)
            fcarry = sb(ph, "f_carry", [128, 44, 2])
            cw = sb(ph, "f_cw", [128, 44, 3])
            cb = sb(ph, "f_cb", [128, 44])
            S.op("pool", lambda e: e.memset(fcarry[:], 0.0), W=[fcarry.b])
            S.dma("sp", cw[:], f_cw_d[l][:, :, :], W=[cw.b])
            S.dma("sp", cb[:], f_cb_d[l][:, :], W=[cb.b])
            sts = [list(range(s, min(s + 4, NFULL))) for s in range(0, NFULL, 4)]

            def ffn_norm(tiles, k):
                for i, ft in enumerate(tiles):
                    rmsnorm_T((junk, ms, xn[i % 2]), hbuf[:, ft, :], [hbuf.bs[ft]], xnT2[k % 2], i * 128)

            def ffn_supertile(tiles, k):
                NT = len(tiles)
                N = NT * 128
                xnT = xnT2[k % 2]
                for b in range(11):
                    wb = wblk[b % 3]
                    wv = wb[:].rearrange("p (c n) -> p c n", c=8)
                    S.dma("sp", wv, Wup_s[l][b], R=[scrB[id(Wup_s[l])]], W=[wb.b])
                    for jj in range(2):
                        res = []
                        for which in range(2):
                            f = which * 22 + b * 2 + jj
                            col = which * 256 + jj * 128
                            p = pb()
                            for c in range(8):
                                S.op("pe", lambda e, c=c, p=p, wv=wv, col=col: e.matmul(p[:, 0:N], wv[:, c, col:col + 128], xnT[:, c, 0:N],
                                                                                      start=(c == 0), stop=(c == 7)),
                                     R=[wb.b, xnT.b], W=[p.b])
                            k = (jj * 2 + which)
                            uu, aa = u[k], acc[k]
                            S.op("pool", lambda e, uu=uu, f=f: e.tensor_copy(uu[:, 0:2], fcarry[:, f, :]), R=[fcarry.b], W=[uu.b])
                            S.op("act", lambda e, uu=uu, p=p: e.copy(uu[:, 2:2 + N], p[:, 0:N]), R=[p.b], W=[uu.b])
                            S.op("pool", lambda e, uu=uu, f=f: e.tensor_copy(fcarry[:, f, :], uu[:, N:N + 2]), R=[uu.b], W=[fcarry.b])
                            S.op("dve", lambda e, uu=uu, aa=aa, f=f: e.tensor_scalar(aa[:, 0:N], uu[:, 2:2 + N], cw[:, f, 2:3], cb[:, f:f + 1], ALU.mult, ALU.add),
                                 R=[uu.b, cw.b, cb.b], W=[aa.b])
                            for j in (1, 0):
                                S.op("dve", lambda e, uu=uu, aa=aa, f=f, j=j: e.scalar_tensor_tensor(
                                    aa[:, 0:N], uu[:, j:j + N], cw[:, f, j:j + 1], aa[:, 0:N], ALU.mult, ALU.add),
                                    R=[uu.b, cw.b, aa.b], W=[aa.b])
                            res.append(aa)
                        g_ = gs[jj]
                        S.op("act", lambda e, g_=g_, a0=res[0]: e.activation(g_[:, 0:N], a0[:, 0:N], AF.Silu), R=[res[0].b], W=[g_.b])
                        S.op("pool", lambda e, g_=g_, a1=res[1], b=b, jj=jj: e.tensor_tensor(act[:, b * 2 + jj, 0:N], g_[:, 0:N], a1[:, 0:N], ALU.mult),
                             R=[g_.b, res[1].b], W=[act.bs[b * 2 + jj]])
                pys = [[pb(), pb()] for _ in range(NT)]
                for jb in range(6):
                    nj = 4 if jb < 5 else 2
                    wb = wblk[jb % 3]
                    wv = wb[:].rearrange("p (j n) -> p j n", j=4)
                    S.dma("sp", wv[:, 0:nj, :], Wdn_s[l][:, jb * 4:jb * 4 + nj, :], R=[scrB[id(Wdn_s[l])]], W=[wb.b])
                    for jl in range(nj):
                        j = jb * 4 + jl
                        for i in range(NT):
                            for half in range(2):
                                S.op("pe", lambda e, i=i, j=j, jl=jl, half=half, wv=wv: e.matmul(
                                    pys[i][half][:], act[:, j, i * 128:(i + 1) * 128], wv[:, jl, half * 512:(half + 1) * 512],
                                    start=(j == 0), stop=(j == 21)), R=[wb.b, act.bs[j]], W=[pys[i][half].b])
                for i, ft in enumerate(tiles):
                    for half in range(2):
                        S.op("dve", lambda e, i=i, ft=ft, half=half: e.scalar_tensor_tensor(
                            hbuf[:, ft, half * 512:(half + 1) * 512], pys[i][half][:], validt[:, ft:ft + 1],
                            hbuf[:, ft, half * 512:(half + 1) * 512], ALU.mult, ALU.add),
                            R=[pys[i][half].b, hbuf.bs[ft], validt.b], W=[hbuf.bs[ft]])

            ffn_norm(sts[0], 0)
            for k, tiles in enumerate(sts):
                if k + 1 < len(sts):
                    ffn_norm(sts[k + 1], k + 1)
                ffn_supertile(tiles, k)
            S.barrier()

    if stage >= 2:
        ffn_phase(0)

    def attn_phase():
        with contextlib.ExitStack() as ph:
            NK = NFULL + 1
            xnT = sb(ph, "a_xnT", [128, 8, 512], BF16)
            xn = [sb(ph, f"a_xn{i}", [128, D], BF16) for i in range(2)]
            ms = sb(ph, "a_ms", [128, 4])
            wblk = [sb(ph, f"a_w{i}", [128, 8, 512], BF16) for i in range(3)]
            wk = [0]

            def wnext(src3, n):
                wb = wblk[wk[0] % 3]
                wk[0] += 1
                S.dma("sp", wb[:, :, 0:n], src3, R=[scrB[id(Wkv_s)], scrB[id(Wq_s)], scrB[id(Wo_s)]], W=[wb.b])
                return wb

            KT = sb(ph, "a_KT", [128, 4, NK * 128], BF16)
            Vt = sb(ph, "a_V", [128, NK, 256], BF16)
            QT = sb(ph, "a_QT", [128, 8, 512], BF16)
            BMf = sb(ph, "a_BMf", [128, 4, 256])
            BM = sb(ph, "a_BM", [128, 16, 256], BF16)
            am = sb(ph, "a_am", [128, 256])
            kbf = sb(ph, "a_kbf", [1, 512])
            kbb = sb(ph, "a_kbb", [1, NK * 128], BF16)
            sinkb = sb(ph, "a_sink", [128, 16])
            sc = [sb(ph, f"a_sc{i}", [128, 2, 256]) for i in range(3)]
            pr = [sb(ph, f"a_pr{i}", [128, 2, 256], BF16) for i in range(3)]
            pT = [sb(ph, f"a_pT{i}", [128, 4, 128], BF16) for i in range(3)]
            st_ = sb(ph, "a_st", [128, 6, 16], F32, nslots=8)
            obf = sb(ph, "a_obf", [128, D], BF16)
            junk = obf
            oT = sb(ph, "a_oT", [128, 8, 128], BF16)
            S.dma("sp", am[:], amask_d[:, :], W=[am.b])
            S.dma("sp", sinkb[:], sink_d[:, :], W=[sinkb.b])
            for q4 in range(4):
                S.dma("sp", BMf[:], band_d[:, q4 * 4:(q4 + 1) * 4, :], W=[BMf.b])
                S.op("dve", lambda e, q4=q4: e.tensor_tensor(BM[:, q4 * 4:(q4 + 1) * 4, :], BMf[:], am[:].unsqueeze(1).to_broadcast([128, 4, 256]), ALU.add),
                     R=[BMf.b, am.b], W=[BM.b])
            for k0_ in range(0, NK * 128, 512):
                kn = min(512, NK * 128 - k0_)
                S.dma("sp", kbf[:, 0:kn], kbias_d[:, k0_:k0_ + kn], W=[kbf.b])
                S.op("dve", lambda e, k0_=k0_, kn=kn: e.tensor_copy(kbb[:, k0_:k0_ + kn], kbf[:, 0:kn]), R=[kbf.b], W=[kbb.b])
            S.op("pool", lambda e: e.memset(KT[:, :, 0:128], 0.0), W=[KT.b])
            S.op("pool", lambda e: e.memset(Vt[:, 0, :], 0.0), W=[Vt.b])
            sts = [list(range(s, min(s + 4, NFULL))) for s in range(0, NFULL, 4)]

            def attn_supertile(tiles):
                NT = len(tiles)
                N = NT * 128
                for i, ft in enumerate(tiles):
                    rmsnorm_T((junk, ms, xn[i % 2]), hbuf[:, ft, :], [hbuf.bs[ft]], xnT, i * 128)
                kc0 = (tiles[0] + 1) * 128
                wkK = wnext(Wkv_s[0], 512)
                for j in range(4):
                    p = pb()
                    for c in range(8):
                        S.op("pe", lambda e, c=c, p=p, j=j, wkK=wkK: e.matmul(p[:, 0:N], wkK[:, c, j * 128:(j + 1) * 128], xnT[:, c, 0:N],
                                                                     start=(c == 0), stop=(c == 7)), R=[wkK.b, xnT.b], W=[p.b])
                    S.op("act", lambda e, p=p, j=j: e.copy(KT[:, j, kc0:kc0 + N], p[:, 0:N]), R=[p.b], W=[KT.b])
                wkV = wnext(Wkv_s[1, :, :, 0:256], 256)
                for i, ft in enumerate(tiles):
                    p = pb()
                    for c in range(8):
                        S.op("pe", lambda e, c=c, p=p, i=i, wkV=wkV: e.matmul(p[:, 0:256], xnT[:, c, i * 128:(i + 1) * 128], wkV[:, c, 0:256],
                                                                     start=(c == 0), stop=(c == 7)), R=[wkV.b, xnT.b], W=[p.b])
                    S.op("dve", lambda e, p=p, ft=ft: e.tensor_copy(Vt[:, ft + 1, :], p[:, 0:256]), R=[p.b], W=[Vt.b])
                for f in range(8):
                    if f % 4 == 0:
                        wqb = wnext(Wq_s[f // 4], 512)
                    p = pb()
                    for c in range(8):
                        S.op("pe", lambda e, c=c, p=p, f=f, wqb=wqb: e.matmul(p[:, 0:N], wqb[:, c, (f % 4) * 128:(f % 4 + 1) * 128], xnT[:, c, 0:N],
                                                                     start=(c == 0), stop=(c == 7)), R=[wqb.b, xnT.b], W=[p.b])
                    S.op("act", lambda e, p=p, f=f: e.activation(QT[:, f, 0:N], p[:, 0:N], AF.Copy, scale=0.125), R=[p.b], W=[QT.b])
                for i, ft in enumerate(tiles):
                    attn_tile(i, ft)

            def attn_tile(i, ft):
                if True:
                    k0 = ft * 128
                    pO = [PB[6], PB[7]]
                    reserved.update((6, 7))
                    PS, PTT = {}, {}

                    def stage_a(f):
                        kv = f // 2
                        ps = pb()
                        for hh in range(2):
                            lo = hh * 64
                            S.op("pe", lambda e, ps=ps, hh=hh, lo=lo, f=f, kv=kv: e.matmul(
                                ps[:, hh * 256:(hh + 1) * 256], QT[lo:lo + 64, f, i * 128:(i + 1) * 128], KT[lo:lo + 64, kv, k0:k0 + 256],
                                start=True, stop=False), R=[QT.b, KT.b], W=[ps.b])
                            S.op("pe", lambda e, ps=ps, hh=hh: e.matmul(
                                ps[:, hh * 256:(hh + 1) * 256], onesb[0:1, 0:128], kbb[0:1, k0:k0 + 256], start=False, stop=True),
                                R=[onesb.b, kbb.b], W=[ps.b])
                        s_, p_ = sc[f % 3], pr[f % 3]
                        sb_ = st_.bs[f]
                        S.op("dve", lambda e, ps=ps, s_=s_, f=f: e.tensor_tensor(s_[:], ps[:].rearrange("p (h k) -> p h k", h=2), BM[:, 2 * f:2 * f + 2, :], ALU.add),
                             R=[ps.b, BM.b], W=[s_.b])
                        S.op("dve", lambda e, s_=s_, f=f: e.tensor_reduce(st_[:, 0, 2 * f:2 * f + 2], s_[:], AX.X, ALU.max), R=[s_.b], W=[sb_])
                        S.op("dve", lambda e, f=f: e.tensor_tensor(st_[:, 0, 2 * f:2 * f + 2], st_[:, 0, 2 * f:2 * f + 2], sinkb[:, 2 * f:2 * f + 2], ALU.max),
                             R=[sb_, sinkb.b], W=[sb_])
                        S.op("dve", lambda e, f=f: e.tensor_scalar(st_[:, 1, 2 * f:2 * f + 2], st_[:, 0, 2 * f:2 * f + 2], -1.0, None, ALU.mult),
                             R=[sb_], W=[sb_])
                        for hh in range(2):
                            h_ = 2 * f + hh
                            S.op("act", lambda e, s_=s_, p_=p_, hh=hh, h_=h_: e.activation(p_[:, hh, :], s_[:, hh, :], AF.Exp, bias=st_[:, 1, h_:h_ + 1],
                                                                                           accum_out=st_[:, 2, h_:h_ + 1]),
                                 R=[s_.b, sb_], W=[p_.b, sb_])

                    def stage_b(f):
                        kv = f // 2
                        p_, t_ = pr[f % 3], pT[f % 3]
                        pt = pb()
                        ptv = pt[:].bitcast(BF16)
                        for hh in range(2):
                            for kb in range(2):
                                S.op("pe", lambda e, ptv=ptv, p_=p_, hh=hh, kb=kb: e.transpose(
                                    ptv[:, (hh * 2 + kb) * 128:(hh * 2 + kb + 1) * 128], p_[:, hh, kb * 128:(kb + 1) * 128], identb[:]),
                                    R=[p_.b, identb.b], W=[pt.b])
                        if f % 2 == 0:
                            S.op("act", lambda e, ptv=ptv, t_=t_: e.copy(t_[:].rearrange("p a b -> p (a b)"), ptv[:, 0:512]), R=[pt.b], W=[t_.b])
                        else:
                            S.op("dve", lambda e, ptv=ptv, t_=t_: e.tensor_copy(t_[:].rearrange("p a b -> p (a b)"), ptv[:, 0:512]), R=[pt.b], W=[t_.b])
                        for hh in range(2):
                            h_ = 2 * f + hh
                            for kb in range(2):
                                S.op("pe", lambda e, t_=t_, hh=hh, kb=kb, h_=h_, kv=kv: e.matmul(
                                    pO[h_ // 8][:, (h_ % 8) * 64:(h_ % 8 + 1) * 64], t_[:, hh * 2 + kb, :], Vt[:, ft + kb, kv * 64:(kv + 1) * 64],
                                    start=(kb == 0), stop=(kb == 1)), R=[t_.b, Vt.b], W=[pO[h_ // 8].b])

                    for f in range(9):
                        if f < 8:
                            stage_a(f)
                        if f >= 1:
                            stage_b(f - 1)
                    reserved.clear()
                    S.op("dve", lambda e: e.tensor_tensor(st_[:, 3, :], sinkb[:], st_[:, 1, :], ALU.add), R=[st_.b, sinkb.b] + st_.bs, W=[st_.b])
                    S.op("act", lambda e: e.activation(st_[:, 3, :], st_[:, 3, :], AF.Exp), R=[st_.b], W=[st_.b])
                    S.op("dve", lambda e: e.tensor_tensor(st_[:, 3, :], st_[:, 3, :], st_[:, 2, :], ALU.add), R=[st_.b], W=[st_.b])
                    S.op("dve", lambda e: e.reciprocal(st_[:, 4, :], st_[:, 3, :]), R=[st_.b], W=[st_.b])
                    for half in range(2):
                        S.op("dve", lambda e, half=half: e.tensor_tensor(
                            obf[:, half * 512:(half + 1) * 512].rearrange("p (h d) -> p h d", h=8), pO[half][:].rearrange("p (h d) -> p h d", h=8),
                            st_[:, 4, half * 8:(half + 1) * 8].unsqueeze(2).to_broadcast([128, 8, 64]), ALU.mult),
                            R=[pO[half].b, st_.b], W=[obf.b])
                    p = pb()
                    pv = p[:].bitcast(BF16)
                    for c in range(8):
                        S.op("pe", lambda e, c=c, pv=pv, p=p: e.transpose(pv[:, c * 128:(c + 1) * 128], obf[:, c * 128:(c + 1) * 128], identb[:]),
                             R=[obf.b, identb.b], W=[p.b])
                    S.op("act", lambda e, pv=pv: e.copy(oT[:].rearrange("p c t -> p (c t)"), pv), R=[p.b], W=[oT.b])
                    py = [pb(), pb()]
                    for half in range(2):
                        wob = wnext(Wo_s[half], 512)
                        for c in range(8):
                            S.op("pe", lambda e, c=c, half=half, wob=wob: e.matmul(py[half][:], oT[:, c, :], wob[:, c, :],
                                                                           start=(c == 0), stop=(c == 7)), R=[oT.b, wob.b], W=[py[half].b])
                        S.op("dve", lambda e, half=half, ft=ft: e.scalar_tensor_tensor(
                            hbuf[:, ft, half * 512:(half + 1) * 512], py[half][:], validt[:, ft:ft + 1],
                            hbuf[:, ft, half * 512:(half + 1) * 512], ALU.mult, ALU.add),
                            R=[py[half].b, hbuf.bs[ft], validt.b], W=[hbuf.bs[ft]])

            for tiles in sts:
                attn_supertile(tiles)
            S.barrier()

    if stage >= 3:
        attn_phase()
    if stage >= 4:
        ffn_phase(1)

    toks = []
    with contextlib.ExitStack() as ph:
        fnw = sb(ph, "fnw", [128, D])
        junk = sb(ph, "o_junk", [128, D], BF16)
        ms = sb(ph, "o_ms", [128, 4])
        ob = [sb(ph, f"o_b{i}", [128, D]) for i in range(2)]
        S.dma("sp", fnw[:], fin_w_d[:, :], W=[fnw.b])
        for t in range(NOWN):
            ft = t + NHALO
            o_ = ob[t % 2]
            if stage >= 5:
                S.op("act", lambda e, ft=ft: e.activation(junk[:], hbuf[:, ft, :], AF.Square, scale=1.0 / 32.0, accum_out=ms[:, 0:1]),
                     R=[hbuf.bs[ft]], W=[junk.b, ms.b])
                S.op("act", lambda e: e.activation(ms[:, 1:2], ms[:, 0:1], AF.Ln, bias=EPS), R=[ms.b, cst.b], W=[ms.b])
                S.op("act", lambda e: e.activation(ms[:, 2:3], ms[:, 1:2], AF.Exp, scale=-0.5), R=[ms.b], W=[ms.b])
                S.op("dve", lambda e, ft=ft, o_=o_: e.scalar_tensor_tensor(o_[:], hbuf[:, ft, :], ms[:, 2:3], fnw[:], ALU.mult, ALU.mult),
                     R=[hbuf.bs[ft], ms.b, fnw.b], W=[o_.b])
            else:
                S.op("dve", lambda e, ft=ft, o_=o_: e.tensor_copy(o_[:], hbuf[:, ft, :]), R=[hbuf.bs[ft]], W=[o_.b])
            toks.append(S.dma("sp", out_d[t * 128:(t + 1) * 128, :], o_[:], R=[o_.b]))
        S.finish(toks)
    root.close()
    return nc


def _t5_bucket(dist):
    n = np.maximum(dist, 0)
    nf = np.maximum(n, 1).astype(np.float32)
    large = 16 + (np.log(nf / 16) / np.log(128 / 16) * 16).astype(np.int32)
    large = np.minimum(large, 31)
    return np.where(n < 16, n, large)


def make_inputs(inp, nhist, cores):
    f32 = np.float32
    A = lambda v: np.ascontiguousarray(np.asarray(v), dtype=f32)
    x = A(inp["x"])

    def pc(v):
        return np.ascontiguousarray(A(v).reshape(8, 128).T)

    common = {
        "a_w_in": A(inp["a_w_in"][0]),
        "a_w_out": A(inp["a_w_out"][0]),
        "ffn_w_up0": A(inp["ffn_w_up"][0]), "ffn_w_up1": A(inp["ffn_w_up"][1]),
        "ffn_w_down0": A(inp["ffn_w_down"][0]), "ffn_w_down1": A(inp["ffn_w_down"][1]),
        "b_w_q": A(inp["b_w_q"][0]), "b_w_o": A(inp["b_w_o"][0]),
        "a_norm_wT": pc(inp["a_norm_w"][0]),
        "ffn_norm_wT0": pc(inp["ffn_norm_w"][0]), "ffn_norm_wT1": pc(inp["ffn_norm_w"][1]),
        "kv_norm_wT": pc(inp["kv_norm_w"]), "b_norm_wT": pc(inp["b_norm_w"][0]),
        "out_norm_wT": A(inp["a_out_norm_w"][0]).reshape(128, 1),
        "final_norm_wb": np.ascontiguousarray(np.broadcast_to(A(inp["final_norm_w"])[None, :], (128, D))),
        "a_conv_wT": np.ascontiguousarray(A(inp["a_conv_w"][0]).T.reshape(32, 128, 4).transpose(1, 0, 2)),
        "a_log_b": np.ascontiguousarray(np.broadcast_to(A(inp["a_a_log"][0])[None, :], (128, 16))),
        "dt_bias_b": np.ascontiguousarray(np.broadcast_to(A(inp["a_dt_bias"][0])[None, :], (128, 16))),
        "sinks_b": np.ascontiguousarray(np.broadcast_to(A(inp["b_sinks"][0])[None, :], (128, 16))),
    }
    for l in range(2):
        common[f"ffn_conv_wT{l}"] = np.ascontiguousarray(A(inp["ffn_conv_w"][l]).T.reshape(44, 128, 3).transpose(1, 0, 2))
        common[f"ffn_conv_bT{l}"] = np.ascontiguousarray(A(inp["ffn_conv_b"][l]).reshape(44, 128).T)
    wkv = A(inp["w_kv"])
    cols = []
    for j in range(4):
        cols += [wkv[:, j * 64:(j + 1) * 64], wkv[:, j * 64:(j + 1) * 64]]
    cols.append(wkv[:, 256:512])
    common["w_kv_dup"] = np.ascontiguousarray(np.concatenate(cols, axis=1))
    qi = np.arange(128)[:, None]
    ki = np.arange(256)[None, :]
    dist = qi + 128 - ki
    bucket = _t5_bucket(dist)
    tab = A(inp["rel_bias_table"])
    common["biasband"] = np.ascontiguousarray(tab[bucket].transpose(0, 2, 1))
    inwin = (dist >= 0) & (dist < 128)
    common["attnmask"] = np.where(inwin, 0.0, NEG).astype(f32)
    common["ident"] = np.eye(128, dtype=f32)
    p_ = np.arange(128)[:, None]
    j_ = np.arange(128)[None, :]
    common["triu"] = (p_ <= j_).astype(f32)
    common["maskL"] = np.where(p_ > j_, 0.0, NEG).astype(f32)
    common["maskU"] = np.where(j_ >= p_, 0.0, NEG).astype(f32)
    NT_ALL = nhist + NFULL
    maps = []
    for c in cores:
        b, j = c // 4, c % 4
        end = 2048 * (j + 1)
        start = end - NT_ALL * 128
        xe = np.zeros((NT_ALL * 128, D), f32)
        s0 = max(start, 0)
        xe[s0 - start:] = x[b, s0:end]
        pos_full = np.arange(end - NFULL * 128, end)
        valid = (pos_full >= 0).astype(f32).reshape(NFULL, 128).T
        kb = np.concatenate([np.full(128, NEG, f32), np.where(pos_full >= 0, 0.0, NEG).astype(f32)])[None, :]
        m = dict(common)
        m["x_ext"] = xe
        m["valid"] = np.ascontiguousarray(valid)
        m["kbias"] = np.ascontiguousarray(kb)
        maps.append(m)
    return maps


_NHIST = 46


def kernel(**inputs):
    nc = build_program(_NHIST)
    maps = make_inputs(inputs, _NHIST, list(range(8)))
    res = run_bass_kernel_spmd(nc, maps, core_ids=list(range(8)))
    out = np.empty((2, 8192, D), np.float32)
    for c in range(8):
        b, j = c // 4, c % 4
        out[b, 2048 * j:2048 * (j + 1)] = res.results[c]["out"]
    return out
```
